# Optimizing a Trainium2 kernel written in Bass

```python
import math
import jax, jax.numpy as jnp
from jax import lax
import numpy as np

D_MODEL = 4096
BATCH = 4
SEQ = 4096
DEPTH = 1

D_MIX = D_MODEL
D_GLA = D_MIX // 2
D_S5 = D_MIX - D_GLA
GLA_HEADS = 4
GLA_DK_TOTAL = D_GLA // 2
GLA_DK = GLA_DK_TOTAL // GLA_HEADS
GLA_DV = D_GLA // GLA_HEADS
GLA_GATE_RANK = 16
GLA_GATE_TAU = 16.0
GLA_CHUNK = 64
S5_GROUP = 16
S5_GROUPS = D_S5 // S5_GROUP
S5_STATE = 64
S5_DT_MIN = 1e-3
S5_DT_MAX = 1e-1
DEEPNORM_ALPHA = (2.0 * DEPTH) ** 0.25
DEEPNORM_BETA = (8.0 * DEPTH) ** -0.25
NORM_EPS = 1e-5
SPLIT_IDX = (
    GLA_DK_TOTAL,
    2 * GLA_DK_TOTAL,
    2 * GLA_DK_TOTAL + D_GLA,
    2 * GLA_DK_TOTAL + D_GLA + GLA_GATE_RANK,
    2 * GLA_DK_TOTAL + 2 * D_GLA + GLA_GATE_RANK,
    2 * GLA_DK_TOTAL + 2 * D_GLA + GLA_GATE_RANK + D_S5,
)
D_IN = 2 * GLA_DK_TOTAL + 2 * D_GLA + GLA_GATE_RANK + 2 * D_S5

kernel_name = 'hymba_gla_s5_deepnorm_adaln'


def layer_norm(r, g, b):
    r32 = r.astype(jnp.float32)
    mu = jnp.mean(r32, axis=-1, keepdims=True)
    var = jnp.mean(jnp.square(r32 - mu), axis=-1, keepdims=True)
    return ((r32 - mu) * lax.rsqrt(var + NORM_EPS) * g.astype(jnp.float32) + b.astype(jnp.float32)).astype(r.dtype)


def gla_chunked(q, k, v, log_a):
    bsz, seq, heads, dk = q.shape
    dv = v.shape[-1]
    n_chunks = seq // GLA_CHUNK

    def to_chunks(t):
        return t.reshape(bsz, n_chunks, GLA_CHUNK, heads, t.shape[-1]).transpose(1, 0, 3, 2, 4)

    qc, kc, vc, gc = to_chunks(q), to_chunks(k), to_chunks(v), to_chunks(log_a)
    causal = jnp.tril(jnp.ones((GLA_CHUNK, GLA_CHUNK), dtype=bool))

    def step(state, inp):
        qn, kn, vn, gn = inp
        b = jnp.cumsum(gn, axis=2)
        b_last = b[:, :, -1:, :]
        q_dec = qn * jnp.exp(b)
        k_inv = kn * jnp.exp(-b)
        att = jnp.where(causal, jnp.einsum('bhid,bhjd->bhij', q_dec, k_inv), 0.0)
        out = jnp.einsum('bhij,bhje->bhie', att, vn) + jnp.einsum('bhid,bhde->bhie', q_dec, state)
        k_end = kn * jnp.exp(b_last - b)
        state = state * jnp.exp(b_last[:, :, 0, :])[..., None] + jnp.einsum('bhcd,bhce->bhde', k_end, vn)
        return state, out

    s0 = jnp.zeros((bsz, heads, dk, dv), jnp.float32)
    _, o = lax.scan(step, s0, (qc, kc, vc, gc))
    return o.transpose(1, 0, 3, 2, 4).reshape(bsz, seq, heads, dv)


def s5_ssm(u, lam_re, lam_im, log_dt, b_re, b_im, c_re, c_im, d_skip):
    bsz, seq, _ = u.shape
    ug = u.reshape(bsz, seq, S5_GROUPS, S5_GROUP)
    dt = jnp.exp(log_dt)[:, None]
    z_re, z_im = lam_re * dt, lam_im * dt
    mag = jnp.exp(z_re)
    ab_re, ab_im = mag * jnp.cos(z_im), mag * jnp.sin(z_im)
    den = lam_re * lam_re + lam_im * lam_im
    n_re, n_im = ab_re - 1.0, ab_im
    f_re = ((n_re * lam_re + n_im * lam_im) / den)[..., None]
    f_im = ((n_im * lam_re - n_re * lam_im) / den)[..., None]
    bb_re = f_re * b_re - f_im * b_im
    bb_im = f_re * b_im + f_im * b_re
    bu_re = jnp.einsum('gph,blgh->blgp', bb_re, ug)
    bu_im = jnp.einsum('gph,blgh->blgp', bb_im, ug)
    a_re = jnp.broadcast_to(ab_re, bu_re.shape)
    a_im = jnp.broadcast_to(ab_im, bu_im.shape)

    def combine(e1, e2):
        a1r, a1i, b1r, b1i = e1
        a2r, a2i, b2r, b2i = e2
        return (a2r * a1r - a2i * a1i,
                a2r * a1i + a2i * a1r,
                a2r * b1r - a2i * b1i + b2r,
                a2r * b1i + a2i * b1r + b2i)

    _, _, s_re, s_im = lax.associative_scan(combine, (a_re, a_im, bu_re, bu_im), axis=1)
    y = jnp.einsum('ghp,blgp->blgh', c_re, s_re) - jnp.einsum('ghp,blgp->blgh', c_im, s_im)
    return y.reshape(bsz, seq, D_S5) + d_skip * u


def setup_inputs(seed: int = 0) -> dict:
    key = jax.random.key(seed)
    ks = jax.random.split(key, 24)
    f32 = jnp.float32
    nrm = lambda k, shape: jax.random.normal(k, shape, f32)
    x = nrm(ks[0], (BATCH, SEQ, D_MODEL))
    c = nrm(ks[1], (BATCH, D_MODEL))
    w_ada = nrm(ks[2], (DEPTH, D_MODEL, 3 * D_MODEL)) * D_MODEL ** -0.5
    b_ada = nrm(ks[3], (DEPTH, 3 * D_MODEL)) * 0.02
    w_in = nrm(ks[4], (DEPTH, D_MODEL, D_IN)) * D_MODEL ** -0.5
    w_gla_gate = nrm(ks[5], (DEPTH, GLA_GATE_RANK, GLA_DK_TOTAL)) * GLA_GATE_RANK ** -0.5
    b_gla_gate = nrm(ks[6], (DEPTH, GLA_DK_TOTAL)) * 0.1
    gla_norm_g = 1.0 + 0.02 * nrm(ks[7], (DEPTH, GLA_DV))
    n_idx = jnp.arange(S5_STATE, dtype=f32)
    s5_lambda_re = -0.5 * (1.0 + 0.01 * nrm(ks[8], (DEPTH, S5_GROUPS, S5_STATE)))
    s5_lambda_im = math.pi * n_idx + 0.01 * nrm(ks[9], (DEPTH, S5_GROUPS, S5_STATE))
    s5_log_dt = jax.random.uniform(ks[10], (DEPTH, S5_GROUPS), f32, math.log(S5_DT_MIN), math.log(S5_DT_MAX))
    s5_b_re = nrm(ks[11], (DEPTH, S5_GROUPS, S5_STATE, S5_GROUP)) * (2.0 * S5_GROUP) ** -0.5
    s5_b_im = nrm(ks[12], (DEPTH, S5_GROUPS, S5_STATE, S5_GROUP)) * (2.0 * S5_GROUP) ** -0.5
    s5_c_re = nrm(ks[13], (DEPTH, S5_GROUPS, S5_GROUP, S5_STATE)) * (2.0 * S5_STATE) ** -0.5
    s5_c_im = nrm(ks[14], (DEPTH, S5_GROUPS, S5_GROUP, S5_STATE)) * (2.0 * S5_STATE) ** -0.5
    s5_d = nrm(ks[15], (DEPTH, D_S5))
    w_glu = nrm(ks[16], (DEPTH, D_S5, D_S5)) * D_S5 ** -0.5
    b_glu = nrm(ks[17], (DEPTH, D_S5)) * 0.02
    w_out = nrm(ks[18], (DEPTH, D_MIX, D_MODEL)) * (D_MIX ** -0.5) * DEEPNORM_BETA
    ln_g = 1.0 + 0.02 * nrm(ks[19], (DEPTH, D_MODEL))
    ln_b = 0.02 * nrm(ks[20], (DEPTH, D_MODEL))
    return {'x': x, 'c': c, 'w_ada': w_ada, 'b_ada': b_ada, 'w_in': w_in,
            'w_gla_gate': w_gla_gate, 'b_gla_gate': b_gla_gate, 'gla_norm_g': gla_norm_g,
            's5_lambda_re': s5_lambda_re, 's5_lambda_im': s5_lambda_im, 's5_log_dt': s5_log_dt,
            's5_b_re': s5_b_re, 's5_b_im': s5_b_im, 's5_c_re': s5_c_re, 's5_c_im': s5_c_im,
            's5_d': s5_d, 'w_glu': w_glu, 'b_glu': b_glu, 'w_out': w_out,
            'ln_g': ln_g, 'ln_b': ln_b}


def reference(x, c, w_ada, b_ada, w_in, w_gla_gate, b_gla_gate, gla_norm_g,
              s5_lambda_re, s5_lambda_im, s5_log_dt, s5_b_re, s5_b_im, s5_c_re, s5_c_im,
              s5_d, w_glu, b_glu, w_out, ln_g, ln_b):
    f32 = jnp.float32
    bsz, seq, _ = x.shape
    for layer in range(DEPTH):
        mod = jax.nn.silu(c) @ w_ada[layer] + b_ada[layer]
        shift, scale, gate = jnp.split(mod[:, None, :], 3, axis=-1)
        h = x * (1.0 + scale) + shift
        proj = h @ w_in[layer]
        q, k, v, g_lr, z_gla, u_s5, z_s5 = jnp.split(proj, SPLIT_IDX, axis=-1)

        qh = q.astype(f32).reshape(bsz, seq, GLA_HEADS, GLA_DK) * GLA_DK ** -0.5
        kh = k.astype(f32).reshape(bsz, seq, GLA_HEADS, GLA_DK)
        vh = v.astype(f32).reshape(bsz, seq, GLA_HEADS, GLA_DV)
        gate_logit = (g_lr @ w_gla_gate[layer] + b_gla_gate[layer]).astype(f32)
        log_a = (jax.nn.log_sigmoid(gate_logit) / GLA_GATE_TAU).reshape(bsz, seq, GLA_HEADS, GLA_DK)
        o = gla_chunked(qh, kh, vh, log_a)
        o = o * lax.rsqrt(jnp.mean(o * o, axis=-1, keepdims=True) + NORM_EPS) * gla_norm_g[layer].astype(f32)
        o_gla = o.reshape(bsz, seq, D_GLA).astype(x.dtype) * jax.nn.silu(z_gla)

        y = s5_ssm(u_s5.astype(f32),
                   s5_lambda_re[layer].astype(f32), s5_lambda_im[layer].astype(f32),
                   s5_log_dt[layer].astype(f32),
                   s5_b_re[layer].astype(f32), s5_b_im[layer].astype(f32),
                   s5_c_re[layer].astype(f32), s5_c_im[layer].astype(f32),
                   s5_d[layer].astype(f32))
        y = jax.nn.gelu(y)
        y = y * jax.nn.sigmoid(y @ w_glu[layer].astype(f32) + b_glu[layer].astype(f32))
        o_s5 = y.astype(x.dtype) * jax.nn.silu(z_s5)

        mixed = jnp.concatenate([o_gla, o_s5], axis=-1) @ w_out[layer]
        x = layer_norm(DEEPNORM_ALPHA * x + gate * mixed, ln_g[layer], ln_b[layer])
    return x
```

```python
import numpy as np
from contextlib import ExitStack
import concourse.bass as bass
import concourse.mybir as mybir
from concourse.bass_utils import run_bass_kernel_spmd

F32 = mybir.dt.float32
BF16 = mybir.dt.bfloat16
I32 = mybir.dt.int32
ALU = mybir.AluOpType
AF = mybir.ActivationFunctionType
AX = mybir.AxisListType

D = 4096
DIN = 10256
ALPHA = 2.0 ** 0.25
EPS = 1e-5
C_Q, C_K, C_V, C_G, C_ZG, C_U, C_ZS = 0, 1024, 2048, 4096, 4112, 6160, 8208


class Prog:
    ENGS = ("pe", "act", "dve", "pool", "sp")

    def __init__(self, nc):
        self.nc = nc
        self.ops = []
        self.last_w = {}
        self.readers = {}

    def op(self, eng, fn, reads=(), writes=(), dma=None):
        idx = len(self.ops)
        deps = set()
        for k in reads:
            if k in self.last_w:
                deps.add(self.last_w[k])
        for k in writes:
            if k in self.last_w:
                deps.add(self.last_w[k])
            for r in self.readers.get(k, ()):
                deps.add(r)
        self.ops.append(dict(eng=eng, fn=fn, deps=deps, dma=dma, needed=False))
        for k in reads:
            self.readers.setdefault(k, []).append(idx)
        for k in writes:
            self.last_w[k] = idx
            self.readers[k] = []
        return idx

    def emit(self, stack, tag):
        nc = self.nc
        ops = self.ops
        for i, o in enumerate(ops):
            latest = {}
            for d in o["deps"]:
                od = ops[d]
                if od["dma"] is not None:
                    continue
                if od["eng"] == "pe" and o["eng"] == "pe" and o["dma"] is None:
                    continue
                if od["eng"] != "pe":
                    od["needed"] = True
                    continue
                if od["eng"] not in latest or d > latest[od["eng"]]:
                    latest[od["eng"]] = d
            for d in latest.values():
                ops[d]["needed"] = True
        sems = {e: stack.enter_context(nc.semaphore(tag + "s_" + e)) for e in self.ENGS}
        dma_sems, dma_cnt = {}, {}
        cnt = {e: 0 for e in self.ENGS}
        ev = [None] * len(ops)
        for i, o in enumerate(ops):
            if o["dma"] is not None:
                name = o["dma"]
                if name not in dma_sems:
                    dma_sems[name] = stack.enter_context(nc.semaphore(tag + "d_" + name))
                    dma_cnt[name] = 0
                dma_cnt[name] += 16
                ev[i] = (dma_sems[name], dma_cnt[name], "dma:" + name)
            elif o["needed"]:
                cnt[o["eng"]] += 1
                ev[i] = (sems[o["eng"]], cnt[o["eng"]], o["eng"])
        final_dma = {n: (dma_sems[n], dma_cnt[n]) for n in dma_sems}
        per_eng = {e: [] for e in self.ENGS}
        for i, o in enumerate(ops):
            per_eng[o["eng"]].append(i)
        with nc.Block() as block:
            getters = dict(pe=block.tensor, act=block.scalar, dve=block.vector,
                           pool=block.gpsimd, sp=block.sync)
            for e in self.ENGS:
                idxs = per_eng[e]

                def body(engine, e=e, idxs=idxs):
                    waited = {}
                    for i in idxs:
                        o = ops[i]
                        need = {}
                        for d in o["deps"]:
                            if ev[d] is None:
                                continue
                            s, v, tg = ev[d]
                            if tg not in need or need[tg][1] < v:
                                need[tg] = (s, v)
                        for tg, (s, v) in need.items():
                            if waited.get(tg, 0) >= v:
                                continue
                            engine.wait_ge(s, v)
                            waited[tg] = v
                        ins = o["fn"](engine)
                        if ev[i] is not None:
                            ins.then_inc(ev[i][0], 16 if o["dma"] is not None else 1)
                    if e == "sp":
                        for n, (s, v) in final_dma.items():
                            engine.wait_ge(s, v)
                        for e2 in ("pe", "act", "dve", "pool"):
                            if cnt[e2] > 0:
                                engine.wait_ge(sems[e2], cnt[e2])
                getters[e](body)


class Ring:
    def __init__(self, nc, st, name, n, shape, dtype):
        self.t = [st.enter_context(nc.sbuf_tensor(f"{name}{i}", shape, dtype)) for i in range(n)]
        self.n = n
        self.i = 0
        self.name = name

    def next(self):
        s = self.i % self.n
        self.i += 1
        return s, self.t[s], (self.name, s)


def dram_rows(ap, r0, nkc, c0, nc_):
    return ap[r0 * 128:(r0 + nkc) * 128, c0:c0 + nc_].rearrange("(kc p) c -> p kc c", p=128)


def make_iota_mask(P, nc, st, name, shape, pattern, base, cm, op, key):
    ti = st.enter_context(nc.sbuf_tensor(name + "_i", shape, I32))
    tf = st.enter_context(nc.sbuf_tensor(name, shape, F32))
    P.op("pool", lambda e: e.iota(ti[:], pattern=pattern, base=base, channel_multiplier=cm),
         writes=[key + "_i"])
    P.op("dve", lambda e: e.tensor_scalar(out=tf[:], in0=ti[:], scalar1=0.0, scalar2=None, op0=op),
         reads=[key + "_i"], writes=[key])
    return tf


def phase0(nc, st, io, scr, pers, L, Lp=0):
    P = Prog(nc)
    with ExitStack() as ls:
        ident = make_iota_mask(P, nc, ls, "ident0", [128, 128], [[1, 128]], 0, -1, ALU.is_equal, "ident")
        c32 = ls.enter_context(nc.sbuf_tensor("c32", [32, 128], F32))
        ba96 = ls.enter_context(nc.sbuf_tensor("ba96", [96, 128], F32))
        scol = ls.enter_context(nc.sbuf_tensor("scol", [128, 32, 2], F32))
        bac = ls.enter_context(nc.sbuf_tensor("bac", [128, 96], F32))
        modc = ls.enter_context(nc.sbuf_tensor("modc", [128, 96], F32))
        g32 = ls.enter_context(nc.sbuf_tensor("g32", [32, 128], F32))
        wa = [ls.enter_context(nc.sbuf_tensor(f"wa{i}", [128, 12288], F32)) for i in range(2)]
        grow = ls.enter_context(nc.sbuf_tensor("grow", [128, 4096], F32))
        wf = [ls.enter_context(nc.sbuf_tensor(f"wf{i}", [128, 4096], F32)) for i in range(2)]
        wb = [ls.enter_context(nc.sbuf_tensor(f"wb{i}", [128, 4096], BF16)) for i in range(2)]
        pst = ls.enter_context(nc.psum_tensor("p0t", [128, 128], F32))
        psm = ls.enter_context(nc.psum_tensor("p0m", [128, 96, 2], F32))

        for r in range(32):
            P.op("pool", lambda e, r=r: e.dma_start(out=scr["win"][r * 128:(r + 1) * 128, :],
                                                  in_=io["w_in"][r * 128:(r + 1) * 128, :],
                                                  max_dma_last_dim=4096),
                 writes=[("win", r)], dma="cast")
        for r in range(16):
            P.op("pool", lambda e, r=r: e.dma_start(out=scr["wglu"][r * 128:(r + 1) * 128, :],
                                                  in_=io["w_glu"][r * 128:(r + 1) * 128, :],
                                                  max_dma_last_dim=4096),
                 writes=[("wglu", r)], dma="cast")

        if Lp:
            P.op("sp", lambda e: e.dma_start(out=pers["flag"][:], in_=io["flag"][:, :]), writes=["flag"], dma="ld0_1")
        P.op("sp", lambda e: e.dma_start(out=c32[:], in_=io["c"][:, :]), writes=["c32"], dma="ld0_2")
        P.op("sp", lambda e: e.dma_start(out=ba96[:], in_=io["b_ada"][:, :]), writes=["ba96"], dma="ld0_3")
        P.op("pe", lambda e: e.transpose(pst[:, 0:32], c32[:], ident[0:32, 0:32]),
             reads=["c32", "ident"], writes=["pst"])
        for j in range(2):
            P.op("act", lambda e, j=j: e.activation(out=scol[:, :, j], in_=pst[:, 0:32], func=AF.Silu),
                 reads=["pst"], writes=["scol"])
        P.op("pe", lambda e: e.transpose(pst[:, 0:96], ba96[:], ident[0:96, 0:96]),
             reads=["ba96", "ident", "scol"], writes=["pst"])
        P.op("dve", lambda e: e.tensor_copy(out=bac[:], in_=pst[:, 0:96]), reads=["pst"], writes=["bac"])
        for kc in range(32):
            P.op("sp", lambda e, kc=kc: e.dma_start(out=wa[kc % 2][:], in_=io["w_ada"][kc * 128:(kc + 1) * 128, :]),
                 writes=[("wa", kc % 2)], dma="wa%d" % (kc % 2))
            for ct in range(96):
                P.op("pe", lambda e, kc=kc, ct=ct: e.matmul(
                    psm[:, ct, :], lhsT=wa[kc % 2][:, ct * 128:(ct + 1) * 128], rhs=scol[:, kc, :],
                    start=(kc == 0 and ct == 0), stop=(kc == 31 and ct == 95), skip_group_check=True),
                    reads=[("wa", kc % 2), "scol"], writes=["psm"])
        P.op("dve", lambda e: e.tensor_tensor(out=modc[:], in0=psm[:, :, 0], in1=bac[:], op=ALU.add),
             reads=["psm", "bac"], writes=["modc"])
        P.op("dve", lambda e: e.tensor_copy(out=pers["sh"][:], in_=modc[:, 0:32]), reads=["modc"], writes=["sh"])
        P.op("dve", lambda e: e.tensor_scalar(out=pers["s1p"][:], in0=modc[:, 32:64], scalar1=1.0, scalar2=None,
                                              op0=ALU.add), reads=["modc"], writes=["s1p"])
        P.op("pe", lambda e: e.transpose(pst[0:32, :], modc[:, 64:96], ident[:]),
             reads=["modc", "ident", "bac"], writes=["pst"])
        P.op("dve", lambda e: e.tensor_copy(out=g32[:], in_=pst[0:32, :]), reads=["pst"], writes=["g32"])
        P.op("sp", lambda e: e.dma_start(out=scr["gate"][:, :], in_=g32[:]), reads=["g32"], writes=["gscr"], dma="ld0_4")
        P.op("sp", lambda e: e.dma_start(
            out=grow[:], in_=scr["gate"].rearrange("a b -> (a b)")[None, :].broadcast_to([128, 4096])),
            reads=["gscr"], writes=["grow"], dma="ld0_5")
        for r in range(32):
            P.op("sp", lambda e, r=r: e.dma_start(out=wf[r % 2][:], in_=io["w_out"][r * 128:(r + 1) * 128, :]),
                 writes=[("wf", r % 2)], dma="wf%d" % (r % 2))
            P.op("dve", lambda e, r=r: e.tensor_tensor(out=wb[r % 2][:], in0=wf[r % 2][:], in1=grow[:], op=ALU.mult),
                 reads=[("wf", r % 2), "grow"], writes=[("wb", r % 2)])
            P.op("sp", lambda e, r=r: e.dma_start(out=scr["wout"][r * 128:(r + 1) * 128, :], in_=wb[r % 2][:]),
                 reads=[("wb", r % 2)], writes=[("wout", r)], dma="wst%d" % (r % 2))
        P.emit(ls, "a")


def load_hT(P, nc, xsrc, pers, xs, hT, psb, ident, t0, tagq):
    for i in range(4):
        xb = xs[i % len(xs)]
        xk = ("xs", i % len(xs))
        P.op("pool", lambda e, xb=xb, i=i: e.dma_start(out=xb[:], in_=xsrc[t0 + i * 128:t0 + (i + 1) * 128, :]),
             writes=[xk], dma="%s_%d" % (tagq, i % len(xs)))
        for kg in range(8):
            b = kg % len(psb)
            for k4 in range(4):
                kc = kg * 4 + k4
                P.op("pe", lambda e, xb=xb, b=b, k4=k4, kc=kc: e.transpose(
                    psb[b][:, k4 * 128:(k4 + 1) * 128], xb[:, kc * 128:(kc + 1) * 128], ident[:]),
                    reads=[xk, "identf"], writes=[("B", b)])
            for k4 in range(4):
                kc = kg * 4 + k4
                if k4 % 2 == 0:
                    P.op("act", lambda e, b=b, k4=k4, kc=kc, i=i: e.activation(
                        out=hT[:, kc, i * 128:(i + 1) * 128], in_=psb[b][:, k4 * 128:(k4 + 1) * 128],
                        func=AF.Identity, bias=pers["sh"][:, kc:kc + 1], scale=pers["s1p"][:, kc:kc + 1]),
                        reads=[("B", b)], writes=[("hT", kc // 8)])
                else:
                    P.op("dve", lambda e, b=b, k4=k4, kc=kc, i=i: e.tensor_scalar(
                        out=hT[:, kc, i * 128:(i + 1) * 128], in0=psb[b][:, k4 * 128:(k4 + 1) * 128],
                        scalar1=pers["s1p"][:, kc:kc + 1], scalar2=pers["sh"][:, kc:kc + 1],
                        op0=ALU.mult, op1=ALU.add),
                        reads=[("B", b)], writes=[("hT", kc // 8)])


def sweep1(nc, st, io, scr, pers, L, Lp=0):
    P = Prog(nc)
    nblk = L // 512
    with ExitStack() as ls:
        sb = lambda name, shape, dt: ls.enter_context(nc.sbuf_tensor(name, shape, dt))
        identf = make_iota_mask(P, nc, ls, "identf1", [128, 128], [[1, 128]], 0, -1, ALU.is_equal, "identf")
        identb = sb("identb1", [128, 128], BF16)
        P.op("dve", lambda e: e.tensor_copy(out=identb[:], in_=identf[:]), reads=["identf"], writes=["identb"])
        m_i = sb("m64i", [128, 64], I32)
        mask64 = sb("mask64", [128, 64], F32)
        for hf in range(2):
            P.op("pool", lambda e, hf=hf: e.iota(m_i[hf * 64:(hf + 1) * 64, :], pattern=[[1, 64]], base=0,
                                               channel_multiplier=-1), writes=["m64i"])
        P.op("dve", lambda e: e.tensor_scalar(out=mask64[:], in0=m_i[:], scalar1=0.0, scalar2=None, op0=ALU.is_ge),
             reads=["m64i"], writes=["mask64"])
        tri = sb("tri", [128, 128], F32)
        P.op("dve", lambda e: e.memset(tri[:], 0.0), writes=["tri"])
        for hf in range(2):
            P.op("dve", lambda e, hf=hf: e.tensor_copy(out=tri[hf * 64:(hf + 1) * 64, hf * 64:(hf + 1) * 64],
                                                     in_=mask64[hf * 64:(hf + 1) * 64, :]),
                 reads=["mask64", "tri"], writes=["tri"])
        wgate = sb("wgate", [16, 1024], F32)
        bgate = sb("bgate", [1, 1024], F32)
        ones = sb("ones1", [1, 128], F32)
        gnb = sb("gnb", [128, 512], F32)
        P.op("sp", lambda e: e.dma_start(out=wgate[:], in_=io["w_gate"][:, :]), writes=["wgate"], dma="c1_1")
        P.op("sp", lambda e: e.dma_start(out=bgate[:], in_=io["b_gate"][:, :]), writes=["bgate"], dma="c1_2")
        P.op("sp", lambda e: e.dma_start(out=gnb[:], in_=io["gnorm"][0:1, :].broadcast_to([128, 512])),
             writes=["gnb"], dma="c1_3")
        P.op("dve", lambda e: e.memset(ones[:], 1.0), writes=["ones"])
        T = sb("Tst", [128, 8, 512], F32)
        Sbf = sb("Sbf", [128, 8, 512], BF16)
        eblp = sb("eblp", [128, 8], F32)
        P.op("dve", lambda e: e.memset(T[:], 0.0), writes=[("T", m) for m in range(8)])
        P.op("pool", lambda e: e.memset(Sbf[:], 0.0), writes=[("Sbf", m) for m in range(8)])
        P.op("dve", lambda e: e.memset(eblp[:], 1.0), writes=[("eblp", m) for m in range(8)])
        xs = [sb(f"xs1_{i}", [128, 4096], F32) for i in range(2)]
        hT = sb("hT1", [128, 32, 512], BF16)
        ring = Ring(nc, ls, "w1_", 3, [128, 4096], BF16)
        wg16 = sb("wg16", [128, 32, 16], BF16)
        glrT = sb("glrT", [16, 512], F32)
        nls = sb("nls", [128, 4, 1024], F32)
        ebt = [sb(f"ebt{e}", [128, 512], F32) for e in range(2)]
        eit = [sb(f"eit{e}", [128, 512], F32) for e in range(2)]
        qdec = [sb(f"qdec{e}", [128, 512], BF16) for e in range(2)]
        kinvT = [sb(f"kinvT{e}", [128, 512], BF16) for e in range(2)]
        kinv_tok = sb("kinvtok", [128, 4, 256], BF16)
        v_tok = sb("vtok", [128, 4, 512], BF16)
        gz = sb("gz", [128, 4, 512], F32)
        zs = sb("zs", [128, 512], F32)
        o_tok = sb("otok", [128, 4, 512], BF16)
        oTs = sb("oTs", [128, 4, 512], BF16)
        att_s = sb("atts", [128, 64], BF16)
        junk = sb("junk1", [128, 512], BF16)
        ssq = sb("ssq", [128, 2], F32)
        B = [ls.enter_context(nc.psum_tensor(f"B{i}", [128, 512], F32)) for i in range(5)]
        B5 = ls.enter_context(nc.psum_tensor("B5", [128, 1024], BF16))
        B6 = ls.enter_context(nc.psum_tensor("B6", [128, 512], F32))
        B7 = ls.enter_context(nc.psum_tensor("B7", [128, 512], F32))

        def wload(r0, nkc, c0, ncol):
            s, wt, wk = ring.next()
            view = wt[:].rearrange("p (a b) -> p a b", b=ncol)
            P.op("sp", lambda e: e.dma_start(out=view, in_=dram_rows(scr["win"], r0, nkc, c0, ncol)),
                 reads=[("win", r) for r in range(r0, r0 + nkc)], writes=[wk], dma="w1_%d" % s)
            return view, wk

        blocks = [("pre", i) for i in range(Lp // 512)] + [("own", i) for i in range(nblk)]
        for mode, blk in blocks:
            own = mode == "own"
            last_pre = (mode == "pre" and blk == Lp // 512 - 1)
            t0 = blk * 512
            load_hT(P, nc, io["x"] if own else io["xp"], pers, xs, hT, B[0:4], identf, t0, "x1")
            P.op("sp", lambda e: e.dma_start(out=wg16[:], in_=dram_rows(scr["win"], 0, 32, C_G, 16)),
                 reads=[("win", r) for r in range(32)], writes=["wg16"], dma="w1g")
            for kc in range(32):
                P.op("pe", lambda e, kc=kc: e.matmul(B[4][0:16, :], lhsT=wg16[:, kc, :], rhs=hT[:, kc, :],
                                                     start=(kc == 0), stop=(kc == 31)),
                     reads=["wg16", ("hT", kc // 8)], writes=["B4"])
            P.op("dve", lambda e: e.tensor_copy(out=glrT[:], in_=B[4][0:16, :]), reads=["B4"], writes=["glrT"])
            for i in range(4):
                for e2 in range(2):
                    sl = slice(e2 * 512, (e2 + 1) * 512)
                    P.op("pe", lambda e, i=i, sl=sl: e.matmul(B[4][:], lhsT=glrT[:, i * 128:(i + 1) * 128],
                                                            rhs=wgate[:, sl], start=True, stop=False),
                         reads=["glrT", "wgate"], writes=["B4"])
                    P.op("pe", lambda e, sl=sl: e.matmul(B[4][:], lhsT=ones[:, :], rhs=bgate[:, sl],
                                                       start=False, stop=True),
                         reads=["ones", "bgate"], writes=["B4"])
                    P.op("act", lambda e, i=i, sl=sl: e.activation(out=nls[:, i, sl], in_=B[4][:], func=AF.Exp, scale=-1.0),
                         reads=["B4"], writes=[("nls", i)])
                    P.op("act", lambda e, i=i, sl=sl: e.activation(out=nls[:, i, sl], in_=nls[:, i, sl], func=AF.Ln, bias=1.0),
                         reads=[("nls", i)], writes=[("nls", i)])
            for h in range(4):
                for e2 in range(2):
                    m = 2 * h + e2
                    for i in range(4):
                        P.op("pe", lambda e, i=i, m=m: e.matmul(B[4][:, i * 128:(i + 1) * 128],
                                                              lhsT=nls[:, i, m * 128:(m + 1) * 128], rhs=tri[:],
                                                              start=True, stop=True),
                             reads=[("nls", i), "tri"], writes=["B4"])
                    P.op("act", lambda e, e2=e2: e.activation(out=ebt[e2][:], in_=B[4][:], func=AF.Exp, scale=-1.0 / 16),
                         reads=["B4"], writes=[("ebt", e2)])
                    P.op("act", lambda e, e2=e2: e.activation(out=eit[e2][:], in_=B[4][:], func=AF.Exp, scale=1.0 / 16),
                         reads=["B4"], writes=[("eit", e2)])
                for which, c0 in (("q", C_Q + 256 * h), ("k", C_K + 256 * h)):
                    if which == "q" and not own:
                        continue
                    pb = (0, 1) if which == "q" else (2, 3)
                    for kh in range(2):
                        view, wk = wload(kh * 16, 16, c0, 256)
                        for e2 in range(2):
                            for k16 in range(16):
                                kc = kh * 16 + k16
                                P.op("pe", lambda e, view=view, e2=e2, k16=k16, kc=kc, pb=pb: e.matmul(
                                    B[pb[e2]][:], lhsT=view[:, k16, e2 * 128:(e2 + 1) * 128], rhs=hT[:, kc, :],
                                    start=(kc == 0), stop=(kc == 31)),
                                    reads=[wk, ("hT", kc // 8)], writes=[("B", pb[e2])])
                    for e2 in range(2):
                        if which == "q":
                            P.op("dve", lambda e, e2=e2, pb=pb: e.scalar_tensor_tensor(
                                out=qdec[e2][:], in0=B[pb[e2]][:], scalar=1.0 / 16, in1=ebt[e2][:],
                                op0=ALU.mult, op1=ALU.mult),
                                reads=[("B", pb[e2]), ("ebt", e2)], writes=[("qdec", e2)])
                        else:
                            P.op("dve", lambda e, e2=e2, pb=pb: e.tensor_tensor(
                                out=kinvT[e2][:], in0=B[pb[e2]][:], in1=eit[e2][:], op=ALU.mult),
                                reads=[("B", pb[e2]), ("eit", e2)], writes=[("kinvT", e2)])
                for e2 in range(2):
                    for i in range(4):
                        P.op("pe", lambda e, e2=e2, i=i: e.transpose(
                            B5[:, (i * 2 + e2) * 128:(i * 2 + e2 + 1) * 128], kinvT[e2][:, i * 128:(i + 1) * 128], identb[:]),
                            reads=[("kinvT", e2), "identb"], writes=["B5"])
                P.op("act", lambda e: e.activation(out=kinv_tok[:].rearrange("p a b -> p (a b)"), in_=B5[:], func=AF.Copy),
                     reads=["B5"], writes=["kinvtok"])
                for which, c0 in (("v", C_V + 512 * h), ("z", C_ZG + 512 * h)):
                    if which == "z" and not own:
                        continue
                    for kq in range(4):
                        view, wk = wload(kq * 8, 8, c0, 512)
                        for i in range(4):
                            for k8 in range(8):
                                kc = kq * 8 + k8
                                P.op("pe", lambda e, view=view, i=i, k8=k8, kc=kc: e.matmul(
                                    B[i][:], lhsT=hT[:, kc, i * 128:(i + 1) * 128], rhs=view[:, k8, :],
                                    start=(kc == 0), stop=(kc == 31)),
                                    reads=[wk, ("hT", kc // 8)], writes=[("B", i)])
                    for i in range(4):
                        if which == "v":
                            P.op("act", lambda e, i=i: e.activation(out=v_tok[:, i, :], in_=B[i][:], func=AF.Copy),
                                 reads=[("B", i)], writes=[("vtok", i)])
                        else:
                            P.op("act", lambda e, i=i: e.activation(out=zs[:], in_=B[i][:], func=AF.Silu),
                                 reads=[("B", i)], writes=["zs"])
                            P.op("dve", lambda e, i=i: e.tensor_tensor(out=gz[:, i, :], in0=zs[:], in1=gnb[:], op=ALU.mult),
                                 reads=["zs", "gnb"], writes=[("gz", i)])
                for c in range(8):
                    par, i = c % 2, c // 2
                    rows = slice(64 * par, 64 * par + 64)
                    cols = slice(64 * c, 64 * c + 64)
                    for e2 in range(2 if own else 0):
                        P.op("pe", lambda e, e2=e2, rows=rows, cols=cols: e.matmul(
                            B7[rows, 0:64], lhsT=kinvT[e2][:, cols], rhs=qdec[e2][:, cols],
                            start=(e2 == 0), stop=(e2 == 1)),
                            reads=[("kinvT", e2), ("qdec", e2)], writes=["B7"])
                    if not own:
                        for e2 in range(2):
                            m = 2 * h + e2
                            ebl_prev = eblp[:, m:m + 1] if c == 0 else ebt[e2][:, 64 * c - 1:64 * c]
                            ebl_cur = ebt[e2][:, 64 * c + 63:64 * c + 64]
                            P.op("pe", lambda e, rows=rows, i=i, e2=e2: e.matmul(
                                B[2 + e2][:], lhsT=kinv_tok[rows, i, e2 * 128:(e2 + 1) * 128], rhs=v_tok[rows, i, :],
                                start=True, stop=True),
                                reads=["kinvtok", ("vtok", i)], writes=[("B", 2 + e2)])
                            P.op("dve", lambda e, m=m, e2=e2, ebl_prev=ebl_prev: e.scalar_tensor_tensor(
                                out=T[:, m, :], in0=T[:, m, :], scalar=ebl_prev, in1=B[2 + e2][:], op0=ALU.mult, op1=ALU.add),
                                reads=[("T", m), ("B", 2 + e2), ("ebt", e2), ("eblp", m)], writes=[("T", m)])
                            if last_pre and c == 7:
                                P.op("act", lambda e, m=m, ebl_cur=ebl_cur: e.activation(
                                    out=Sbf[:, m, :], in_=T[:, m, :], func=AF.Copy, scale=ebl_cur),
                                    reads=[("T", m), ("ebt", e2)], writes=[("Sbf", m)])
                        continue
                    P.op("dve", lambda e, rows=rows: e.tensor_tensor(out=att_s[rows, :], in0=B7[rows, 0:64],
                                                                   in1=mask64[rows, :], op=ALU.mult),
                         reads=["B7", "mask64"], writes=["atts"])
                    P.op("pe", lambda e, rows=rows, i=i: e.matmul(B6[rows, :], lhsT=att_s[rows, :], rhs=v_tok[rows, i, :],
                                                                start=True, stop=False),
                         reads=["atts", ("vtok", i)], writes=["B6"])
                    for e2 in range(2):
                        m = 2 * h + e2
                        P.op("pe", lambda e, rows=rows, cols=cols, e2=e2, m=m: e.matmul(
                            B6[rows, :], lhsT=qdec[e2][:, cols], rhs=Sbf[:, m, :], start=False, stop=(e2 == 1)),
                            reads=[("qdec", e2), ("Sbf", m)], writes=["B6"])
                    P.op("act", lambda e, rows=rows: e.activation(out=junk[rows, :], in_=B6[rows, :], func=AF.Square,
                                                                accum_out=ssq[rows, 0:1]),
                         reads=["B6"], writes=["ssq", "junk"])
                    P.op("dve", lambda e, rows=rows: e.tensor_scalar(out=ssq[rows, 1:2], in0=ssq[rows, 0:1],
                                                                   scalar1=1.0 / 512, scalar2=EPS, op0=ALU.mult, op1=ALU.add),
                         reads=["ssq"], writes=["ssq"])
                    P.op("act", lambda e, rows=rows: e.activation(out=ssq[rows, 1:2], in_=ssq[rows, 1:2], func=AF.Sqrt),
                         reads=["ssq"], writes=["ssq"])
                    P.op("dve", lambda e, rows=rows: e.reciprocal(out=ssq[rows, 1:2], in_=ssq[rows, 1:2]),
                         reads=["ssq"], writes=["ssq"])
                    P.op("dve", lambda e, rows=rows, i=i: e.scalar_tensor_tensor(
                        out=o_tok[rows, i, :], in0=B6[rows, :], scalar=ssq[rows, 1:2], in1=gz[rows, i, :],
                        op0=ALU.mult, op1=ALU.mult),
                        reads=["B6", "ssq", ("gz", i)], writes=[("otok", i)])
                    for e2 in range(2):
                        m = 2 * h + e2
                        ebl_prev = eblp[:, m:m + 1] if c == 0 else ebt[e2][:, 64 * c - 1:64 * c]
                        ebl_cur = ebt[e2][:, 64 * c + 63:64 * c + 64]
                        P.op("pe", lambda e, rows=rows, i=i, e2=e2: e.matmul(
                            B[2 + e2][:], lhsT=kinv_tok[rows, i, e2 * 128:(e2 + 1) * 128], rhs=v_tok[rows, i, :],
                            start=True, stop=True),
                            reads=["kinvtok", ("vtok", i)], writes=[("B", 2 + e2)])
                        P.op("dve", lambda e, m=m, e2=e2, ebl_prev=ebl_prev: e.scalar_tensor_tensor(
                            out=T[:, m, :], in0=T[:, m, :], scalar=ebl_prev, in1=B[2 + e2][:], op0=ALU.mult, op1=ALU.add),
                            reads=[("T", m), ("B", 2 + e2), ("ebt", e2), ("eblp", m)], writes=[("T", m)])
                        P.op("act", lambda e, m=m, ebl_cur=ebl_cur: e.activation(
                            out=Sbf[:, m, :], in_=T[:, m, :], func=AF.Copy, scale=ebl_cur),
                            reads=[("T", m), ("ebt", e2)], writes=[("Sbf", m)])
                for e2 in range(2):
                    m = 2 * h + e2
                    P.op("dve", lambda e, m=m, e2=e2: e.tensor_copy(out=eblp[:, m:m + 1], in_=ebt[e2][:, 511:512]),
                         reads=[("ebt", e2)], writes=[("eblp", m)])
                for half in range(2 if own else 0):
                    for cc2 in range(2):
                        cc = half * 2 + cc2
                        for i in range(4):
                            P.op("pe", lambda e, cc=cc, cc2=cc2, i=i: e.transpose(
                                B5[:, (cc2 * 4 + i) * 128:(cc2 * 4 + i + 1) * 128], o_tok[:, i, cc * 128:(cc + 1) * 128], identb[:]),
                                reads=[("otok", i), "identb"], writes=["B5"])
                    P.op("dve", lambda e, half=half: e.tensor_copy(
                        out=oTs[:, half * 2:half * 2 + 2, :].rearrange("p a b -> p (a b)"), in_=B5[:]),
                        reads=["B5"], writes=["oTs"])
                if own:
                    P.op("pool", lambda e, h=h, t0=t0: e.dma_start(out=dram_rows(scr["oT"], h * 4, 4, t0, 512), in_=oTs[:]),
                         reads=["oTs"], writes=[("oTscr", blk, h)], dma="o1")
            if last_pre:
                allT = [("T", m) for m in range(8)]
                allS = [("Sbf", m) for m in range(8)]
                P.op("dve", lambda e: e.tensor_scalar(out=T[:].rearrange("p a b -> p (a b)"), in0=T[:].rearrange("p a b -> p (a b)"),
                                                      scalar1=pers["flag"][:, 0:1], scalar2=None, op0=ALU.mult),
                     reads=allT + ["flag"], writes=allT)
                P.op("dve", lambda e: e.tensor_scalar(out=Sbf[:].rearrange("p a b -> p (a b)"), in0=Sbf[:].rearrange("p a b -> p (a b)"),
                                                      scalar1=pers["flag"][:, 0:1], scalar2=None, op0=ALU.mult),
                     reads=allS + ["flag"], writes=allS)
        P.emit(ls, "b")


TWO_PI = 6.283185307179586
PI = 3.141592653589793


def s5_prologue(nc, st, io, scr, pers):
    P = Prog(nc)
    with ExitStack() as ls:
        sb = lambda name, shape, dt=F32: ls.enter_context(nc.sbuf_tensor(name, shape, dt))
        cnt = [0]

        def dve(fn, r, w):
            P.op("dve", fn, reads=r, writes=w)

        def TT(out, a, b, op, r, w):
            dve(lambda e: e.tensor_tensor(out=out, in0=a, in1=b, op=op), r, w)

        def TS(out, a, s1, s2, op0, op1, r, w):
            if op1 is None:
                dve(lambda e: e.tensor_scalar(out=out, in0=a, scalar1=s1, scalar2=None, op0=op0), r, w)
            else:
                dve(lambda e: e.tensor_scalar(out=out, in0=a, scalar1=s1, scalar2=s2, op0=op0, op1=op1), r, w)

        def ACT(out, a, func, r, w, **kw):
            P.op("act", lambda e: e.activation(out=out, in_=a, func=func, **kw), reads=r, writes=w)

        identf = make_iota_mask(P, nc, ls, "identfp", [128, 128], [[1, 128]], 0, -1, ALU.is_equal, "identf")
        maskc = make_iota_mask(P, nc, ls, "maskc", [128, 8, 16], [[16, 8], [0, 16]], 15, -1, ALU.is_ge, "maskc")
        rep = make_iota_mask(P, nc, ls, "rep", [16, 8, 16], [[0, 8], [1, 16]], 0, -1, ALU.is_equal, "rep")
        pt = ls.enter_context(nc.psum_tensor("pqt", [128, 128], F32))
        pm = [ls.enter_context(nc.psum_tensor(f"pqm{i}", [128, 128], F32)) for i in range(2)]

        def transp(out_sb, in_ap, npart, nfree, rk, wk, odt_copy="dve"):
            P.op("pe", lambda e: e.transpose(pt[0:nfree, 0:npart], in_ap, identf[0:npart, 0:npart]),
                 reads=rk + ["identf"], writes=["pt"])
            dve(lambda e: e.tensor_copy(out=out_sb, in_=pt[0:nfree, 0:npart]), ["pt"], wk)

        lt = [sb(f"lt{i}", [64, 128]) for i in range(2)]
        ldt = sb("ldt", [64, 2]); dtx = sb("dtx", [64, 2, 64])
        P.op("sp", lambda e: e.dma_start(out=lt[0][:], in_=io["lam_re"][:, :]), writes=["lt0"], dma="q0_1")
        P.op("sp", lambda e: e.dma_start(out=lt[1][:], in_=io["lam_im"][:, :]), writes=["lt1"], dma="q0_2")
        P.op("sp", lambda e: e.dma_start(out=ldt[:], in_=io["log_dt"][:, :]), writes=["ldt"], dma="q0_3")
        ACT(ldt[:], ldt[:], AF.Exp, ["ldt"], ["ldt"])
        dve(lambda e: e.tensor_copy(out=dtx[:], in_=ldt[:, :, None].to_broadcast([64, 2, 64])), ["ldt"], ["dtx"])
        zt = [sb(f"ztt{i}", [64, 128]) for i in range(2)]
        for i in range(2):
            TT(zt[i][:], lt[i][:], dtx[:].rearrange("p a b -> p (a b)"), ALU.mult, [f"lt{i}", "dtx"], [f"ztt{i}"])
        lam = [sb(f"lam{i}", [128, 64]) for i in range(2)]
        z = [sb(f"z{i}", [128, 64]) for i in range(2)]
        for i in range(2):
            transp(lam[i][:], lt[i][:], 64, 128, [f"lt{i}"], [f"lam{i}"])
            transp(z[i][:], zt[i][:], 64, 128, [f"ztt{i}"], [f"z{i}"])
        ki = sb("ki", [128, 64], I32); kf = sb("kf", [128, 64]); rr = sb("rr", [128, 64]); mm_ = sb("mm_", [128, 64])
        xs_ = sb("xsft", [128, 64])

        def sin_of(out, x_ap, xk, ok):
            TS(ki[:], x_ap, 1.0 / TWO_PI, None, ALU.mult, None, [xk], ["ki"])
            dve(lambda e: e.tensor_copy(out=kf[:], in_=ki[:]), ["ki"], ["kf"])
            dve(lambda e: e.scalar_tensor_tensor(out=rr[:], in0=kf[:], scalar=-TWO_PI, in1=x_ap, op0=ALU.mult, op1=ALU.add),
                ["kf", xk], ["rr"])
            TS(mm_[:], rr[:], PI, -TWO_PI, ALU.is_gt, ALU.mult, ["rr"], ["mm_"])
            TT(rr[:], rr[:], mm_[:], ALU.add, ["rr", "mm_"], ["rr"])
            TS(mm_[:], rr[:], -PI, TWO_PI, ALU.is_lt, ALU.mult, ["rr"], ["mm_"])
            TT(rr[:], rr[:], mm_[:], ALU.add, ["rr", "mm_"], ["rr"])
            ACT(out, rr[:], AF.Sin, ["rr"], [ok])

        sn = sb("sn", [128, 64]); cs = sb("cs", [128, 64]); mag = sb("mag", [128, 64]); imag = sb("imag", [128, 64])
        sin_of(sn[:], z[1][:], "z1", "sn")
        TS(xs_[:], z[1][:], PI / 2, None, ALU.add, None, ["z1"], ["xsft"])
        sin_of(cs[:], xs_[:], "xsft", "cs")
        ACT(mag[:], z[0][:], AF.Exp, ["z0"], ["mag"])
        ACT(imag[:], z[0][:], AF.Exp, ["z0"], ["imag"], scale=-1.0)
        APr = sb("APr", [128, 9, 64]); APi = sb("APi", [128, 9, 64]); AMr = sb("AMr", [128, 8, 64]); AMi = sb("AMi", [128, 8, 64])
        t1 = sb("t1", [128, 64]); t2 = sb("t2", [128, 64])
        for Tn, nm in ((APr, "APr"), (AMr, "AMr")):
            dve(lambda e, Tn=Tn: e.memset(Tn[:, 0, :], 1.0), [], [nm])
        for Tn, nm in ((APi, "APi"), (AMi, "AMi")):
            dve(lambda e, Tn=Tn: e.memset(Tn[:, 0, :], 0.0), [], [nm])
        TT(APr[:, 1, :], mag[:], cs[:], ALU.mult, ["mag", "cs"], ["APr"])
        TT(APi[:, 1, :], mag[:], sn[:], ALU.mult, ["mag", "sn"], ["APi"])
        TT(AMr[:, 1, :], imag[:], cs[:], ALU.mult, ["imag", "cs"], ["AMr"])
        dve(lambda e: e.scalar_tensor_tensor(out=AMi[:, 1, :], in0=imag[:], scalar=-1.0, in1=sn[:], op0=ALU.mult, op1=ALU.mult),
            ["imag", "sn"], ["AMi"])

        def cmul(o_re, o_im, a_re, a_im, b_re, b_im, tmpa, tmpb, r, w):
            TT(tmpa, a_re, b_re, ALU.mult, r, ["cm_a"])
            TT(tmpb, a_im, b_im, ALU.mult, r, ["cm_b"])
            TT(o_re, tmpa, tmpb, ALU.subtract, ["cm_a", "cm_b"], w)
            TT(tmpa, a_re, b_im, ALU.mult, r + w, ["cm_a"])
            TT(tmpb, a_im, b_re, ALU.mult, r + w, ["cm_b"])
            TT(o_im, tmpa, tmpb, ALU.add, ["cm_a", "cm_b"], w)

        for k in range(2, 9):
            cmul(APr[:, k, :], APi[:, k, :], APr[:, k - 1, :], APi[:, k - 1, :], APr[:, 1, :], APi[:, 1, :],
                 t1[:], t2[:], ["APr", "APi"], ["APr", "APi"])
        for k in range(2, 8):
            cmul(AMr[:, k, :], AMi[:, k, :], AMr[:, k - 1, :], AMi[:, k - 1, :], AMr[:, 1, :], AMi[:, 1, :],
                 t1[:], t2[:], ["AMr", "AMi"], ["AMr", "AMi"])
        for c in range(2):
            dve(lambda e, c=c: e.tensor_copy(out=pers["A8c"][:, c, :], in_=APr[:, 8, :]), ["APr"], ["A8c"])
        dve(lambda e: e.tensor_copy(out=pers["A8n"][:, 1, :], in_=APi[:, 8, :]), ["APi"], ["A8n"])
        TS(pers["A8n"][:, 0, :], APi[:, 8, :], -1.0, None, ALU.mult, None, ["APi"], ["A8n"])
        den = sb("den", [128, 64]); nre = sb("nre", [128, 64]); fre = sb("fre", [128, 64]); fim = sb("fim", [128, 64])
        TT(t1[:], lam[0][:], lam[0][:], ALU.mult, ["lam0"], ["t1"])
        TT(t2[:], lam[1][:], lam[1][:], ALU.mult, ["lam1"], ["t2"])
        TT(den[:], t1[:], t2[:], ALU.add, ["t1", "t2"], ["den"])
        dve(lambda e: e.reciprocal(out=den[:], in_=den[:]), ["den"], ["den"])
        TS(nre[:], APr[:, 1, :], -1.0, None, ALU.add, None, ["APr"], ["nre"])
        TT(t1[:], nre[:], lam[0][:], ALU.mult, ["nre", "lam0"], ["t1"])
        TT(t2[:], APi[:, 1, :], lam[1][:], ALU.mult, ["APi", "lam1"], ["t2"])
        TT(fre[:], t1[:], t2[:], ALU.add, ["t1", "t2"], ["fre"])
        TT(fre[:], fre[:], den[:], ALU.mult, ["fre", "den"], ["fre"])
        TT(t1[:], APi[:, 1, :], lam[0][:], ALU.mult, ["APi", "lam0"], ["t1"])
        TT(t2[:], nre[:], lam[1][:], ALU.mult, ["nre", "lam1"], ["t2"])
        TT(fim[:], t1[:], t2[:], ALU.subtract, ["t1", "t2"], ["fim"])
        TT(fim[:], fim[:], den[:], ALU.mult, ["fim", "den"], ["fim"])
        Bt = [sb(f"Bt{i}", [128, 64, 16]) for i in range(2)]
        bb = [sb(f"bb{i}", [128, 64, 16]) for i in range(2)]
        Ct = [sb(f"Ct{i}", [128, 64, 16]) for i in range(2)]
        u1 = sb("u1", [128, 64, 16]); u2 = sb("u2", [128, 64, 16])
        for i, nm in enumerate(("b_re", "b_im")):
            for g2 in range(2):
                P.op("sp", lambda e, i=i, nm=nm, g2=g2: e.dma_start(
                    out=Bt[i][g2 * 64:(g2 + 1) * 64, :, :],
                    in_=io[nm].rearrange("(t g2) p h -> g2 p t h", g2=2)[g2]), writes=[f"Bt{i}"], dma="q1_%d_%d" % (i, g2))
        bc = lambda ap: ap[:, :, None].to_broadcast([128, 64, 16])
        cmul(bb[0][:], bb[1][:], bc(fre[:]), bc(fim[:]), Bt[0][:], Bt[1][:], u1[:], u2[:],
             ["fre", "fim", "Bt0", "Bt1"], ["bb0", "bb1"])
        cin = sb("cin", [128, 128])
        for i, nm in enumerate(("c_re", "c_im")):
            for tb in range(8):
                for tl in range(8):
                    tt_ = tb * 8 + tl
                    P.op("sp", lambda e, nm=nm, tl=tl, tt_=tt_: e.dma_start(
                        out=cin[tl * 16:(tl + 1) * 16, :].rearrange("h (g p) -> h g p", g=2),
                        in_=io[nm][2 * tt_:2 * tt_ + 2].rearrange("g h p -> h g p")), writes=["cin"], dma="q2")
                transp(Ct[i][:, tb * 8:(tb + 1) * 8, :].rearrange("p a b -> p (a b)"), cin[:], 128, 128, ["cin"], [f"Ct{i}"])
        d16 = sb("d16", [128, 16]); dgh = sb("dgh", [16, 128]); dcol = sb("dcol", [128, 128])
        P.op("sp", lambda e: e.dma_start(out=d16[:], in_=io["s5_d"][:, :]), writes=["d16"], dma="q0_4")
        transp(dgh[:], d16[:], 128, 16, ["d16"], ["dgh"])
        P.op("pe", lambda e: e.matmul(pt[:, :], lhsT=rep[:].rearrange("p a b -> p (a b)"), rhs=dgh[:], start=True, stop=True),
             reads=["rep", "dgh"], writes=["pt"])
        dve(lambda e: e.tensor_copy(out=dcol[:], in_=pt[:, :]), ["pt"], ["dcol"])
        bg16 = sb("bg16", [16, 128])
        P.op("sp", lambda e: e.dma_start(out=bg16[:], in_=io["b_glu"][:, :]), writes=["bg16"], dma="q0_5")
        transp(pers["bgluc"][:], bg16[:], 16, 128, ["bg16"], ["bgluc"])
        NB = 8
        X = [sb(f"X{i}", [128, NB, 8, 16]) for i in range(2)]
        Y = [sb(f"Y{i}", [128, NB, 8, 16]) for i in range(2)]
        Zm = [sb(f"Zm{i}", [128, NB, 8, 16]) for i in range(2)]
        Fb = sb("Fb", [128, NB, 2, 128], BF16)
        ZTb = sb("ZTb", [128, NB, 2, 128], BF16)
        Mb = sb("Mb", [128, 2 * NB, 128], BF16)
        w1 = sb("w1", [128, NB, 16]); w2 = sb("w2", [128, NB, 16]); mtmp = sb("mtmp", [128, 128])
        for q in range(64 // NB):
            ts = slice(q * NB, (q + 1) * NB)
            bcp = lambda ap: ap[:, :, None].to_broadcast([128, NB, 16])
            for j in range(8):
                cmul(X[0][:, :, j, :], X[1][:, :, j, :], bcp(AMr[:, j, ts]), bcp(AMi[:, j, ts]), bb[0][:, ts, :], bb[1][:, ts, :],
                     w1[:], w2[:], ["AMr", "AMi", "bb0", "bb1"], ["X0", "X1"])
                cmul(Y[0][:, :, j, :], Y[1][:, :, j, :], bcp(APr[:, j, ts]), bcp(APi[:, j, ts]), Ct[0][:, ts, :], Ct[1][:, ts, :],
                     w1[:], w2[:], ["APr", "APi", "Ct0", "Ct1"], ["Y0", "Y1"])
                cmul(Zm[0][:, :, j, :], Zm[1][:, :, j, :], bcp(APr[:, 7 - j, ts]), bcp(APi[:, 7 - j, ts]), bb[0][:, ts, :], bb[1][:, ts, :],
                     w1[:], w2[:], ["APr", "APi", "bb0", "bb1"], ["Z0", "Z1"])
                TT(w1[:], bcp(APr[:, j + 1, ts]), Ct[0][:, ts, :], ALU.mult, ["APr", "Ct0"], ["cm_a"])
                TT(w2[:], bcp(APi[:, j + 1, ts]), Ct[1][:, ts, :], ALU.mult, ["APi", "Ct1"], ["cm_b"])
                TT(Fb[:, :, 0, j * 16:(j + 1) * 16], w1[:], w2[:], ALU.subtract, ["cm_a", "cm_b"], ["Fb"])
                TT(w1[:], bcp(APr[:, j + 1, ts]), Ct[1][:, ts, :], ALU.mult, ["APr", "Ct1", "Fb"], ["cm_a"])
                TT(w2[:], bcp(APi[:, j + 1, ts]), Ct[0][:, ts, :], ALU.mult, ["APi", "Ct0", "Fb"], ["cm_b"])
                dve(lambda e, j=j: e.scalar_tensor_tensor(out=Fb[:, :, 1, j * 16:(j + 1) * 16], in0=w1[:], scalar=-1.0, in1=w2[:],
                                                          op0=ALU.mult, op1=ALU.subtract), ["cm_a", "cm_b"], ["Fb"])
            TS(X[1][:], X[1][:], -1.0, None, ALU.mult, None, ["X1"], ["X1"])
            for tl in range(NB):
                t = q * NB + tl
                for c in range(2):
                    P.op("pe", lambda e, c=c, tl=tl: e.transpose(pt[:, :], Zm[c][:, tl, :, :].rearrange("p a b -> p (a b)"), identf[:]),
                         reads=[f"Z{c}", "identf"], writes=["pt"])
                    dve(lambda e, c=c, tl=tl: e.tensor_copy(out=ZTb[:, tl, c, :], in_=pt[:, :]), ["pt"], ["ZTb"])
                for g2 in range(2):
                    rows = slice(g2 * 64, (g2 + 1) * 64)
                    pmb = pm[g2]
                    for c in range(2):
                        P.op("pe", lambda e, c=c, tl=tl, rows=rows, pmb=pmb: e.matmul(
                            pmb[:, :], lhsT=X[c][rows, tl, :, :].rearrange("p a b -> p (a b)"),
                            rhs=Y[c][rows, tl, :, :].rearrange("p a b -> p (a b)"), start=(c == 0), stop=(c == 1)),
                            reads=["X0", "X1", "Y0", "Y1"], writes=[("pm", g2)])
                    g = 2 * t + g2
                    TT(mtmp[:], pmb[:, :], maskc[:].rearrange("p a b -> p (a b)"), ALU.mult, [("pm", g2), "maskc"], ["mtmp"])
                    dve(lambda e, g=g, tl=tl, g2=g2: e.scalar_tensor_tensor(
                        out=Mb[:, tl * 2 + g2, :], in0=identf[:], scalar=dcol[:, g:g + 1], in1=mtmp[:],
                        op0=ALU.mult, op1=ALU.add), ["mtmp", "identf", "dcol"], ["Mb"])
            P.op("sp", lambda e, q=q: e.dma_start(out=scr["mint"][q * 2 * NB:(q + 1) * 2 * NB].rearrange("g a b -> a g b"), in_=Mb[:]),
                 reads=["Mb"], writes=[("mint", q)], dma="q3m")
            P.op("sp", lambda e, q=q: e.dma_start(out=scr["zt"][q * NB:(q + 1) * NB].rearrange("t c a b -> a t c b"), in_=ZTb[:]),
                 reads=["ZTb"], writes=[("zts", q)], dma="q3z")
            P.op("sp", lambda e, q=q: e.dma_start(out=scr["ff"][q * NB:(q + 1) * NB].rearrange("t c a b -> a t c b"), in_=Fb[:]),
                 reads=["Fb"], writes=[("ffs", q)], dma="q3f")
        P.emit(ls, "q")


import os
S2_STOP = int(os.environ.get("S2_STOP", "0"))


def sweep2(nc, st, io, scr, pers, L, Lp=0):
    P = Prog(nc)
    TB = 256
    NCH = TB // 8
    nblk = L // TB
    GELU_C = 1.5957691216057308
    with ExitStack() as ls:
        sb = lambda name, shape, dt=F32: ls.enter_context(nc.sbuf_tensor(name, shape, dt))
        identf = make_iota_mask(P, nc, ls, "identf2", [128, 128], [[1, 128]], 0, -1, ALU.is_equal, "identf")
        identb = sb("identb2", [128, 128], BF16)
        P.op("dve", lambda e: e.tensor_copy(out=identb[:], in_=identf[:]), reads=["identf"], writes=["identb"])
        xs = [sb("xs2_0", [128, 4096]), sb("xs2_1", [128, 4096])]
        hT = sb("hT2", [128, 32, TB], BF16)
        ring = Ring(nc, ls, "w2_", 3, [128, 8, 512], BF16)
        uT = sb("uT", [128, 4, TB], BF16)
        UUs = [sb(f"UU{i}", [NCH, 8, 8, 16], BF16) for i in range(2)]
        UD = sb("UD", [128, 128, NCH], BF16)
        ZTc = [sb(f"ZTc{i}", [128, 8, 2, 128], BF16) for i in range(2)]
        Fc = [sb(f"Fc{i}", [128, 8, 2, 128], BF16) for i in range(2)]
        Mc = [sb(f"Mc{i}", [128, 16, 128], BF16) for i in range(2)]
        W = sb("Wst", [128, NCH, 2, 64])
        Sbf = [sb(f"Sbf2_{i}", [128, NCH + 1, 2, 64], BF16) for i in range(2)]
        for i_ in range(2):
            P.op("pool", lambda e, i_=i_: e.memset(Sbf[i_][:], 0.0), writes=["Sbf"])
        carry = sb("carry", [128, 2, 64])
        tA = sb("tA", [128, 2, 64]); tBm = sb("tBm", [128, 2, 64])
        ysbs = [sb(f"ysb{i}", [128, 8 * NCH]) for i in range(2)]
        ysqs = [sb(f"ysq{i}", [128, 8 * NCH]) for i in range(2)]
        ysgs = [sb(f"ysg{i}", [128, 8 * NCH]) for i in range(2)]
        YDs = [sb(f"YD{i}", [128, 8, NCH], BF16) for i in range(2)]
        YYs = [sb(f"YY{i}", [NCH, 8, 8, 16], BF16) for i in range(2)]
        ygT = sb("ygT", [128, 16, TB], BF16)
        sg = sb("sg2", [128, TB]); zs = sb("zs2", [128, TB])
        oT5 = sb("oT5", [128, 4, TB], BF16)
        B = [ls.enter_context(nc.psum_tensor(f"C{i}", [128, 512], F32)) for i in range(4)]
        B5 = ls.enter_context(nc.psum_tensor("C5", [128, 1024], BF16))
        B5b = ls.enter_context(nc.psum_tensor("C5b", [128, 1024], BF16))
        B6 = ls.enter_context(nc.psum_tensor("C6", [128, 512], F32))
        B7 = ls.enter_context(nc.psum_tensor("C7", [128, 512], F32))
        P.op("dve", lambda e: e.memset(carry[:], 0.0), writes=["carry"])

        def wload(src, r0, nkc, c0, ncol, rk):
            s, wt, wk = ring.next()
            view = wt[:, 0:nkc, 0:ncol]
            P.op("sp", lambda e: e.dma_start(out=view, in_=dram_rows(src, r0, nkc, c0, ncol)), reads=rk, writes=[wk], dma="w2_%d" % s)
            return view, wk

        blocks = [("pre", i) for i in range(Lp // TB)] + [("own", i) for i in range(nblk)]
        for mode, blk in blocks:
            own = mode == "own"
            last_pre = (mode == "pre" and blk == Lp // TB - 1)
            xsrc = io["x"] if own else io["xp"]
            t0 = blk * TB
            for i in range(TB // 128):
                xb = xs[i % 2]
                xkey = ("xs", i % 2)
                P.op("pool", lambda e, i=i, t0=t0, xsrc=xsrc, xb=xb: e.dma_start(out=xb[:], in_=xsrc[t0 + i * 128:t0 + (i + 1) * 128, :]),
                     writes=[xkey], dma="x2_%d" % (i % 2))
                for kg in range(8):
                    b = kg % 4
                    for k4 in range(4):
                        kc = kg * 4 + k4
                        P.op("pe", lambda e, b=b, k4=k4, kc=kc, xb=xb: e.transpose(
                            B[b][:, k4 * 128:(k4 + 1) * 128], xb[:, kc * 128:(kc + 1) * 128], identf[:]),
                            reads=[xkey, "identf"], writes=[("B", b)])
                    for k4 in range(4):
                        kc = kg * 4 + k4
                        P.op("act", lambda e, b=b, k4=k4, kc=kc, i=i: e.activation(
                            out=hT[:, kc, i * 128:(i + 1) * 128], in_=B[b][:, k4 * 128:(k4 + 1) * 128],
                            func=AF.Identity, bias=pers["sh"][:, kc:kc + 1], scale=pers["s1p"][:, kc:kc + 1]),
                            reads=[("B", b)], writes=[("hT", kc // 8)])
            for cg in range(4):
                for kq in range(4):
                    view, wk = wload(scr["win"], kq * 8, 8, C_U + cg * 512, 512, [])
                    for c4 in range(4):
                        for k8 in range(8):
                            kc = kq * 8 + k8
                            P.op("pe", lambda e, view=view, c4=c4, k8=k8, kc=kc: e.matmul(
                                B[c4][:, 0:TB], lhsT=view[:, k8, c4 * 128:(c4 + 1) * 128], rhs=hT[:, kc, :],
                                start=(kc == 0), stop=(kc == 31)),
                                reads=[wk, ("hT", kc // 8)], writes=[("B", c4)])
                for c4 in range(4):
                    ct = cg * 4 + c4
                    pp = ct % 2
                    UU = UUs[pp]
                    T5, T5k = (B5, "B5") if pp == 0 else (B6[:].bitcast(BF16), "B67_0")
                    T5b, T5bk = (B5b, "B5b") if pp == 0 else (B7[:].bitcast(BF16), "B67_1")
                    T5 = T5 if pp == 1 else T5[:]
                    T5b = T5b if pp == 1 else T5b[:]
                    P.op("act", lambda e, c4=c4: e.activation(out=uT[:, c4, :], in_=B[c4][:, 0:TB], func=AF.Copy),
                         reads=[("B", c4)], writes=[("uT", c4)])
                    for j in range(8):
                        P.op("pe", lambda e, c4=c4, j=j, T5=T5: e.transpose(
                            T5[0:NCH, j * 128:(j + 1) * 128], uT[:, c4, j::8], identb[:]),
                            reads=[("uT", c4), "identb"], writes=[T5k])
                    P.op("act", lambda e, UU=UU, T5=T5: e.activation(
                        out=UU[:].rearrange("n g j h -> n j g h"),
                        in_=T5[0:NCH, :].rearrange("n (j g h) -> n j g h", j=8, g=8), func=AF.Copy), reads=[T5k], writes=[("UU", pp)])
                    for gl in range(8):
                        P.op("pe", lambda e, gl=gl, UU=UU, T5b=T5b: e.transpose(
                            T5b[:, gl * NCH:(gl + 1) * NCH], UU[:, gl, :, :].rearrange("n j h -> n (j h)"), identb[0:NCH, 0:NCH]),
                            reads=[("UU", pp), "identb"], writes=[T5bk])
                    P.op("act", lambda e, ct=ct, T5b=T5b: e.activation(
                        out=UD[:, ct * 8:(ct + 1) * 8, :].rearrange("p g n -> p (g n)"), in_=T5b[:, 0:8 * NCH], func=AF.Copy),
                        reads=[T5bk], writes=[("UD", ct)])
            if S2_STOP == 1:
                continue
            for q in range(8):
                zc = ZTc[q % 2]
                P.op("sp", lambda e, zc=zc, q=q: e.dma_start(out=zc[:], in_=scr["zt"][q * 8:(q + 1) * 8].rearrange("t c a b -> a t c b")),
                     writes=[("ZTc", q % 2)], dma="m2z%d" % (q % 2))
                for tl in range(8):
                    t = q * 8 + tl
                    for g2 in range(2):
                        for c, Bc in ((0, B6), (1, B7)):
                            P.op("pe", lambda e, zc=zc, tl=tl, g2=g2, c=c, Bc=Bc, t=t: e.matmul(
                                Bc[g2 * 64:(g2 + 1) * 64, tl * NCH:(tl + 1) * NCH], lhsT=zc[:, tl, c, g2 * 64:(g2 + 1) * 64],
                                rhs=UD[:, 2 * t + g2, :], start=True, stop=True),
                                reads=[("ZTc", q % 2), ("UD", (2 * t + g2) // 8)], writes=["B67_%d" % c])
                for c, Bc in ((0, B6), (1, B7)):
                    P.op("act", lambda e, c=c, Bc=Bc, q=q: e.activation(
                        out=W[:, :, c, q * 8:(q + 1) * 8].rearrange("p n t -> p t n"),
                        in_=Bc[:, 0:8 * NCH].rearrange("p (t n) -> p t n", t=8), func=AF.Copy),
                        reads=["B67_%d" % c], writes=["W"])
            if S2_STOP == 2:
                continue
            for g2_ in range(2):
                hp = slice(g2_ * 64, (g2_ + 1) * 64)
                P.op("act", lambda e, g2_=g2_, hp=hp: e.activation(out=Sbf[g2_][hp, 0, :, :], in_=carry[hp], func=AF.Copy),
                     reads=["carry"], writes=["Sbf"])
            for n in range(NCH):
                prev = carry[:] if n == 0 else W[:, n - 1, :, :]
                P.op("dve", lambda e, prev=prev: e.tensor_tensor(out=tA[:], in0=pers["A8c"][:], in1=prev, op=ALU.mult),
                     reads=["W", "carry"], writes=["tA"])
                P.op("dve", lambda e, prev=prev: e.tensor_tensor(out=tBm[:, 0, :], in0=pers["A8n"][:, 0, :], in1=prev[:, 1, :], op=ALU.mult),
                     reads=["W", "carry"], writes=["tB"])
                P.op("dve", lambda e, prev=prev: e.tensor_tensor(out=tBm[:, 1, :], in0=pers["A8n"][:, 1, :], in1=prev[:, 0, :], op=ALU.mult),
                     reads=["W", "carry"], writes=["tB"])
                P.op("dve", lambda e, n=n: e.tensor_tensor(out=W[:, n, :, :], in0=W[:, n, :, :], in1=tA[:], op=ALU.add),
                     reads=["W", "tA"], writes=["W"])
                P.op("dve", lambda e, n=n: e.tensor_tensor(out=W[:, n, :, :], in0=W[:, n, :, :], in1=tBm[:], op=ALU.add),
                     reads=["W", "tB"], writes=["W"])
            P.op("dve", lambda e: e.tensor_copy(out=carry[:], in_=W[:, NCH - 1, :, :]), reads=["W"], writes=["carry"])
            if last_pre:
                P.op("dve", lambda e: e.tensor_scalar(out=carry[:].rearrange("p a b -> p (a b)"), in0=carry[:].rearrange("p a b -> p (a b)"),
                                                      scalar1=pers["flag"][:, 0:1], scalar2=None, op0=ALU.mult),
                     reads=["carry", "flag"], writes=["carry"])
            if not own:
                continue
            for g2_ in range(2):
                hp = slice(g2_ * 64, (g2_ + 1) * 64)
                P.op("act", lambda e, g2_=g2_, hp=hp: e.activation(
                    out=Sbf[g2_][hp, 1:NCH + 1, :, :].rearrange("p n c t -> p (n c t)"),
                    in_=W[hp].rearrange("p n c t -> p (n c t)"), func=AF.Copy),
                    reads=["W"], writes=["Sbf"])
            if S2_STOP == 3:
                continue
            for ct in range(16):
                q = ct // 2
                if ct % 2 == 0:
                    fc, mc = Fc[q % 2], Mc[q % 2]
                    P.op("sp", lambda e, fc=fc, q=q: e.dma_start(out=fc[:], in_=scr["ff"][q * 8:(q + 1) * 8].rearrange("t c a b -> a t c b")),
                         writes=[("Fc", q % 2)], dma="m2f%d" % (q % 2))
                    P.op("sp", lambda e, mc=mc, q=q: e.dma_start(out=mc[:], in_=scr["mint"][q * 16:(q + 1) * 16].rearrange("g a b -> a g b")),
                         writes=[("Mc", q % 2)], dma="m2m%d" % (q % 2))
                yb = B[ct % 2]
                for gl in range(8):
                    g = ct * 8 + gl
                    t, g2 = g // 2, g % 2
                    tl = t - q * 8
                    rows = slice(g2 * 64, (g2 + 1) * 64)
                    osl = yb[:, gl * NCH:(gl + 1) * NCH]
                    P.op("pe", lambda e, osl=osl, mc=mc, g=g, q=q: e.matmul(osl, lhsT=mc[:, g - q * 16, :], rhs=UD[:, g, :],
                                                                       start=True, stop=False),
                         reads=[("Mc", q % 2), ("UD", ct)], writes=[("B", ct % 2)])
                    for c in range(2):
                        P.op("pe", lambda e, osl=osl, fc=fc, tl=tl, c=c, g2=g2, t=t: e.matmul(
                            osl, lhsT=fc[:, tl, c, :], rhs=Sbf[g2][:, 0:NCH, c, t], start=False, stop=(c == 1)),
                            reads=[("Fc", q % 2), "Sbf"], writes=[("B", ct % 2)])
                ysl = yb[:, 0:8 * NCH]
                pp = ct % 2
                ysb, ysq, ysg, YD, YY = ysbs[pp], ysqs[pp], ysgs[pp], YDs[pp], YYs[pp]
                T5, T5k = (B5[:], "B5") if pp == 0 else (B6[:].bitcast(BF16), "B67_0")
                T5b, T5bk = (B5b[:], "B5b") if pp == 0 else (B7[:].bitcast(BF16), "B67_1")
                kk = lambda s: (s, pp)
                P.op("act", lambda e, ysl=ysl, ysb=ysb: e.activation(out=ysb[:], in_=ysl, func=AF.Copy), reads=[("B", ct % 2)], writes=[kk("ysb")])
                P.op("act", lambda e, ysl=ysl, ysq=ysq: e.activation(out=ysq[:], in_=ysl, func=AF.Square), reads=[("B", ct % 2)], writes=[kk("ysq")])
                P.op("dve", lambda e, ysq=ysq: e.tensor_scalar(out=ysq[:], in0=ysq[:], scalar1=0.044715, scalar2=1.0, op0=ALU.mult, op1=ALU.add),
                     reads=[kk("ysq")], writes=[kk("ysq")])
                P.op("dve", lambda e, ysq=ysq, ysb=ysb: e.tensor_tensor(out=ysq[:], in0=ysq[:], in1=ysb[:], op=ALU.mult),
                     reads=[kk("ysq"), kk("ysb")], writes=[kk("ysq")])
                P.op("act", lambda e, ysq=ysq, ysg=ysg: e.activation(out=ysg[:], in_=ysq[:], func=AF.Sigmoid, scale=GELU_C),
                     reads=[kk("ysq")], writes=[kk("ysg")])
                P.op("dve", lambda e, YD=YD, ysb=ysb, ysg=ysg: e.tensor_tensor(out=YD[:].rearrange("p g n -> p (g n)"), in0=ysb[:], in1=ysg[:], op=ALU.mult),
                     reads=[kk("ysb"), kk("ysg")], writes=[kk("YD")])
                for gl in range(8):
                    P.op("pe", lambda e, gl=gl, T5=T5, YD=YD: e.transpose(T5[0:NCH, gl * 128:(gl + 1) * 128], YD[:, gl, :], identb[:]),
                         reads=[kk("YD"), "identb"], writes=[T5k])
                P.op("dve", lambda e, YY=YY, T5=T5: e.tensor_copy(out=YY[:].rearrange("n j g h -> n g j h"),
                                                    in_=T5[0:NCH, :].rearrange("n (g j h) -> n g j h", g=8, j=8)),
                     reads=[T5k], writes=[kk("YY")])
                for j in range(8):
                    P.op("pe", lambda e, j=j, YY=YY, T5b=T5b: e.transpose(T5b[:, j * NCH:(j + 1) * NCH], YY[:, j, :, :].rearrange("n g h -> n (g h)"), identb[0:NCH, 0:NCH]),
                         reads=[kk("YY"), "identb"], writes=[T5bk])
                P.op("act", lambda e, ct=ct, T5b=T5b: e.activation(
                    out=ygT[:, ct, :].rearrange("p (n j) -> p j n", j=8),
                    in_=T5b[:, 0:8 * NCH].rearrange("p (j n) -> p j n", j=8), func=AF.Copy),
                    reads=[T5bk], writes=[("ygT", ct)])
            if S2_STOP == 4:
                continue
            for cgo in range(4):
                for kq in range(2):
                    view, wk = wload(scr["wglu"], kq * 8, 8, cgo * 512, 512, [])
                    for c4 in range(4):
                        for k8 in range(8):
                            kc = kq * 8 + k8
                            P.op("pe", lambda e, view=view, c4=c4, k8=k8, kc=kc: e.matmul(
                                B[c4][:, 0:TB], lhsT=view[:, k8, c4 * 128:(c4 + 1) * 128], rhs=ygT[:, kc, :],
                                start=(kc == 0), stop=(kc == 15)),
                                reads=[wk, ("ygT", kc)], writes=[("B", c4)])
                for kq in range(4):
                    view, wk = wload(scr["win"], kq * 8, 8, C_ZS + cgo * 512, 512, [])
                    for c4 in range(4):
                        for k8 in range(8):
                            kc = kq * 8 + k8
                            P.op("pe", lambda e, view=view, c4=c4, k8=k8, kc=kc: e.matmul(
                                B[c4][:, TB:2 * TB], lhsT=view[:, k8, c4 * 128:(c4 + 1) * 128], rhs=hT[:, kc, :],
                                start=(kc == 0), stop=(kc == 31)),
                                reads=[wk, ("hT", kc // 8)], writes=[("B", c4)])
                for c4 in range(4):
                    ct = cgo * 4 + c4
                    P.op("act", lambda e, c4=c4, ct=ct: e.activation(out=sg[:], in_=B[c4][:, 0:TB], func=AF.Sigmoid,
                                                                    bias=pers["bgluc"][:, ct:ct + 1]),
                         reads=[("B", c4)], writes=["sg"])
                    P.op("act", lambda e, c4=c4: e.activation(out=zs[:], in_=B[c4][:, TB:2 * TB], func=AF.Sigmoid),
                         reads=[("B", c4)], writes=["zs"])
                    P.op("dve", lambda e, c4=c4: e.tensor_tensor(out=zs[:], in0=zs[:], in1=B[c4][:, TB:2 * TB], op=ALU.mult),
                         reads=["zs", ("B", c4)], writes=["zs"])
                    P.op("dve", lambda e, ct=ct: e.tensor_tensor(out=sg[:], in0=sg[:], in1=ygT[:, ct, :], op=ALU.mult),
                         reads=["sg", ("ygT", ct)], writes=["sg"])
                    P.op("dve", lambda e, c4=c4: e.tensor_tensor(out=oT5[:, c4, :], in0=sg[:], in1=zs[:], op=ALU.mult),
                         reads=["sg", "zs"], writes=["oT5"])
                P.op("pool", lambda e, cgo=cgo, t0=t0: e.dma_start(out=dram_rows(scr["oT"], 16 + cgo * 4, 4, t0, TB), in_=oT5[:]),
                     reads=["oT5"], writes=[("oTs", blk, cgo)], dma="o2")
        P.emit(ls, "d")


def sweep3(nc, st, io, scr, L):
    P = Prog(nc)
    nblk = L // 512
    with ExitStack() as ls:
        oT = ls.enter_context(nc.sbuf_tensor("oT3", [128, 32, 512], BF16))
        r = [ls.enter_context(nc.sbuf_tensor(f"r3_{i}", [128, 4096], F32)) for i in range(4)]
        lng = ls.enter_context(nc.sbuf_tensor("lng", [128, 4096], F32))
        lnb = ls.enter_context(nc.sbuf_tensor("lnb", [128, 4096], F32))
        stats = [ls.enter_context(nc.sbuf_tensor(f"st3_{i}", [128, 8, 6], F32)) for i in range(4)]
        mv = [ls.enter_context(nc.sbuf_tensor(f"mv3_{i}", [128, 4], F32)) for i in range(4)]
        ring = Ring(nc, ls, "w3_", 4, [128, 8, 512], BF16)
        ps = [ls.enter_context(nc.psum_tensor(f"ps3_{i}", [128, 512], F32)) for i in range(8)]
        P.op("sp", lambda e: e.dma_start(out=lng[:], in_=io["ln_g"][0:1, :].broadcast_to([128, 4096])),
             writes=["lng"], dma="c3_1")
        P.op("sp", lambda e: e.dma_start(out=lnb[:], in_=io["ln_b"][0:1, :].broadcast_to([128, 4096])),
             writes=["lnb"], dma="c3_2")
        for blk in range(nblk):
            t0 = blk * 512
            for q in range(4):
                P.op("sp", lambda e, q=q, t0=t0: e.dma_start(
                    out=oT[:, q * 8:(q + 1) * 8, :], in_=dram_rows(scr["oT"], q * 8, 8, t0, 512)),
                    writes=[("oT", q)], dma="oT_%d" % q)
            for i in range(4):
                P.op("pool", lambda e, i=i, t0=t0: e.dma_start(out=r[i][:], in_=io["x"][t0 + i * 128:t0 + (i + 1) * 128, :]),
                     writes=[("r", i, c) for c in range(8)], dma="x3_%d" % i)
            for cg in range(8):
                for kq in range(4):
                    s, wt, wk = ring.next()
                    P.op("sp", lambda e, wt=wt, kq=kq, cg=cg: e.dma_start(
                        out=wt[:], in_=dram_rows(scr["wout"], kq * 8, 8, cg * 512, 512)),
                        writes=[wk], dma="w3_%d" % s)
                    for i in range(4):
                        b = (cg % 2) * 4 + i
                        for k8 in range(8):
                            kc = kq * 8 + k8
                            P.op("pe", lambda e, b=b, kc=kc, i=i, wt=wt, k8=k8: e.matmul(
                                ps[b][:], lhsT=oT[:, kc, i * 128:(i + 1) * 128], rhs=wt[:, k8, :],
                                start=(kc == 0), stop=(kc == 31)),
                                reads=[wk, ("oT", kq)], writes=[("ps", b)])
                for i in range(4):
                    b = (cg % 2) * 4 + i
                    sl = slice(cg * 512, (cg + 1) * 512)
                    P.op("dve", lambda e, i=i, b=b, sl=sl: e.scalar_tensor_tensor(
                        out=r[i][:, sl], in0=r[i][:, sl], scalar=ALPHA, in1=ps[b][:], op0=ALU.mult, op1=ALU.add),
                        reads=[("ps", b), ("r", i, cg)], writes=[("r", i, cg)])
                    P.op("dve", lambda e, i=i, cg=cg, sl=sl: e.bn_stats(out=stats[i][:, cg, :], in_=r[i][:, sl]),
                         reads=[("r", i, cg)], writes=[("st", i)])
            for i in range(4):
                rk = [("r", i, c) for c in range(8)]
                P.op("dve", lambda e, i=i: e.bn_aggr(out=mv[i][:, 0:2], in_=stats[i][:].rearrange("p a b -> p (a b)")),
                     reads=[("st", i)], writes=[("mv", i)])
                P.op("dve", lambda e, i=i: e.tensor_scalar(out=mv[i][:, 2:3], in0=mv[i][:, 1:2], scalar1=EPS,
                                                           scalar2=None, op0=ALU.add),
                     reads=[("mv", i)], writes=[("mv", i)])
                P.op("act", lambda e, i=i: e.activation(out=mv[i][:, 2:3], in_=mv[i][:, 2:3], func=AF.Sqrt),
                     reads=[("mv", i)], writes=[("mv", i)])
                P.op("dve", lambda e, i=i: e.reciprocal(out=mv[i][:, 2:3], in_=mv[i][:, 2:3]),
                     reads=[("mv", i)], writes=[("mv", i)])
                P.op("dve", lambda e, i=i: e.tensor_scalar(out=mv[i][:, 3:4], in0=mv[i][:, 0:1], scalar1=mv[i][:, 2:3],
                                                           scalar2=-1.0, op0=ALU.mult, op1=ALU.mult),
                     reads=[("mv", i)], writes=[("mv", i)])
                P.op("act", lambda e, i=i: e.activation(out=r[i][:], in_=r[i][:], func=AF.Identity,
                                                        bias=mv[i][:, 3:4], scale=mv[i][:, 2:3]),
                     reads=[("mv", i)] + rk, writes=rk)
                P.op("pool", lambda e, i=i: e.tensor_tensor(out=r[i][:], in0=r[i][:], in1=lng[:], op=ALU.mult),
                     reads=rk + ["lng"], writes=rk)
                P.op("dve", lambda e, i=i: e.tensor_tensor(out=r[i][:], in0=r[i][:], in1=lnb[:], op=ALU.add),
                     reads=rk + ["lnb"], writes=rk)
                P.op("pool", lambda e, i=i, t0=t0: e.dma_start(out=io["y"][t0 + i * 128:t0 + (i + 1) * 128, :], in_=r[i][:]),
                     reads=rk, writes=[("y", blk, i)], dma="y3_%d" % i)
        P.emit(ls, "c")


def build(L, stages=("p0", "s1", "s2", "s3"), dbg=False, Lp=0):
    nc = bass.Bass("TRN2", target_bir_lowering=False)
    io = {}

    def inp(name, shape):
        io[name] = nc.dram_tensor(name, shape, F32, kind="ExternalInput").ap()
    inp("x", [L, D]); inp("c", [32, 128]); inp("w_ada", [D, 3 * D]); inp("b_ada", [96, 128])
    if Lp:
        inp("xp", [Lp, D]); inp("flag", [128, 1])
    inp("w_in", [D, DIN]); inp("w_gate", [16, 1024]); inp("b_gate", [1, 1024]); inp("gnorm", [1, 512])
    inp("lam_re", [64, 128]); inp("lam_im", [64, 128]); inp("log_dt", [64, 2])
    inp("b_re", [128, 64, 16]); inp("b_im", [128, 64, 16]); inp("c_re", [128, 16, 64]); inp("c_im", [128, 16, 64])
    inp("s5_d", [128, 16]); inp("w_glu", [2048, 2048]); inp("b_glu", [16, 128])
    inp("w_out", [D, D]); inp("ln_g", [1, D]); inp("ln_b", [1, D])
    io["y"] = nc.dram_tensor("y", [L, D], F32, kind="ExternalOutput").ap()
    scr = {}
    okind = ("ExternalInput" if ("s1" not in stages and "s2" not in stages and "s2p" not in stages) else "ExternalOutput") if dbg else "Internal"
    scr["oT"] = nc.dram_tensor("oT_scr", [D, L], BF16, kind=okind).ap()
    scr["win"] = nc.dram_tensor("win_scr", [D, DIN], BF16).ap()
    scr["wglu"] = nc.dram_tensor("wglu_scr", [2048, 2048], BF16).ap()
    scr["wout"] = nc.dram_tensor("wout_scr", [D, D], BF16).ap()
    scr["gate"] = nc.dram_tensor("gate_scr", [32, 128], F32).ap()
    scr["mint"] = nc.dram_tensor("mint_scr", [128, 128, 128], BF16).ap()
    scr["zt"] = nc.dram_tensor("zt_scr", [64, 2, 128, 128], BF16).ap()
    scr["ff"] = nc.dram_tensor("ff_scr", [64, 2, 128, 128], BF16).ap()
    with ExitStack() as st:
        pers = {}
        pers["sh"] = st.enter_context(nc.sbuf_tensor("sh_c", [128, 32], F32))
        pers["s1p"] = st.enter_context(nc.sbuf_tensor("s1p_c", [128, 32], F32))
        pers["flag"] = st.enter_context(nc.sbuf_tensor("flagt", [128, 1], F32))
        pers["A8c"] = st.enter_context(nc.sbuf_tensor("A8c", [128, 2, 64], F32))
        pers["A8n"] = st.enter_context(nc.sbuf_tensor("A8n", [128, 2, 64], F32))
        pers["bgluc"] = st.enter_context(nc.sbuf_tensor("bgluc", [128, 16], F32))
        if "p0" in stages:
            phase0(nc, st, io, scr, pers, L, Lp)
        if "s1" in stages:
            sweep1(nc, st, io, scr, pers, L, Lp)
        if "s2" in stages or "s2p" in stages:
            s5_prologue(nc, st, io, scr, pers)
        if "s2" in stages:
            sweep2(nc, st, io, scr, pers, L, Lp)
        if "s3" in stages:
            sweep3(nc, st, io, scr, L)
    return nc


def make_in_map(inputs, b, L, s=0, Lp=0):
    f = lambda a: np.ascontiguousarray(a, dtype=np.float32)
    i = inputs
    m = {
        "x": f(i["x"][b, s * L:(s + 1) * L]), "c": f(i["c"][b].reshape(32, 128)), "w_ada": f(i["w_ada"][0]),
        "b_ada": f(i["b_ada"][0].reshape(96, 128)), "w_in": f(i["w_in"][0]), "w_gate": f(i["w_gla_gate"][0]),
        "b_gate": f(i["b_gla_gate"][0].reshape(1, 1024)), "gnorm": f(i["gla_norm_g"][0].reshape(1, 512)),
        "lam_re": f(i["s5_lambda_re"][0].reshape(64, 128)), "lam_im": f(i["s5_lambda_im"][0].reshape(64, 128)),
        "log_dt": f(i["s5_log_dt"][0].reshape(64, 2)), "b_re": f(i["s5_b_re"][0]), "b_im": f(i["s5_b_im"][0]),
        "c_re": f(i["s5_c_re"][0]), "c_im": f(i["s5_c_im"][0]), "s5_d": f(i["s5_d"][0].reshape(128, 16)),
        "w_glu": f(i["w_glu"][0]), "b_glu": f(i["b_glu"][0].reshape(16, 128)), "w_out": f(i["w_out"][0]),
        "ln_g": f(i["ln_g"][0].reshape(1, D)), "ln_b": f(i["ln_b"][0].reshape(1, D)),
    }
    if Lp:
        m["xp"] = f(i["x"][b, 0:Lp])
        m["flag"] = np.full((128, 1), float(s), dtype=np.float32)
    return m


def kernel(**inputs):
    B, S = inputs["x"].shape[0], inputs["x"].shape[1]
    L = S // 2
    nc = build(L, Lp=L)
    in_maps = [make_in_map(inputs, b, L, s, L) for b in range(B) for s in range(2)]
    res = run_bass_kernel_spmd(nc, in_maps, core_ids=list(range(2 * B)))
    out = np.empty((B, S, D), dtype=np.float32)
    for b in range(B):
        for s in range(2):
            out[b, s * L:(s + 1) * L] = np.asarray(res.results[2 * b + s]["y"], dtype=np.float32)
    return out
```

```python
import numpy as np
from contextlib import ExitStack
import concourse.bass as bass
import concourse.mybir as mybir
from concourse.bass_utils import run_bass_kernel_spmd

F32 = mybir.dt.float32
BF16 = mybir.dt.bfloat16
I32 = mybir.dt.int32
ALU = mybir.AluOpType
AF = mybir.ActivationFunctionType
AX = mybir.AxisListType

D = 4096
DIN = 10256
ALPHA = 2.0 ** 0.25
EPS = 1e-5
C_Q, C_K, C_V, C_G, C_ZG, C_U, C_ZS = 0, 1024, 2048, 4096, 4112, 6160, 8208


class Prog:
    ENGS = ("pe", "act", "dve", "pool", "sp")

    def __init__(self, nc):
        self.nc = nc
        self.ops = []
        self.last_w = {}
        self.readers = {}

    def op(self, eng, fn, reads=(), writes=(), dma=None):
        idx = len(self.ops)
        deps = set()
        for k in reads:
            if k in self.last_w:
                deps.add(self.last_w[k])
        for k in writes:
            if k in self.last_w:
                deps.add(self.last_w[k])
            for r in self.readers.get(k, ()):
                deps.add(r)
        self.ops.append(dict(eng=eng, fn=fn, deps=deps, dma=dma, needed=False))
        for k in reads:
            self.readers.setdefault(k, []).append(idx)
        for k in writes:
            self.last_w[k] = idx
            self.readers[k] = []
        return idx

    def emit(self, stack, tag):
        nc = self.nc
        ops = self.ops
        for i, o in enumerate(ops):
            latest = {}
            for d in o["deps"]:
                od = ops[d]
                if od["dma"] is not None:
                    continue
                if od["eng"] == "pe" and o["eng"] == "pe" and o["dma"] is None:
                    continue
                if od["eng"] != "pe":
                    od["needed"] = True
                    continue
                if od["eng"] not in latest or d > latest[od["eng"]]:
                    latest[od["eng"]] = d
            for d in latest.values():
                ops[d]["needed"] = True
        sems = {e: stack.enter_context(nc.semaphore(tag + "s_" + e)) for e in self.ENGS}
        dma_sems, dma_cnt = {}, {}
        cnt = {e: 0 for e in self.ENGS}
        ev = [None] * len(ops)
        for i, o in enumerate(ops):
            if o["dma"] is not None:
                name = o["dma"]
                if name not in dma_sems:
                    dma_sems[name] = stack.enter_context(nc.semaphore(tag + "d_" + name))
                    dma_cnt[name] = 0
                dma_cnt[name] += 16
                ev[i] = (dma_sems[name], dma_cnt[name], "dma:" + name)
            elif o["needed"]:
                cnt[o["eng"]] += 1
                ev[i] = (sems[o["eng"]], cnt[o["eng"]], o["eng"])
        final_dma = {n: (dma_sems[n], dma_cnt[n]) for n in dma_sems}
        per_eng = {e: [] for e in self.ENGS}
        for i, o in enumerate(ops):
            per_eng[o["eng"]].append(i)
        with nc.Block() as block:
            getters = dict(pe=block.tensor, act=block.scalar, dve=block.vector,
                           pool=block.gpsimd, sp=block.sync)
            for e in self.ENGS:
                idxs = per_eng[e]

                def body(engine, e=e, idxs=idxs):
                    waited = {}
                    for i in idxs:
                        o = ops[i]
                        need = {}
                        for d in o["deps"]:
                            if ev[d] is None:
                                continue
                            s, v, tg = ev[d]
                            if tg not in need or need[tg][1] < v:
                                need[tg] = (s, v)
                        for tg, (s, v) in need.items():
                            if waited.get(tg, 0) >= v:
                                continue
                            engine.wait_ge(s, v)
                            waited[tg] = v
                        ins = o["fn"](engine)
                        if ev[i] is not None:
                            ins.then_inc(ev[i][0], 16 if o["dma"] is not None else 1)
                    if e == "sp":
                        for n, (s, v) in final_dma.items():
                            engine.wait_ge(s, v)
                        for e2 in ("pe", "act", "dve", "pool"):
                            if cnt[e2] > 0:
                                engine.wait_ge(sems[e2], cnt[e2])
                getters[e](body)


class Ring:
    def __init__(self, nc, st, name, n, shape, dtype):
        self.t = [st.enter_context(nc.sbuf_tensor(f"{name}{i}", shape, dtype)) for i in range(n)]
        self.n = n
        self.i = 0
        self.name = name

    def next(self):
        s = self.i % self.n
        self.i += 1
        return s, self.t[s], (self.name, s)


def dram_rows(ap, r0, nkc, c0, nc_):
    return ap[r0 * 128:(r0 + nkc) * 128, c0:c0 + nc_].rearrange("(kc p) c -> p kc c", p=128)


def make_iota_mask(P, nc, st, name, shape, pattern, base, cm, op, key):
    ti = st.enter_context(nc.sbuf_tensor(name + "_i", shape, I32))
    tf = st.enter_context(nc.sbuf_tensor(name, shape, F32))
    P.op("pool", lambda e: e.iota(ti[:], pattern=pattern, base=base, channel_multiplier=cm),
         writes=[key + "_i"])
    P.op("dve", lambda e: e.tensor_scalar(out=tf[:], in0=ti[:], scalar1=0.0, scalar2=None, op0=op),
         reads=[key + "_i"], writes=[key])
    return tf


def phase0(nc, st, io, scr, pers, L, Lp=0):
    P = Prog(nc)
    with ExitStack() as ls:
        ident = make_iota_mask(P, nc, ls, "ident0", [128, 128], [[1, 128]], 0, -1, ALU.is_equal, "ident")
        c32 = ls.enter_context(nc.sbuf_tensor("c32", [32, 128], F32))
        ba96 = ls.enter_context(nc.sbuf_tensor("ba96", [96, 128], F32))
        scol = ls.enter_context(nc.sbuf_tensor("scol", [128, 32, 2], F32))
        bac = ls.enter_context(nc.sbuf_tensor("bac", [128, 96], F32))
        modc = ls.enter_context(nc.sbuf_tensor("modc", [128, 96], F32))
        g32 = ls.enter_context(nc.sbuf_tensor("g32", [32, 128], F32))
        wa = [ls.enter_context(nc.sbuf_tensor(f"wa{i}", [128, 12288], F32)) for i in range(2)]
        grow = ls.enter_context(nc.sbuf_tensor("grow", [128, 4096], F32))
        wf = [ls.enter_context(nc.sbuf_tensor(f"wf{i}", [128, 4096], F32)) for i in range(2)]
        wb = [ls.enter_context(nc.sbuf_tensor(f"wb{i}", [128, 4096], BF16)) for i in range(2)]
        pst = ls.enter_context(nc.psum_tensor("p0t", [128, 512], F32))[:, 0:128]
        psm = ls.enter_context(nc.psum_tensor("p0m", [128, 256, 2], F32))[:, 0:96, :]

        for r in range(32):
            P.op("pool", lambda e, r=r: e.dma_start(out=scr["win"][r * 128:(r + 1) * 128, :],
                                                  in_=io["w_in"][r * 128:(r + 1) * 128, :],
                                                  max_dma_last_dim=4096),
                 writes=[("win", r)], dma="cast")
        for r in range(16):
            P.op("pool", lambda e, r=r: e.dma_start(out=scr["wglu"][r * 128:(r + 1) * 128, :],
                                                  in_=io["w_glu"][r * 128:(r + 1) * 128, :],
                                                  max_dma_last_dim=4096),
                 writes=[("wglu", r)], dma="cast")

        if Lp:
            P.op("sp", lambda e: e.dma_start(out=pers["flag"][:], in_=io["flag"][:, :]), writes=["flag"], dma="ld0_1")
        P.op("sp", lambda e: e.dma_start(out=c32[:], in_=io["c"][:, :]), writes=["c32"], dma="ld0_2")
        P.op("sp", lambda e: e.dma_start(out=ba96[:], in_=io["b_ada"][:, :]), writes=["ba96"], dma="ld0_3")
        P.op("pe", lambda e: e.transpose(pst[:, 0:32], c32[:], ident[0:32, 0:32]),
             reads=["c32", "ident"], writes=["pst"])
        for j in range(2):
            P.op("act", lambda e, j=j: e.activation(out=scol[:, :, j], in_=pst[:, 0:32], func=AF.Silu),
                 reads=["pst"], writes=["scol"])
        P.op("pe", lambda e: e.transpose(pst[:, 0:96], ba96[:], ident[0:96, 0:96]),
             reads=["ba96", "ident", "scol"], writes=["pst"])
        P.op("dve", lambda e: e.tensor_copy(out=bac[:], in_=pst[:, 0:96]), reads=["pst"], writes=["bac"])
        for kc in range(32):
            P.op("sp", lambda e, kc=kc: e.dma_start(out=wa[kc % 2][:], in_=io["w_ada"][kc * 128:(kc + 1) * 128, :]),
                 writes=[("wa", kc % 2)], dma="wa%d" % (kc % 2))
            for ct in range(96):
                P.op("pe", lambda e, kc=kc, ct=ct: e.matmul(
                    psm[:, ct, :], lhsT=wa[kc % 2][:, ct * 128:(ct + 1) * 128], rhs=scol[:, kc, :],
                    start=(kc == 0 and ct == 0), stop=(kc == 31 and ct == 95), skip_group_check=True),
                    reads=[("wa", kc % 2), "scol"], writes=["psm"])
        P.op("dve", lambda e: e.tensor_tensor(out=modc[:], in0=psm[:, :, 0], in1=bac[:], op=ALU.add),
             reads=["psm", "bac"], writes=["modc"])
        P.op("dve", lambda e: e.tensor_copy(out=pers["sh"][:], in_=modc[:, 0:32]), reads=["modc"], writes=["sh"])
        P.op("dve", lambda e: e.tensor_scalar(out=pers["s1p"][:], in0=modc[:, 32:64], scalar1=1.0, scalar2=None,
                                              op0=ALU.add), reads=["modc"], writes=["s1p"])
        P.op("pe", lambda e: e.transpose(pst[0:32, :], modc[:, 64:96], ident[:]),
             reads=["modc", "ident", "bac"], writes=["pst"])
        P.op("dve", lambda e: e.tensor_copy(out=g32[:], in_=pst[0:32, :]), reads=["pst"], writes=["g32"])
        P.op("sp", lambda e: e.dma_start(out=scr["gate"][:, :], in_=g32[:]), reads=["g32"], writes=["gscr"], dma="ld0_4")
        P.op("sp", lambda e: e.dma_start(
            out=grow[:], in_=scr["gate"].rearrange("a b -> (a b)")[None, :].broadcast_to([128, 4096])),
            reads=["gscr"], writes=["grow"], dma="ld0_5")
        for r in range(32):
            P.op("sp", lambda e, r=r: e.dma_start(out=wf[r % 2][:], in_=io["w_out"][r * 128:(r + 1) * 128, :]),
                 writes=[("wf", r % 2)], dma="wf%d" % (r % 2))
            P.op("dve", lambda e, r=r: e.tensor_tensor(out=wb[r % 2][:], in0=wf[r % 2][:], in1=grow[:], op=ALU.mult),
                 reads=[("wf", r % 2), "grow"], writes=[("wb", r % 2)])
            P.op("sp", lambda e, r=r: e.dma_start(out=scr["wout"][r * 128:(r + 1) * 128, :], in_=wb[r % 2][:]),
                 reads=[("wb", r % 2)], writes=[("wout", r)], dma="wst%d" % (r % 2))
        P.emit(ls, "a")


def load_hT(P, nc, xsrc, pers, xs, hT, psb, ident, t0, tagq):
    for i in range(4):
        xb = xs[i % len(xs)]
        xk = ("xs", i % len(xs))
        P.op("pool", lambda e, xb=xb, i=i: e.dma_start(out=xb[:], in_=xsrc[t0 + i * 128:t0 + (i + 1) * 128, :]),
             writes=[xk], dma="%s_%d" % (tagq, i % len(xs)))
        for kg in range(8):
            b = kg % len(psb)
            for k4 in range(4):
                kc = kg * 4 + k4
                P.op("pe", lambda e, xb=xb, b=b, k4=k4, kc=kc: e.transpose(
                    psb[b][:, k4 * 128:(k4 + 1) * 128], xb[:, kc * 128:(kc + 1) * 128], ident[:]),
                    reads=[xk, "identf"], writes=[("B", b)])
            for k4 in range(4):
                kc = kg * 4 + k4
                if kg % 2 == 0:
                    P.op("act", lambda e, b=b, k4=k4, kc=kc, i=i: e.activation(
                        out=hT[:, kc, i * 128:(i + 1) * 128], in_=psb[b][:, k4 * 128:(k4 + 1) * 128],
                        func=AF.Identity, bias=pers["sh"][:, kc:kc + 1], scale=pers["s1p"][:, kc:kc + 1]),
                        reads=[("B", b)], writes=[("hT", kc // 8)])
                else:
                    P.op("dve", lambda e, b=b, k4=k4, kc=kc, i=i: e.tensor_scalar(
                        out=hT[:, kc, i * 128:(i + 1) * 128], in0=psb[b][:, k4 * 128:(k4 + 1) * 128],
                        scalar1=pers["s1p"][:, kc:kc + 1], scalar2=pers["sh"][:, kc:kc + 1],
                        op0=ALU.mult, op1=ALU.add),
                        reads=[("B", b)], writes=[("hT", kc // 8)])


def sweep1(nc, st, io, scr, pers, L, Lp=0):
    P = Prog(nc)
    nblk = L // 512
    with ExitStack() as ls:
        sb = lambda name, shape, dt: ls.enter_context(nc.sbuf_tensor(name, shape, dt))
        identf = make_iota_mask(P, nc, ls, "identf1", [128, 128], [[1, 128]], 0, -1, ALU.is_equal, "identf")
        identb = sb("identb1", [128, 128], BF16)
        P.op("dve", lambda e: e.tensor_copy(out=identb[:], in_=identf[:]), reads=["identf"], writes=["identb"])
        m_i = sb("m64i", [128, 64], I32)
        mask64 = sb("mask64", [128, 64], F32)
        for hf in range(2):
            P.op("pool", lambda e, hf=hf: e.iota(m_i[hf * 64:(hf + 1) * 64, :], pattern=[[1, 64]], base=0,
                                               channel_multiplier=-1), writes=["m64i"])
        P.op("dve", lambda e: e.tensor_scalar(out=mask64[:], in0=m_i[:], scalar1=0.0, scalar2=None, op0=ALU.is_ge),
             reads=["m64i"], writes=["mask64"])
        tri = sb("tri", [128, 128], F32)
        P.op("dve", lambda e: e.memset(tri[:], 0.0), writes=["tri"])
        for hf in range(2):
            P.op("dve", lambda e, hf=hf: e.tensor_copy(out=tri[hf * 64:(hf + 1) * 64, hf * 64:(hf + 1) * 64],
                                                     in_=mask64[hf * 64:(hf + 1) * 64, :]),
                 reads=["mask64", "tri"], writes=["tri"])
        wgate = sb("wgate", [16, 1024], F32)
        bgate = sb("bgate", [1, 1024], F32)
        ones = sb("ones1", [1, 128], F32)
        gnb = sb("gnb", [128, 512], F32)
        P.op("sp", lambda e: e.dma_start(out=wgate[:], in_=io["w_gate"][:, :]), writes=["wgate"], dma="c1_1")
        P.op("sp", lambda e: e.dma_start(out=bgate[:], in_=io["b_gate"][:, :]), writes=["bgate"], dma="c1_2")
        P.op("sp", lambda e: e.dma_start(out=gnb[:], in_=io["gnorm"][0:1, :].broadcast_to([128, 512])),
             writes=["gnb"], dma="c1_3")
        P.op("dve", lambda e: e.memset(ones[:], 1.0), writes=["ones"])
        T = sb("Tst", [128, 8, 512], F32)
        Sbf = sb("Sbf", [128, 8, 512], BF16)
        eblp = sb("eblp", [128, 8], F32)
        P.op("dve", lambda e: e.memset(T[:], 0.0), writes=[("T", m) for m in range(8)])
        P.op("pool", lambda e: e.memset(Sbf[:], 0.0), writes=[("Sbf", m) for m in range(8)])
        P.op("dve", lambda e: e.memset(eblp[:], 1.0), writes=[("eblp", m) for m in range(8)])
        xs = [sb(f"xs1_{i}", [128, 4096], F32) for i in range(2)]
        hT = sb("hT1", [128, 32, 512], BF16)
        ring = Ring(nc, ls, "w1_", 3, [128, 4096], BF16)
        wg16 = sb("wg16", [128, 32, 16], BF16)
        glrT = sb("glrT", [16, 512], F32)
        nls = sb("nls", [128, 4, 1024], F32)
        ebt = [sb(f"ebt{e}", [128, 512], F32) for e in range(2)]
        eit = [sb(f"eit{e}", [128, 512], F32) for e in range(2)]
        qdec = [sb(f"qdec{e}", [128, 512], BF16) for e in range(2)]
        kinvT = [sb(f"kinvT{e}", [128, 512], BF16) for e in range(2)]
        kinv_tok = sb("kinvtok", [128, 4, 256], BF16)
        v_tok = sb("vtok", [128, 4, 512], BF16)
        gz = sb("gz", [128, 4, 512], F32)
        zs = sb("zs", [128, 512], F32)
        o_tok = sb("otok", [128, 4, 512], BF16)
        oTs = sb("oTs", [128, 4, 512], BF16)
        att_s = sb("atts", [128, 64], BF16)
        junk = sb("junk1", [128, 512], BF16)
        ssq = sb("ssq", [128, 2], F32)
        B = [ls.enter_context(nc.psum_tensor(f"B{i}", [128, 512], F32)) for i in range(5)]
        B5 = ls.enter_context(nc.psum_tensor("B5", [128, 1024], BF16))
        B6 = ls.enter_context(nc.psum_tensor("B6", [128, 512], F32))
        B7 = ls.enter_context(nc.psum_tensor("B7", [128, 512], F32))

        def wload(r0, nkc, c0, ncol):
            s, wt, wk = ring.next()
            view = wt[:].rearrange("p (a b) -> p a b", b=ncol)
            P.op("sp", lambda e: e.dma_start(out=view, in_=dram_rows(scr["win"], r0, nkc, c0, ncol)),
                 reads=[("win", r) for r in range(r0, r0 + nkc)], writes=[wk], dma="w1_%d" % s)
            return view, wk

        blocks = [("pre", i) for i in range(Lp // 512)] + [("own", i) for i in range(nblk)]
        for mode, blk in blocks:
            own = mode == "own"
            last_pre = (mode == "pre" and blk == Lp // 512 - 1)
            t0 = blk * 512
            load_hT(P, nc, io["x"] if own else io["xp"], pers, xs, hT, B[0:4], identf, t0, "x1")
            P.op("sp", lambda e: e.dma_start(out=wg16[:], in_=dram_rows(scr["win"], 0, 32, C_G, 16)),
                 reads=[("win", r) for r in range(32)], writes=["wg16"], dma="w1g")
            for kc in range(32):
                P.op("pe", lambda e, kc=kc: e.matmul(B[4][0:16, :], lhsT=wg16[:, kc, :], rhs=hT[:, kc, :],
                                                     start=(kc == 0), stop=(kc == 31)),
                     reads=["wg16", ("hT", kc // 8)], writes=["B4"])
            P.op("dve", lambda e: e.tensor_copy(out=glrT[:], in_=B[4][0:16, :]), reads=["B4"], writes=["glrT"])
            for i in range(4):
                for e2 in range(2):
                    sl = slice(e2 * 512, (e2 + 1) * 512)
                    P.op("pe", lambda e, i=i, sl=sl: e.matmul(B[4][:], lhsT=glrT[:, i * 128:(i + 1) * 128],
                                                            rhs=wgate[:, sl], start=True, stop=False),
                         reads=["glrT", "wgate"], writes=["B4"])
                    P.op("pe", lambda e, sl=sl: e.matmul(B[4][:], lhsT=ones[:, :], rhs=bgate[:, sl],
                                                       start=False, stop=True),
                         reads=["ones", "bgate"], writes=["B4"])
                    P.op("act", lambda e, i=i, sl=sl: e.activation(out=nls[:, i, sl], in_=B[4][:], func=AF.Exp, scale=-1.0),
                         reads=["B4"], writes=[("nls", i)])
                    P.op("act", lambda e, i=i, sl=sl: e.activation(out=nls[:, i, sl], in_=nls[:, i, sl], func=AF.Ln, bias=1.0),
                         reads=[("nls", i)], writes=[("nls", i)])
            for h in range(4):
                for e2 in range(2):
                    m = 2 * h + e2
                    for i in range(4):
                        P.op("pe", lambda e, i=i, m=m: e.matmul(B[4][:, i * 128:(i + 1) * 128],
                                                              lhsT=nls[:, i, m * 128:(m + 1) * 128], rhs=tri[:],
                                                              start=True, stop=True),
                             reads=[("nls", i), "tri"], writes=["B4"])
                    P.op("act", lambda e, e2=e2: e.activation(out=ebt[e2][:], in_=B[4][:], func=AF.Exp, scale=-1.0 / 16),
                         reads=["B4"], writes=[("ebt", e2)])
                    P.op("act", lambda e, e2=e2: e.activation(out=eit[e2][:], in_=B[4][:], func=AF.Exp, scale=1.0 / 16),
                         reads=["B4"], writes=[("eit", e2)])
                for which, c0 in (("q", C_Q + 256 * h), ("k", C_K + 256 * h)):
                    if which == "q" and not own:
                        continue
                    pb = (0, 1) if which == "q" else (2, 3)
                    for kh in range(2):
                        view, wk = wload(kh * 16, 16, c0, 256)
                        for e2 in range(2):
                            for k16 in range(16):
                                kc = kh * 16 + k16
                                P.op("pe", lambda e, view=view, e2=e2, k16=k16, kc=kc, pb=pb: e.matmul(
                                    B[pb[e2]][:], lhsT=view[:, k16, e2 * 128:(e2 + 1) * 128], rhs=hT[:, kc, :],
                                    start=(kc == 0), stop=(kc == 31)),
                                    reads=[wk, ("hT", kc // 8)], writes=[("B", pb[e2])])
                    for e2 in range(2):
                        if which == "q":
                            P.op("dve", lambda e, e2=e2, pb=pb: e.scalar_tensor_tensor(
                                out=qdec[e2][:], in0=B[pb[e2]][:], scalar=1.0 / 16, in1=ebt[e2][:],
                                op0=ALU.mult, op1=ALU.mult),
                                reads=[("B", pb[e2]), ("ebt", e2)], writes=[("qdec", e2)])
                        else:
                            P.op("dve", lambda e, e2=e2, pb=pb: e.tensor_tensor(
                                out=kinvT[e2][:], in0=B[pb[e2]][:], in1=eit[e2][:], op=ALU.mult),
                                reads=[("B", pb[e2]), ("eit", e2)], writes=[("kinvT", e2)])
                for e2 in range(2):
                    for i in range(4):
                        P.op("pe", lambda e, e2=e2, i=i: e.transpose(
                            B5[:, (i * 2 + e2) * 128:(i * 2 + e2 + 1) * 128], kinvT[e2][:, i * 128:(i + 1) * 128], identb[:]),
                            reads=[("kinvT", e2), "identb"], writes=["B5"])
                P.op("act", lambda e: e.activation(out=kinv_tok[:].rearrange("p a b -> p (a b)"), in_=B5[:], func=AF.Copy),
                     reads=["B5"], writes=["kinvtok"])
                for which, c0 in (("v", C_V + 512 * h), ("z", C_ZG + 512 * h)):
                    if which == "z" and not own:
                        continue
                    for kq in range(4):
                        view, wk = wload(kq * 8, 8, c0, 512)
                        for i in range(4):
                            for k8 in range(8):
                                kc = kq * 8 + k8
                                P.op("pe", lambda e, view=view, i=i, k8=k8, kc=kc: e.matmul(
                                    B[i][:], lhsT=hT[:, kc, i * 128:(i + 1) * 128], rhs=view[:, k8, :],
                                    start=(kc == 0), stop=(kc == 31)),
                                    reads=[wk, ("hT", kc // 8)], writes=[("B", i)])
                    for i in range(4):
                        if which == "v":
                            P.op("act", lambda e, i=i: e.activation(out=v_tok[:, i, :], in_=B[i][:], func=AF.Copy),
                                 reads=[("B", i)], writes=[("vtok", i)])
                        else:
                            P.op("act", lambda e, i=i: e.activation(out=zs[:], in_=B[i][:], func=AF.Silu),
                                 reads=[("B", i)], writes=["zs"])
                            P.op("dve", lambda e, i=i: e.tensor_tensor(out=gz[:, i, :], in0=zs[:], in1=gnb[:], op=ALU.mult),
                                 reads=["zs", "gnb"], writes=[("gz", i)])
                for c in range(8):
                    par, i = c % 2, c // 2
                    rows = slice(64 * par, 64 * par + 64)
                    cols = slice(64 * c, 64 * c + 64)
                    for e2 in range(2 if own else 0):
                        P.op("pe", lambda e, e2=e2, rows=rows, cols=cols: e.matmul(
                            B7[rows, 0:64], lhsT=kinvT[e2][:, cols], rhs=qdec[e2][:, cols],
                            start=(e2 == 0), stop=(e2 == 1)),
                            reads=[("kinvT", e2), ("qdec", e2)], writes=["B7"])
                    if not own:
                        for e2 in range(2):
                            m = 2 * h + e2
                            ebl_prev = eblp[:, m:m + 1] if c == 0 else ebt[e2][:, 64 * c - 1:64 * c]
                            ebl_cur = ebt[e2][:, 64 * c + 63:64 * c + 64]
                            P.op("pe", lambda e, rows=rows, i=i, e2=e2: e.matmul(
                                B[2 + e2][:], lhsT=kinv_tok[rows, i, e2 * 128:(e2 + 1) * 128], rhs=v_tok[rows, i, :],
                                start=True, stop=True),
                                reads=["kinvtok", ("vtok", i)], writes=[("B", 2 + e2)])
                            P.op("dve", lambda e, m=m, e2=e2, ebl_prev=ebl_prev: e.scalar_tensor_tensor(
                                out=T[:, m, :], in0=T[:, m, :], scalar=ebl_prev, in1=B[2 + e2][:], op0=ALU.mult, op1=ALU.add),
                                reads=[("T", m), ("B", 2 + e2), ("ebt", e2), ("eblp", m)], writes=[("T", m)])
                            if last_pre and c == 7:
                                P.op("act", lambda e, m=m, ebl_cur=ebl_cur: e.activation(
                                    out=Sbf[:, m, :], in_=T[:, m, :], func=AF.Copy, scale=ebl_cur),
                                    reads=[("T", m), ("ebt", e2)], writes=[("Sbf", m)])
                        continue
                    P.op("dve", lambda e, rows=rows: e.tensor_tensor(out=att_s[rows, :], in0=B7[rows, 0:64],
                                                                   in1=mask64[rows, :], op=ALU.mult),
                         reads=["B7", "mask64"], writes=["atts"])
                    P.op("pe", lambda e, rows=rows, i=i: e.matmul(B6[rows, :], lhsT=att_s[rows, :], rhs=v_tok[rows, i, :],
                                                                start=True, stop=False),
                         reads=["atts", ("vtok", i)], writes=["B6"])
                    for e2 in range(2):
                        m = 2 * h + e2
                        P.op("pe", lambda e, rows=rows, cols=cols, e2=e2, m=m: e.matmul(
                            B6[rows, :], lhsT=qdec[e2][:, cols], rhs=Sbf[:, m, :], start=False, stop=(e2 == 1)),
                            reads=[("qdec", e2), ("Sbf", m)], writes=["B6"])
                    P.op("act", lambda e, rows=rows: e.activation(out=junk[rows, :], in_=B6[rows, :], func=AF.Square,
                                                                accum_out=ssq[rows, 0:1]),
                         reads=["B6"], writes=["ssq", "junk"])
                    P.op("dve", lambda e, rows=rows: e.tensor_scalar(out=ssq[rows, 1:2], in0=ssq[rows, 0:1],
                                                                   scalar1=1.0 / 512, scalar2=EPS, op0=ALU.mult, op1=ALU.add),
                         reads=["ssq"], writes=["ssq"])
                    P.op("act", lambda e, rows=rows: e.activation(out=ssq[rows, 1:2], in_=ssq[rows, 1:2], func=AF.Sqrt),
                         reads=["ssq"], writes=["ssq"])
                    P.op("dve", lambda e, rows=rows: e.reciprocal(out=ssq[rows, 1:2], in_=ssq[rows, 1:2]),
                         reads=["ssq"], writes=["ssq"])
                    P.op("dve", lambda e, rows=rows, i=i: e.scalar_tensor_tensor(
                        out=o_tok[rows, i, :], in0=B6[rows, :], scalar=ssq[rows, 1:2], in1=gz[rows, i, :],
                        op0=ALU.mult, op1=ALU.mult),
                        reads=["B6", "ssq", ("gz", i)], writes=[("otok", i)])
                    for e2 in range(2):
                        m = 2 * h + e2
                        ebl_prev = eblp[:, m:m + 1] if c == 0 else ebt[e2][:, 64 * c - 1:64 * c]
                        ebl_cur = ebt[e2][:, 64 * c + 63:64 * c + 64]
                        P.op("pe", lambda e, rows=rows, i=i, e2=e2: e.matmul(
                            B[2 + e2][:], lhsT=kinv_tok[rows, i, e2 * 128:(e2 + 1) * 128], rhs=v_tok[rows, i, :],
                            start=True, stop=True),
                            reads=["kinvtok", ("vtok", i)], writes=[("B", 2 + e2)])
                        P.op("dve", lambda e, m=m, e2=e2, ebl_prev=ebl_prev: e.scalar_tensor_tensor(
                            out=T[:, m, :], in0=T[:, m, :], scalar=ebl_prev, in1=B[2 + e2][:], op0=ALU.mult, op1=ALU.add),
                            reads=[("T", m), ("B", 2 + e2), ("ebt", e2), ("eblp", m)], writes=[("T", m)])
                        P.op("act", lambda e, m=m, ebl_cur=ebl_cur: e.activation(
                            out=Sbf[:, m, :], in_=T[:, m, :], func=AF.Copy, scale=ebl_cur),
                            reads=[("T", m), ("ebt", e2)], writes=[("Sbf", m)])
                for e2 in range(2):
                    m = 2 * h + e2
                    P.op("dve", lambda e, m=m, e2=e2: e.tensor_copy(out=eblp[:, m:m + 1], in_=ebt[e2][:, 511:512]),
                         reads=[("ebt", e2)], writes=[("eblp", m)])
                for half in range(2 if own else 0):
                    for cc2 in range(2):
                        cc = half * 2 + cc2
                        for i in range(4):
                            P.op("pe", lambda e, cc=cc, cc2=cc2, i=i: e.transpose(
                                B5[:, (cc2 * 4 + i) * 128:(cc2 * 4 + i + 1) * 128], o_tok[:, i, cc * 128:(cc + 1) * 128], identb[:]),
                                reads=[("otok", i), "identb"], writes=["B5"])
                    P.op("dve", lambda e, half=half: e.tensor_copy(
                        out=oTs[:, half * 2:half * 2 + 2, :].rearrange("p a b -> p (a b)"), in_=B5[:]),
                        reads=["B5"], writes=["oTs"])
                if own:
                    P.op("pool", lambda e, h=h, t0=t0: e.dma_start(out=dram_rows(scr["oT"], h * 4, 4, t0, 512), in_=oTs[:]),
                         reads=["oTs"], writes=[("oTscr", blk, h)], dma="o1")
            if last_pre:
                allT = [("T", m) for m in range(8)]
                allS = [("Sbf", m) for m in range(8)]
                P.op("dve", lambda e: e.tensor_scalar(out=T[:].rearrange("p a b -> p (a b)"), in0=T[:].rearrange("p a b -> p (a b)"),
                                                      scalar1=pers["flag"][:, 0:1], scalar2=None, op0=ALU.mult),
                     reads=allT + ["flag"], writes=allT)
                P.op("dve", lambda e: e.tensor_scalar(out=Sbf[:].rearrange("p a b -> p (a b)"), in0=Sbf[:].rearrange("p a b -> p (a b)"),
                                                      scalar1=pers["flag"][:, 0:1], scalar2=None, op0=ALU.mult),
                     reads=allS + ["flag"], writes=allS)
        P.emit(ls, "b")


TWO_PI = 6.283185307179586
PI = 3.141592653589793


def s5_prologue(nc, st, io, scr, pers):
    P = Prog(nc)
    with ExitStack() as ls:
        sb = lambda name, shape, dt=F32: ls.enter_context(nc.sbuf_tensor(name, shape, dt))
        cnt = [0]

        def dve(fn, r, w):
            P.op("dve", fn, reads=r, writes=w)

        def TT(out, a, b, op, r, w):
            dve(lambda e: e.tensor_tensor(out=out, in0=a, in1=b, op=op), r, w)

        def TS(out, a, s1, s2, op0, op1, r, w):
            if op1 is None:
                dve(lambda e: e.tensor_scalar(out=out, in0=a, scalar1=s1, scalar2=None, op0=op0), r, w)
            else:
                dve(lambda e: e.tensor_scalar(out=out, in0=a, scalar1=s1, scalar2=s2, op0=op0, op1=op1), r, w)

        def ACT(out, a, func, r, w, **kw):
            P.op("act", lambda e: e.activation(out=out, in_=a, func=func, **kw), reads=r, writes=w)

        identf = make_iota_mask(P, nc, ls, "identfp", [128, 128], [[1, 128]], 0, -1, ALU.is_equal, "identf")
        maskc = make_iota_mask(P, nc, ls, "maskc", [128, 8, 16], [[16, 8], [0, 16]], 15, -1, ALU.is_ge, "maskc")
        rep = make_iota_mask(P, nc, ls, "rep", [16, 8, 16], [[0, 8], [1, 16]], 0, -1, ALU.is_equal, "rep")
        pt = ls.enter_context(nc.psum_tensor("pqt", [128, 512], F32))[:, 0:128]
        pm = [ls.enter_context(nc.psum_tensor(f"pqm{i}", [128, 512], F32))[:, 0:128] for i in range(2)]

        def transp(out_sb, in_ap, npart, nfree, rk, wk, odt_copy="dve"):
            P.op("pe", lambda e: e.transpose(pt[0:nfree, 0:npart], in_ap, identf[0:npart, 0:npart]),
                 reads=rk + ["identf"], writes=["pt"])
            dve(lambda e: e.tensor_copy(out=out_sb, in_=pt[0:nfree, 0:npart]), ["pt"], wk)

        lt = [sb(f"lt{i}", [64, 128]) for i in range(2)]
        ldt = sb("ldt", [64, 2]); dtx = sb("dtx", [64, 2, 64])
        P.op("sp", lambda e: e.dma_start(out=lt[0][:], in_=io["lam_re"][:, :]), writes=["lt0"], dma="q0_1")
        P.op("sp", lambda e: e.dma_start(out=lt[1][:], in_=io["lam_im"][:, :]), writes=["lt1"], dma="q0_2")
        P.op("sp", lambda e: e.dma_start(out=ldt[:], in_=io["log_dt"][:, :]), writes=["ldt"], dma="q0_3")
        ACT(ldt[:], ldt[:], AF.Exp, ["ldt"], ["ldt"])
        dve(lambda e: e.tensor_copy(out=dtx[:], in_=ldt[:, :, None].to_broadcast([64, 2, 64])), ["ldt"], ["dtx"])
        zt = [sb(f"ztt{i}", [64, 128]) for i in range(2)]
        for i in range(2):
            TT(zt[i][:], lt[i][:], dtx[:].rearrange("p a b -> p (a b)"), ALU.mult, [f"lt{i}", "dtx"], [f"ztt{i}"])
        lam = [sb(f"lam{i}", [128, 64]) for i in range(2)]
        z = [sb(f"z{i}", [128, 64]) for i in range(2)]
        for i in range(2):
            transp(lam[i][:], lt[i][:], 64, 128, [f"lt{i}"], [f"lam{i}"])
            transp(z[i][:], zt[i][:], 64, 128, [f"ztt{i}"], [f"z{i}"])
        ki = sb("ki", [128, 64], I32); kf = sb("kf", [128, 64]); rr = sb("rr", [128, 64]); mm_ = sb("mm_", [128, 64])
        xs_ = sb("xsft", [128, 64])

        def sin_of(out, x_ap, xk, ok):
            TS(ki[:], x_ap, 1.0 / TWO_PI, None, ALU.mult, None, [xk], ["ki"])
            dve(lambda e: e.tensor_copy(out=kf[:], in_=ki[:]), ["ki"], ["kf"])
            dve(lambda e: e.scalar_tensor_tensor(out=rr[:], in0=kf[:], scalar=-TWO_PI, in1=x_ap, op0=ALU.mult, op1=ALU.add),
                ["kf", xk], ["rr"])
            TS(mm_[:], rr[:], PI, -TWO_PI, ALU.is_gt, ALU.mult, ["rr"], ["mm_"])
            TT(rr[:], rr[:], mm_[:], ALU.add, ["rr", "mm_"], ["rr"])
            TS(mm_[:], rr[:], -PI, TWO_PI, ALU.is_lt, ALU.mult, ["rr"], ["mm_"])
            TT(rr[:], rr[:], mm_[:], ALU.add, ["rr", "mm_"], ["rr"])
            ACT(out, rr[:], AF.Sin, ["rr"], [ok])

        sn = sb("sn", [128, 64]); cs = sb("cs", [128, 64]); mag = sb("mag", [128, 64]); imag = sb("imag", [128, 64])
        sin_of(sn[:], z[1][:], "z1", "sn")
        TS(xs_[:], z[1][:], PI / 2, None, ALU.add, None, ["z1"], ["xsft"])
        sin_of(cs[:], xs_[:], "xsft", "cs")
        ACT(mag[:], z[0][:], AF.Exp, ["z0"], ["mag"])
        ACT(imag[:], z[0][:], AF.Exp, ["z0"], ["imag"], scale=-1.0)
        APr = sb("APr", [128, 9, 64]); APi = sb("APi", [128, 9, 64]); AMr = sb("AMr", [128, 8, 64]); AMi = sb("AMi", [128, 8, 64])
        t1 = sb("t1", [128, 64]); t2 = sb("t2", [128, 64])
        for Tn, nm in ((APr, "APr"), (AMr, "AMr")):
            dve(lambda e, Tn=Tn: e.memset(Tn[:, 0, :], 1.0), [], [nm])
        for Tn, nm in ((APi, "APi"), (AMi, "AMi")):
            dve(lambda e, Tn=Tn: e.memset(Tn[:, 0, :], 0.0), [], [nm])
        TT(APr[:, 1, :], mag[:], cs[:], ALU.mult, ["mag", "cs"], ["APr"])
        TT(APi[:, 1, :], mag[:], sn[:], ALU.mult, ["mag", "sn"], ["APi"])
        TT(AMr[:, 1, :], imag[:], cs[:], ALU.mult, ["imag", "cs"], ["AMr"])
        dve(lambda e: e.scalar_tensor_tensor(out=AMi[:, 1, :], in0=imag[:], scalar=-1.0, in1=sn[:], op0=ALU.mult, op1=ALU.mult),
            ["imag", "sn"], ["AMi"])

        def cmul(o_re, o_im, a_re, a_im, b_re, b_im, tmpa, tmpb, r, w):
            TT(tmpa, a_re, b_re, ALU.mult, r, ["cm_a"])
            TT(tmpb, a_im, b_im, ALU.mult, r, ["cm_b"])
            TT(o_re, tmpa, tmpb, ALU.subtract, ["cm_a", "cm_b"], w)
            TT(tmpa, a_re, b_im, ALU.mult, r + w, ["cm_a"])
            TT(tmpb, a_im, b_re, ALU.mult, r + w, ["cm_b"])
            TT(o_im, tmpa, tmpb, ALU.add, ["cm_a", "cm_b"], w)

        for k in range(2, 9):
            cmul(APr[:, k, :], APi[:, k, :], APr[:, k - 1, :], APi[:, k - 1, :], APr[:, 1, :], APi[:, 1, :],
                 t1[:], t2[:], ["APr", "APi"], ["APr", "APi"])
        for k in range(2, 8):
            cmul(AMr[:, k, :], AMi[:, k, :], AMr[:, k - 1, :], AMi[:, k - 1, :], AMr[:, 1, :], AMi[:, 1, :],
                 t1[:], t2[:], ["AMr", "AMi"], ["AMr", "AMi"])
        for c in range(2):
            dve(lambda e, c=c: e.tensor_copy(out=pers["A8c"][:, c, :], in_=APr[:, 8, :]), ["APr"], ["A8c"])
        dve(lambda e: e.tensor_copy(out=pers["A8n"][:, 1, :], in_=APi[:, 8, :]), ["APi"], ["A8n"])
        TS(pers["A8n"][:, 0, :], APi[:, 8, :], -1.0, None, ALU.mult, None, ["APi"], ["A8n"])
        den = sb("den", [128, 64]); nre = sb("nre", [128, 64]); fre = sb("fre", [128, 64]); fim = sb("fim", [128, 64])
        TT(t1[:], lam[0][:], lam[0][:], ALU.mult, ["lam0"], ["t1"])
        TT(t2[:], lam[1][:], lam[1][:], ALU.mult, ["lam1"], ["t2"])
        TT(den[:], t1[:], t2[:], ALU.add, ["t1", "t2"], ["den"])
        dve(lambda e: e.reciprocal(out=den[:], in_=den[:]), ["den"], ["den"])
        TS(nre[:], APr[:, 1, :], -1.0, None, ALU.add, None, ["APr"], ["nre"])
        TT(t1[:], nre[:], lam[0][:], ALU.mult, ["nre", "lam0"], ["t1"])
        TT(t2[:], APi[:, 1, :], lam[1][:], ALU.mult, ["APi", "lam1"], ["t2"])
        TT(fre[:], t1[:], t2[:], ALU.add, ["t1", "t2"], ["fre"])
        TT(fre[:], fre[:], den[:], ALU.mult, ["fre", "den"], ["fre"])
        TT(t1[:], APi[:, 1, :], lam[0][:], ALU.mult, ["APi", "lam0"], ["t1"])
        TT(t2[:], nre[:], lam[1][:], ALU.mult, ["nre", "lam1"], ["t2"])
        TT(fim[:], t1[:], t2[:], ALU.subtract, ["t1", "t2"], ["fim"])
        TT(fim[:], fim[:], den[:], ALU.mult, ["fim", "den"], ["fim"])
        Bt = [sb(f"Bt{i}", [128, 64, 16]) for i in range(2)]
        bb = [sb(f"bb{i}", [128, 64, 16]) for i in range(2)]
        Ct = [sb(f"Ct{i}", [128, 64, 16]) for i in range(2)]
        u1 = sb("u1", [128, 64, 16]); u2 = sb("u2", [128, 64, 16])
        for i, nm in enumerate(("b_re", "b_im")):
            for g2 in range(2):
                P.op("sp", lambda e, i=i, nm=nm, g2=g2: e.dma_start(
                    out=Bt[i][g2 * 64:(g2 + 1) * 64, :, :],
                    in_=io[nm].rearrange("(t g2) p h -> g2 p t h", g2=2)[g2]), writes=[f"Bt{i}"], dma="q1_%d_%d" % (i, g2))
        bc = lambda ap: ap[:, :, None].to_broadcast([128, 64, 16])
        cmul(bb[0][:], bb[1][:], bc(fre[:]), bc(fim[:]), Bt[0][:], Bt[1][:], u1[:], u2[:],
             ["fre", "fim", "Bt0", "Bt1"], ["bb0", "bb1"])
        cin = sb("cin", [128, 128])
        for i, nm in enumerate(("c_re", "c_im")):
            for tb in range(8):
                for tl in range(8):
                    tt_ = tb * 8 + tl
                    P.op("sp", lambda e, nm=nm, tl=tl, tt_=tt_: e.dma_start(
                        out=cin[tl * 16:(tl + 1) * 16, :].rearrange("h (g p) -> h g p", g=2),
                        in_=io[nm][2 * tt_:2 * tt_ + 2].rearrange("g h p -> h g p")), writes=["cin"], dma="q2")
                transp(Ct[i][:, tb * 8:(tb + 1) * 8, :].rearrange("p a b -> p (a b)"), cin[:], 128, 128, ["cin"], [f"Ct{i}"])
        d16 = sb("d16", [128, 16]); dgh = sb("dgh", [16, 128]); dcol = sb("dcol", [128, 128])
        P.op("sp", lambda e: e.dma_start(out=d16[:], in_=io["s5_d"][:, :]), writes=["d16"], dma="q0_4")
        transp(dgh[:], d16[:], 128, 16, ["d16"], ["dgh"])
        P.op("pe", lambda e: e.matmul(pt[:, :], lhsT=rep[:].rearrange("p a b -> p (a b)"), rhs=dgh[:], start=True, stop=True),
             reads=["rep", "dgh"], writes=["pt"])
        dve(lambda e: e.tensor_copy(out=dcol[:], in_=pt[:, :]), ["pt"], ["dcol"])
        bg16 = sb("bg16", [16, 128])
        P.op("sp", lambda e: e.dma_start(out=bg16[:], in_=io["b_glu"][:, :]), writes=["bg16"], dma="q0_5")
        transp(pers["bgluc"][:], bg16[:], 16, 128, ["bg16"], ["bgluc"])
        NB = 8
        X = [sb(f"X{i}", [128, NB, 8, 16]) for i in range(2)]
        Y = [sb(f"Y{i}", [128, NB, 8, 16]) for i in range(2)]
        Zm = [sb(f"Zm{i}", [128, NB, 8, 16]) for i in range(2)]
        Fb = sb("Fb", [128, NB, 2, 128], BF16)
        ZTb = sb("ZTb", [128, NB, 2, 128], BF16)
        Mb = sb("Mb", [128, 2 * NB, 128], BF16)
        w1 = sb("w1", [128, NB, 16]); w2 = sb("w2", [128, NB, 16]); mtmp = sb("mtmp", [128, 128])
        for q in range(64 // NB):
            ts = slice(q * NB, (q + 1) * NB)
            bcp = lambda ap: ap[:, :, None].to_broadcast([128, NB, 16])
            for j in range(8):
                cmul(X[0][:, :, j, :], X[1][:, :, j, :], bcp(AMr[:, j, ts]), bcp(AMi[:, j, ts]), bb[0][:, ts, :], bb[1][:, ts, :],
                     w1[:], w2[:], ["AMr", "AMi", "bb0", "bb1"], ["X0", "X1"])
                cmul(Y[0][:, :, j, :], Y[1][:, :, j, :], bcp(APr[:, j, ts]), bcp(APi[:, j, ts]), Ct[0][:, ts, :], Ct[1][:, ts, :],
                     w1[:], w2[:], ["APr", "APi", "Ct0", "Ct1"], ["Y0", "Y1"])
                cmul(Zm[0][:, :, j, :], Zm[1][:, :, j, :], bcp(APr[:, 7 - j, ts]), bcp(APi[:, 7 - j, ts]), bb[0][:, ts, :], bb[1][:, ts, :],
                     w1[:], w2[:], ["APr", "APi", "bb0", "bb1"], ["Z0", "Z1"])
                TT(w1[:], bcp(APr[:, j + 1, ts]), Ct[0][:, ts, :], ALU.mult, ["APr", "Ct0"], ["cm_a"])
                TT(w2[:], bcp(APi[:, j + 1, ts]), Ct[1][:, ts, :], ALU.mult, ["APi", "Ct1"], ["cm_b"])
                TT(Fb[:, :, 0, j * 16:(j + 1) * 16], w1[:], w2[:], ALU.subtract, ["cm_a", "cm_b"], ["Fb"])
                TT(w1[:], bcp(APr[:, j + 1, ts]), Ct[1][:, ts, :], ALU.mult, ["APr", "Ct1", "Fb"], ["cm_a"])
                TT(w2[:], bcp(APi[:, j + 1, ts]), Ct[0][:, ts, :], ALU.mult, ["APi", "Ct0", "Fb"], ["cm_b"])
                dve(lambda e, j=j: e.scalar_tensor_tensor(out=Fb[:, :, 1, j * 16:(j + 1) * 16], in0=w1[:], scalar=-1.0, in1=w2[:],
                                                          op0=ALU.mult, op1=ALU.subtract), ["cm_a", "cm_b"], ["Fb"])
            TS(X[1][:], X[1][:], -1.0, None, ALU.mult, None, ["X1"], ["X1"])
            for tl in range(NB):
                t = q * NB + tl
                for c in range(2):
                    P.op("pe", lambda e, c=c, tl=tl: e.transpose(pt[:, :], Zm[c][:, tl, :, :].rearrange("p a b -> p (a b)"), identf[:]),
                         reads=[f"Z{c}", "identf"], writes=["pt"])
                    dve(lambda e, c=c, tl=tl: e.tensor_copy(out=ZTb[:, tl, c, :], in_=pt[:, :]), ["pt"], ["ZTb"])
                for g2 in range(2):
                    rows = slice(g2 * 64, (g2 + 1) * 64)
                    pmb = pm[g2]
                    for c in range(2):
                        P.op("pe", lambda e, c=c, tl=tl, rows=rows, pmb=pmb: e.matmul(
                            pmb[:, :], lhsT=X[c][rows, tl, :, :].rearrange("p a b -> p (a b)"),
                            rhs=Y[c][rows, tl, :, :].rearrange("p a b -> p (a b)"), start=(c == 0), stop=(c == 1)),
                            reads=["X0", "X1", "Y0", "Y1"], writes=[("pm", g2)])
                    g = 2 * t + g2
                    TT(mtmp[:], pmb[:, :], maskc[:].rearrange("p a b -> p (a b)"), ALU.mult, [("pm", g2), "maskc"], ["mtmp"])
                    dve(lambda e, g=g, tl=tl, g2=g2: e.scalar_tensor_tensor(
                        out=Mb[:, tl * 2 + g2, :], in0=identf[:], scalar=dcol[:, g:g + 1], in1=mtmp[:],
                        op0=ALU.mult, op1=ALU.add), ["mtmp", "identf", "dcol"], ["Mb"])
            P.op("sp", lambda e, q=q: e.dma_start(out=scr["mint"][q * 2 * NB:(q + 1) * 2 * NB].rearrange("g a b -> a g b"), in_=Mb[:]),
                 reads=["Mb"], writes=[("mint", q)], dma="q3m")
            P.op("sp", lambda e, q=q: e.dma_start(out=scr["zt"][q * NB:(q + 1) * NB].rearrange("t c a b -> a t c b"), in_=ZTb[:]),
                 reads=["ZTb"], writes=[("zts", q)], dma="q3z")
            P.op("sp", lambda e, q=q: e.dma_start(out=scr["ff"][q * NB:(q + 1) * NB].rearrange("t c a b -> a t c b"), in_=Fb[:]),
                 reads=["Fb"], writes=[("ffs", q)], dma="q3f")
        P.emit(ls, "q")


import os
S2_STOP = int(os.environ.get("S2_STOP", "0"))


def sweep2(nc, st, io, scr, pers, L, Lp=0):
    P = Prog(nc)
    TB = 256
    NCH = TB // 8
    nblk = L // TB
    GELU_C = 1.5957691216057308
    with ExitStack() as ls:
        sb = lambda name, shape, dt=F32: ls.enter_context(nc.sbuf_tensor(name, shape, dt))
        identf = make_iota_mask(P, nc, ls, "identf2", [128, 128], [[1, 128]], 0, -1, ALU.is_equal, "identf")
        identb = sb("identb2", [128, 128], BF16)
        P.op("dve", lambda e: e.tensor_copy(out=identb[:], in_=identf[:]), reads=["identf"], writes=["identb"])
        xs = [sb("xs2_0", [128, 4096]), sb("xs2_1", [128, 4096])]
        hT = sb("hT2", [128, 32, TB], BF16)
        ring = Ring(nc, ls, "w2_", 3, [128, 8, 512], BF16)
        uT = sb("uT", [128, 4, TB], BF16)
        UUs = [sb(f"UU{i}", [NCH, 8, 8, 16], BF16) for i in range(2)]
        UD = sb("UD", [128, 128, NCH], BF16)
        ZTc = [sb(f"ZTc{i}", [128, 8, 2, 128], BF16) for i in range(2)]
        Fc = [sb(f"Fc{i}", [128, 8, 2, 128], BF16) for i in range(2)]
        Mc = [sb(f"Mc{i}", [128, 16, 128], BF16) for i in range(2)]
        W = sb("Wst", [128, NCH, 2, 64])
        Sbf = [sb(f"Sbf2_{i}", [128, NCH + 1, 2, 64], BF16) for i in range(2)]
        for i_ in range(2):
            P.op("pool", lambda e, i_=i_: e.memset(Sbf[i_][:], 0.0), writes=["Sbf"])
        carry = sb("carry", [128, 2, 64])
        tA = sb("tA", [128, 2, 64]); tBm = sb("tBm", [128, 2, 64])
        ysbs = [sb(f"ysb{i}", [128, 8 * NCH]) for i in range(2)]
        ysqs = [sb(f"ysq{i}", [128, 8 * NCH]) for i in range(2)]
        ysgs = [sb(f"ysg{i}", [128, 8 * NCH]) for i in range(2)]
        YDs = [sb(f"YD{i}", [128, 8, NCH], BF16) for i in range(2)]
        YYs = [sb(f"YY{i}", [NCH, 8, 8, 16], BF16) for i in range(2)]
        ygT = sb("ygT", [128, 16, TB], BF16)
        sg = sb("sg2", [128, TB]); zs = sb("zs2", [128, TB])
        oT5 = sb("oT5", [128, 4, TB], BF16)
        B = [ls.enter_context(nc.psum_tensor(f"C{i}", [128, 512], F32)) for i in range(4)]
        B5 = ls.enter_context(nc.psum_tensor("C5", [128, 1024], BF16))
        B5b = ls.enter_context(nc.psum_tensor("C5b", [128, 1024], BF16))
        B6 = ls.enter_context(nc.psum_tensor("C6", [128, 512], F32))
        B7 = ls.enter_context(nc.psum_tensor("C7", [128, 512], F32))
        P.op("dve", lambda e: e.memset(carry[:], 0.0), writes=["carry"])

        def wload(src, r0, nkc, c0, ncol, rk):
            s, wt, wk = ring.next()
            view = wt[:, 0:nkc, 0:ncol]
            P.op("sp", lambda e: e.dma_start(out=view, in_=dram_rows(src, r0, nkc, c0, ncol)), reads=rk, writes=[wk], dma="w2_%d" % s)
            return view, wk

        blocks = [("pre", i) for i in range(Lp // TB)] + [("own", i) for i in range(nblk)]
        for mode, blk in blocks:
            own = mode == "own"
            last_pre = (mode == "pre" and blk == Lp // TB - 1)
            xsrc = io["x"] if own else io["xp"]
            t0 = blk * TB
            for i in range(TB // 128):
                xb = xs[i % 2]
                xkey = ("xs", i % 2)
                P.op("pool", lambda e, i=i, t0=t0, xsrc=xsrc, xb=xb: e.dma_start(out=xb[:], in_=xsrc[t0 + i * 128:t0 + (i + 1) * 128, :]),
                     writes=[xkey], dma="x2_%d" % (i % 2))
                for kg in range(8):
                    b = kg % 4
                    for k4 in range(4):
                        kc = kg * 4 + k4
                        P.op("pe", lambda e, b=b, k4=k4, kc=kc, xb=xb: e.transpose(
                            B[b][:, k4 * 128:(k4 + 1) * 128], xb[:, kc * 128:(kc + 1) * 128], identf[:]),
                            reads=[xkey, "identf"], writes=[("B", b)])
                    for k4 in range(4):
                        kc = kg * 4 + k4
                        P.op("act", lambda e, b=b, k4=k4, kc=kc, i=i: e.activation(
                            out=hT[:, kc, i * 128:(i + 1) * 128], in_=B[b][:, k4 * 128:(k4 + 1) * 128],
                            func=AF.Identity, bias=pers["sh"][:, kc:kc + 1], scale=pers["s1p"][:, kc:kc + 1]),
                            reads=[("B", b)], writes=[("hT", kc // 8)])
            for cg in range(4):
                for kq in range(4):
                    view, wk = wload(scr["win"], kq * 8, 8, C_U + cg * 512, 512, [])
                    for c4 in range(4):
                        for k8 in range(8):
                            kc = kq * 8 + k8
                            P.op("pe", lambda e, view=view, c4=c4, k8=k8, kc=kc: e.matmul(
                                B[c4][:, 0:TB], lhsT=view[:, k8, c4 * 128:(c4 + 1) * 128], rhs=hT[:, kc, :],
                                start=(kc == 0), stop=(kc == 31)),
                                reads=[wk, ("hT", kc // 8)], writes=[("B", c4)])
                for c4 in range(4):
                    ct = cg * 4 + c4
                    pp = ct % 2
                    UU = UUs[pp]
                    T5, T5k = (B5, "B5") if pp == 0 else (B6[:].bitcast(BF16), "B67_0")
                    T5b, T5bk = (B5b, "B5b") if pp == 0 else (B7[:].bitcast(BF16), "B67_1")
                    T5 = T5 if pp == 1 else T5[:]
                    T5b = T5b if pp == 1 else T5b[:]
                    P.op("act", lambda e, c4=c4: e.activation(out=uT[:, c4, :], in_=B[c4][:, 0:TB], func=AF.Copy),
                         reads=[("B", c4)], writes=[("uT", c4)])
                    for j in range(8):
                        P.op("pe", lambda e, c4=c4, j=j, T5=T5: e.transpose(
                            T5[0:NCH, j * 128:(j + 1) * 128], uT[:, c4, j::8], identb[:]),
                            reads=[("uT", c4), "identb"], writes=[T5k])
                    P.op("act", lambda e, UU=UU, T5=T5: e.activation(
                        out=UU[:].rearrange("n g j h -> n j g h"),
                        in_=T5[0:NCH, :].rearrange("n (j g h) -> n j g h", j=8, g=8), func=AF.Copy), reads=[T5k], writes=[("UU", pp)])
                    for gl in range(8):
                        P.op("pe", lambda e, gl=gl, UU=UU, T5b=T5b: e.transpose(
                            T5b[:, gl * NCH:(gl + 1) * NCH], UU[:, gl, :, :].rearrange("n j h -> n (j h)"), identb[0:NCH, 0:NCH]),
                            reads=[("UU", pp), "identb"], writes=[T5bk])
                    P.op("act", lambda e, ct=ct, T5b=T5b: e.activation(
                        out=UD[:, ct * 8:(ct + 1) * 8, :].rearrange("p g n -> p (g n)"), in_=T5b[:, 0:8 * NCH], func=AF.Copy),
                        reads=[T5bk], writes=[("UD", ct)])
            if S2_STOP == 1:
                continue
            for q in range(8):
                zc = ZTc[q % 2]
                P.op("sp", lambda e, zc=zc, q=q: e.dma_start(out=zc[:], in_=scr["zt"][q * 8:(q + 1) * 8].rearrange("t c a b -> a t c b")),
                     writes=[("ZTc", q % 2)], dma="m2z%d" % (q % 2))
                for tl in range(8):
                    t = q * 8 + tl
                    for g2 in range(2):
                        for c, Bc in ((0, B6), (1, B7)):
                            P.op("pe", lambda e, zc=zc, tl=tl, g2=g2, c=c, Bc=Bc, t=t: e.matmul(
                                Bc[g2 * 64:(g2 + 1) * 64, tl * NCH:(tl + 1) * NCH], lhsT=zc[:, tl, c, g2 * 64:(g2 + 1) * 64],
                                rhs=UD[:, 2 * t + g2, :], start=True, stop=True),
                                reads=[("ZTc", q % 2), ("UD", (2 * t + g2) // 8)], writes=["B67_%d" % c])
                for c, Bc in ((0, B6), (1, B7)):
                    P.op("act", lambda e, c=c, Bc=Bc, q=q: e.activation(
                        out=W[:, :, c, q * 8:(q + 1) * 8].rearrange("p n t -> p t n"),
                        in_=Bc[:, 0:8 * NCH].rearrange("p (t n) -> p t n", t=8), func=AF.Copy),
                        reads=["B67_%d" % c], writes=["W"])
            if S2_STOP == 2:
                continue
            for g2_ in range(2):
                hp = slice(g2_ * 64, (g2_ + 1) * 64)
                P.op("act", lambda e, g2_=g2_, hp=hp: e.activation(out=Sbf[g2_][hp, 0, :, :], in_=carry[hp], func=AF.Copy),
                     reads=["carry"], writes=["Sbf"])
            for n in range(NCH):
                prev = carry[:] if n == 0 else W[:, n - 1, :, :]
                P.op("dve", lambda e, prev=prev: e.tensor_tensor(out=tA[:], in0=pers["A8c"][:], in1=prev, op=ALU.mult),
                     reads=["W", "carry"], writes=["tA"])
                P.op("dve", lambda e, prev=prev: e.tensor_tensor(out=tBm[:, 0, :], in0=pers["A8n"][:, 0, :], in1=prev[:, 1, :], op=ALU.mult),
                     reads=["W", "carry"], writes=["tB"])
                P.op("dve", lambda e, prev=prev: e.tensor_tensor(out=tBm[:, 1, :], in0=pers["A8n"][:, 1, :], in1=prev[:, 0, :], op=ALU.mult),
                     reads=["W", "carry"], writes=["tB"])
                P.op("dve", lambda e, n=n: e.tensor_tensor(out=W[:, n, :, :], in0=W[:, n, :, :], in1=tA[:], op=ALU.add),
                     reads=["W", "tA"], writes=["W"])
                P.op("dve", lambda e, n=n: e.tensor_tensor(out=W[:, n, :, :], in0=W[:, n, :, :], in1=tBm[:], op=ALU.add),
                     reads=["W", "tB"], writes=["W"])
            P.op("dve", lambda e: e.tensor_copy(out=carry[:], in_=W[:, NCH - 1, :, :]), reads=["W"], writes=["carry"])
            if last_pre:
                P.op("dve", lambda e: e.tensor_scalar(out=carry[:].rearrange("p a b -> p (a b)"), in0=carry[:].rearrange("p a b -> p (a b)"),
                                                      scalar1=pers["flag"][:, 0:1], scalar2=None, op0=ALU.mult),
                     reads=["carry", "flag"], writes=["carry"])
            if not own:
                continue
            for g2_ in range(2):
                hp = slice(g2_ * 64, (g2_ + 1) * 64)
                P.op("act", lambda e, g2_=g2_, hp=hp: e.activation(
                    out=Sbf[g2_][hp, 1:NCH + 1, :, :].rearrange("p n c t -> p (n c t)"),
                    in_=W[hp].rearrange("p n c t -> p (n c t)"), func=AF.Copy),
                    reads=["W"], writes=["Sbf"])
            if S2_STOP == 3:
                continue
            for ct in range(16):
                q = ct // 2
                if ct % 2 == 0:
                    fc, mc = Fc[q % 2], Mc[q % 2]
                    P.op("sp", lambda e, fc=fc, q=q: e.dma_start(out=fc[:], in_=scr["ff"][q * 8:(q + 1) * 8].rearrange("t c a b -> a t c b")),
                         writes=[("Fc", q % 2)], dma="m2f%d" % (q % 2))
                    P.op("sp", lambda e, mc=mc, q=q: e.dma_start(out=mc[:], in_=scr["mint"][q * 16:(q + 1) * 16].rearrange("g a b -> a g b")),
                         writes=[("Mc", q % 2)], dma="m2m%d" % (q % 2))
                yb = B[ct % 2]
                for gl in range(8):
                    g = ct * 8 + gl
                    t, g2 = g // 2, g % 2
                    tl = t - q * 8
                    rows = slice(g2 * 64, (g2 + 1) * 64)
                    osl = yb[:, gl * NCH:(gl + 1) * NCH]
                    P.op("pe", lambda e, osl=osl, mc=mc, g=g, q=q: e.matmul(osl, lhsT=mc[:, g - q * 16, :], rhs=UD[:, g, :],
                                                                       start=True, stop=False),
                         reads=[("Mc", q % 2), ("UD", ct)], writes=[("B", ct % 2)])
                    for c in range(2):
                        P.op("pe", lambda e, osl=osl, fc=fc, tl=tl, c=c, g2=g2, t=t: e.matmul(
                            osl, lhsT=fc[:, tl, c, :], rhs=Sbf[g2][:, 0:NCH, c, t], start=False, stop=(c == 1)),
                            reads=[("Fc", q % 2), "Sbf"], writes=[("B", ct % 2)])
                ysl = yb[:, 0:8 * NCH]
                pp = ct % 2
                ysb, ysq, ysg, YD, YY = ysbs[pp], ysqs[pp], ysgs[pp], YDs[pp], YYs[pp]
                T5, T5k = (B5[:], "B5") if pp == 0 else (B6[:].bitcast(BF16), "B67_0")
                T5b, T5bk = (B5b[:], "B5b") if pp == 0 else (B7[:].bitcast(BF16), "B67_1")
                kk = lambda s: (s, pp)
                P.op("act", lambda e, ysl=ysl, ysb=ysb: e.activation(out=ysb[:], in_=ysl, func=AF.Copy), reads=[("B", ct % 2)], writes=[kk("ysb")])
                P.op("act", lambda e, ysl=ysl, ysq=ysq: e.activation(out=ysq[:], in_=ysl, func=AF.Square), reads=[("B", ct % 2)], writes=[kk("ysq")])
                P.op("dve", lambda e, ysq=ysq: e.tensor_scalar(out=ysq[:], in0=ysq[:], scalar1=0.044715, scalar2=1.0, op0=ALU.mult, op1=ALU.add),
                     reads=[kk("ysq")], writes=[kk("ysq")])
                P.op("dve", lambda e, ysq=ysq, ysb=ysb: e.tensor_tensor(out=ysq[:], in0=ysq[:], in1=ysb[:], op=ALU.mult),
                     reads=[kk("ysq"), kk("ysb")], writes=[kk("ysq")])
                P.op("act", lambda e, ysq=ysq, ysg=ysg: e.activation(out=ysg[:], in_=ysq[:], func=AF.Sigmoid, scale=GELU_C),
                     reads=[kk("ysq")], writes=[kk("ysg")])
                P.op("dve", lambda e, YD=YD, ysb=ysb, ysg=ysg: e.tensor_tensor(out=YD[:].rearrange("p g n -> p (g n)"), in0=ysb[:], in1=ysg[:], op=ALU.mult),
                     reads=[kk("ysb"), kk("ysg")], writes=[kk("YD")])
                for gl in range(8):
                    P.op("pe", lambda e, gl=gl, T5=T5, YD=YD: e.transpose(T5[0:NCH, gl * 128:(gl + 1) * 128], YD[:, gl, :], identb[:]),
                         reads=[kk("YD"), "identb"], writes=[T5k])
                P.op("dve", lambda e, YY=YY, T5=T5: e.tensor_copy(out=YY[:].rearrange("n j g h -> n g j h"),
                                                    in_=T5[0:NCH, :].rearrange("n (g j h) -> n g j h", g=8, j=8)),
                     reads=[T5k], writes=[kk("YY")])
                for j in range(8):
                    P.op("pe", lambda e, j=j, YY=YY, T5b=T5b: e.transpose(T5b[:, j * NCH:(j + 1) * NCH], YY[:, j, :, :].rearrange("n g h -> n (g h)"), identb[0:NCH, 0:NCH]),
                         reads=[kk("YY"), "identb"], writes=[T5bk])
                P.op("act", lambda e, ct=ct, T5b=T5b: e.activation(
                    out=ygT[:, ct, :].rearrange("p (n j) -> p j n", j=8),
                    in_=T5b[:, 0:8 * NCH].rearrange("p (j n) -> p j n", j=8), func=AF.Copy),
                    reads=[T5bk], writes=[("ygT", ct)])
            if S2_STOP == 4:
                continue
            for cgo in range(4):
                for kq in range(2):
                    view, wk = wload(scr["wglu"], kq * 8, 8, cgo * 512, 512, [])
                    for c4 in range(4):
                        for k8 in range(8):
                            kc = kq * 8 + k8
                            P.op("pe", lambda e, view=view, c4=c4, k8=k8, kc=kc: e.matmul(
                                B[c4][:, 0:TB], lhsT=view[:, k8, c4 * 128:(c4 + 1) * 128], rhs=ygT[:, kc, :],
                                start=(kc == 0), stop=(kc == 15)),
                                reads=[wk, ("ygT", kc)], writes=[("B", c4)])
                for kq in range(4):
                    view, wk = wload(scr["win"], kq * 8, 8, C_ZS + cgo * 512, 512, [])
                    for c4 in range(4):
                        for k8 in range(8):
                            kc = kq * 8 + k8
                            P.op("pe", lambda e, view=view, c4=c4, k8=k8, kc=kc: e.matmul(
                                B[c4][:, TB:2 * TB], lhsT=view[:, k8, c4 * 128:(c4 + 1) * 128], rhs=hT[:, kc, :],
                                start=(kc == 0), stop=(kc == 31)),
                                reads=[wk, ("hT", kc // 8)], writes=[("B", c4)])
                for c4 in range(4):
                    ct = cgo * 4 + c4
                    P.op("act", lambda e, c4=c4, ct=ct: e.activation(out=sg[:], in_=B[c4][:, 0:TB], func=AF.Sigmoid,
                                                                    bias=pers["bgluc"][:, ct:ct + 1]),
                         reads=[("B", c4)], writes=["sg"])
                    P.op("act", lambda e, c4=c4: e.activation(out=zs[:], in_=B[c4][:, TB:2 * TB], func=AF.Sigmoid),
                         reads=[("B", c4)], writes=["zs"])
                    P.op("dve", lambda e, c4=c4: e.tensor_tensor(out=zs[:], in0=zs[:], in1=B[c4][:, TB:2 * TB], op=ALU.mult),
                         reads=["zs", ("B", c4)], writes=["zs"])
                    P.op("dve", lambda e, ct=ct: e.tensor_tensor(out=sg[:], in0=sg[:], in1=ygT[:, ct, :], op=ALU.mult),
                         reads=["sg", ("ygT", ct)], writes=["sg"])
                    P.op("dve", lambda e, c4=c4: e.tensor_tensor(out=oT5[:, c4, :], in0=sg[:], in1=zs[:], op=ALU.mult),
                         reads=["sg", "zs"], writes=["oT5"])
                P.op("pool", lambda e, cgo=cgo, t0=t0: e.dma_start(out=dram_rows(scr["oT"], 16 + cgo * 4, 4, t0, TB), in_=oT5[:]),
                     reads=["oT5"], writes=[("oTs", blk, cgo)], dma="o2")
        P.emit(ls, "d")


def sweep3(nc, st, io, scr, L):
    P = Prog(nc)
    nblk = L // 512
    with ExitStack() as ls:
        oT = ls.enter_context(nc.sbuf_tensor("oT3", [128, 32, 512], BF16))
        r = [ls.enter_context(nc.sbuf_tensor(f"r3_{i}", [128, 4096], F32)) for i in range(4)]
        lng = ls.enter_context(nc.sbuf_tensor("lng", [128, 4096], F32))
        lnb = ls.enter_context(nc.sbuf_tensor("lnb", [128, 4096], F32))
        stats = [ls.enter_context(nc.sbuf_tensor(f"st3_{i}", [128, 8, 6], F32)) for i in range(4)]
        mv = [ls.enter_context(nc.sbuf_tensor(f"mv3_{i}", [128, 4], F32)) for i in range(4)]
        ring = Ring(nc, ls, "w3_", 4, [128, 8, 512], BF16)
        ps = [ls.enter_context(nc.psum_tensor(f"ps3_{i}", [128, 512], F32)) for i in range(8)]
        P.op("sp", lambda e: e.dma_start(out=lng[:], in_=io["ln_g"][0:1, :].broadcast_to([128, 4096])),
             writes=["lng"], dma="c3_1")
        P.op("sp", lambda e: e.dma_start(out=lnb[:], in_=io["ln_b"][0:1, :].broadcast_to([128, 4096])),
             writes=["lnb"], dma="c3_2")
        for blk in range(nblk):
            t0 = blk * 512
            for q in range(4):
                P.op("sp", lambda e, q=q, t0=t0: e.dma_start(
                    out=oT[:, q * 8:(q + 1) * 8, :], in_=dram_rows(scr["oT"], q * 8, 8, t0, 512)),
                    writes=[("oT", q)], dma="oT_%d" % q)
            for i in range(4):
                P.op("pool", lambda e, i=i, t0=t0: e.dma_start(out=r[i][:], in_=io["x"][t0 + i * 128:t0 + (i + 1) * 128, :]),
                     writes=[("r", i, c) for c in range(8)], dma="x3_%d" % i)
            for cg in range(8):
                for kq in range(4):
                    s, wt, wk = ring.next()
                    P.op("sp", lambda e, wt=wt, kq=kq, cg=cg: e.dma_start(
                        out=wt[:], in_=dram_rows(scr["wout"], kq * 8, 8, cg * 512, 512)),
                        writes=[wk], dma="w3_%d" % s)
                    for i in range(4):
                        b = (cg % 2) * 4 + i
                        for k8 in range(8):
                            kc = kq * 8 + k8
                            P.op("pe", lambda e, b=b, kc=kc, i=i, wt=wt, k8=k8: e.matmul(
                                ps[b][:], lhsT=oT[:, kc, i * 128:(i + 1) * 128], rhs=wt[:, k8, :],
                                start=(kc == 0), stop=(kc == 31)),
                                reads=[wk, ("oT", kq)], writes=[("ps", b)])
                for i in range(4):
                    b = (cg % 2) * 4 + i
                    sl = slice(cg * 512, (cg + 1) * 512)
                    P.op("dve", lambda e, i=i, b=b, sl=sl: e.scalar_tensor_tensor(
                        out=r[i][:, sl], in0=r[i][:, sl], scalar=ALPHA, in1=ps[b][:], op0=ALU.mult, op1=ALU.add),
                        reads=[("ps", b), ("r", i, cg)], writes=[("r", i, cg)])
                    P.op("dve", lambda e, i=i, cg=cg, sl=sl: e.bn_stats(out=stats[i][:, cg, :], in_=r[i][:, sl]),
                         reads=[("r", i, cg)], writes=[("st", i)])
            for i in range(4):
                rk = [("r", i, c) for c in range(8)]
                P.op("dve", lambda e, i=i: e.bn_aggr(out=mv[i][:, 0:2], in_=stats[i][:].rearrange("p a b -> p (a b)")),
                     reads=[("st", i)], writes=[("mv", i)])
                P.op("dve", lambda e, i=i: e.tensor_scalar(out=mv[i][:, 2:3], in0=mv[i][:, 1:2], scalar1=EPS,
                                                           scalar2=None, op0=ALU.add),
                     reads=[("mv", i)], writes=[("mv", i)])
                P.op("act", lambda e, i=i: e.activation(out=mv[i][:, 2:3], in_=mv[i][:, 2:3], func=AF.Sqrt),
                     reads=[("mv", i)], writes=[("mv", i)])
                P.op("dve", lambda e, i=i: e.reciprocal(out=mv[i][:, 2:3], in_=mv[i][:, 2:3]),
                     reads=[("mv", i)], writes=[("mv", i)])
                P.op("dve", lambda e, i=i: e.tensor_scalar(out=mv[i][:, 3:4], in0=mv[i][:, 0:1], scalar1=mv[i][:, 2:3],
                                                           scalar2=-1.0, op0=ALU.mult, op1=ALU.mult),
                     reads=[("mv", i)], writes=[("mv", i)])
                P.op("act", lambda e, i=i: e.activation(out=r[i][:], in_=r[i][:], func=AF.Identity,
                                                        bias=mv[i][:, 3:4], scale=mv[i][:, 2:3]),
                     reads=[("mv", i)] + rk, writes=rk)
                P.op("pool", lambda e, i=i: e.tensor_tensor(out=r[i][:], in0=r[i][:], in1=lng[:], op=ALU.mult),
                     reads=rk + ["lng"], writes=rk)
                P.op("dve", lambda e, i=i: e.tensor_tensor(out=r[i][:], in0=r[i][:], in1=lnb[:], op=ALU.add),
                     reads=rk + ["lnb"], writes=rk)
                P.op("pool", lambda e, i=i, t0=t0: e.dma_start(out=io["y"][t0 + i * 128:t0 + (i + 1) * 128, :], in_=r[i][:]),
                     reads=rk, writes=[("y", blk, i)], dma="y3_%d" % i)
        P.emit(ls, "c")


def build(L, stages=("p0", "s1", "s2", "s3"), dbg=False, Lp=0):
    nc = bass.Bass("TRN2", target_bir_lowering=False)
    io = {}

    def inp(name, shape):
        io[name] = nc.dram_tensor(name, shape, F32, kind="ExternalInput").ap()
    inp("x", [L, D]); inp("c", [32, 128]); inp("w_ada", [D, 3 * D]); inp("b_ada", [96, 128])
    if Lp:
        inp("xp", [Lp, D]); inp("flag", [128, 1])
    inp("w_in", [D, DIN]); inp("w_gate", [16, 1024]); inp("b_gate", [1, 1024]); inp("gnorm", [1, 512])
    inp("lam_re", [64, 128]); inp("lam_im", [64, 128]); inp("log_dt", [64, 2])
    inp("b_re", [128, 64, 16]); inp("b_im", [128, 64, 16]); inp("c_re", [128, 16, 64]); inp("c_im", [128, 16, 64])
    inp("s5_d", [128, 16]); inp("w_glu", [2048, 2048]); inp("b_glu", [16, 128])
    inp("w_out", [D, D]); inp("ln_g", [1, D]); inp("ln_b", [1, D])
    io["y"] = nc.dram_tensor("y", [L, D], F32, kind="ExternalOutput").ap()
    scr = {}
    okind = ("ExternalInput" if ("s1" not in stages and "s2" not in stages and "s2p" not in stages) else "ExternalOutput") if dbg else "Internal"
    scr["oT"] = nc.dram_tensor("oT_scr", [D, L], BF16, kind=okind).ap()
    scr["win"] = nc.dram_tensor("win_scr", [D, DIN], BF16).ap()
    scr["wglu"] = nc.dram_tensor("wglu_scr", [2048, 2048], BF16).ap()
    scr["wout"] = nc.dram_tensor("wout_scr", [D, D], BF16).ap()
    scr["gate"] = nc.dram_tensor("gate_scr", [32, 128], F32).ap()
    scr["mint"] = nc.dram_tensor("mint_scr", [128, 128, 128], BF16).ap()
    scr["zt"] = nc.dram_tensor("zt_scr", [64, 2, 128, 128], BF16).ap()
    scr["ff"] = nc.dram_tensor("ff_scr", [64, 2, 128, 128], BF16).ap()
    with ExitStack() as st:
        pers = {}
        pers["sh"] = st.enter_context(nc.sbuf_tensor("sh_c", [128, 32], F32))
        pers["s1p"] = st.enter_context(nc.sbuf_tensor("s1p_c", [128, 32], F32))
        pers["flag"] = st.enter_context(nc.sbuf_tensor("flagt", [128, 1], F32))
        pers["A8c"] = st.enter_context(nc.sbuf_tensor("A8c", [128, 2, 64], F32))
        pers["A8n"] = st.enter_context(nc.sbuf_tensor("A8n", [128, 2, 64], F32))
        pers["bgluc"] = st.enter_context(nc.sbuf_tensor("bgluc", [128, 16], F32))
        if "p0" in stages:
            phase0(nc, st, io, scr, pers, L, Lp)
        if "s1" in stages:
            sweep1(nc, st, io, scr, pers, L, Lp)
        if "s2" in stages or "s2p" in stages:
            s5_prologue(nc, st, io, scr, pers)
        if "s2" in stages:
            sweep2(nc, st, io, scr, pers, L, Lp)
        if "s3" in stages:
            sweep3(nc, st, io, scr, L)
    return nc


def make_in_map(inputs, b, L, s=0, Lp=0):
    f = lambda a: np.ascontiguousarray(a, dtype=np.float32)
    i = inputs
    m = {
        "x": f(i["x"][b, s * L:(s + 1) * L]), "c": f(i["c"][b].reshape(32, 128)), "w_ada": f(i["w_ada"][0]),
        "b_ada": f(i["b_ada"][0].reshape(96, 128)), "w_in": f(i["w_in"][0]), "w_gate": f(i["w_gla_gate"][0]),
        "b_gate": f(i["b_gla_gate"][0].reshape(1, 1024)), "gnorm": f(i["gla_norm_g"][0].reshape(1, 512)),
        "lam_re": f(i["s5_lambda_re"][0].reshape(64, 128)), "lam_im": f(i["s5_lambda_im"][0].reshape(64, 128)),
        "log_dt": f(i["s5_log_dt"][0].reshape(64, 2)), "b_re": f(i["s5_b_re"][0]), "b_im": f(i["s5_b_im"][0]),
        "c_re": f(i["s5_c_re"][0]), "c_im": f(i["s5_c_im"][0]), "s5_d": f(i["s5_d"][0].reshape(128, 16)),
        "w_glu": f(i["w_glu"][0]), "b_glu": f(i["b_glu"][0].reshape(16, 128)), "w_out": f(i["w_out"][0]),
        "ln_g": f(i["ln_g"][0].reshape(1, D)), "ln_b": f(i["ln_b"][0].reshape(1, D)),
    }
    if Lp:
        m["xp"] = f(i["x"][b, 0:Lp])
        m["flag"] = np.full((128, 1), float(s), dtype=np.float32)
    return m


def kernel(**inputs):
    B, S = inputs["x"].shape[0], inputs["x"].shape[1]
    L = S // 2
    nc = build(L, Lp=L)
    in_maps = [make_in_map(inputs, b, L, s, L) for b in range(B) for s in range(2)]
    res = run_bass_kernel_spmd(nc, in_maps, core_ids=list(range(2 * B)))
    out = np.empty((B, S, D), dtype=np.float32)
    for b in range(B):
        for s in range(2):
            out[b, s * L:(s + 1) * L] = np.asarray(res.results[2 * b + s]["y"], dtype=np.float32)
    return out
```

```python
import numpy as np
from contextlib import ExitStack
import concourse.bass as bass
import concourse.mybir as mybir
from concourse.bass_utils import run_bass_kernel_spmd

F32 = mybir.dt.float32
BF16 = mybir.dt.bfloat16
I32 = mybir.dt.int32
ALU = mybir.AluOpType
AF = mybir.ActivationFunctionType
AX = mybir.AxisListType

D = 4096
DIN = 10256
ALPHA = 2.0 ** 0.25
EPS = 1e-5
C_Q, C_K, C_V, C_G, C_ZG, C_U, C_ZS = 0, 1024, 2048, 4096, 4112, 6160, 8208


class Prog:
    ENGS = ("pe", "act", "dve", "pool", "sp")

    def __init__(self, nc):
        self.nc = nc
        self.ops = []
        self.last_w = {}
        self.readers = {}

    def op(self, eng, fn, reads=(), writes=(), dma=None):
        idx = len(self.ops)
        deps = set()
        for k in reads:
            if k in self.last_w:
                deps.add(self.last_w[k])
        for k in writes:
            if k in self.last_w:
                deps.add(self.last_w[k])
            for r in self.readers.get(k, ()):
                deps.add(r)
        self.ops.append(dict(eng=eng, fn=fn, deps=deps, dma=dma, needed=False))
        for k in reads:
            self.readers.setdefault(k, []).append(idx)
        for k in writes:
            self.last_w[k] = idx
            self.readers[k] = []
        return idx

    def emit(self, stack, tag):
        nc = self.nc
        ops = self.ops
        for i, o in enumerate(ops):
            latest = {}
            for d in o["deps"]:
                od = ops[d]
                if od["dma"] is not None:
                    continue
                if od["eng"] == "pe" and o["eng"] == "pe" and o["dma"] is None:
                    continue
                if od["eng"] != "pe":
                    od["needed"] = True
                    continue
                if od["eng"] not in latest or d > latest[od["eng"]]:
                    latest[od["eng"]] = d
            for d in latest.values():
                ops[d]["needed"] = True
        sems = {e: stack.enter_context(nc.semaphore(tag + "s_" + e)) for e in self.ENGS}
        dma_sems, dma_cnt = {}, {}
        cnt = {e: 0 for e in self.ENGS}
        ev = [None] * len(ops)
        for i, o in enumerate(ops):
            if o["dma"] is not None:
                name = o["dma"]
                if name not in dma_sems:
                    dma_sems[name] = stack.enter_context(nc.semaphore(tag + "d_" + name))
                    dma_cnt[name] = 0
                dma_cnt[name] += 16
                ev[i] = (dma_sems[name], dma_cnt[name], "dma:" + name)
            elif o["needed"]:
                cnt[o["eng"]] += 1
                ev[i] = (sems[o["eng"]], cnt[o["eng"]], o["eng"])
        final_dma = {n: (dma_sems[n], dma_cnt[n]) for n in dma_sems}
        per_eng = {e: [] for e in self.ENGS}
        for i, o in enumerate(ops):
            per_eng[o["eng"]].append(i)
        with nc.Block() as block:
            getters = dict(pe=block.tensor, act=block.scalar, dve=block.vector,
                           pool=block.gpsimd, sp=block.sync)
            for e in self.ENGS:
                idxs = per_eng[e]

                def body(engine, e=e, idxs=idxs):
                    waited = {}
                    for i in idxs:
                        o = ops[i]
                        need = {}
                        for d in o["deps"]:
                            if ev[d] is None:
                                continue
                            s, v, tg = ev[d]
                            if tg not in need or need[tg][1] < v:
                                need[tg] = (s, v)
                        for tg, (s, v) in need.items():
                            if waited.get(tg, 0) >= v:
                                continue
                            engine.wait_ge(s, v)
                            waited[tg] = v
                        ins = o["fn"](engine)
                        if ev[i] is not None:
                            ins.then_inc(ev[i][0], 16 if o["dma"] is not None else 1)
                    if e == "sp":
                        for n, (s, v) in final_dma.items():
                            engine.wait_ge(s, v)
                        for e2 in ("pe", "act", "dve", "pool"):
                            if cnt[e2] > 0:
                                engine.wait_ge(sems[e2], cnt[e2])
                getters[e](body)


class Ring:
    def __init__(self, nc, st, name, n, shape, dtype):
        self.t = [st.enter_context(nc.sbuf_tensor(f"{name}{i}", shape, dtype)) for i in range(n)]
        self.n = n
        self.i = 0
        self.name = name

    def next(self):
        s = self.i % self.n
        self.i += 1
        return s, self.t[s], (self.name, s)


def dram_rows(ap, r0, nkc, c0, nc_):
    return ap[r0 * 128:(r0 + nkc) * 128, c0:c0 + nc_].rearrange("(kc p) c -> p kc c", p=128)


def make_iota_mask(P, nc, st, name, shape, pattern, base, cm, op, key):
    ti = st.enter_context(nc.sbuf_tensor(name + "_i", shape, I32))
    tf = st.enter_context(nc.sbuf_tensor(name, shape, F32))
    P.op("pool", lambda e: e.iota(ti[:], pattern=pattern, base=base, channel_multiplier=cm),
         writes=[key + "_i"])
    P.op("dve", lambda e: e.tensor_scalar(out=tf[:], in0=ti[:], scalar1=0.0, scalar2=None, op0=op),
         reads=[key + "_i"], writes=[key])
    return tf


def phase0(nc, st, io, scr, pers, L, Lp=0):
    P = Prog(nc)
    with ExitStack() as ls:
        ident = make_iota_mask(P, nc, ls, "ident0", [128, 128], [[1, 128]], 0, -1, ALU.is_equal, "ident")
        c32 = ls.enter_context(nc.sbuf_tensor("c32", [32, 128], F32))
        ba96 = ls.enter_context(nc.sbuf_tensor("ba96", [96, 128], F32))
        scol = ls.enter_context(nc.sbuf_tensor("scol", [128, 32, 2], F32))
        bac = ls.enter_context(nc.sbuf_tensor("bac", [128, 96], F32))
        modc = ls.enter_context(nc.sbuf_tensor("modc", [128, 96], F32))
        g32 = ls.enter_context(nc.sbuf_tensor("g32", [32, 128], F32))
        wa = [ls.enter_context(nc.sbuf_tensor(f"wa{i}", [128, 12288], F32)) for i in range(2)]
        grow = ls.enter_context(nc.sbuf_tensor("grow", [128, 4096], F32))
        wf = [ls.enter_context(nc.sbuf_tensor(f"wf{i}", [128, 4096], F32)) for i in range(2)]
        wb = [ls.enter_context(nc.sbuf_tensor(f"wb{i}", [128, 4096], BF16)) for i in range(2)]
        pst = ls.enter_context(nc.psum_tensor("p0t", [128, 512], F32))[:, 0:128]
        psm = ls.enter_context(nc.psum_tensor("p0m", [128, 256, 2], F32))[:, 0:96, :]

        for r in range(32):
            P.op("pool", lambda e, r=r: e.dma_start(out=scr["win"][r * 128:(r + 1) * 128, :],
                                                  in_=io["w_in"][r * 128:(r + 1) * 128, :],
                                                  max_dma_last_dim=4096),
                 writes=[("win", r)], dma="cast")
        for r in range(16):
            P.op("pool", lambda e, r=r: e.dma_start(out=scr["wglu"][r * 128:(r + 1) * 128, :],
                                                  in_=io["w_glu"][r * 128:(r + 1) * 128, :],
                                                  max_dma_last_dim=4096),
                 writes=[("wglu", r)], dma="cast")

        if Lp:
            P.op("sp", lambda e: e.dma_start(out=pers["flag"][:], in_=io["flag"][:, :]), writes=["flag"], dma="ld0_1")
        P.op("sp", lambda e: e.dma_start(out=c32[:], in_=io["c"][:, :]), writes=["c32"], dma="ld0_2")
        P.op("sp", lambda e: e.dma_start(out=ba96[:], in_=io["b_ada"][:, :]), writes=["ba96"], dma="ld0_3")
        P.op("pe", lambda e: e.transpose(pst[:, 0:32], c32[:], ident[0:32, 0:32]),
             reads=["c32", "ident"], writes=["pst"])
        for j in range(2):
            P.op("act", lambda e, j=j: e.activation(out=scol[:, :, j], in_=pst[:, 0:32], func=AF.Silu),
                 reads=["pst"], writes=["scol"])
        P.op("pe", lambda e: e.transpose(pst[:, 0:96], ba96[:], ident[0:96, 0:96]),
             reads=["ba96", "ident", "scol"], writes=["pst"])
        P.op("dve", lambda e: e.tensor_copy(out=bac[:], in_=pst[:, 0:96]), reads=["pst"], writes=["bac"])
        for kc in range(32):
            P.op("sp", lambda e, kc=kc: e.dma_start(out=wa[kc % 2][:], in_=io["w_ada"][kc * 128:(kc + 1) * 128, :]),
                 writes=[("wa", kc % 2)], dma="wa%d" % (kc % 2))
            for ct in range(96):
                P.op("pe", lambda e, kc=kc, ct=ct: e.matmul(
                    psm[:, ct, :], lhsT=wa[kc % 2][:, ct * 128:(ct + 1) * 128], rhs=scol[:, kc, :],
                    start=(kc == 0 and ct == 0), stop=(kc == 31 and ct == 95), skip_group_check=True),
                    reads=[("wa", kc % 2), "scol"], writes=["psm"])
        P.op("dve", lambda e: e.tensor_tensor(out=modc[:], in0=psm[:, :, 0], in1=bac[:], op=ALU.add),
             reads=["psm", "bac"], writes=["modc"])
        P.op("dve", lambda e: e.tensor_copy(out=pers["sh"][:], in_=modc[:, 0:32]), reads=["modc"], writes=["sh"])
        P.op("dve", lambda e: e.tensor_scalar(out=pers["s1p"][:], in0=modc[:, 32:64], scalar1=1.0, scalar2=None,
                                              op0=ALU.add), reads=["modc"], writes=["s1p"])
        P.op("pe", lambda e: e.transpose(pst[0:32, :], modc[:, 64:96], ident[:]),
             reads=["modc", "ident", "bac"], writes=["pst"])
        P.op("dve", lambda e: e.tensor_copy(out=g32[:], in_=pst[0:32, :]), reads=["pst"], writes=["g32"])
        P.op("sp", lambda e: e.dma_start(out=scr["gate"][:, :], in_=g32[:]), reads=["g32"], writes=["gscr"], dma="ld0_4")
        P.op("sp", lambda e: e.dma_start(
            out=grow[:], in_=scr["gate"].rearrange("a b -> (a b)")[None, :].broadcast_to([128, 4096])),
            reads=["gscr"], writes=["grow"], dma="ld0_5")
        for r in range(32):
            P.op("sp", lambda e, r=r: e.dma_start(out=wf[r % 2][:], in_=io["w_out"][r * 128:(r + 1) * 128, :]),
                 writes=[("wf", r % 2)], dma="wf%d" % (r % 2))
            P.op("dve", lambda e, r=r: e.tensor_tensor(out=wb[r % 2][:], in0=wf[r % 2][:], in1=grow[:], op=ALU.mult),
                 reads=[("wf", r % 2), "grow"], writes=[("wb", r % 2)])
            P.op("sp", lambda e, r=r: e.dma_start(out=scr["wout"][r * 128:(r + 1) * 128, :], in_=wb[r % 2][:]),
                 reads=[("wb", r % 2)], writes=[("wout", r)], dma="wst%d" % (r % 2))
        P.emit(ls, "a")


def load_hT(P, nc, xsrc, pers, xs, hT, psb, ident, t0, tagq):
    for i in range(4):
        xb = xs[i % len(xs)]
        xk = ("xs", i % len(xs))
        P.op("pool", lambda e, xb=xb, i=i: e.dma_start(out=xb[:], in_=xsrc[t0 + i * 128:t0 + (i + 1) * 128, :]),
             writes=[xk], dma="%s_%d" % (tagq, i % len(xs)))
        for kg in range(8):
            b = kg % len(psb)
            for k4 in range(4):
                kc = kg * 4 + k4
                P.op("pe", lambda e, xb=xb, b=b, k4=k4, kc=kc: e.transpose(
                    psb[b][:, k4 * 128:(k4 + 1) * 128], xb[:, kc * 128:(kc + 1) * 128], ident[:]),
                    reads=[xk, "identf"], writes=[("B", b)])
            for k4 in range(4):
                kc = kg * 4 + k4
                if kg % 2 == 0:
                    P.op("act", lambda e, b=b, k4=k4, kc=kc, i=i: e.activation(
                        out=hT[:, kc, i * 128:(i + 1) * 128], in_=psb[b][:, k4 * 128:(k4 + 1) * 128],
                        func=AF.Identity, bias=pers["sh"][:, kc:kc + 1], scale=pers["s1p"][:, kc:kc + 1]),
                        reads=[("B", b)], writes=[("hT", kc // 8)])
                else:
                    P.op("dve", lambda e, b=b, k4=k4, kc=kc, i=i: e.tensor_scalar(
                        out=hT[:, kc, i * 128:(i + 1) * 128], in0=psb[b][:, k4 * 128:(k4 + 1) * 128],
                        scalar1=pers["s1p"][:, kc:kc + 1], scalar2=pers["sh"][:, kc:kc + 1],
                        op0=ALU.mult, op1=ALU.add),
                        reads=[("B", b)], writes=[("hT", kc // 8)])


def sweep1(nc, st, io, scr, pers, L, Lp=0):
    P = Prog(nc)
    nblk = L // 512
    with ExitStack() as ls:
        sb = lambda name, shape, dt: ls.enter_context(nc.sbuf_tensor(name, shape, dt))
        identf = make_iota_mask(P, nc, ls, "identf1", [128, 128], [[1, 128]], 0, -1, ALU.is_equal, "identf")
        identb = sb("identb1", [128, 128], BF16)
        P.op("dve", lambda e: e.tensor_copy(out=identb[:], in_=identf[:]), reads=["identf"], writes=["identb"])
        m_i = sb("m64i", [128, 64], I32)
        mask64 = sb("mask64", [128, 64], F32)
        for hf in range(2):
            P.op("pool", lambda e, hf=hf: e.iota(m_i[hf * 64:(hf + 1) * 64, :], pattern=[[1, 64]], base=0,
                                               channel_multiplier=-1), writes=["m64i"])
        P.op("dve", lambda e: e.tensor_scalar(out=mask64[:], in0=m_i[:], scalar1=0.0, scalar2=None, op0=ALU.is_ge),
             reads=["m64i"], writes=["mask64"])
        tri = sb("tri", [128, 128], F32)
        P.op("dve", lambda e: e.memset(tri[:], 0.0), writes=["tri"])
        for hf in range(2):
            P.op("dve", lambda e, hf=hf: e.tensor_copy(out=tri[hf * 64:(hf + 1) * 64, hf * 64:(hf + 1) * 64],
                                                     in_=mask64[hf * 64:(hf + 1) * 64, :]),
                 reads=["mask64", "tri"], writes=["tri"])
        wgate = sb("wgate", [16, 1024], F32)
        bgate = sb("bgate", [1, 1024], F32)
        ones = sb("ones1", [1, 128], F32)
        gnb = sb("gnb", [128, 512], F32)
        P.op("sp", lambda e: e.dma_start(out=wgate[:], in_=io["w_gate"][:, :]), writes=["wgate"], dma="c1_1")
        P.op("sp", lambda e: e.dma_start(out=bgate[:], in_=io["b_gate"][:, :]), writes=["bgate"], dma="c1_2")
        P.op("sp", lambda e: e.dma_start(out=gnb[:], in_=io["gnorm"][0:1, :].broadcast_to([128, 512])),
             writes=["gnb"], dma="c1_3")
        P.op("dve", lambda e: e.memset(ones[:], 1.0), writes=["ones"])
        T = sb("Tst", [128, 8, 512], F32)
        Sbf = sb("Sbf", [128, 8, 512], BF16)
        eblp = sb("eblp", [128, 8], F32)
        P.op("dve", lambda e: e.memset(T[:], 0.0), writes=[("T", m) for m in range(8)])
        P.op("pool", lambda e: e.memset(Sbf[:], 0.0), writes=[("Sbf", m) for m in range(8)])
        P.op("dve", lambda e: e.memset(eblp[:], 1.0), writes=[("eblp", m) for m in range(8)])
        xs = [sb(f"xs1_{i}", [128, 4096], F32) for i in range(2)]
        hT = sb("hT1", [128, 32, 512], BF16)
        ring = Ring(nc, ls, "w1_", 3, [128, 4096], BF16)
        wg16 = sb("wg16", [128, 32, 16], BF16)
        glrT = sb("glrT", [16, 512], F32)
        nls = sb("nls", [128, 4, 1024], F32)
        ebt = [sb(f"ebt{e}", [128, 512], F32) for e in range(2)]
        eit = [sb(f"eit{e}", [128, 512], F32) for e in range(2)]
        qdec = [sb(f"qdec{e}", [128, 512], BF16) for e in range(2)]
        kinvT = [sb(f"kinvT{e}", [128, 512], BF16) for e in range(2)]
        kinv_tok = sb("kinvtok", [128, 4, 256], BF16)
        v_tok = sb("vtok", [128, 4, 512], BF16)
        gz = sb("gz", [128, 4, 512], F32)
        zs = sb("zs", [128, 512], F32)
        o_tok = sb("otok", [128, 4, 512], BF16)
        oTs = sb("oTs", [128, 4, 512], BF16)
        att_s = sb("atts", [128, 64], BF16)
        junk = sb("junk1", [128, 512], BF16)
        ssq = sb("ssq", [128, 2], F32)
        B = [ls.enter_context(nc.psum_tensor(f"B{i}", [128, 512], F32)) for i in range(5)]
        B5 = ls.enter_context(nc.psum_tensor("B5", [128, 1024], BF16))
        B6 = ls.enter_context(nc.psum_tensor("B6", [128, 512], F32))
        B7 = ls.enter_context(nc.psum_tensor("B7", [128, 512], F32))

        def wload(r0, nkc, c0, ncol):
            s, wt, wk = ring.next()
            view = wt[:].rearrange("p (a b) -> p a b", b=ncol)
            P.op("sp", lambda e: e.dma_start(out=view, in_=dram_rows(scr["win"], r0, nkc, c0, ncol)),
                 reads=[("win", r) for r in range(r0, r0 + nkc)], writes=[wk], dma="w1_%d" % s)
            return view, wk

        blocks = [("pre", i) for i in range(Lp // 512)] + [("own", i) for i in range(nblk)]
        for mode, blk in blocks:
            own = mode == "own"
            last_pre = (mode == "pre" and blk == Lp // 512 - 1)
            t0 = blk * 512
            load_hT(P, nc, io["x"] if own else io["xp"], pers, xs, hT, B[0:4], identf, t0, "x1")
            tok0 = t0 + (Lp if own else 0)
            for q in range(4):
                P.op("pool", lambda e, q=q, tok0=tok0: e.dma_start(
                    out=dram_rows(scr["hT"], q * 8, 8, tok0, 512), in_=hT[:, q * 8:(q + 1) * 8, :]),
                    reads=[("hT", q)], writes=[("hTscr", tok0, q)], dma="hs_%d" % q)
            P.op("sp", lambda e: e.dma_start(out=wg16[:], in_=dram_rows(scr["win"], 0, 32, C_G, 16)),
                 reads=[("win", r) for r in range(32)], writes=["wg16"], dma="w1g")
            for kc in range(32):
                P.op("pe", lambda e, kc=kc: e.matmul(B[4][0:16, :], lhsT=wg16[:, kc, :], rhs=hT[:, kc, :],
                                                     start=(kc == 0), stop=(kc == 31)),
                     reads=["wg16", ("hT", kc // 8)], writes=["B4"])
            P.op("dve", lambda e: e.tensor_copy(out=glrT[:], in_=B[4][0:16, :]), reads=["B4"], writes=["glrT"])
            for i in range(4):
                for e2 in range(2):
                    sl = slice(e2 * 512, (e2 + 1) * 512)
                    P.op("pe", lambda e, i=i, sl=sl: e.matmul(B[4][:], lhsT=glrT[:, i * 128:(i + 1) * 128],
                                                            rhs=wgate[:, sl], start=True, stop=False),
                         reads=["glrT", "wgate"], writes=["B4"])
                    P.op("pe", lambda e, sl=sl: e.matmul(B[4][:], lhsT=ones[:, :], rhs=bgate[:, sl],
                                                       start=False, stop=True),
                         reads=["ones", "bgate"], writes=["B4"])
                    P.op("act", lambda e, i=i, sl=sl: e.activation(out=nls[:, i, sl], in_=B[4][:], func=AF.Exp, scale=-1.0),
                         reads=["B4"], writes=[("nls", i)])
                    P.op("act", lambda e, i=i, sl=sl: e.activation(out=nls[:, i, sl], in_=nls[:, i, sl], func=AF.Ln, bias=1.0),
                         reads=[("nls", i)], writes=[("nls", i)])
            for h in range(4):
                for e2 in range(2):
                    m = 2 * h + e2
                    for i in range(4):
                        P.op("pe", lambda e, i=i, m=m: e.matmul(B[4][:, i * 128:(i + 1) * 128],
                                                              lhsT=nls[:, i, m * 128:(m + 1) * 128], rhs=tri[:],
                                                              start=True, stop=True),
                             reads=[("nls", i), "tri"], writes=["B4"])
                    P.op("act", lambda e, e2=e2: e.activation(out=ebt[e2][:], in_=B[4][:], func=AF.Exp, scale=-1.0 / 16),
                         reads=["B4"], writes=[("ebt", e2)])
                    P.op("act", lambda e, e2=e2: e.activation(out=eit[e2][:], in_=B[4][:], func=AF.Exp, scale=1.0 / 16),
                         reads=["B4"], writes=[("eit", e2)])
                for which, c0 in (("q", C_Q + 256 * h), ("k", C_K + 256 * h)):
                    if which == "q" and not own:
                        continue
                    pb = (0, 1) if which == "q" else (2, 3)
                    for kh in range(2):
                        view, wk = wload(kh * 16, 16, c0, 256)
                        for e2 in range(2):
                            for k16 in range(16):
                                kc = kh * 16 + k16
                                P.op("pe", lambda e, view=view, e2=e2, k16=k16, kc=kc, pb=pb: e.matmul(
                                    B[pb[e2]][:], lhsT=view[:, k16, e2 * 128:(e2 + 1) * 128], rhs=hT[:, kc, :],
                                    start=(kc == 0), stop=(kc == 31)),
                                    reads=[wk, ("hT", kc // 8)], writes=[("B", pb[e2])])
                    for e2 in range(2):
                        if which == "q":
                            P.op("dve", lambda e, e2=e2, pb=pb: e.scalar_tensor_tensor(
                                out=qdec[e2][:], in0=B[pb[e2]][:], scalar=1.0 / 16, in1=ebt[e2][:],
                                op0=ALU.mult, op1=ALU.mult),
                                reads=[("B", pb[e2]), ("ebt", e2)], writes=[("qdec", e2)])
                        else:
                            P.op("dve", lambda e, e2=e2, pb=pb: e.tensor_tensor(
                                out=kinvT[e2][:], in0=B[pb[e2]][:], in1=eit[e2][:], op=ALU.mult),
                                reads=[("B", pb[e2]), ("eit", e2)], writes=[("kinvT", e2)])
                for e2 in range(2):
                    for i in range(4):
                        P.op("pe", lambda e, e2=e2, i=i: e.transpose(
                            B5[:, (i * 2 + e2) * 128:(i * 2 + e2 + 1) * 128], kinvT[e2][:, i * 128:(i + 1) * 128], identb[:]),
                            reads=[("kinvT", e2), "identb"], writes=["B5"])
                P.op("act", lambda e: e.activation(out=kinv_tok[:].rearrange("p a b -> p (a b)"), in_=B5[:], func=AF.Copy),
                     reads=["B5"], writes=["kinvtok"])
                for which, c0 in (("v", C_V + 512 * h), ("z", C_ZG + 512 * h)):
                    if which == "z" and not own:
                        continue
                    for kq in range(4):
                        view, wk = wload(kq * 8, 8, c0, 512)
                        for i in range(4):
                            for k8 in range(8):
                                kc = kq * 8 + k8
                                P.op("pe", lambda e, view=view, i=i, k8=k8, kc=kc: e.matmul(
                                    B[i][:], lhsT=hT[:, kc, i * 128:(i + 1) * 128], rhs=view[:, k8, :],
                                    start=(kc == 0), stop=(kc == 31)),
                                    reads=[wk, ("hT", kc // 8)], writes=[("B", i)])
                    for i in range(4):
                        if which == "v":
                            P.op("act", lambda e, i=i: e.activation(out=v_tok[:, i, :], in_=B[i][:], func=AF.Copy),
                                 reads=[("B", i)], writes=[("vtok", i)])
                        else:
                            P.op("act", lambda e, i=i: e.activation(out=zs[:], in_=B[i][:], func=AF.Silu),
                                 reads=[("B", i)], writes=["zs"])
                            P.op("dve", lambda e, i=i: e.tensor_tensor(out=gz[:, i, :], in0=zs[:], in1=gnb[:], op=ALU.mult),
                                 reads=["zs", "gnb"], writes=[("gz", i)])
                for c in range(8):
                    par, i = c % 2, c // 2
                    rows = slice(64 * par, 64 * par + 64)
                    cols = slice(64 * c, 64 * c + 64)
                    for e2 in range(2 if own else 0):
                        P.op("pe", lambda e, e2=e2, rows=rows, cols=cols: e.matmul(
                            B7[rows, 0:64], lhsT=kinvT[e2][:, cols], rhs=qdec[e2][:, cols],
                            start=(e2 == 0), stop=(e2 == 1)),
                            reads=[("kinvT", e2), ("qdec", e2)], writes=["B7"])
                    if not own:
                        for e2 in range(2):
                            m = 2 * h + e2
                            ebl_prev = eblp[:, m:m + 1] if c == 0 else ebt[e2][:, 64 * c - 1:64 * c]
                            ebl_cur = ebt[e2][:, 64 * c + 63:64 * c + 64]
                            P.op("pe", lambda e, rows=rows, i=i, e2=e2: e.matmul(
                                B[2 + e2][:], lhsT=kinv_tok[rows, i, e2 * 128:(e2 + 1) * 128], rhs=v_tok[rows, i, :],
                                start=True, stop=True),
                                reads=["kinvtok", ("vtok", i)], writes=[("B", 2 + e2)])
                            P.op("dve", lambda e, m=m, e2=e2, ebl_prev=ebl_prev: e.scalar_tensor_tensor(
                                out=T[:, m, :], in0=T[:, m, :], scalar=ebl_prev, in1=B[2 + e2][:], op0=ALU.mult, op1=ALU.add),
                                reads=[("T", m), ("B", 2 + e2), ("ebt", e2), ("eblp", m)], writes=[("T", m)])
                            if last_pre and c == 7:
                                P.op("act", lambda e, m=m, ebl_cur=ebl_cur: e.activation(
                                    out=Sbf[:, m, :], in_=T[:, m, :], func=AF.Copy, scale=ebl_cur),
                                    reads=[("T", m), ("ebt", e2)], writes=[("Sbf", m)])
                        continue
                    P.op("dve", lambda e, rows=rows: e.tensor_tensor(out=att_s[rows, :], in0=B7[rows, 0:64],
                                                                   in1=mask64[rows, :], op=ALU.mult),
                         reads=["B7", "mask64"], writes=["atts"])
                    P.op("pe", lambda e, rows=rows, i=i: e.matmul(B6[rows, :], lhsT=att_s[rows, :], rhs=v_tok[rows, i, :],
                                                                start=True, stop=False),
                         reads=["atts", ("vtok", i)], writes=["B6"])
                    for e2 in range(2):
                        m = 2 * h + e2
                        P.op("pe", lambda e, rows=rows, cols=cols, e2=e2, m=m: e.matmul(
                            B6[rows, :], lhsT=qdec[e2][:, cols], rhs=Sbf[:, m, :], start=False, stop=(e2 == 1)),
                            reads=[("qdec", e2), ("Sbf", m)], writes=["B6"])
                    P.op("act", lambda e, rows=rows: e.activation(out=junk[rows, :], in_=B6[rows, :], func=AF.Square,
                                                                accum_out=ssq[rows, 0:1]),
                         reads=["B6"], writes=["ssq", "junk"])
                    P.op("dve", lambda e, rows=rows: e.tensor_scalar(out=ssq[rows, 1:2], in0=ssq[rows, 0:1],
                                                                   scalar1=1.0 / 512, scalar2=EPS, op0=ALU.mult, op1=ALU.add),
                         reads=["ssq"], writes=["ssq"])
                    P.op("act", lambda e, rows=rows: e.activation(out=ssq[rows, 1:2], in_=ssq[rows, 1:2], func=AF.Sqrt),
                         reads=["ssq"], writes=["ssq"])
                    P.op("dve", lambda e, rows=rows: e.reciprocal(out=ssq[rows, 1:2], in_=ssq[rows, 1:2]),
                         reads=["ssq"], writes=["ssq"])
                    P.op("dve", lambda e, rows=rows, i=i: e.scalar_tensor_tensor(
                        out=o_tok[rows, i, :], in0=B6[rows, :], scalar=ssq[rows, 1:2], in1=gz[rows, i, :],
                        op0=ALU.mult, op1=ALU.mult),
                        reads=["B6", "ssq", ("gz", i)], writes=[("otok", i)])
                    for e2 in range(2):
                        m = 2 * h + e2
                        ebl_prev = eblp[:, m:m + 1] if c == 0 else ebt[e2][:, 64 * c - 1:64 * c]
                        ebl_cur = ebt[e2][:, 64 * c + 63:64 * c + 64]
                        P.op("pe", lambda e, rows=rows, i=i, e2=e2: e.matmul(
                            B[2 + e2][:], lhsT=kinv_tok[rows, i, e2 * 128:(e2 + 1) * 128], rhs=v_tok[rows, i, :],
                            start=True, stop=True),
                            reads=["kinvtok", ("vtok", i)], writes=[("B", 2 + e2)])
                        P.op("dve", lambda e, m=m, e2=e2, ebl_prev=ebl_prev: e.scalar_tensor_tensor(
                            out=T[:, m, :], in0=T[:, m, :], scalar=ebl_prev, in1=B[2 + e2][:], op0=ALU.mult, op1=ALU.add),
                            reads=[("T", m), ("B", 2 + e2), ("ebt", e2), ("eblp", m)], writes=[("T", m)])
                        P.op("act", lambda e, m=m, ebl_cur=ebl_cur: e.activation(
                            out=Sbf[:, m, :], in_=T[:, m, :], func=AF.Copy, scale=ebl_cur),
                            reads=[("T", m), ("ebt", e2)], writes=[("Sbf", m)])
                for e2 in range(2):
                    m = 2 * h + e2
                    P.op("dve", lambda e, m=m, e2=e2: e.tensor_copy(out=eblp[:, m:m + 1], in_=ebt[e2][:, 511:512]),
                         reads=[("ebt", e2)], writes=[("eblp", m)])
                for half in range(2 if own else 0):
                    for cc2 in range(2):
                        cc = half * 2 + cc2
                        for i in range(4):
                            P.op("pe", lambda e, cc=cc, cc2=cc2, i=i: e.transpose(
                                B5[:, (cc2 * 4 + i) * 128:(cc2 * 4 + i + 1) * 128], o_tok[:, i, cc * 128:(cc + 1) * 128], identb[:]),
                                reads=[("otok", i), "identb"], writes=["B5"])
                    P.op("dve", lambda e, half=half: e.tensor_copy(
                        out=oTs[:, half * 2:half * 2 + 2, :].rearrange("p a b -> p (a b)"), in_=B5[:]),
                        reads=["B5"], writes=["oTs"])
                if own:
                    P.op("pool", lambda e, h=h, t0=t0: e.dma_start(out=dram_rows(scr["oT"], h * 4, 4, t0, 512), in_=oTs[:]),
                         reads=["oTs"], writes=[("oTscr", blk, h)], dma="o1")
            if last_pre:
                allT = [("T", m) for m in range(8)]
                allS = [("Sbf", m) for m in range(8)]
                P.op("dve", lambda e: e.tensor_scalar(out=T[:].rearrange("p a b -> p (a b)"), in0=T[:].rearrange("p a b -> p (a b)"),
                                                      scalar1=pers["flag"][:, 0:1], scalar2=None, op0=ALU.mult),
                     reads=allT + ["flag"], writes=allT)
                P.op("dve", lambda e: e.tensor_scalar(out=Sbf[:].rearrange("p a b -> p (a b)"), in0=Sbf[:].rearrange("p a b -> p (a b)"),
                                                      scalar1=pers["flag"][:, 0:1], scalar2=None, op0=ALU.mult),
                     reads=allS + ["flag"], writes=allS)
        P.emit(ls, "b")


TWO_PI = 6.283185307179586
PI = 3.141592653589793


def s5_prologue(nc, st, io, scr, pers):
    P = Prog(nc)
    with ExitStack() as ls:
        sb = lambda name, shape, dt=F32: ls.enter_context(nc.sbuf_tensor(name, shape, dt))
        cnt = [0]

        def dve(fn, r, w):
            P.op("dve", fn, reads=r, writes=w)

        def TT(out, a, b, op, r, w):
            dve(lambda e: e.tensor_tensor(out=out, in0=a, in1=b, op=op), r, w)

        def TS(out, a, s1, s2, op0, op1, r, w):
            if op1 is None:
                dve(lambda e: e.tensor_scalar(out=out, in0=a, scalar1=s1, scalar2=None, op0=op0), r, w)
            else:
                dve(lambda e: e.tensor_scalar(out=out, in0=a, scalar1=s1, scalar2=s2, op0=op0, op1=op1), r, w)

        def ACT(out, a, func, r, w, **kw):
            P.op("act", lambda e: e.activation(out=out, in_=a, func=func, **kw), reads=r, writes=w)

        identf = make_iota_mask(P, nc, ls, "identfp", [128, 128], [[1, 128]], 0, -1, ALU.is_equal, "identf")
        maskc = make_iota_mask(P, nc, ls, "maskc", [128, 8, 16], [[16, 8], [0, 16]], 15, -1, ALU.is_ge, "maskc")
        rep = make_iota_mask(P, nc, ls, "rep", [16, 8, 16], [[0, 8], [1, 16]], 0, -1, ALU.is_equal, "rep")
        pt = ls.enter_context(nc.psum_tensor("pqt", [128, 512], F32))[:, 0:128]
        pm = [ls.enter_context(nc.psum_tensor(f"pqm{i}", [128, 512], F32))[:, 0:128] for i in range(2)]

        def transp(out_sb, in_ap, npart, nfree, rk, wk, odt_copy="dve"):
            P.op("pe", lambda e: e.transpose(pt[0:nfree, 0:npart], in_ap, identf[0:npart, 0:npart]),
                 reads=rk + ["identf"], writes=["pt"])
            dve(lambda e: e.tensor_copy(out=out_sb, in_=pt[0:nfree, 0:npart]), ["pt"], wk)

        lt = [sb(f"lt{i}", [64, 128]) for i in range(2)]
        ldt = sb("ldt", [64, 2]); dtx = sb("dtx", [64, 2, 64])
        P.op("sp", lambda e: e.dma_start(out=lt[0][:], in_=io["lam_re"][:, :]), writes=["lt0"], dma="q0_1")
        P.op("sp", lambda e: e.dma_start(out=lt[1][:], in_=io["lam_im"][:, :]), writes=["lt1"], dma="q0_2")
        P.op("sp", lambda e: e.dma_start(out=ldt[:], in_=io["log_dt"][:, :]), writes=["ldt"], dma="q0_3")
        ACT(ldt[:], ldt[:], AF.Exp, ["ldt"], ["ldt"])
        dve(lambda e: e.tensor_copy(out=dtx[:], in_=ldt[:, :, None].to_broadcast([64, 2, 64])), ["ldt"], ["dtx"])
        zt = [sb(f"ztt{i}", [64, 128]) for i in range(2)]
        for i in range(2):
            TT(zt[i][:], lt[i][:], dtx[:].rearrange("p a b -> p (a b)"), ALU.mult, [f"lt{i}", "dtx"], [f"ztt{i}"])
        lam = [sb(f"lam{i}", [128, 64]) for i in range(2)]
        z = [sb(f"z{i}", [128, 64]) for i in range(2)]
        for i in range(2):
            transp(lam[i][:], lt[i][:], 64, 128, [f"lt{i}"], [f"lam{i}"])
            transp(z[i][:], zt[i][:], 64, 128, [f"ztt{i}"], [f"z{i}"])
        ki = sb("ki", [128, 64], I32); kf = sb("kf", [128, 64]); rr = sb("rr", [128, 64]); mm_ = sb("mm_", [128, 64])
        xs_ = sb("xsft", [128, 64])

        def sin_of(out, x_ap, xk, ok):
            TS(ki[:], x_ap, 1.0 / TWO_PI, None, ALU.mult, None, [xk], ["ki"])
            dve(lambda e: e.tensor_copy(out=kf[:], in_=ki[:]), ["ki"], ["kf"])
            dve(lambda e: e.scalar_tensor_tensor(out=rr[:], in0=kf[:], scalar=-TWO_PI, in1=x_ap, op0=ALU.mult, op1=ALU.add),
                ["kf", xk], ["rr"])
            TS(mm_[:], rr[:], PI, -TWO_PI, ALU.is_gt, ALU.mult, ["rr"], ["mm_"])
            TT(rr[:], rr[:], mm_[:], ALU.add, ["rr", "mm_"], ["rr"])
            TS(mm_[:], rr[:], -PI, TWO_PI, ALU.is_lt, ALU.mult, ["rr"], ["mm_"])
            TT(rr[:], rr[:], mm_[:], ALU.add, ["rr", "mm_"], ["rr"])
            ACT(out, rr[:], AF.Sin, ["rr"], [ok])

        sn = sb("sn", [128, 64]); cs = sb("cs", [128, 64]); mag = sb("mag", [128, 64]); imag = sb("imag", [128, 64])
        sin_of(sn[:], z[1][:], "z1", "sn")
        TS(xs_[:], z[1][:], PI / 2, None, ALU.add, None, ["z1"], ["xsft"])
        sin_of(cs[:], xs_[:], "xsft", "cs")
        ACT(mag[:], z[0][:], AF.Exp, ["z0"], ["mag"])
        ACT(imag[:], z[0][:], AF.Exp, ["z0"], ["imag"], scale=-1.0)
        APr = sb("APr", [128, 9, 64]); APi = sb("APi", [128, 9, 64]); AMr = sb("AMr", [128, 8, 64]); AMi = sb("AMi", [128, 8, 64])
        t1 = sb("t1", [128, 64]); t2 = sb("t2", [128, 64])
        for Tn, nm in ((APr, "APr"), (AMr, "AMr")):
            dve(lambda e, Tn=Tn: e.memset(Tn[:, 0, :], 1.0), [], [nm])
        for Tn, nm in ((APi, "APi"), (AMi, "AMi")):
            dve(lambda e, Tn=Tn: e.memset(Tn[:, 0, :], 0.0), [], [nm])
        TT(APr[:, 1, :], mag[:], cs[:], ALU.mult, ["mag", "cs"], ["APr"])
        TT(APi[:, 1, :], mag[:], sn[:], ALU.mult, ["mag", "sn"], ["APi"])
        TT(AMr[:, 1, :], imag[:], cs[:], ALU.mult, ["imag", "cs"], ["AMr"])
        dve(lambda e: e.scalar_tensor_tensor(out=AMi[:, 1, :], in0=imag[:], scalar=-1.0, in1=sn[:], op0=ALU.mult, op1=ALU.mult),
            ["imag", "sn"], ["AMi"])

        def cmul(o_re, o_im, a_re, a_im, b_re, b_im, tmpa, tmpb, r, w):
            TT(tmpa, a_re, b_re, ALU.mult, r, ["cm_a"])
            TT(tmpb, a_im, b_im, ALU.mult, r, ["cm_b"])
            TT(o_re, tmpa, tmpb, ALU.subtract, ["cm_a", "cm_b"], w)
            TT(tmpa, a_re, b_im, ALU.mult, r + w, ["cm_a"])
            TT(tmpb, a_im, b_re, ALU.mult, r + w, ["cm_b"])
            TT(o_im, tmpa, tmpb, ALU.add, ["cm_a", "cm_b"], w)

        for k in range(2, 9):
            cmul(APr[:, k, :], APi[:, k, :], APr[:, k - 1, :], APi[:, k - 1, :], APr[:, 1, :], APi[:, 1, :],
                 t1[:], t2[:], ["APr", "APi"], ["APr", "APi"])
        for k in range(2, 8):
            cmul(AMr[:, k, :], AMi[:, k, :], AMr[:, k - 1, :], AMi[:, k - 1, :], AMr[:, 1, :], AMi[:, 1, :],
                 t1[:], t2[:], ["AMr", "AMi"], ["AMr", "AMi"])
        for c in range(2):
            dve(lambda e, c=c: e.tensor_copy(out=pers["A8c"][:, c, :], in_=APr[:, 8, :]), ["APr"], ["A8c"])
        dve(lambda e: e.tensor_copy(out=pers["A8n"][:, 1, :], in_=APi[:, 8, :]), ["APi"], ["A8n"])
        TS(pers["A8n"][:, 0, :], APi[:, 8, :], -1.0, None, ALU.mult, None, ["APi"], ["A8n"])
        den = sb("den", [128, 64]); nre = sb("nre", [128, 64]); fre = sb("fre", [128, 64]); fim = sb("fim", [128, 64])
        TT(t1[:], lam[0][:], lam[0][:], ALU.mult, ["lam0"], ["t1"])
        TT(t2[:], lam[1][:], lam[1][:], ALU.mult, ["lam1"], ["t2"])
        TT(den[:], t1[:], t2[:], ALU.add, ["t1", "t2"], ["den"])
        dve(lambda e: e.reciprocal(out=den[:], in_=den[:]), ["den"], ["den"])
        TS(nre[:], APr[:, 1, :], -1.0, None, ALU.add, None, ["APr"], ["nre"])
        TT(t1[:], nre[:], lam[0][:], ALU.mult, ["nre", "lam0"], ["t1"])
        TT(t2[:], APi[:, 1, :], lam[1][:], ALU.mult, ["APi", "lam1"], ["t2"])
        TT(fre[:], t1[:], t2[:], ALU.add, ["t1", "t2"], ["fre"])
        TT(fre[:], fre[:], den[:], ALU.mult, ["fre", "den"], ["fre"])
        TT(t1[:], APi[:, 1, :], lam[0][:], ALU.mult, ["APi", "lam0"], ["t1"])
        TT(t2[:], nre[:], lam[1][:], ALU.mult, ["nre", "lam1"], ["t2"])
        TT(fim[:], t1[:], t2[:], ALU.subtract, ["t1", "t2"], ["fim"])
        TT(fim[:], fim[:], den[:], ALU.mult, ["fim", "den"], ["fim"])
        Bt = [sb(f"Bt{i}", [128, 64, 16]) for i in range(2)]
        bb = [sb(f"bb{i}", [128, 64, 16]) for i in range(2)]
        Ct = [sb(f"Ct{i}", [128, 64, 16]) for i in range(2)]
        u1 = sb("u1", [128, 64, 16]); u2 = sb("u2", [128, 64, 16])
        for i, nm in enumerate(("b_re", "b_im")):
            for g2 in range(2):
                P.op("sp", lambda e, i=i, nm=nm, g2=g2: e.dma_start(
                    out=Bt[i][g2 * 64:(g2 + 1) * 64, :, :],
                    in_=io[nm].rearrange("(t g2) p h -> g2 p t h", g2=2)[g2]), writes=[f"Bt{i}"], dma="q1_%d_%d" % (i, g2))
        bc = lambda ap: ap[:, :, None].to_broadcast([128, 64, 16])
        cmul(bb[0][:], bb[1][:], bc(fre[:]), bc(fim[:]), Bt[0][:], Bt[1][:], u1[:], u2[:],
             ["fre", "fim", "Bt0", "Bt1"], ["bb0", "bb1"])
        cin = sb("cin", [128, 128])
        for i, nm in enumerate(("c_re", "c_im")):
            for tb in range(8):
                for tl in range(8):
                    tt_ = tb * 8 + tl
                    P.op("sp", lambda e, nm=nm, tl=tl, tt_=tt_: e.dma_start(
                        out=cin[tl * 16:(tl + 1) * 16, :].rearrange("h (g p) -> h g p", g=2),
                        in_=io[nm][2 * tt_:2 * tt_ + 2].rearrange("g h p -> h g p")), writes=["cin"], dma="q2")
                transp(Ct[i][:, tb * 8:(tb + 1) * 8, :].rearrange("p a b -> p (a b)"), cin[:], 128, 128, ["cin"], [f"Ct{i}"])
        d16 = sb("d16", [128, 16]); dgh = sb("dgh", [16, 128]); dcol = sb("dcol", [128, 128])
        P.op("sp", lambda e: e.dma_start(out=d16[:], in_=io["s5_d"][:, :]), writes=["d16"], dma="q0_4")
        transp(dgh[:], d16[:], 128, 16, ["d16"], ["dgh"])
        P.op("pe", lambda e: e.matmul(pt[:, :], lhsT=rep[:].rearrange("p a b -> p (a b)"), rhs=dgh[:], start=True, stop=True),
             reads=["rep", "dgh"], writes=["pt"])
        dve(lambda e: e.tensor_copy(out=dcol[:], in_=pt[:, :]), ["pt"], ["dcol"])
        bg16 = sb("bg16", [16, 128])
        P.op("sp", lambda e: e.dma_start(out=bg16[:], in_=io["b_glu"][:, :]), writes=["bg16"], dma="q0_5")
        transp(pers["bgluc"][:], bg16[:], 16, 128, ["bg16"], ["bgluc"])
        NB = 8
        X = [sb(f"X{i}", [128, NB, 8, 16]) for i in range(2)]
        Y = [sb(f"Y{i}", [128, NB, 8, 16]) for i in range(2)]
        Zm = [sb(f"Zm{i}", [128, NB, 8, 16]) for i in range(2)]
        Fb = sb("Fb", [128, NB, 2, 128], BF16)
        ZTb = sb("ZTb", [128, NB, 2, 128], BF16)
        Mb = sb("Mb", [128, 2 * NB, 128], BF16)
        w1 = sb("w1", [128, NB, 16]); w2 = sb("w2", [128, NB, 16]); mtmp = sb("mtmp", [128, 128])
        for q in range(64 // NB):
            ts = slice(q * NB, (q + 1) * NB)
            bcp = lambda ap: ap[:, :, None].to_broadcast([128, NB, 16])
            for j in range(8):
                cmul(X[0][:, :, j, :], X[1][:, :, j, :], bcp(AMr[:, j, ts]), bcp(AMi[:, j, ts]), bb[0][:, ts, :], bb[1][:, ts, :],
                     w1[:], w2[:], ["AMr", "AMi", "bb0", "bb1"], ["X0", "X1"])
                cmul(Y[0][:, :, j, :], Y[1][:, :, j, :], bcp(APr[:, j, ts]), bcp(APi[:, j, ts]), Ct[0][:, ts, :], Ct[1][:, ts, :],
                     w1[:], w2[:], ["APr", "APi", "Ct0", "Ct1"], ["Y0", "Y1"])
                cmul(Zm[0][:, :, j, :], Zm[1][:, :, j, :], bcp(APr[:, 7 - j, ts]), bcp(APi[:, 7 - j, ts]), bb[0][:, ts, :], bb[1][:, ts, :],
                     w1[:], w2[:], ["APr", "APi", "bb0", "bb1"], ["Z0", "Z1"])
                TT(w1[:], bcp(APr[:, j + 1, ts]), Ct[0][:, ts, :], ALU.mult, ["APr", "Ct0"], ["cm_a"])
                TT(w2[:], bcp(APi[:, j + 1, ts]), Ct[1][:, ts, :], ALU.mult, ["APi", "Ct1"], ["cm_b"])
                TT(Fb[:, :, 0, j * 16:(j + 1) * 16], w1[:], w2[:], ALU.subtract, ["cm_a", "cm_b"], ["Fb"])
                TT(w1[:], bcp(APr[:, j + 1, ts]), Ct[1][:, ts, :], ALU.mult, ["APr", "Ct1", "Fb"], ["cm_a"])
                TT(w2[:], bcp(APi[:, j + 1, ts]), Ct[0][:, ts, :], ALU.mult, ["APi", "Ct0", "Fb"], ["cm_b"])
                dve(lambda e, j=j: e.scalar_tensor_tensor(out=Fb[:, :, 1, j * 16:(j + 1) * 16], in0=w1[:], scalar=-1.0, in1=w2[:],
                                                          op0=ALU.mult, op1=ALU.subtract), ["cm_a", "cm_b"], ["Fb"])
            TS(X[1][:], X[1][:], -1.0, None, ALU.mult, None, ["X1"], ["X1"])
            for tl in range(NB):
                t = q * NB + tl
                for c in range(2):
                    P.op("pe", lambda e, c=c, tl=tl: e.transpose(pt[:, :], Zm[c][:, tl, :, :].rearrange("p a b -> p (a b)"), identf[:]),
                         reads=[f"Z{c}", "identf"], writes=["pt"])
                    dve(lambda e, c=c, tl=tl: e.tensor_copy(out=ZTb[:, tl, c, :], in_=pt[:, :]), ["pt"], ["ZTb"])
                for g2 in range(2):
                    rows = slice(g2 * 64, (g2 + 1) * 64)
                    pmb = pm[g2]
                    for c in range(2):
                        P.op("pe", lambda e, c=c, tl=tl, rows=rows, pmb=pmb: e.matmul(
                            pmb[:, :], lhsT=X[c][rows, tl, :, :].rearrange("p a b -> p (a b)"),
                            rhs=Y[c][rows, tl, :, :].rearrange("p a b -> p (a b)"), start=(c == 0), stop=(c == 1)),
                            reads=["X0", "X1", "Y0", "Y1"], writes=[("pm", g2)])
                    g = 2 * t + g2
                    TT(mtmp[:], pmb[:, :], maskc[:].rearrange("p a b -> p (a b)"), ALU.mult, [("pm", g2), "maskc"], ["mtmp"])
                    dve(lambda e, g=g, tl=tl, g2=g2: e.scalar_tensor_tensor(
                        out=Mb[:, tl * 2 + g2, :], in0=identf[:], scalar=dcol[:, g:g + 1], in1=mtmp[:],
                        op0=ALU.mult, op1=ALU.add), ["mtmp", "identf", "dcol"], ["Mb"])
            P.op("sp", lambda e, q=q: e.dma_start(out=scr["mint"][q * 2 * NB:(q + 1) * 2 * NB].rearrange("g a b -> a g b"), in_=Mb[:]),
                 reads=["Mb"], writes=[("mint", q)], dma="q3m")
            P.op("sp", lambda e, q=q: e.dma_start(out=scr["zt"][q * NB:(q + 1) * NB].rearrange("t c a b -> a t c b"), in_=ZTb[:]),
                 reads=["ZTb"], writes=[("zts", q)], dma="q3z")
            P.op("sp", lambda e, q=q: e.dma_start(out=scr["ff"][q * NB:(q + 1) * NB].rearrange("t c a b -> a t c b"), in_=Fb[:]),
                 reads=["Fb"], writes=[("ffs", q)], dma="q3f")
        P.emit(ls, "q")


import os
S2_STOP = int(os.environ.get("S2_STOP", "0"))


def sweep2(nc, st, io, scr, pers, L, Lp=0):
    P = Prog(nc)
    TB = 256
    NCH = TB // 8
    nblk = L // TB
    GELU_C = 1.5957691216057308
    with ExitStack() as ls:
        sb = lambda name, shape, dt=F32: ls.enter_context(nc.sbuf_tensor(name, shape, dt))
        identf = make_iota_mask(P, nc, ls, "identf2", [128, 128], [[1, 128]], 0, -1, ALU.is_equal, "identf")
        identb = sb("identb2", [128, 128], BF16)
        P.op("dve", lambda e: e.tensor_copy(out=identb[:], in_=identf[:]), reads=["identf"], writes=["identb"])
        hTs = [sb(f"hT2_{i}", [128, 32, TB], BF16) for i in range(2)]
        hpar = 0
        ring = Ring(nc, ls, "w2_", 3, [128, 8, 512], BF16)
        uT = sb("uT", [128, 4, TB], BF16)
        UUs = [sb(f"UU{i}", [NCH, 8, 8, 16], BF16) for i in range(2)]
        UD = sb("UD", [128, 128, NCH], BF16)
        ZTc = [sb(f"ZTc{i}", [128, 8, 2, 128], BF16) for i in range(2)]
        Fc = [sb(f"Fc{i}", [128, 8, 2, 128], BF16) for i in range(2)]
        Mc = [sb(f"Mc{i}", [128, 16, 128], BF16) for i in range(2)]
        W = sb("Wst", [128, NCH, 2, 64])
        Sbf = [sb(f"Sbf2_{i}", [128, NCH + 1, 2, 64], BF16) for i in range(2)]
        for i_ in range(2):
            P.op("pool", lambda e, i_=i_: e.memset(Sbf[i_][:], 0.0), writes=["Sbf"])
        carry = sb("carry", [128, 2, 64])
        tA = sb("tA", [128, 2, 64]); tBm = sb("tBm", [128, 2, 64])
        ysbs = [sb(f"ysb{i}", [128, 8 * NCH]) for i in range(2)]
        ysqs = [sb(f"ysq{i}", [128, 8 * NCH]) for i in range(2)]
        ysgs = [sb(f"ysg{i}", [128, 8 * NCH]) for i in range(2)]
        YDs = [sb(f"YD{i}", [128, 8, NCH], BF16) for i in range(2)]
        YYs = [sb(f"YY{i}", [NCH, 8, 8, 16], BF16) for i in range(2)]
        ygT = sb("ygT", [128, 16, TB], BF16)
        sg = sb("sg2", [128, TB]); zs = sb("zs2", [128, TB])
        oT5 = sb("oT5", [128, 4, TB], BF16)
        B = [ls.enter_context(nc.psum_tensor(f"C{i}", [128, 512], F32)) for i in range(4)]
        B5 = ls.enter_context(nc.psum_tensor("C5", [128, 1024], BF16))
        B5b = ls.enter_context(nc.psum_tensor("C5b", [128, 1024], BF16))
        B6 = ls.enter_context(nc.psum_tensor("C6", [128, 512], F32))
        B7 = ls.enter_context(nc.psum_tensor("C7", [128, 512], F32))
        P.op("dve", lambda e: e.memset(carry[:], 0.0), writes=["carry"])

        def wload(src, r0, nkc, c0, ncol, rk):
            s, wt, wk = ring.next()
            view = wt[:, 0:nkc, 0:ncol]
            P.op("sp", lambda e: e.dma_start(out=view, in_=dram_rows(src, r0, nkc, c0, ncol)), reads=rk, writes=[wk], dma="w2_%d" % s)
            return view, wk

        blocks = [("pre", i) for i in range(Lp // TB)] + [("own", i) for i in range(nblk)]
        for mode, blk in blocks:
            own = mode == "own"
            last_pre = (mode == "pre" and blk == Lp // TB - 1)
            xsrc = io["x"] if own else io["xp"]
            t0 = blk * TB
            tok0 = t0 + (Lp if own else 0)
            hT = hTs[hpar % 2]
            hkq = lambda q_, hp_=hpar % 2: ("hT", hp_, q_)
            for q in range(4):
                P.op("sp", lambda e, q=q, tok0=tok0, hT=hT: e.dma_start(
                    out=hT[:, q * 8:(q + 1) * 8, :], in_=dram_rows(scr["hT"], q * 8, 8, tok0, TB)),
                    writes=[hkq(q)], dma="h2_%d_%d" % (hpar % 2, q))
            hpar += 1
            for cg in range(4):
                for kq in range(4):
                    view, wk = wload(scr["win"], kq * 8, 8, C_U + cg * 512, 512, [])
                    for c4 in range(4):
                        for k8 in range(8):
                            kc = kq * 8 + k8
                            P.op("pe", lambda e, view=view, c4=c4, k8=k8, kc=kc, hT=hT: e.matmul(
                                B[c4][:, 0:TB], lhsT=view[:, k8, c4 * 128:(c4 + 1) * 128], rhs=hT[:, kc, :],
                                start=(kc == 0), stop=(kc == 31)),
                                reads=[wk, hkq(kc // 8)], writes=[("B", c4)])
                for c4 in range(4):
                    ct = cg * 4 + c4
                    pp = ct % 2
                    UU = UUs[pp]
                    T5, T5k = (B5, "B5") if pp == 0 else (B6[:].bitcast(BF16), "B67_0")
                    T5b, T5bk = (B5b, "B5b") if pp == 0 else (B7[:].bitcast(BF16), "B67_1")
                    T5 = T5 if pp == 1 else T5[:]
                    T5b = T5b if pp == 1 else T5b[:]
                    P.op("act", lambda e, c4=c4: e.activation(out=uT[:, c4, :], in_=B[c4][:, 0:TB], func=AF.Copy),
                         reads=[("B", c4)], writes=[("uT", c4)])
                    for j in range(8):
                        P.op("pe", lambda e, c4=c4, j=j, T5=T5: e.transpose(
                            T5[0:NCH, j * 128:(j + 1) * 128], uT[:, c4, j::8], identb[:]),
                            reads=[("uT", c4), "identb"], writes=[T5k])
                    P.op("act", lambda e, UU=UU, T5=T5: e.activation(
                        out=UU[:].rearrange("n g j h -> n j g h"),
                        in_=T5[0:NCH, :].rearrange("n (j g h) -> n j g h", j=8, g=8), func=AF.Copy), reads=[T5k], writes=[("UU", pp)])
                    for gl in range(8):
                        P.op("pe", lambda e, gl=gl, UU=UU, T5b=T5b: e.transpose(
                            T5b[:, gl * NCH:(gl + 1) * NCH], UU[:, gl, :, :].rearrange("n j h -> n (j h)"), identb[0:NCH, 0:NCH]),
                            reads=[("UU", pp), "identb"], writes=[T5bk])
                    P.op("act", lambda e, ct=ct, T5b=T5b: e.activation(
                        out=UD[:, ct * 8:(ct + 1) * 8, :].rearrange("p g n -> p (g n)"), in_=T5b[:, 0:8 * NCH], func=AF.Copy),
                        reads=[T5bk], writes=[("UD", ct)])
            if S2_STOP == 1:
                continue
            for q in range(8):
                zc = ZTc[q % 2]
                P.op("sp", lambda e, zc=zc, q=q: e.dma_start(out=zc[:], in_=scr["zt"][q * 8:(q + 1) * 8].rearrange("t c a b -> a t c b")),
                     writes=[("ZTc", q % 2)], dma="m2z%d" % (q % 2))
                for tl in range(8):
                    t = q * 8 + tl
                    for g2 in range(2):
                        for c, Bc in ((0, B6), (1, B7)):
                            P.op("pe", lambda e, zc=zc, tl=tl, g2=g2, c=c, Bc=Bc, t=t: e.matmul(
                                Bc[g2 * 64:(g2 + 1) * 64, tl * NCH:(tl + 1) * NCH], lhsT=zc[:, tl, c, g2 * 64:(g2 + 1) * 64],
                                rhs=UD[:, 2 * t + g2, :], start=True, stop=True),
                                reads=[("ZTc", q % 2), ("UD", (2 * t + g2) // 8)], writes=["B67_%d" % c])
                for c, Bc in ((0, B6), (1, B7)):
                    P.op("act", lambda e, c=c, Bc=Bc, q=q: e.activation(
                        out=W[:, :, c, q * 8:(q + 1) * 8].rearrange("p n t -> p t n"),
                        in_=Bc[:, 0:8 * NCH].rearrange("p (t n) -> p t n", t=8), func=AF.Copy),
                        reads=["B67_%d" % c], writes=["W"])
            if S2_STOP == 2:
                continue
            for g2_ in range(2):
                hp = slice(g2_ * 64, (g2_ + 1) * 64)
                P.op("act", lambda e, g2_=g2_, hp=hp: e.activation(out=Sbf[g2_][hp, 0, :, :], in_=carry[hp], func=AF.Copy),
                     reads=["carry"], writes=["Sbf"])
            for n in range(NCH):
                prev = carry[:] if n == 0 else W[:, n - 1, :, :]
                P.op("dve", lambda e, prev=prev: e.tensor_tensor(out=tA[:], in0=pers["A8c"][:], in1=prev, op=ALU.mult),
                     reads=["W", "carry"], writes=["tA"])
                P.op("dve", lambda e, prev=prev: e.tensor_tensor(out=tBm[:, 0, :], in0=pers["A8n"][:, 0, :], in1=prev[:, 1, :], op=ALU.mult),
                     reads=["W", "carry"], writes=["tB"])
                P.op("dve", lambda e, prev=prev: e.tensor_tensor(out=tBm[:, 1, :], in0=pers["A8n"][:, 1, :], in1=prev[:, 0, :], op=ALU.mult),
                     reads=["W", "carry"], writes=["tB"])
                P.op("dve", lambda e, n=n: e.tensor_tensor(out=W[:, n, :, :], in0=W[:, n, :, :], in1=tA[:], op=ALU.add),
                     reads=["W", "tA"], writes=["W"])
                P.op("dve", lambda e, n=n: e.tensor_tensor(out=W[:, n, :, :], in0=W[:, n, :, :], in1=tBm[:], op=ALU.add),
                     reads=["W", "tB"], writes=["W"])
            P.op("dve", lambda e: e.tensor_copy(out=carry[:], in_=W[:, NCH - 1, :, :]), reads=["W"], writes=["carry"])
            if last_pre:
                P.op("dve", lambda e: e.tensor_scalar(out=carry[:].rearrange("p a b -> p (a b)"), in0=carry[:].rearrange("p a b -> p (a b)"),
                                                      scalar1=pers["flag"][:, 0:1], scalar2=None, op0=ALU.mult),
                     reads=["carry", "flag"], writes=["carry"])
            if not own:
                continue
            for g2_ in range(2):
                hp = slice(g2_ * 64, (g2_ + 1) * 64)
                P.op("act", lambda e, g2_=g2_, hp=hp: e.activation(
                    out=Sbf[g2_][hp, 1:NCH + 1, :, :].rearrange("p n c t -> p (n c t)"),
                    in_=W[hp].rearrange("p n c t -> p (n c t)"), func=AF.Copy),
                    reads=["W"], writes=["Sbf"])
            if S2_STOP == 3:
                continue
            for ct in range(16):
                q = ct // 2
                if ct % 2 == 0:
                    fc, mc = Fc[q % 2], Mc[q % 2]
                    P.op("sp", lambda e, fc=fc, q=q: e.dma_start(out=fc[:], in_=scr["ff"][q * 8:(q + 1) * 8].rearrange("t c a b -> a t c b")),
                         writes=[("Fc", q % 2)], dma="m2f%d" % (q % 2))
                    P.op("sp", lambda e, mc=mc, q=q: e.dma_start(out=mc[:], in_=scr["mint"][q * 16:(q + 1) * 16].rearrange("g a b -> a g b")),
                         writes=[("Mc", q % 2)], dma="m2m%d" % (q % 2))
                yb = B[ct % 2]
                for gl in range(8):
                    g = ct * 8 + gl
                    t, g2 = g // 2, g % 2
                    tl = t - q * 8
                    rows = slice(g2 * 64, (g2 + 1) * 64)
                    osl = yb[:, gl * NCH:(gl + 1) * NCH]
                    P.op("pe", lambda e, osl=osl, mc=mc, g=g, q=q: e.matmul(osl, lhsT=mc[:, g - q * 16, :], rhs=UD[:, g, :],
                                                                       start=True, stop=False),
                         reads=[("Mc", q % 2), ("UD", ct)], writes=[("B", ct % 2)])
                    for c in range(2):
                        P.op("pe", lambda e, osl=osl, fc=fc, tl=tl, c=c, g2=g2, t=t: e.matmul(
                            osl, lhsT=fc[:, tl, c, :], rhs=Sbf[g2][:, 0:NCH, c, t], start=False, stop=(c == 1)),
                            reads=[("Fc", q % 2), "Sbf"], writes=[("B", ct % 2)])
                ysl = yb[:, 0:8 * NCH]
                pp = ct % 2
                ysb, ysq, ysg, YD, YY = ysbs[pp], ysqs[pp], ysgs[pp], YDs[pp], YYs[pp]
                T5, T5k = (B5[:], "B5") if pp == 0 else (B6[:].bitcast(BF16), "B67_0")
                T5b, T5bk = (B5b[:], "B5b") if pp == 0 else (B7[:].bitcast(BF16), "B67_1")
                kk = lambda s: (s, pp)
                P.op("act", lambda e, ysl=ysl, ysb=ysb: e.activation(out=ysb[:], in_=ysl, func=AF.Copy), reads=[("B", ct % 2)], writes=[kk("ysb")])
                P.op("act", lambda e, ysl=ysl, ysq=ysq: e.activation(out=ysq[:], in_=ysl, func=AF.Square), reads=[("B", ct % 2)], writes=[kk("ysq")])
                P.op("dve", lambda e, ysq=ysq: e.tensor_scalar(out=ysq[:], in0=ysq[:], scalar1=0.044715, scalar2=1.0, op0=ALU.mult, op1=ALU.add),
                     reads=[kk("ysq")], writes=[kk("ysq")])
                P.op("dve", lambda e, ysq=ysq, ysb=ysb: e.tensor_tensor(out=ysq[:], in0=ysq[:], in1=ysb[:], op=ALU.mult),
                     reads=[kk("ysq"), kk("ysb")], writes=[kk("ysq")])
                P.op("act", lambda e, ysq=ysq, ysg=ysg: e.activation(out=ysg[:], in_=ysq[:], func=AF.Sigmoid, scale=GELU_C),
                     reads=[kk("ysq")], writes=[kk("ysg")])
                P.op("dve", lambda e, YD=YD, ysb=ysb, ysg=ysg: e.tensor_tensor(out=YD[:].rearrange("p g n -> p (g n)"), in0=ysb[:], in1=ysg[:], op=ALU.mult),
                     reads=[kk("ysb"), kk("ysg")], writes=[kk("YD")])
                for gl in range(8):
                    P.op("pe", lambda e, gl=gl, T5=T5, YD=YD: e.transpose(T5[0:NCH, gl * 128:(gl + 1) * 128], YD[:, gl, :], identb[:]),
                         reads=[kk("YD"), "identb"], writes=[T5k])
                P.op("dve", lambda e, YY=YY, T5=T5: e.tensor_copy(out=YY[:].rearrange("n j g h -> n g j h"),
                                                    in_=T5[0:NCH, :].rearrange("n (g j h) -> n g j h", g=8, j=8)),
                     reads=[T5k], writes=[kk("YY")])
                for j in range(8):
                    P.op("pe", lambda e, j=j, YY=YY, T5b=T5b: e.transpose(T5b[:, j * NCH:(j + 1) * NCH], YY[:, j, :, :].rearrange("n g h -> n (g h)"), identb[0:NCH, 0:NCH]),
                         reads=[kk("YY"), "identb"], writes=[T5bk])
                P.op("act", lambda e, ct=ct, T5b=T5b: e.activation(
                    out=ygT[:, ct, :].rearrange("p (n j) -> p j n", j=8),
                    in_=T5b[:, 0:8 * NCH].rearrange("p (j n) -> p j n", j=8), func=AF.Copy),
                    reads=[T5bk], writes=[("ygT", ct)])
            if S2_STOP == 4:
                continue
            for cgo in range(4):
                for kq in range(2):
                    view, wk = wload(scr["wglu"], kq * 8, 8, cgo * 512, 512, [])
                    for c4 in range(4):
                        for k8 in range(8):
                            kc = kq * 8 + k8
                            P.op("pe", lambda e, view=view, c4=c4, k8=k8, kc=kc: e.matmul(
                                B[c4][:, 0:TB], lhsT=view[:, k8, c4 * 128:(c4 + 1) * 128], rhs=ygT[:, kc, :],
                                start=(kc == 0), stop=(kc == 15)),
                                reads=[wk, ("ygT", kc)], writes=[("B", c4)])
                for kq in range(4):
                    view, wk = wload(scr["win"], kq * 8, 8, C_ZS + cgo * 512, 512, [])
                    for c4 in range(4):
                        for k8 in range(8):
                            kc = kq * 8 + k8
                            P.op("pe", lambda e, view=view, c4=c4, k8=k8, kc=kc, hT=hT: e.matmul(
                                B[c4][:, TB:2 * TB], lhsT=view[:, k8, c4 * 128:(c4 + 1) * 128], rhs=hT[:, kc, :],
                                start=(kc == 0), stop=(kc == 31)),
                                reads=[wk, hkq(kc // 8)], writes=[("B", c4)])
                for c4 in range(4):
                    ct = cgo * 4 + c4
                    P.op("act", lambda e, c4=c4, ct=ct: e.activation(out=sg[:], in_=B[c4][:, 0:TB], func=AF.Sigmoid,
                                                                    bias=pers["bgluc"][:, ct:ct + 1]),
                         reads=[("B", c4)], writes=["sg"])
                    P.op("act", lambda e, c4=c4: e.activation(out=zs[:], in_=B[c4][:, TB:2 * TB], func=AF.Sigmoid),
                         reads=[("B", c4)], writes=["zs"])
                    P.op("dve", lambda e, c4=c4: e.tensor_tensor(out=zs[:], in0=zs[:], in1=B[c4][:, TB:2 * TB], op=ALU.mult),
                         reads=["zs", ("B", c4)], writes=["zs"])
                    P.op("dve", lambda e, ct=ct: e.tensor_tensor(out=sg[:], in0=sg[:], in1=ygT[:, ct, :], op=ALU.mult),
                         reads=["sg", ("ygT", ct)], writes=["sg"])
                    P.op("dve", lambda e, c4=c4: e.tensor_tensor(out=oT5[:, c4, :], in0=sg[:], in1=zs[:], op=ALU.mult),
                         reads=["sg", "zs"], writes=["oT5"])
                P.op("pool", lambda e, cgo=cgo, t0=t0: e.dma_start(out=dram_rows(scr["oT"], 16 + cgo * 4, 4, t0, TB), in_=oT5[:]),
                     reads=["oT5"], writes=[("oTs", blk, cgo)], dma="o2")
        P.emit(ls, "d")


def sweep3(nc, st, io, scr, L):
    P = Prog(nc)
    nblk = L // 512
    with ExitStack() as ls:
        oT = ls.enter_context(nc.sbuf_tensor("oT3", [128, 32, 512], BF16))
        r = [ls.enter_context(nc.sbuf_tensor(f"r3_{i}", [128, 4096], F32)) for i in range(4)]
        lng = ls.enter_context(nc.sbuf_tensor("lng", [128, 4096], F32))
        lnb = ls.enter_context(nc.sbuf_tensor("lnb", [128, 4096], F32))
        stats = [ls.enter_context(nc.sbuf_tensor(f"st3_{i}", [128, 8, 6], F32)) for i in range(4)]
        mv = [ls.enter_context(nc.sbuf_tensor(f"mv3_{i}", [128, 4], F32)) for i in range(4)]
        ring = Ring(nc, ls, "w3_", 4, [128, 8, 512], BF16)
        ps = [ls.enter_context(nc.psum_tensor(f"ps3_{i}", [128, 512], F32)) for i in range(8)]
        P.op("sp", lambda e: e.dma_start(out=lng[:], in_=io["ln_g"][0:1, :].broadcast_to([128, 4096])),
             writes=["lng"], dma="c3_1")
        P.op("sp", lambda e: e.dma_start(out=lnb[:], in_=io["ln_b"][0:1, :].broadcast_to([128, 4096])),
             writes=["lnb"], dma="c3_2")
        for blk in range(nblk):
            t0 = blk * 512
            for q in range(4):
                P.op("sp", lambda e, q=q, t0=t0: e.dma_start(
                    out=oT[:, q * 8:(q + 1) * 8, :], in_=dram_rows(scr["oT"], q * 8, 8, t0, 512)),
                    writes=[("oT", q)], dma="oT_%d" % q)
            for i in range(4):
                P.op("pool", lambda e, i=i, t0=t0: e.dma_start(out=r[i][:], in_=io["x"][t0 + i * 128:t0 + (i + 1) * 128, :]),
                     writes=[("r", i, c) for c in range(8)], dma="x3_%d" % i)
            for cg in range(8):
                for kq in range(4):
                    s, wt, wk = ring.next()
                    P.op("sp", lambda e, wt=wt, kq=kq, cg=cg: e.dma_start(
                        out=wt[:], in_=dram_rows(scr["wout"], kq * 8, 8, cg * 512, 512)),
                        writes=[wk], dma="w3_%d" % s)
                    for i in range(4):
                        b = (cg % 2) * 4 + i
                        for k8 in range(8):
                            kc = kq * 8 + k8
                            P.op("pe", lambda e, b=b, kc=kc, i=i, wt=wt, k8=k8: e.matmul(
                                ps[b][:], lhsT=oT[:, kc, i * 128:(i + 1) * 128], rhs=wt[:, k8, :],
                                start=(kc == 0), stop=(kc == 31)),
                                reads=[wk, ("oT", kq)], writes=[("ps", b)])
                for i in range(4):
                    b = (cg % 2) * 4 + i
                    sl = slice(cg * 512, (cg + 1) * 512)
                    P.op("dve", lambda e, i=i, b=b, sl=sl: e.scalar_tensor_tensor(
                        out=r[i][:, sl], in0=r[i][:, sl], scalar=ALPHA, in1=ps[b][:], op0=ALU.mult, op1=ALU.add),
                        reads=[("ps", b), ("r", i, cg)], writes=[("r", i, cg)])
                    P.op("dve", lambda e, i=i, cg=cg, sl=sl: e.bn_stats(out=stats[i][:, cg, :], in_=r[i][:, sl]),
                         reads=[("r", i, cg)], writes=[("st", i)])
            for i in range(4):
                rk = [("r", i, c) for c in range(8)]
                P.op("dve", lambda e, i=i: e.bn_aggr(out=mv[i][:, 0:2], in_=stats[i][:].rearrange("p a b -> p (a b)")),
                     reads=[("st", i)], writes=[("mv", i)])
                P.op("dve", lambda e, i=i: e.tensor_scalar(out=mv[i][:, 2:3], in0=mv[i][:, 1:2], scalar1=EPS,
                                                           scalar2=None, op0=ALU.add),
                     reads=[("mv", i)], writes=[("mv", i)])
                P.op("act", lambda e, i=i: e.activation(out=mv[i][:, 2:3], in_=mv[i][:, 2:3], func=AF.Sqrt),
                     reads=[("mv", i)], writes=[("mv", i)])
                P.op("dve", lambda e, i=i: e.reciprocal(out=mv[i][:, 2:3], in_=mv[i][:, 2:3]),
                     reads=[("mv", i)], writes=[("mv", i)])
                P.op("dve", lambda e, i=i: e.tensor_scalar(out=mv[i][:, 3:4], in0=mv[i][:, 0:1], scalar1=mv[i][:, 2:3],
                                                           scalar2=-1.0, op0=ALU.mult, op1=ALU.mult),
                     reads=[("mv", i)], writes=[("mv", i)])
                P.op("act", lambda e, i=i: e.activation(out=r[i][:], in_=r[i][:], func=AF.Identity,
                                                        bias=mv[i][:, 3:4], scale=mv[i][:, 2:3]),
                     reads=[("mv", i)] + rk, writes=rk)
                P.op("pool", lambda e, i=i: e.tensor_tensor(out=r[i][:], in0=r[i][:], in1=lng[:], op=ALU.mult),
                     reads=rk + ["lng"], writes=rk)
                P.op("dve", lambda e, i=i: e.tensor_tensor(out=r[i][:], in0=r[i][:], in1=lnb[:], op=ALU.add),
                     reads=rk + ["lnb"], writes=rk)
                P.op("pool", lambda e, i=i, t0=t0: e.dma_start(out=io["y"][t0 + i * 128:t0 + (i + 1) * 128, :], in_=r[i][:]),
                     reads=rk, writes=[("y", blk, i)], dma="y3_%d" % i)
        P.emit(ls, "c")


def build(L, stages=("p0", "s1", "s2", "s3"), dbg=False, Lp=0):
    nc = bass.Bass("TRN2", target_bir_lowering=False)
    io = {}

    def inp(name, shape):
        io[name] = nc.dram_tensor(name, shape, F32, kind="ExternalInput").ap()
    inp("x", [L, D]); inp("c", [32, 128]); inp("w_ada", [D, 3 * D]); inp("b_ada", [96, 128])
    if Lp:
        inp("xp", [Lp, D]); inp("flag", [128, 1])
    inp("w_in", [D, DIN]); inp("w_gate", [16, 1024]); inp("b_gate", [1, 1024]); inp("gnorm", [1, 512])
    inp("lam_re", [64, 128]); inp("lam_im", [64, 128]); inp("log_dt", [64, 2])
    inp("b_re", [128, 64, 16]); inp("b_im", [128, 64, 16]); inp("c_re", [128, 16, 64]); inp("c_im", [128, 16, 64])
    inp("s5_d", [128, 16]); inp("w_glu", [2048, 2048]); inp("b_glu", [16, 128])
    inp("w_out", [D, D]); inp("ln_g", [1, D]); inp("ln_b", [1, D])
    io["y"] = nc.dram_tensor("y", [L, D], F32, kind="ExternalOutput").ap()
    scr = {}
    okind = ("ExternalInput" if ("s1" not in stages and "s2" not in stages and "s2p" not in stages) else "ExternalOutput") if dbg else "Internal"
    scr["oT"] = nc.dram_tensor("oT_scr", [D, L], BF16, kind=okind).ap()
    scr["win"] = nc.dram_tensor("win_scr", [D, DIN], BF16).ap()
    scr["wglu"] = nc.dram_tensor("wglu_scr", [2048, 2048], BF16).ap()
    scr["wout"] = nc.dram_tensor("wout_scr", [D, D], BF16).ap()
    scr["gate"] = nc.dram_tensor("gate_scr", [32, 128], F32).ap()
    scr["hT"] = nc.dram_tensor("hT_scr", [D, L + Lp], BF16).ap()
    scr["mint"] = nc.dram_tensor("mint_scr", [128, 128, 128], BF16).ap()
    scr["zt"] = nc.dram_tensor("zt_scr", [64, 2, 128, 128], BF16).ap()
    scr["ff"] = nc.dram_tensor("ff_scr", [64, 2, 128, 128], BF16).ap()
    with ExitStack() as st:
        pers = {}
        pers["sh"] = st.enter_context(nc.sbuf_tensor("sh_c", [128, 32], F32))
        pers["s1p"] = st.enter_context(nc.sbuf_tensor("s1p_c", [128, 32], F32))
        pers["flag"] = st.enter_context(nc.sbuf_tensor("flagt", [128, 1], F32))
        pers["A8c"] = st.enter_context(nc.sbuf_tensor("A8c", [128, 2, 64], F32))
        pers["A8n"] = st.enter_context(nc.sbuf_tensor("A8n", [128, 2, 64], F32))
        pers["bgluc"] = st.enter_context(nc.sbuf_tensor("bgluc", [128, 16], F32))
        if "p0" in stages:
            phase0(nc, st, io, scr, pers, L, Lp)
        if "s1" in stages:
            sweep1(nc, st, io, scr, pers, L, Lp)
        if "s2" in stages or "s2p" in stages:
            s5_prologue(nc, st, io, scr, pers)
        if "s2" in stages:
            sweep2(nc, st, io, scr, pers, L, Lp)
        if "s3" in stages:
            sweep3(nc, st, io, scr, L)
    return nc


def make_in_map(inputs, b, L, s=0, Lp=0):
    f = lambda a: np.ascontiguousarray(a, dtype=np.float32)
    i = inputs
    m = {
        "x": f(i["x"][b, s * L:(s + 1) * L]), "c": f(i["c"][b].reshape(32, 128)), "w_ada": f(i["w_ada"][0]),
        "b_ada": f(i["b_ada"][0].reshape(96, 128)), "w_in": f(i["w_in"][0]), "w_gate": f(i["w_gla_gate"][0]),
        "b_gate": f(i["b_gla_gate"][0].reshape(1, 1024)), "gnorm": f(i["gla_norm_g"][0].reshape(1, 512)),
        "lam_re": f(i["s5_lambda_re"][0].reshape(64, 128)), "lam_im": f(i["s5_lambda_im"][0].reshape(64, 128)),
        "log_dt": f(i["s5_log_dt"][0].reshape(64, 2)), "b_re": f(i["s5_b_re"][0]), "b_im": f(i["s5_b_im"][0]),
        "c_re": f(i["s5_c_re"][0]), "c_im": f(i["s5_c_im"][0]), "s5_d": f(i["s5_d"][0].reshape(128, 16)),
        "w_glu": f(i["w_glu"][0]), "b_glu": f(i["b_glu"][0].reshape(16, 128)), "w_out": f(i["w_out"][0]),
        "ln_g": f(i["ln_g"][0].reshape(1, D)), "ln_b": f(i["ln_b"][0].reshape(1, D)),
    }
    if Lp:
        m["xp"] = f(i["x"][b, 0:Lp])
        m["flag"] = np.full((128, 1), float(s), dtype=np.float32)
    return m


def kernel(**inputs):
    B, S = inputs["x"].shape[0], inputs["x"].shape[1]
    L = S // 2
    nc = build(L, Lp=L)
    in_maps = [make_in_map(inputs, b, L, s, L) for b in range(B) for s in range(2)]
    res = run_bass_kernel_spmd(nc, in_maps, core_ids=list(range(2 * B)))
    out = np.empty((B, S, D), dtype=np.float32)
    for b in range(B):
        for s in range(2):
            out[b, s * L:(s + 1) * L] = np.asarray(res.results[2 * b + s]["y"], dtype=np.float32)
    return out
```

```python
import numpy as np
from contextlib import ExitStack
import concourse.bass as bass
import concourse.mybir as mybir
from concourse.bass_utils import run_bass_kernel_spmd

F32 = mybir.dt.float32
BF16 = mybir.dt.bfloat16
I32 = mybir.dt.int32
ALU = mybir.AluOpType
AF = mybir.ActivationFunctionType
AX = mybir.AxisListType

D = 4096
DIN = 10256
ALPHA = 2.0 ** 0.25
EPS = 1e-5
C_Q, C_K, C_V, C_G, C_ZG, C_U, C_ZS = 0, 1024, 2048, 4096, 4112, 6160, 8208


class Prog:
    ENGS = ("pe", "act", "dve", "pool", "sp")

    def __init__(self, nc):
        self.nc = nc
        self.ops = []
        self.last_w = {}
        self.readers = {}

    def op(self, eng, fn, reads=(), writes=(), dma=None):
        idx = len(self.ops)
        deps = set()
        for k in reads:
            if k in self.last_w:
                deps.add(self.last_w[k])
        for k in writes:
            if k in self.last_w:
                deps.add(self.last_w[k])
            for r in self.readers.get(k, ()):
                deps.add(r)
        self.ops.append(dict(eng=eng, fn=fn, deps=deps, dma=dma, needed=False))
        for k in reads:
            self.readers.setdefault(k, []).append(idx)
        for k in writes:
            self.last_w[k] = idx
            self.readers[k] = []
        return idx

    def emit(self, stack, tag):
        nc = self.nc
        ops = self.ops
        for i, o in enumerate(ops):
            latest = {}
            for d in o["deps"]:
                od = ops[d]
                if od["dma"] is not None:
                    continue
                if od["eng"] == "pe" and o["eng"] == "pe" and o["dma"] is None:
                    continue
                if od["eng"] != "pe":
                    od["needed"] = True
                    continue
                if od["eng"] not in latest or d > latest[od["eng"]]:
                    latest[od["eng"]] = d
            for d in latest.values():
                ops[d]["needed"] = True
        sems = {e: stack.enter_context(nc.semaphore(tag + "s_" + e)) for e in self.ENGS}
        dma_sems, dma_cnt = {}, {}
        cnt = {e: 0 for e in self.ENGS}
        ev = [None] * len(ops)
        for i, o in enumerate(ops):
            if o["dma"] is not None:
                name = o["dma"]
                if name not in dma_sems:
                    dma_sems[name] = stack.enter_context(nc.semaphore(tag + "d_" + name))
                    dma_cnt[name] = 0
                dma_cnt[name] += 16
                ev[i] = (dma_sems[name], dma_cnt[name], "dma:" + name)
            elif o["needed"]:
                cnt[o["eng"]] += 1
                ev[i] = (sems[o["eng"]], cnt[o["eng"]], o["eng"])
        final_dma = {n: (dma_sems[n], dma_cnt[n]) for n in dma_sems}
        per_eng = {e: [] for e in self.ENGS}
        for i, o in enumerate(ops):
            per_eng[o["eng"]].append(i)
        with nc.Block() as block:
            getters = dict(pe=block.tensor, act=block.scalar, dve=block.vector,
                           pool=block.gpsimd, sp=block.sync)
            for e in self.ENGS:
                idxs = per_eng[e]

                def body(engine, e=e, idxs=idxs):
                    waited = {}
                    for i in idxs:
                        o = ops[i]
                        need = {}
                        for d in o["deps"]:
                            if ev[d] is None:
                                continue
                            s, v, tg = ev[d]
                            if tg not in need or need[tg][1] < v:
                                need[tg] = (s, v)
                        for tg, (s, v) in need.items():
                            if waited.get(tg, 0) >= v:
                                continue
                            engine.wait_ge(s, v)
                            waited[tg] = v
                        ins = o["fn"](engine)
                        if ev[i] is not None:
                            ins.then_inc(ev[i][0], 16 if o["dma"] is not None else 1)
                    if e == "sp":
                        for n, (s, v) in final_dma.items():
                            engine.wait_ge(s, v)
                        for e2 in ("pe", "act", "dve", "pool"):
                            if cnt[e2] > 0:
                                engine.wait_ge(sems[e2], cnt[e2])
                getters[e](body)


class Ring:
    def __init__(self, nc, st, name, n, shape, dtype):
        self.t = [st.enter_context(nc.sbuf_tensor(f"{name}{i}", shape, dtype)) for i in range(n)]
        self.n = n
        self.i = 0
        self.name = name

    def next(self):
        s = self.i % self.n
        self.i += 1
        return s, self.t[s], (self.name, s)


def dram_rows(ap, r0, nkc, c0, nc_):
    return ap[r0 * 128:(r0 + nkc) * 128, c0:c0 + nc_].rearrange("(kc p) c -> p kc c", p=128)


def make_iota_mask(P, nc, st, name, shape, pattern, base, cm, op, key):
    ti = st.enter_context(nc.sbuf_tensor(name + "_i", shape, I32))
    tf = st.enter_context(nc.sbuf_tensor(name, shape, F32))
    P.op("pool", lambda e: e.iota(ti[:], pattern=pattern, base=base, channel_multiplier=cm),
         writes=[key + "_i"])
    P.op("dve", lambda e: e.tensor_scalar(out=tf[:], in0=ti[:], scalar1=0.0, scalar2=None, op0=op),
         reads=[key + "_i"], writes=[key])
    return tf


def phase0(nc, st, io, scr, pers, L, Lp=0):
    P = Prog(nc)
    with ExitStack() as ls:
        ident = make_iota_mask(P, nc, ls, "ident0", [128, 128], [[1, 128]], 0, -1, ALU.is_equal, "ident")
        c32 = ls.enter_context(nc.sbuf_tensor("c32", [32, 128], F32))
        ba96 = ls.enter_context(nc.sbuf_tensor("ba96", [96, 128], F32))
        scol = ls.enter_context(nc.sbuf_tensor("scol", [128, 32, 2], F32))
        bac = ls.enter_context(nc.sbuf_tensor("bac", [128, 96], F32))
        modc = ls.enter_context(nc.sbuf_tensor("modc", [128, 96], F32))
        g32 = ls.enter_context(nc.sbuf_tensor("g32", [32, 128], F32))
        wa = [ls.enter_context(nc.sbuf_tensor(f"wa{i}", [128, 12288], F32)) for i in range(2)]
        grow = ls.enter_context(nc.sbuf_tensor("grow", [128, 4096], F32))
        wf = [ls.enter_context(nc.sbuf_tensor(f"wf{i}", [128, 4096], F32)) for i in range(2)]
        wb = [ls.enter_context(nc.sbuf_tensor(f"wb{i}", [128, 4096], BF16)) for i in range(2)]
        pst = ls.enter_context(nc.psum_tensor("p0t", [128, 512], F32))[:, 0:128]
        psm = ls.enter_context(nc.psum_tensor("p0m", [128, 256, 2], F32))[:, 0:96, :]

        for r in range(32):
            P.op("pool", lambda e, r=r: e.dma_start(out=scr["win"][r * 128:(r + 1) * 128, :],
                                                  in_=io["w_in"][r * 128:(r + 1) * 128, :],
                                                  max_dma_last_dim=4096),
                 writes=[("win", r)], dma="cast")
        for r in range(16):
            P.op("pool", lambda e, r=r: e.dma_start(out=scr["wglu"][r * 128:(r + 1) * 128, :],
                                                  in_=io["w_glu"][r * 128:(r + 1) * 128, :],
                                                  max_dma_last_dim=4096),
                 writes=[("wglu", r)], dma="cast")

        if Lp:
            P.op("sp", lambda e: e.dma_start(out=pers["flag"][:], in_=io["flag"][:, :]), writes=["flag"], dma="ld0_1")
        P.op("sp", lambda e: e.dma_start(out=c32[:], in_=io["c"][:, :]), writes=["c32"], dma="ld0_2")
        P.op("sp", lambda e: e.dma_start(out=ba96[:], in_=io["b_ada"][:, :]), writes=["ba96"], dma="ld0_3")
        P.op("pe", lambda e: e.transpose(pst[:, 0:32], c32[:], ident[0:32, 0:32]),
             reads=["c32", "ident"], writes=["pst"])
        for j in range(2):
            P.op("act", lambda e, j=j: e.activation(out=scol[:, :, j], in_=pst[:, 0:32], func=AF.Silu),
                 reads=["pst"], writes=["scol"])
        P.op("pe", lambda e: e.transpose(pst[:, 0:96], ba96[:], ident[0:96, 0:96]),
             reads=["ba96", "ident", "scol"], writes=["pst"])
        P.op("dve", lambda e: e.tensor_copy(out=bac[:], in_=pst[:, 0:96]), reads=["pst"], writes=["bac"])
        for kc in range(32):
            P.op("sp", lambda e, kc=kc: e.dma_start(out=wa[kc % 2][:], in_=io["w_ada"][kc * 128:(kc + 1) * 128, :]),
                 writes=[("wa", kc % 2)], dma="wa%d" % (kc % 2))
            for ct in range(96):
                P.op("pe", lambda e, kc=kc, ct=ct: e.matmul(
                    psm[:, ct, :], lhsT=wa[kc % 2][:, ct * 128:(ct + 1) * 128], rhs=scol[:, kc, :],
                    start=(kc == 0 and ct == 0), stop=(kc == 31 and ct == 95), skip_group_check=True),
                    reads=[("wa", kc % 2), "scol"], writes=["psm"])
        P.op("dve", lambda e: e.tensor_tensor(out=modc[:], in0=psm[:, :, 0], in1=bac[:], op=ALU.add),
             reads=["psm", "bac"], writes=["modc"])
        P.op("dve", lambda e: e.tensor_copy(out=pers["sh"][:], in_=modc[:, 0:32]), reads=["modc"], writes=["sh"])
        P.op("dve", lambda e: e.tensor_scalar(out=pers["s1p"][:], in0=modc[:, 32:64], scalar1=1.0, scalar2=None,
                                              op0=ALU.add), reads=["modc"], writes=["s1p"])
        P.op("pe", lambda e: e.transpose(pst[0:32, :], modc[:, 64:96], ident[:]),
             reads=["modc", "ident", "bac"], writes=["pst"])
        P.op("dve", lambda e: e.tensor_copy(out=g32[:], in_=pst[0:32, :]), reads=["pst"], writes=["g32"])
        P.op("sp", lambda e: e.dma_start(out=scr["gate"][:, :], in_=g32[:]), reads=["g32"], writes=["gscr"], dma="ld0_4")
        P.op("sp", lambda e: e.dma_start(
            out=grow[:], in_=scr["gate"].rearrange("a b -> (a b)")[None, :].broadcast_to([128, 4096])),
            reads=["gscr"], writes=["grow"], dma="ld0_5")
        for r in range(32):
            P.op("sp", lambda e, r=r: e.dma_start(out=wf[r % 2][:], in_=io["w_out"][r * 128:(r + 1) * 128, :]),
                 writes=[("wf", r % 2)], dma="wf%d" % (r % 2))
            P.op("dve", lambda e, r=r: e.tensor_tensor(out=wb[r % 2][:], in0=wf[r % 2][:], in1=grow[:], op=ALU.mult),
                 reads=[("wf", r % 2), "grow"], writes=[("wb", r % 2)])
            P.op("sp", lambda e, r=r: e.dma_start(out=scr["wout"][r * 128:(r + 1) * 128, :], in_=wb[r % 2][:]),
                 reads=[("wb", r % 2)], writes=[("wout", r)], dma="wst%d" % (r % 2))
        P.emit(ls, "a")


def load_hT(P, nc, xsrc, pers, xs, hT, psb, ident, t0, tagq):
    for i in range(4):
        xb = xs[i % len(xs)]
        xk = ("xs", i % len(xs))
        P.op("pool", lambda e, xb=xb, i=i: e.dma_start(out=xb[:], in_=xsrc[t0 + i * 128:t0 + (i + 1) * 128, :]),
             writes=[xk], dma="%s_%d" % (tagq, i % len(xs)))
        for kg in range(8):
            b = kg % len(psb)
            for k4 in range(4):
                kc = kg * 4 + k4
                P.op("pe", lambda e, xb=xb, b=b, k4=k4, kc=kc: e.transpose(
                    psb[b][:, k4 * 128:(k4 + 1) * 128], xb[:, kc * 128:(kc + 1) * 128], ident[:]),
                    reads=[xk, "identf"], writes=[("B", b)])
            for k4 in range(4):
                kc = kg * 4 + k4
                if kg % 2 == 0:
                    P.op("act", lambda e, b=b, k4=k4, kc=kc, i=i: e.activation(
                        out=hT[:, kc, i * 128:(i + 1) * 128], in_=psb[b][:, k4 * 128:(k4 + 1) * 128],
                        func=AF.Identity, bias=pers["sh"][:, kc:kc + 1], scale=pers["s1p"][:, kc:kc + 1]),
                        reads=[("B", b)], writes=[("hT", kc // 8)])
                else:
                    P.op("dve", lambda e, b=b, k4=k4, kc=kc, i=i: e.tensor_scalar(
                        out=hT[:, kc, i * 128:(i + 1) * 128], in0=psb[b][:, k4 * 128:(k4 + 1) * 128],
                        scalar1=pers["s1p"][:, kc:kc + 1], scalar2=pers["sh"][:, kc:kc + 1],
                        op0=ALU.mult, op1=ALU.add),
                        reads=[("B", b)], writes=[("hT", kc // 8)])


def sweep1(nc, st, io, scr, pers, L, Lp=0):
    P = Prog(nc)
    nblk = L // 512
    with ExitStack() as ls:
        sb = lambda name, shape, dt: ls.enter_context(nc.sbuf_tensor(name, shape, dt))
        identf = make_iota_mask(P, nc, ls, "identf1", [128, 128], [[1, 128]], 0, -1, ALU.is_equal, "identf")
        identb = sb("identb1", [128, 128], BF16)
        P.op("dve", lambda e: e.tensor_copy(out=identb[:], in_=identf[:]), reads=["identf"], writes=["identb"])
        m_i = sb("m64i", [128, 64], I32)
        mask64 = sb("mask64", [128, 64], F32)
        for hf in range(2):
            P.op("pool", lambda e, hf=hf: e.iota(m_i[hf * 64:(hf + 1) * 64, :], pattern=[[1, 64]], base=0,
                                               channel_multiplier=-1), writes=["m64i"])
        P.op("dve", lambda e: e.tensor_scalar(out=mask64[:], in0=m_i[:], scalar1=0.0, scalar2=None, op0=ALU.is_ge),
             reads=["m64i"], writes=["mask64"])
        tri = sb("tri", [128, 128], F32)
        P.op("dve", lambda e: e.memset(tri[:], 0.0), writes=["tri"])
        for hf in range(2):
            P.op("dve", lambda e, hf=hf: e.tensor_copy(out=tri[hf * 64:(hf + 1) * 64, hf * 64:(hf + 1) * 64],
                                                     in_=mask64[hf * 64:(hf + 1) * 64, :]),
                 reads=["mask64", "tri"], writes=["tri"])
        wgate = sb("wgate", [16, 1024], F32)
        bgate = sb("bgate", [1, 1024], F32)
        ones = sb("ones1", [1, 128], F32)
        gnb = sb("gnb", [128, 512], F32)
        P.op("sp", lambda e: e.dma_start(out=wgate[:], in_=io["w_gate"][:, :]), writes=["wgate"], dma="c1_1")
        P.op("sp", lambda e: e.dma_start(out=bgate[:], in_=io["b_gate"][:, :]), writes=["bgate"], dma="c1_2")
        P.op("sp", lambda e: e.dma_start(out=gnb[:], in_=io["gnorm"][0:1, :].broadcast_to([128, 512])),
             writes=["gnb"], dma="c1_3")
        P.op("dve", lambda e: e.memset(ones[:], 1.0), writes=["ones"])
        T = sb("Tst", [128, 8, 512], F32)
        Sbf = sb("Sbf", [128, 8, 512], BF16)
        eblp = sb("eblp", [128, 8], F32)
        P.op("dve", lambda e: e.memset(T[:], 0.0), writes=[("T", m) for m in range(8)])
        P.op("pool", lambda e: e.memset(Sbf[:], 0.0), writes=[("Sbf", m) for m in range(8)])
        P.op("dve", lambda e: e.memset(eblp[:], 1.0), writes=[("eblp", m) for m in range(8)])
        xs = [sb(f"xs1_{i}", [128, 4096], F32) for i in range(2)]
        hT = sb("hT1", [128, 32, 512], BF16)
        ring = Ring(nc, ls, "w1_", 3, [128, 4096], BF16)
        wg16 = sb("wg16", [128, 32, 16], BF16)
        glrT = sb("glrT", [16, 512], F32)
        nls = sb("nls", [128, 4, 1024], F32)
        ebt = [sb(f"ebt{e}", [128, 512], F32) for e in range(2)]
        eit = [sb(f"eit{e}", [128, 512], F32) for e in range(2)]
        qdec = [sb(f"qdec{e}", [128, 512], BF16) for e in range(2)]
        kinvT = [sb(f"kinvT{e}", [128, 512], BF16) for e in range(2)]
        kinv_tok = sb("kinvtok", [128, 4, 256], BF16)
        v_tok = sb("vtok", [128, 4, 512], BF16)
        gz = sb("gz", [128, 4, 512], F32)
        zs = sb("zs", [128, 512], F32)
        o_tok = sb("otok", [128, 4, 512], BF16)
        oTs = sb("oTs", [128, 4, 512], BF16)
        att_s = sb("atts", [128, 64], BF16)
        junk = sb("junk1", [128, 512], BF16)
        ssq = sb("ssq", [128, 2], F32)
        B = [ls.enter_context(nc.psum_tensor(f"B{i}", [128, 512], F32)) for i in range(5)]
        B5 = ls.enter_context(nc.psum_tensor("B5", [128, 1024], BF16))
        B6 = ls.enter_context(nc.psum_tensor("B6", [128, 512], F32))
        B7 = ls.enter_context(nc.psum_tensor("B7", [128, 512], F32))

        def wload(r0, nkc, c0, ncol):
            s, wt, wk = ring.next()
            view = wt[:].rearrange("p (a b) -> p a b", b=ncol)
            P.op("sp", lambda e: e.dma_start(out=view, in_=dram_rows(scr["win"], r0, nkc, c0, ncol)),
                 reads=[("win", r) for r in range(r0, r0 + nkc)], writes=[wk], dma="w1_%d" % s)
            return view, wk

        blocks = [("pre", i) for i in range(Lp // 512)] + [("own", i) for i in range(nblk)]
        for mode, blk in blocks:
            own = mode == "own"
            last_pre = (mode == "pre" and blk == Lp // 512 - 1)
            t0 = blk * 512
            load_hT(P, nc, io["x"] if own else io["xp"], pers, xs, hT, B[0:4], identf, t0, "x1")
            tok0 = t0 + (Lp if own else 0)
            for q in range(4):
                P.op("pool", lambda e, q=q, tok0=tok0: e.dma_start(
                    out=dram_rows(scr["hT"], q * 8, 8, tok0, 512), in_=hT[:, q * 8:(q + 1) * 8, :]),
                    reads=[("hT", q)], writes=[("hTscr", tok0, q)], dma="hs_%d" % q)
            P.op("sp", lambda e: e.dma_start(out=wg16[:], in_=dram_rows(scr["win"], 0, 32, C_G, 16)),
                 reads=[("win", r) for r in range(32)], writes=["wg16"], dma="w1g")
            for kc in range(32):
                P.op("pe", lambda e, kc=kc: e.matmul(B[4][0:16, :], lhsT=wg16[:, kc, :], rhs=hT[:, kc, :],
                                                     start=(kc == 0), stop=(kc == 31)),
                     reads=["wg16", ("hT", kc // 8)], writes=["B4"])
            P.op("dve", lambda e: e.tensor_copy(out=glrT[:], in_=B[4][0:16, :]), reads=["B4"], writes=["glrT"])
            for i in range(4):
                for e2 in range(2):
                    sl = slice(e2 * 512, (e2 + 1) * 512)
                    P.op("pe", lambda e, i=i, sl=sl: e.matmul(B[4][:], lhsT=glrT[:, i * 128:(i + 1) * 128],
                                                            rhs=wgate[:, sl], start=True, stop=False),
                         reads=["glrT", "wgate"], writes=["B4"])
                    P.op("pe", lambda e, sl=sl: e.matmul(B[4][:], lhsT=ones[:, :], rhs=bgate[:, sl],
                                                       start=False, stop=True),
                         reads=["ones", "bgate"], writes=["B4"])
                    P.op("act", lambda e, i=i, sl=sl: e.activation(out=nls[:, i, sl], in_=B[4][:], func=AF.Exp, scale=-1.0),
                         reads=["B4"], writes=[("nls", i)])
                    P.op("act", lambda e, i=i, sl=sl: e.activation(out=nls[:, i, sl], in_=nls[:, i, sl], func=AF.Ln, bias=1.0),
                         reads=[("nls", i)], writes=[("nls", i)])
            for h in range(4):
                for e2 in range(2):
                    m = 2 * h + e2
                    for i in range(4):
                        P.op("pe", lambda e, i=i, m=m: e.matmul(B[4][:, i * 128:(i + 1) * 128],
                                                              lhsT=nls[:, i, m * 128:(m + 1) * 128], rhs=tri[:],
                                                              start=True, stop=True),
                             reads=[("nls", i), "tri"], writes=["B4"])
                    P.op("act", lambda e, e2=e2: e.activation(out=ebt[e2][:], in_=B[4][:], func=AF.Exp, scale=-1.0 / 16),
                         reads=["B4"], writes=[("ebt", e2)])
                    P.op("act", lambda e, e2=e2: e.activation(out=eit[e2][:], in_=B[4][:], func=AF.Exp, scale=1.0 / 16),
                         reads=["B4"], writes=[("eit", e2)])
                for which, c0 in (("q", C_Q + 256 * h), ("k", C_K + 256 * h)):
                    if which == "q" and not own:
                        continue
                    pb = (0, 1) if which == "q" else (2, 3)
                    for kh in range(2):
                        view, wk = wload(kh * 16, 16, c0, 256)
                        for e2 in range(2):
                            for k16 in range(16):
                                kc = kh * 16 + k16
                                P.op("pe", lambda e, view=view, e2=e2, k16=k16, kc=kc, pb=pb: e.matmul(
                                    B[pb[e2]][:], lhsT=view[:, k16, e2 * 128:(e2 + 1) * 128], rhs=hT[:, kc, :],
                                    start=(kc == 0), stop=(kc == 31)),
                                    reads=[wk, ("hT", kc // 8)], writes=[("B", pb[e2])])
                    for e2 in range(2):
                        if which == "q":
                            P.op("dve", lambda e, e2=e2, pb=pb: e.scalar_tensor_tensor(
                                out=qdec[e2][:], in0=B[pb[e2]][:], scalar=1.0 / 16, in1=ebt[e2][:],
                                op0=ALU.mult, op1=ALU.mult),
                                reads=[("B", pb[e2]), ("ebt", e2)], writes=[("qdec", e2)])
                        else:
                            P.op("dve", lambda e, e2=e2, pb=pb: e.tensor_tensor(
                                out=kinvT[e2][:], in0=B[pb[e2]][:], in1=eit[e2][:], op=ALU.mult),
                                reads=[("B", pb[e2]), ("eit", e2)], writes=[("kinvT", e2)])
                for e2 in range(2):
                    for i in range(4):
                        P.op("pe", lambda e, e2=e2, i=i: e.transpose(
                            B5[:, (i * 2 + e2) * 128:(i * 2 + e2 + 1) * 128], kinvT[e2][:, i * 128:(i + 1) * 128], identb[:]),
                            reads=[("kinvT", e2), "identb"], writes=["B5"])
                P.op("act", lambda e: e.activation(out=kinv_tok[:].rearrange("p a b -> p (a b)"), in_=B5[:], func=AF.Copy),
                     reads=["B5"], writes=["kinvtok"])
                for which, c0 in (("v", C_V + 512 * h), ("z", C_ZG + 512 * h)):
                    if which == "z" and not own:
                        continue
                    for kq in range(4):
                        view, wk = wload(kq * 8, 8, c0, 512)
                        for i in range(4):
                            for k8 in range(8):
                                kc = kq * 8 + k8
                                P.op("pe", lambda e, view=view, i=i, k8=k8, kc=kc: e.matmul(
                                    B[i][:], lhsT=hT[:, kc, i * 128:(i + 1) * 128], rhs=view[:, k8, :],
                                    start=(kc == 0), stop=(kc == 31)),
                                    reads=[wk, ("hT", kc // 8)], writes=[("B", i)])
                    for i in range(4):
                        if which == "v":
                            P.op("act", lambda e, i=i: e.activation(out=v_tok[:, i, :], in_=B[i][:], func=AF.Copy),
                                 reads=[("B", i)], writes=[("vtok", i)])
                        else:
                            P.op("act", lambda e, i=i: e.activation(out=zs[:], in_=B[i][:], func=AF.Silu),
                                 reads=[("B", i)], writes=["zs"])
                            P.op("dve", lambda e, i=i: e.tensor_tensor(out=gz[:, i, :], in0=zs[:], in1=gnb[:], op=ALU.mult),
                                 reads=["zs", "gnb"], writes=[("gz", i)])
                for c in range(8):
                    par, i = c % 2, c // 2
                    rows = slice(64 * par, 64 * par + 64)
                    cols = slice(64 * c, 64 * c + 64)
                    for e2 in range(2 if own else 0):
                        P.op("pe", lambda e, e2=e2, rows=rows, cols=cols: e.matmul(
                            B7[rows, 0:64], lhsT=kinvT[e2][:, cols], rhs=qdec[e2][:, cols],
                            start=(e2 == 0), stop=(e2 == 1)),
                            reads=[("kinvT", e2), ("qdec", e2)], writes=["B7"])
                    if not own:
                        for e2 in range(2):
                            m = 2 * h + e2
                            ebl_prev = eblp[:, m:m + 1] if c == 0 else ebt[e2][:, 64 * c - 1:64 * c]
                            ebl_cur = ebt[e2][:, 64 * c + 63:64 * c + 64]
                            P.op("pe", lambda e, rows=rows, i=i, e2=e2: e.matmul(
                                B[2 + e2][:], lhsT=kinv_tok[rows, i, e2 * 128:(e2 + 1) * 128], rhs=v_tok[rows, i, :],
                                start=True, stop=True),
                                reads=["kinvtok", ("vtok", i)], writes=[("B", 2 + e2)])
                            P.op("dve", lambda e, m=m, e2=e2, ebl_prev=ebl_prev: e.scalar_tensor_tensor(
                                out=T[:, m, :], in0=T[:, m, :], scalar=ebl_prev, in1=B[2 + e2][:], op0=ALU.mult, op1=ALU.add),
                                reads=[("T", m), ("B", 2 + e2), ("ebt", e2), ("eblp", m)], writes=[("T", m)])
                            if last_pre and c == 7:
                                P.op("act", lambda e, m=m, ebl_cur=ebl_cur: e.activation(
                                    out=Sbf[:, m, :], in_=T[:, m, :], func=AF.Copy, scale=ebl_cur),
                                    reads=[("T", m), ("ebt", e2)], writes=[("Sbf", m)])
                        continue
                    P.op("dve", lambda e, rows=rows: e.tensor_tensor(out=att_s[rows, :], in0=B7[rows, 0:64],
                                                                   in1=mask64[rows, :], op=ALU.mult),
                         reads=["B7", "mask64"], writes=["atts"])
                    P.op("pe", lambda e, rows=rows, i=i: e.matmul(B6[rows, :], lhsT=att_s[rows, :], rhs=v_tok[rows, i, :],
                                                                start=True, stop=False),
                         reads=["atts", ("vtok", i)], writes=["B6"])
                    for e2 in range(2):
                        m = 2 * h + e2
                        P.op("pe", lambda e, rows=rows, cols=cols, e2=e2, m=m: e.matmul(
                            B6[rows, :], lhsT=qdec[e2][:, cols], rhs=Sbf[:, m, :], start=False, stop=(e2 == 1)),
                            reads=[("qdec", e2), ("Sbf", m)], writes=["B6"])
                    P.op("act", lambda e, rows=rows: e.activation(out=junk[rows, :], in_=B6[rows, :], func=AF.Square,
                                                                accum_out=ssq[rows, 0:1]),
                         reads=["B6"], writes=["ssq", "junk"])
                    P.op("dve", lambda e, rows=rows: e.tensor_scalar(out=ssq[rows, 1:2], in0=ssq[rows, 0:1],
                                                                   scalar1=1.0 / 512, scalar2=EPS, op0=ALU.mult, op1=ALU.add),
                         reads=["ssq"], writes=["ssq"])
                    P.op("act", lambda e, rows=rows: e.activation(out=ssq[rows, 1:2], in_=ssq[rows, 1:2], func=AF.Sqrt),
                         reads=["ssq"], writes=["ssq"])
                    P.op("dve", lambda e, rows=rows: e.reciprocal(out=ssq[rows, 1:2], in_=ssq[rows, 1:2]),
                         reads=["ssq"], writes=["ssq"])
                    P.op("dve", lambda e, rows=rows, i=i: e.scalar_tensor_tensor(
                        out=o_tok[rows, i, :], in0=B6[rows, :], scalar=ssq[rows, 1:2], in1=gz[rows, i, :],
                        op0=ALU.mult, op1=ALU.mult),
                        reads=["B6", "ssq", ("gz", i)], writes=[("otok", i)])
                    for e2 in range(2):
                        m = 2 * h + e2
                        ebl_prev = eblp[:, m:m + 1] if c == 0 else ebt[e2][:, 64 * c - 1:64 * c]
                        ebl_cur = ebt[e2][:, 64 * c + 63:64 * c + 64]
                        P.op("pe", lambda e, rows=rows, i=i, e2=e2: e.matmul(
                            B[2 + e2][:], lhsT=kinv_tok[rows, i, e2 * 128:(e2 + 1) * 128], rhs=v_tok[rows, i, :],
                            start=True, stop=True),
                            reads=["kinvtok", ("vtok", i)], writes=[("B", 2 + e2)])
                        P.op("dve", lambda e, m=m, e2=e2, ebl_prev=ebl_prev: e.scalar_tensor_tensor(
                            out=T[:, m, :], in0=T[:, m, :], scalar=ebl_prev, in1=B[2 + e2][:], op0=ALU.mult, op1=ALU.add),
                            reads=[("T", m), ("B", 2 + e2), ("ebt", e2), ("eblp", m)], writes=[("T", m)])
                        P.op("act", lambda e, m=m, ebl_cur=ebl_cur: e.activation(
                            out=Sbf[:, m, :], in_=T[:, m, :], func=AF.Copy, scale=ebl_cur),
                            reads=[("T", m), ("ebt", e2)], writes=[("Sbf", m)])
                for e2 in range(2):
                    m = 2 * h + e2
                    P.op("dve", lambda e, m=m, e2=e2: e.tensor_copy(out=eblp[:, m:m + 1], in_=ebt[e2][:, 511:512]),
                         reads=[("ebt", e2)], writes=[("eblp", m)])
                for half in range(2 if own else 0):
                    for cc2 in range(2):
                        cc = half * 2 + cc2
                        for i in range(4):
                            P.op("pe", lambda e, cc=cc, cc2=cc2, i=i: e.transpose(
                                B5[:, (cc2 * 4 + i) * 128:(cc2 * 4 + i + 1) * 128], o_tok[:, i, cc * 128:(cc + 1) * 128], identb[:]),
                                reads=[("otok", i), "identb"], writes=["B5"])
                    P.op("dve", lambda e, half=half: e.tensor_copy(
                        out=oTs[:, half * 2:half * 2 + 2, :].rearrange("p a b -> p (a b)"), in_=B5[:]),
                        reads=["B5"], writes=["oTs"])
                if own:
                    P.op("pool", lambda e, h=h, t0=t0: e.dma_start(out=dram_rows(scr["oT"], h * 4, 4, t0, 512), in_=oTs[:]),
                         reads=["oTs"], writes=[("oTscr", blk, h)], dma="o1")
            if last_pre:
                allT = [("T", m) for m in range(8)]
                allS = [("Sbf", m) for m in range(8)]
                P.op("dve", lambda e: e.tensor_scalar(out=T[:].rearrange("p a b -> p (a b)"), in0=T[:].rearrange("p a b -> p (a b)"),
                                                      scalar1=pers["flag"][:, 0:1], scalar2=None, op0=ALU.mult),
                     reads=allT + ["flag"], writes=allT)
                P.op("dve", lambda e: e.tensor_scalar(out=Sbf[:].rearrange("p a b -> p (a b)"), in0=Sbf[:].rearrange("p a b -> p (a b)"),
                                                      scalar1=pers["flag"][:, 0:1], scalar2=None, op0=ALU.mult),
                     reads=allS + ["flag"], writes=allS)
        P.emit(ls, "b")


TWO_PI = 6.283185307179586
PI = 3.141592653589793


def s5_prologue(nc, st, io, scr, pers):
    P = Prog(nc)
    with ExitStack() as ls:
        sb = lambda name, shape, dt=F32: ls.enter_context(nc.sbuf_tensor(name, shape, dt))
        cnt = [0]

        def dve(fn, r, w):
            P.op("dve", fn, reads=r, writes=w)

        def TT(out, a, b, op, r, w):
            dve(lambda e: e.tensor_tensor(out=out, in0=a, in1=b, op=op), r, w)

        def TS(out, a, s1, s2, op0, op1, r, w):
            if op1 is None:
                dve(lambda e: e.tensor_scalar(out=out, in0=a, scalar1=s1, scalar2=None, op0=op0), r, w)
            else:
                dve(lambda e: e.tensor_scalar(out=out, in0=a, scalar1=s1, scalar2=s2, op0=op0, op1=op1), r, w)

        def ACT(out, a, func, r, w, **kw):
            P.op("act", lambda e: e.activation(out=out, in_=a, func=func, **kw), reads=r, writes=w)

        identf = make_iota_mask(P, nc, ls, "identfp", [128, 128], [[1, 128]], 0, -1, ALU.is_equal, "identf")
        maskc = make_iota_mask(P, nc, ls, "maskc", [128, 8, 16], [[16, 8], [0, 16]], 15, -1, ALU.is_ge, "maskc")
        rep = make_iota_mask(P, nc, ls, "rep", [16, 8, 16], [[0, 8], [1, 16]], 0, -1, ALU.is_equal, "rep")
        pt = ls.enter_context(nc.psum_tensor("pqt", [128, 512], F32))[:, 0:128]
        pm = [ls.enter_context(nc.psum_tensor(f"pqm{i}", [128, 512], F32))[:, 0:128] for i in range(2)]

        def transp(out_sb, in_ap, npart, nfree, rk, wk, odt_copy="dve"):
            P.op("pe", lambda e: e.transpose(pt[0:nfree, 0:npart], in_ap, identf[0:npart, 0:npart]),
                 reads=rk + ["identf"], writes=["pt"])
            dve(lambda e: e.tensor_copy(out=out_sb, in_=pt[0:nfree, 0:npart]), ["pt"], wk)

        lt = [sb(f"lt{i}", [64, 128]) for i in range(2)]
        ldt = sb("ldt", [64, 2]); dtx = sb("dtx", [64, 2, 64])
        P.op("sp", lambda e: e.dma_start(out=lt[0][:], in_=io["lam_re"][:, :]), writes=["lt0"], dma="q0_1")
        P.op("sp", lambda e: e.dma_start(out=lt[1][:], in_=io["lam_im"][:, :]), writes=["lt1"], dma="q0_2")
        P.op("sp", lambda e: e.dma_start(out=ldt[:], in_=io["log_dt"][:, :]), writes=["ldt"], dma="q0_3")
        ACT(ldt[:], ldt[:], AF.Exp, ["ldt"], ["ldt"])
        dve(lambda e: e.tensor_copy(out=dtx[:], in_=ldt[:, :, None].to_broadcast([64, 2, 64])), ["ldt"], ["dtx"])
        zt = [sb(f"ztt{i}", [64, 128]) for i in range(2)]
        for i in range(2):
            TT(zt[i][:], lt[i][:], dtx[:].rearrange("p a b -> p (a b)"), ALU.mult, [f"lt{i}", "dtx"], [f"ztt{i}"])
        lam = [sb(f"lam{i}", [128, 64]) for i in range(2)]
        z = [sb(f"z{i}", [128, 64]) for i in range(2)]
        for i in range(2):
            transp(lam[i][:], lt[i][:], 64, 128, [f"lt{i}"], [f"lam{i}"])
            transp(z[i][:], zt[i][:], 64, 128, [f"ztt{i}"], [f"z{i}"])
        ki = sb("ki", [128, 64], I32); kf = sb("kf", [128, 64]); rr = sb("rr", [128, 64]); mm_ = sb("mm_", [128, 64])
        xs_ = sb("xsft", [128, 64])

        def sin_of(out, x_ap, xk, ok):
            TS(ki[:], x_ap, 1.0 / TWO_PI, None, ALU.mult, None, [xk], ["ki"])
            dve(lambda e: e.tensor_copy(out=kf[:], in_=ki[:]), ["ki"], ["kf"])
            dve(lambda e: e.scalar_tensor_tensor(out=rr[:], in0=kf[:], scalar=-TWO_PI, in1=x_ap, op0=ALU.mult, op1=ALU.add),
                ["kf", xk], ["rr"])
            TS(mm_[:], rr[:], PI, -TWO_PI, ALU.is_gt, ALU.mult, ["rr"], ["mm_"])
            TT(rr[:], rr[:], mm_[:], ALU.add, ["rr", "mm_"], ["rr"])
            TS(mm_[:], rr[:], -PI, TWO_PI, ALU.is_lt, ALU.mult, ["rr"], ["mm_"])
            TT(rr[:], rr[:], mm_[:], ALU.add, ["rr", "mm_"], ["rr"])
            ACT(out, rr[:], AF.Sin, ["rr"], [ok])

        sn = sb("sn", [128, 64]); cs = sb("cs", [128, 64]); mag = sb("mag", [128, 64]); imag = sb("imag", [128, 64])
        sin_of(sn[:], z[1][:], "z1", "sn")
        TS(xs_[:], z[1][:], PI / 2, None, ALU.add, None, ["z1"], ["xsft"])
        sin_of(cs[:], xs_[:], "xsft", "cs")
        ACT(mag[:], z[0][:], AF.Exp, ["z0"], ["mag"])
        ACT(imag[:], z[0][:], AF.Exp, ["z0"], ["imag"], scale=-1.0)
        APr = sb("APr", [128, 9, 64]); APi = sb("APi", [128, 9, 64]); AMr = sb("AMr", [128, 8, 64]); AMi = sb("AMi", [128, 8, 64])
        t1 = sb("t1", [128, 64]); t2 = sb("t2", [128, 64])
        for Tn, nm in ((APr, "APr"), (AMr, "AMr")):
            dve(lambda e, Tn=Tn: e.memset(Tn[:, 0, :], 1.0), [], [nm])
        for Tn, nm in ((APi, "APi"), (AMi, "AMi")):
            dve(lambda e, Tn=Tn: e.memset(Tn[:, 0, :], 0.0), [], [nm])
        TT(APr[:, 1, :], mag[:], cs[:], ALU.mult, ["mag", "cs"], ["APr"])
        TT(APi[:, 1, :], mag[:], sn[:], ALU.mult, ["mag", "sn"], ["APi"])
        TT(AMr[:, 1, :], imag[:], cs[:], ALU.mult, ["imag", "cs"], ["AMr"])
        dve(lambda e: e.scalar_tensor_tensor(out=AMi[:, 1, :], in0=imag[:], scalar=-1.0, in1=sn[:], op0=ALU.mult, op1=ALU.mult),
            ["imag", "sn"], ["AMi"])

        def cmul(o_re, o_im, a_re, a_im, b_re, b_im, tmpa, tmpb, r, w):
            TT(tmpa, a_re, b_re, ALU.mult, r, ["cm_a"])
            TT(tmpb, a_im, b_im, ALU.mult, r, ["cm_b"])
            TT(o_re, tmpa, tmpb, ALU.subtract, ["cm_a", "cm_b"], w)
            TT(tmpa, a_re, b_im, ALU.mult, r + w, ["cm_a"])
            TT(tmpb, a_im, b_re, ALU.mult, r + w, ["cm_b"])
            TT(o_im, tmpa, tmpb, ALU.add, ["cm_a", "cm_b"], w)

        for k in range(2, 9):
            cmul(APr[:, k, :], APi[:, k, :], APr[:, k - 1, :], APi[:, k - 1, :], APr[:, 1, :], APi[:, 1, :],
                 t1[:], t2[:], ["APr", "APi"], ["APr", "APi"])
        for k in range(2, 8):
            cmul(AMr[:, k, :], AMi[:, k, :], AMr[:, k - 1, :], AMi[:, k - 1, :], AMr[:, 1, :], AMi[:, 1, :],
                 t1[:], t2[:], ["AMr", "AMi"], ["AMr", "AMi"])
        for c in range(2):
            dve(lambda e, c=c: e.tensor_copy(out=pers["A8c"][:, c, :], in_=APr[:, 8, :]), ["APr"], ["A8c"])
        dve(lambda e: e.tensor_copy(out=pers["A8n"][:, 1, :], in_=APi[:, 8, :]), ["APi"], ["A8n"])
        TS(pers["A8n"][:, 0, :], APi[:, 8, :], -1.0, None, ALU.mult, None, ["APi"], ["A8n"])
        den = sb("den", [128, 64]); nre = sb("nre", [128, 64]); fre = sb("fre", [128, 64]); fim = sb("fim", [128, 64])
        TT(t1[:], lam[0][:], lam[0][:], ALU.mult, ["lam0"], ["t1"])
        TT(t2[:], lam[1][:], lam[1][:], ALU.mult, ["lam1"], ["t2"])
        TT(den[:], t1[:], t2[:], ALU.add, ["t1", "t2"], ["den"])
        dve(lambda e: e.reciprocal(out=den[:], in_=den[:]), ["den"], ["den"])
        TS(nre[:], APr[:, 1, :], -1.0, None, ALU.add, None, ["APr"], ["nre"])
        TT(t1[:], nre[:], lam[0][:], ALU.mult, ["nre", "lam0"], ["t1"])
        TT(t2[:], APi[:, 1, :], lam[1][:], ALU.mult, ["APi", "lam1"], ["t2"])
        TT(fre[:], t1[:], t2[:], ALU.add, ["t1", "t2"], ["fre"])
        TT(fre[:], fre[:], den[:], ALU.mult, ["fre", "den"], ["fre"])
        TT(t1[:], APi[:, 1, :], lam[0][:], ALU.mult, ["APi", "lam0"], ["t1"])
        TT(t2[:], nre[:], lam[1][:], ALU.mult, ["nre", "lam1"], ["t2"])
        TT(fim[:], t1[:], t2[:], ALU.subtract, ["t1", "t2"], ["fim"])
        TT(fim[:], fim[:], den[:], ALU.mult, ["fim", "den"], ["fim"])
        Bt = [sb(f"Bt{i}", [128, 64, 16]) for i in range(2)]
        bb = [sb(f"bb{i}", [128, 64, 16]) for i in range(2)]
        Ct = [sb(f"Ct{i}", [128, 64, 16]) for i in range(2)]
        u1 = sb("u1", [128, 64, 16]); u2 = sb("u2", [128, 64, 16])
        for i, nm in enumerate(("b_re", "b_im")):
            for g2 in range(2):
                P.op("sp", lambda e, i=i, nm=nm, g2=g2: e.dma_start(
                    out=Bt[i][g2 * 64:(g2 + 1) * 64, :, :],
                    in_=io[nm].rearrange("(t g2) p h -> g2 p t h", g2=2)[g2]), writes=[f"Bt{i}"], dma="q1_%d_%d" % (i, g2))
        bc = lambda ap: ap[:, :, None].to_broadcast([128, 64, 16])
        cmul(bb[0][:], bb[1][:], bc(fre[:]), bc(fim[:]), Bt[0][:], Bt[1][:], u1[:], u2[:],
             ["fre", "fim", "Bt0", "Bt1"], ["bb0", "bb1"])
        cin = sb("cin", [128, 128])
        for i, nm in enumerate(("c_re", "c_im")):
            for tb in range(8):
                for tl in range(8):
                    tt_ = tb * 8 + tl
                    P.op("sp", lambda e, nm=nm, tl=tl, tt_=tt_: e.dma_start(
                        out=cin[tl * 16:(tl + 1) * 16, :].rearrange("h (g p) -> h g p", g=2),
                        in_=io[nm][2 * tt_:2 * tt_ + 2].rearrange("g h p -> h g p")), writes=["cin"], dma="q2")
                transp(Ct[i][:, tb * 8:(tb + 1) * 8, :].rearrange("p a b -> p (a b)"), cin[:], 128, 128, ["cin"], [f"Ct{i}"])
        d16 = sb("d16", [128, 16]); dgh = sb("dgh", [16, 128]); dcol = sb("dcol", [128, 128])
        P.op("sp", lambda e: e.dma_start(out=d16[:], in_=io["s5_d"][:, :]), writes=["d16"], dma="q0_4")
        transp(dgh[:], d16[:], 128, 16, ["d16"], ["dgh"])
        P.op("pe", lambda e: e.matmul(pt[:, :], lhsT=rep[:].rearrange("p a b -> p (a b)"), rhs=dgh[:], start=True, stop=True),
             reads=["rep", "dgh"], writes=["pt"])
        dve(lambda e: e.tensor_copy(out=dcol[:], in_=pt[:, :]), ["pt"], ["dcol"])
        bg16 = sb("bg16", [16, 128])
        P.op("sp", lambda e: e.dma_start(out=bg16[:], in_=io["b_glu"][:, :]), writes=["bg16"], dma="q0_5")
        transp(pers["bgluc"][:], bg16[:], 16, 128, ["bg16"], ["bgluc"])
        NB = 8
        X = [sb(f"X{i}", [128, NB, 8, 16]) for i in range(2)]
        Y = [sb(f"Y{i}", [128, NB, 8, 16]) for i in range(2)]
        Zm = [sb(f"Zm{i}", [128, NB, 8, 16]) for i in range(2)]
        Fb = sb("Fb", [128, NB, 2, 128], BF16)
        ZTb = sb("ZTb", [128, NB, 2, 128], BF16)
        Mb = sb("Mb", [128, 2 * NB, 128], BF16)
        w1 = sb("w1", [128, NB, 16]); w2 = sb("w2", [128, NB, 16]); mtmp = sb("mtmp", [128, 128])
        for q in range(64 // NB):
            ts = slice(q * NB, (q + 1) * NB)
            bcp = lambda ap: ap[:, :, None].to_broadcast([128, NB, 16])
            for j in range(8):
                cmul(X[0][:, :, j, :], X[1][:, :, j, :], bcp(AMr[:, j, ts]), bcp(AMi[:, j, ts]), bb[0][:, ts, :], bb[1][:, ts, :],
                     w1[:], w2[:], ["AMr", "AMi", "bb0", "bb1"], ["X0", "X1"])
                cmul(Y[0][:, :, j, :], Y[1][:, :, j, :], bcp(APr[:, j, ts]), bcp(APi[:, j, ts]), Ct[0][:, ts, :], Ct[1][:, ts, :],
                     w1[:], w2[:], ["APr", "APi", "Ct0", "Ct1"], ["Y0", "Y1"])
                cmul(Zm[0][:, :, j, :], Zm[1][:, :, j, :], bcp(APr[:, 7 - j, ts]), bcp(APi[:, 7 - j, ts]), bb[0][:, ts, :], bb[1][:, ts, :],
                     w1[:], w2[:], ["APr", "APi", "bb0", "bb1"], ["Z0", "Z1"])
                TT(w1[:], bcp(APr[:, j + 1, ts]), Ct[0][:, ts, :], ALU.mult, ["APr", "Ct0"], ["cm_a"])
                TT(w2[:], bcp(APi[:, j + 1, ts]), Ct[1][:, ts, :], ALU.mult, ["APi", "Ct1"], ["cm_b"])
                TT(Fb[:, :, 0, j * 16:(j + 1) * 16], w1[:], w2[:], ALU.subtract, ["cm_a", "cm_b"], ["Fb"])
                TT(w1[:], bcp(APr[:, j + 1, ts]), Ct[1][:, ts, :], ALU.mult, ["APr", "Ct1", "Fb"], ["cm_a"])
                TT(w2[:], bcp(APi[:, j + 1, ts]), Ct[0][:, ts, :], ALU.mult, ["APi", "Ct0", "Fb"], ["cm_b"])
                dve(lambda e, j=j: e.scalar_tensor_tensor(out=Fb[:, :, 1, j * 16:(j + 1) * 16], in0=w1[:], scalar=-1.0, in1=w2[:],
                                                          op0=ALU.mult, op1=ALU.subtract), ["cm_a", "cm_b"], ["Fb"])
            TS(X[1][:], X[1][:], -1.0, None, ALU.mult, None, ["X1"], ["X1"])
            for tl in range(NB):
                t = q * NB + tl
                for c in range(2):
                    P.op("pe", lambda e, c=c, tl=tl: e.transpose(pt[:, :], Zm[c][:, tl, :, :].rearrange("p a b -> p (a b)"), identf[:]),
                         reads=[f"Z{c}", "identf"], writes=["pt"])
                    dve(lambda e, c=c, tl=tl: e.tensor_copy(out=ZTb[:, tl, c, :], in_=pt[:, :]), ["pt"], ["ZTb"])
                for g2 in range(2):
                    rows = slice(g2 * 64, (g2 + 1) * 64)
                    pmb = pm[g2]
                    for c in range(2):
                        P.op("pe", lambda e, c=c, tl=tl, rows=rows, pmb=pmb: e.matmul(
                            pmb[:, :], lhsT=X[c][rows, tl, :, :].rearrange("p a b -> p (a b)"),
                            rhs=Y[c][rows, tl, :, :].rearrange("p a b -> p (a b)"), start=(c == 0), stop=(c == 1)),
                            reads=["X0", "X1", "Y0", "Y1"], writes=[("pm", g2)])
                    g = 2 * t + g2
                    TT(mtmp[:], pmb[:, :], maskc[:].rearrange("p a b -> p (a b)"), ALU.mult, [("pm", g2), "maskc"], ["mtmp"])
                    dve(lambda e, g=g, tl=tl, g2=g2: e.scalar_tensor_tensor(
                        out=Mb[:, tl * 2 + g2, :], in0=identf[:], scalar=dcol[:, g:g + 1], in1=mtmp[:],
                        op0=ALU.mult, op1=ALU.add), ["mtmp", "identf", "dcol"], ["Mb"])
            P.op("sp", lambda e, q=q: e.dma_start(out=scr["mint"][q * 2 * NB:(q + 1) * 2 * NB].rearrange("g a b -> a g b"), in_=Mb[:]),
                 reads=["Mb"], writes=[("mint", q)], dma="q3m")
            P.op("sp", lambda e, q=q: e.dma_start(out=scr["zt"][q * NB:(q + 1) * NB].rearrange("t c a b -> a t c b"), in_=ZTb[:]),
                 reads=["ZTb"], writes=[("zts", q)], dma="q3z")
            P.op("sp", lambda e, q=q: e.dma_start(out=scr["ff"][q * NB:(q + 1) * NB].rearrange("t c a b -> a t c b"), in_=Fb[:]),
                 reads=["Fb"], writes=[("ffs", q)], dma="q3f")
        P.emit(ls, "q")


import os
S2_STOP = int(os.environ.get("S2_STOP", "0"))


def sweep2(nc, st, io, scr, pers, L, Lp=0):
    P = Prog(nc)
    TB = 256
    NCH = TB // 8
    nblk = L // TB
    GELU_C = 1.5957691216057308
    with ExitStack() as ls:
        sb = lambda name, shape, dt=F32: ls.enter_context(nc.sbuf_tensor(name, shape, dt))
        identf = make_iota_mask(P, nc, ls, "identf2", [128, 128], [[1, 128]], 0, -1, ALU.is_equal, "identf")
        identb = sb("identb2", [128, 128], BF16)
        P.op("dve", lambda e: e.tensor_copy(out=identb[:], in_=identf[:]), reads=["identf"], writes=["identb"])
        hTs = [sb(f"hT2_{i}", [128, 32, TB], BF16) for i in range(2)]
        hpar = 0
        ring = Ring(nc, ls, "w2_", 3, [128, 8, 512], BF16)
        uT = sb("uT", [128, 4, TB], BF16)
        UUs = [sb(f"UU{i}", [NCH, 8, 8, 16], BF16) for i in range(2)]
        UD = sb("UD", [128, 128, NCH], BF16)
        ZTc = [sb(f"ZTc{i}", [128, 8, 2, 128], BF16) for i in range(2)]
        Fc = [sb(f"Fc{i}", [128, 8, 2, 128], BF16) for i in range(2)]
        Mc = [sb(f"Mc{i}", [128, 16, 128], BF16) for i in range(2)]
        W = sb("Wst", [128, NCH, 2, 64])
        Sbf = [sb(f"Sbf2_{i}", [128, NCH + 1, 2, 64], BF16) for i in range(2)]
        for i_ in range(2):
            P.op("pool", lambda e, i_=i_: e.memset(Sbf[i_][:], 0.0), writes=["Sbf"])
        carry = sb("carry", [128, 2, 64])
        tA = sb("tA", [128, 2, 64]); tBm = sb("tBm", [128, 2, 64])
        ysbs = [sb(f"ysb{i}", [128, 8 * NCH]) for i in range(2)]
        ysqs = [sb(f"ysq{i}", [128, 8 * NCH]) for i in range(2)]
        ysgs = [sb(f"ysg{i}", [128, 8 * NCH]) for i in range(2)]
        YDs = [sb(f"YD{i}", [128, 8, NCH], BF16) for i in range(2)]
        YYs = [sb(f"YY{i}", [NCH, 8, 8, 16], BF16) for i in range(2)]
        ygT = sb("ygT", [128, 16, TB], BF16)
        sg = sb("sg2", [128, TB])
        zcb = sb("zc2", [128, 16, TB], BF16); zgb = sb("zg2", [128, 16, TB], BF16)
        oT5 = sb("oT5", [128, 4, TB], BF16)
        B = [ls.enter_context(nc.psum_tensor(f"C{i}", [128, 512], F32)) for i in range(4)]
        B5 = ls.enter_context(nc.psum_tensor("C5", [128, 1024], BF16))
        B5b = ls.enter_context(nc.psum_tensor("C5b", [128, 1024], BF16))
        B6 = ls.enter_context(nc.psum_tensor("C6", [128, 512], F32))
        B7 = ls.enter_context(nc.psum_tensor("C7", [128, 512], F32))
        P.op("dve", lambda e: e.memset(carry[:], 0.0), writes=["carry"])

        def wload(src, r0, nkc, c0, ncol, rk):
            s, wt, wk = ring.next()
            view = wt[:, 0:nkc, 0:ncol]
            P.op("sp", lambda e: e.dma_start(out=view, in_=dram_rows(src, r0, nkc, c0, ncol)), reads=rk, writes=[wk], dma="w2_%d" % s)
            return view, wk

        blocks = [("pre", i) for i in range(Lp // TB)] + [("own", i) for i in range(nblk)]
        for mode, blk in blocks:
            own = mode == "own"
            last_pre = (mode == "pre" and blk == Lp // TB - 1)
            xsrc = io["x"] if own else io["xp"]
            t0 = blk * TB
            tok0 = t0 + (Lp if own else 0)
            hT = hTs[hpar % 2]
            hkq = lambda q_, hp_=hpar % 2: ("hT", hp_, q_)
            for q in range(4):
                P.op("sp", lambda e, q=q, tok0=tok0, hT=hT: e.dma_start(
                    out=hT[:, q * 8:(q + 1) * 8, :], in_=dram_rows(scr["hT"], q * 8, 8, tok0, TB)),
                    writes=[hkq(q)], dma="h2_%d_%d" % (hpar % 2, q))
            hpar += 1
            for cg in range(4):
                for kq in range(4):
                    view, wk = wload(scr["win"], kq * 8, 8, C_U + cg * 512, 512, [])
                    for c4 in range(4):
                        for k8 in range(8):
                            kc = kq * 8 + k8
                            P.op("pe", lambda e, view=view, c4=c4, k8=k8, kc=kc, hT=hT: e.matmul(
                                B[c4][:, 0:TB], lhsT=view[:, k8, c4 * 128:(c4 + 1) * 128], rhs=hT[:, kc, :],
                                start=(kc == 0), stop=(kc == 31)),
                                reads=[wk, hkq(kc // 8)], writes=[("B", c4)])
                for c4 in range(4):
                    ct = cg * 4 + c4
                    pp = ct % 2
                    UU = UUs[pp]
                    T5, T5k = (B5, "B5") if pp == 0 else (B6[:].bitcast(BF16), "B67_0")
                    T5b, T5bk = (B5b, "B5b") if pp == 0 else (B7[:].bitcast(BF16), "B67_1")
                    T5 = T5 if pp == 1 else T5[:]
                    T5b = T5b if pp == 1 else T5b[:]
                    P.op("act", lambda e, c4=c4: e.activation(out=uT[:, c4, :], in_=B[c4][:, 0:TB], func=AF.Copy),
                         reads=[("B", c4)], writes=[("uT", c4)])
                    for j in range(8):
                        P.op("pe", lambda e, c4=c4, j=j, T5=T5: e.transpose(
                            T5[0:NCH, j * 128:(j + 1) * 128], uT[:, c4, j::8], identb[:]),
                            reads=[("uT", c4), "identb"], writes=[T5k])
                    P.op("act", lambda e, UU=UU, T5=T5: e.activation(
                        out=UU[:].rearrange("n g j h -> n j g h"),
                        in_=T5[0:NCH, :].rearrange("n (j g h) -> n j g h", j=8, g=8), func=AF.Copy), reads=[T5k], writes=[("UU", pp)])
                    for gl in range(8):
                        P.op("pe", lambda e, gl=gl, UU=UU, T5b=T5b: e.transpose(
                            T5b[:, gl * NCH:(gl + 1) * NCH], UU[:, gl, :, :].rearrange("n j h -> n (j h)"), identb[0:NCH, 0:NCH]),
                            reads=[("UU", pp), "identb"], writes=[T5bk])
                    P.op("act", lambda e, ct=ct, T5b=T5b: e.activation(
                        out=UD[:, ct * 8:(ct + 1) * 8, :].rearrange("p g n -> p (g n)"), in_=T5b[:, 0:8 * NCH], func=AF.Copy),
                        reads=[T5bk], writes=[("UD", ct)])
            if S2_STOP == 1:
                continue
            for q in range(8):
                zc = ZTc[q % 2]
                P.op("sp", lambda e, zc=zc, q=q: e.dma_start(out=zc[:], in_=scr["zt"][q * 8:(q + 1) * 8].rearrange("t c a b -> a t c b")),
                     writes=[("ZTc", q % 2)], dma="m2z%d" % (q % 2))
                for tl in range(8):
                    t = q * 8 + tl
                    for g2 in range(2):
                        for c, Bc in ((0, B6), (1, B7)):
                            P.op("pe", lambda e, zc=zc, tl=tl, g2=g2, c=c, Bc=Bc, t=t: e.matmul(
                                Bc[g2 * 64:(g2 + 1) * 64, tl * NCH:(tl + 1) * NCH], lhsT=zc[:, tl, c, g2 * 64:(g2 + 1) * 64],
                                rhs=UD[:, 2 * t + g2, :], start=True, stop=True),
                                reads=[("ZTc", q % 2), ("UD", (2 * t + g2) // 8)], writes=["B67_%d" % c])
                for c, Bc in ((0, B6), (1, B7)):
                    P.op("act", lambda e, c=c, Bc=Bc, q=q: e.activation(
                        out=W[:, :, c, q * 8:(q + 1) * 8].rearrange("p n t -> p t n"),
                        in_=Bc[:, 0:8 * NCH].rearrange("p (t n) -> p t n", t=8), func=AF.Copy),
                        reads=["B67_%d" % c], writes=["W"])
            if own:
                for cgo in range(4):
                    for kq in range(4):
                        view, wk = wload(scr["win"], kq * 8, 8, C_ZS + cgo * 512, 512, [])
                        for c4 in range(4):
                            for k8 in range(8):
                                kc = kq * 8 + k8
                                P.op("pe", lambda e, view=view, c4=c4, k8=k8, kc=kc, hT=hT: e.matmul(
                                    B[c4][:, 0:TB], lhsT=view[:, k8, c4 * 128:(c4 + 1) * 128], rhs=hT[:, kc, :],
                                    start=(kc == 0), stop=(kc == 31)),
                                    reads=[wk, hkq(kc // 8)], writes=[("B", c4)])
                    for c4 in range(4):
                        ct = cgo * 4 + c4
                        P.op("act", lambda e, c4=c4, ct=ct: e.activation(out=zcb[:, ct, :], in_=B[c4][:, 0:TB], func=AF.Copy),
                             reads=[("B", c4)], writes=[("zc", ct)])
                        P.op("act", lambda e, c4=c4, ct=ct: e.activation(out=zgb[:, ct, :], in_=B[c4][:, 0:TB], func=AF.Sigmoid),
                             reads=[("B", c4)], writes=[("zg", ct)])
                        P.op("pool", lambda e, ct=ct: e.tensor_tensor(out=zcb[:, ct, :], in0=zcb[:, ct, :], in1=zgb[:, ct, :], op=ALU.mult),
                             reads=[("zc", ct), ("zg", ct)], writes=[("zc", ct)])
            if S2_STOP == 2:
                continue
            for g2_ in range(2):
                hp = slice(g2_ * 64, (g2_ + 1) * 64)
                P.op("act", lambda e, g2_=g2_, hp=hp: e.activation(out=Sbf[g2_][hp, 0, :, :], in_=carry[hp], func=AF.Copy),
                     reads=["carry"], writes=["Sbf"])
            for n in range(NCH):
                prev = carry[:] if n == 0 else W[:, n - 1, :, :]
                P.op("dve", lambda e, prev=prev: e.tensor_tensor(out=tA[:], in0=pers["A8c"][:], in1=prev, op=ALU.mult),
                     reads=["W", "carry"], writes=["tA"])
                P.op("dve", lambda e, prev=prev: e.tensor_tensor(out=tBm[:, 0, :], in0=pers["A8n"][:, 0, :], in1=prev[:, 1, :], op=ALU.mult),
                     reads=["W", "carry"], writes=["tB"])
                P.op("dve", lambda e, prev=prev: e.tensor_tensor(out=tBm[:, 1, :], in0=pers["A8n"][:, 1, :], in1=prev[:, 0, :], op=ALU.mult),
                     reads=["W", "carry"], writes=["tB"])
                P.op("dve", lambda e, n=n: e.tensor_tensor(out=W[:, n, :, :], in0=W[:, n, :, :], in1=tA[:], op=ALU.add),
                     reads=["W", "tA"], writes=["W"])
                P.op("dve", lambda e, n=n: e.tensor_tensor(out=W[:, n, :, :], in0=W[:, n, :, :], in1=tBm[:], op=ALU.add),
                     reads=["W", "tB"], writes=["W"])
            P.op("dve", lambda e: e.tensor_copy(out=carry[:], in_=W[:, NCH - 1, :, :]), reads=["W"], writes=["carry"])
            if last_pre:
                P.op("dve", lambda e: e.tensor_scalar(out=carry[:].rearrange("p a b -> p (a b)"), in0=carry[:].rearrange("p a b -> p (a b)"),
                                                      scalar1=pers["flag"][:, 0:1], scalar2=None, op0=ALU.mult),
                     reads=["carry", "flag"], writes=["carry"])
            if not own:
                continue
            for g2_ in range(2):
                hp = slice(g2_ * 64, (g2_ + 1) * 64)
                P.op("act", lambda e, g2_=g2_, hp=hp: e.activation(
                    out=Sbf[g2_][hp, 1:NCH + 1, :, :].rearrange("p n c t -> p (n c t)"),
                    in_=W[hp].rearrange("p n c t -> p (n c t)"), func=AF.Copy),
                    reads=["W"], writes=["Sbf"])
            if S2_STOP == 3:
                continue
            for ct in range(16):
                q = ct // 2
                if ct % 2 == 0:
                    fc, mc = Fc[q % 2], Mc[q % 2]
                    P.op("sp", lambda e, fc=fc, q=q: e.dma_start(out=fc[:], in_=scr["ff"][q * 8:(q + 1) * 8].rearrange("t c a b -> a t c b")),
                         writes=[("Fc", q % 2)], dma="m2f%d" % (q % 2))
                    P.op("sp", lambda e, mc=mc, q=q: e.dma_start(out=mc[:], in_=scr["mint"][q * 16:(q + 1) * 16].rearrange("g a b -> a g b")),
                         writes=[("Mc", q % 2)], dma="m2m%d" % (q % 2))
                yb = B[ct % 2]
                for gl in range(8):
                    g = ct * 8 + gl
                    t, g2 = g // 2, g % 2
                    tl = t - q * 8
                    rows = slice(g2 * 64, (g2 + 1) * 64)
                    osl = yb[:, gl * NCH:(gl + 1) * NCH]
                    P.op("pe", lambda e, osl=osl, mc=mc, g=g, q=q: e.matmul(osl, lhsT=mc[:, g - q * 16, :], rhs=UD[:, g, :],
                                                                       start=True, stop=False),
                         reads=[("Mc", q % 2), ("UD", ct)], writes=[("B", ct % 2)])
                    for c in range(2):
                        P.op("pe", lambda e, osl=osl, fc=fc, tl=tl, c=c, g2=g2, t=t: e.matmul(
                            osl, lhsT=fc[:, tl, c, :], rhs=Sbf[g2][:, 0:NCH, c, t], start=False, stop=(c == 1)),
                            reads=[("Fc", q % 2), "Sbf"], writes=[("B", ct % 2)])
                ysl = yb[:, 0:8 * NCH]
                pp = ct % 2
                ysb, ysq, ysg, YD, YY = ysbs[pp], ysqs[pp], ysgs[pp], YDs[pp], YYs[pp]
                T5, T5k = (B5[:], "B5") if pp == 0 else (B6[:].bitcast(BF16), "B67_0")
                T5b, T5bk = (B5b[:], "B5b") if pp == 0 else (B7[:].bitcast(BF16), "B67_1")
                kk = lambda s: (s, pp)
                P.op("act", lambda e, ysl=ysl, ysb=ysb: e.activation(out=ysb[:], in_=ysl, func=AF.Copy), reads=[("B", ct % 2)], writes=[kk("ysb")])
                P.op("act", lambda e, ysl=ysl, ysq=ysq: e.activation(out=ysq[:], in_=ysl, func=AF.Square), reads=[("B", ct % 2)], writes=[kk("ysq")])
                P.op("dve", lambda e, ysq=ysq: e.tensor_scalar(out=ysq[:], in0=ysq[:], scalar1=0.044715, scalar2=1.0, op0=ALU.mult, op1=ALU.add),
                     reads=[kk("ysq")], writes=[kk("ysq")])
                P.op("dve", lambda e, ysq=ysq, ysb=ysb: e.tensor_tensor(out=ysq[:], in0=ysq[:], in1=ysb[:], op=ALU.mult),
                     reads=[kk("ysq"), kk("ysb")], writes=[kk("ysq")])
                P.op("act", lambda e, ysq=ysq, ysg=ysg: e.activation(out=ysg[:], in_=ysq[:], func=AF.Sigmoid, scale=GELU_C),
                     reads=[kk("ysq")], writes=[kk("ysg")])
                P.op("dve", lambda e, YD=YD, ysb=ysb, ysg=ysg: e.tensor_tensor(out=YD[:].rearrange("p g n -> p (g n)"), in0=ysb[:], in1=ysg[:], op=ALU.mult),
                     reads=[kk("ysb"), kk("ysg")], writes=[kk("YD")])
                for gl in range(8):
                    P.op("pe", lambda e, gl=gl, T5=T5, YD=YD: e.transpose(T5[0:NCH, gl * 128:(gl + 1) * 128], YD[:, gl, :], identb[:]),
                         reads=[kk("YD"), "identb"], writes=[T5k])
                P.op("dve", lambda e, YY=YY, T5=T5: e.tensor_copy(out=YY[:].rearrange("n j g h -> n g j h"),
                                                    in_=T5[0:NCH, :].rearrange("n (g j h) -> n g j h", g=8, j=8)),
                     reads=[T5k], writes=[kk("YY")])
                for j in range(8):
                    P.op("pe", lambda e, j=j, YY=YY, T5b=T5b: e.transpose(T5b[:, j * NCH:(j + 1) * NCH], YY[:, j, :, :].rearrange("n g h -> n (g h)"), identb[0:NCH, 0:NCH]),
                         reads=[kk("YY"), "identb"], writes=[T5bk])
                P.op("act", lambda e, ct=ct, T5b=T5b: e.activation(
                    out=ygT[:, ct, :].rearrange("p (n j) -> p j n", j=8),
                    in_=T5b[:, 0:8 * NCH].rearrange("p (j n) -> p j n", j=8), func=AF.Copy),
                    reads=[T5bk], writes=[("ygT", ct)])
            if S2_STOP == 4:
                continue
            for cgo in range(4):
                for kq in range(2):
                    view, wk = wload(scr["wglu"], kq * 8, 8, cgo * 512, 512, [])
                    for c4 in range(4):
                        for k8 in range(8):
                            kc = kq * 8 + k8
                            P.op("pe", lambda e, view=view, c4=c4, k8=k8, kc=kc: e.matmul(
                                B[c4][:, 0:TB], lhsT=view[:, k8, c4 * 128:(c4 + 1) * 128], rhs=ygT[:, kc, :],
                                start=(kc == 0), stop=(kc == 15)),
                                reads=[wk, ("ygT", kc)], writes=[("B", c4)])
                for c4 in range(4):
                    ct = cgo * 4 + c4
                    P.op("act", lambda e, c4=c4, ct=ct: e.activation(out=sg[:], in_=B[c4][:, 0:TB], func=AF.Sigmoid,
                                                                    bias=pers["bgluc"][:, ct:ct + 1]),
                         reads=[("B", c4)], writes=["sg"])
                    P.op("dve", lambda e, ct=ct: e.tensor_tensor(out=sg[:], in0=sg[:], in1=ygT[:, ct, :], op=ALU.mult),
                         reads=["sg", ("ygT", ct)], writes=["sg"])
                    P.op("dve", lambda e, c4=c4, ct=ct: e.tensor_tensor(out=oT5[:, c4, :], in0=sg[:], in1=zcb[:, ct, :], op=ALU.mult),
                         reads=["sg", ("zc", ct)], writes=["oT5"])
                P.op("pool", lambda e, cgo=cgo, t0=t0: e.dma_start(out=dram_rows(scr["oT"], 16 + cgo * 4, 4, t0, TB), in_=oT5[:]),
                     reads=["oT5"], writes=[("oTs", blk, cgo)], dma="o2")
        P.emit(ls, "d")


def sweep3(nc, st, io, scr, L):
    P = Prog(nc)
    nblk = L // 512
    with ExitStack() as ls:
        oT = ls.enter_context(nc.sbuf_tensor("oT3", [128, 32, 512], BF16))
        r = [ls.enter_context(nc.sbuf_tensor(f"r3_{i}", [128, 4096], F32)) for i in range(4)]
        lng = ls.enter_context(nc.sbuf_tensor("lng", [128, 4096], F32))
        lnb = ls.enter_context(nc.sbuf_tensor("lnb", [128, 4096], F32))
        stats = [ls.enter_context(nc.sbuf_tensor(f"st3_{i}", [128, 8, 6], F32)) for i in range(4)]
        mv = [ls.enter_context(nc.sbuf_tensor(f"mv3_{i}", [128, 4], F32)) for i in range(4)]
        ring = Ring(nc, ls, "w3_", 4, [128, 8, 512], BF16)
        ps = [ls.enter_context(nc.psum_tensor(f"ps3_{i}", [128, 512], F32)) for i in range(8)]
        P.op("sp", lambda e: e.dma_start(out=lng[:], in_=io["ln_g"][0:1, :].broadcast_to([128, 4096])),
             writes=["lng"], dma="c3_1")
        P.op("sp", lambda e: e.dma_start(out=lnb[:], in_=io["ln_b"][0:1, :].broadcast_to([128, 4096])),
             writes=["lnb"], dma="c3_2")
        for blk in range(nblk):
            t0 = blk * 512
            for q in range(4):
                P.op("sp", lambda e, q=q, t0=t0: e.dma_start(
                    out=oT[:, q * 8:(q + 1) * 8, :], in_=dram_rows(scr["oT"], q * 8, 8, t0, 512)),
                    writes=[("oT", q)], dma="oT_%d" % q)
            for i in range(4):
                P.op("pool", lambda e, i=i, t0=t0: e.dma_start(out=r[i][:], in_=io["x"][t0 + i * 128:t0 + (i + 1) * 128, :]),
                     writes=[("r", i, c) for c in range(8)], dma="x3_%d" % i)
            for cg in range(8):
                for kq in range(4):
                    s, wt, wk = ring.next()
                    P.op("sp", lambda e, wt=wt, kq=kq, cg=cg: e.dma_start(
                        out=wt[:], in_=dram_rows(scr["wout"], kq * 8, 8, cg * 512, 512)),
                        writes=[wk], dma="w3_%d" % s)
                    for i in range(4):
                        b = (cg % 2) * 4 + i
                        for k8 in range(8):
                            kc = kq * 8 + k8
                            P.op("pe", lambda e, b=b, kc=kc, i=i, wt=wt, k8=k8: e.matmul(
                                ps[b][:], lhsT=oT[:, kc, i * 128:(i + 1) * 128], rhs=wt[:, k8, :],
                                start=(kc == 0), stop=(kc == 31)),
                                reads=[wk, ("oT", kq)], writes=[("ps", b)])
                for i in range(4):
                    b = (cg % 2) * 4 + i
                    sl = slice(cg * 512, (cg + 1) * 512)
                    P.op("dve", lambda e, i=i, b=b, sl=sl: e.scalar_tensor_tensor(
                        out=r[i][:, sl], in0=r[i][:, sl], scalar=ALPHA, in1=ps[b][:], op0=ALU.mult, op1=ALU.add),
                        reads=[("ps", b), ("r", i, cg)], writes=[("r", i, cg)])
                    P.op("dve", lambda e, i=i, cg=cg, sl=sl: e.bn_stats(out=stats[i][:, cg, :], in_=r[i][:, sl]),
                         reads=[("r", i, cg)], writes=[("st", i)])
            for i in range(4):
                rk = [("r", i, c) for c in range(8)]
                P.op("dve", lambda e, i=i: e.bn_aggr(out=mv[i][:, 0:2], in_=stats[i][:].rearrange("p a b -> p (a b)")),
                     reads=[("st", i)], writes=[("mv", i)])
                P.op("dve", lambda e, i=i: e.tensor_scalar(out=mv[i][:, 2:3], in0=mv[i][:, 1:2], scalar1=EPS,
                                                           scalar2=None, op0=ALU.add),
                     reads=[("mv", i)], writes=[("mv", i)])
                P.op("act", lambda e, i=i: e.activation(out=mv[i][:, 2:3], in_=mv[i][:, 2:3], func=AF.Sqrt),
                     reads=[("mv", i)], writes=[("mv", i)])
                P.op("dve", lambda e, i=i: e.reciprocal(out=mv[i][:, 2:3], in_=mv[i][:, 2:3]),
                     reads=[("mv", i)], writes=[("mv", i)])
                P.op("dve", lambda e, i=i: e.tensor_scalar(out=mv[i][:, 3:4], in0=mv[i][:, 0:1], scalar1=mv[i][:, 2:3],
                                                           scalar2=-1.0, op0=ALU.mult, op1=ALU.mult),
                     reads=[("mv", i)], writes=[("mv", i)])
                P.op("act", lambda e, i=i: e.activation(out=r[i][:], in_=r[i][:], func=AF.Identity,
                                                        bias=mv[i][:, 3:4], scale=mv[i][:, 2:3]),
                     reads=[("mv", i)] + rk, writes=rk)
                P.op("pool", lambda e, i=i: e.tensor_tensor(out=r[i][:], in0=r[i][:], in1=lng[:], op=ALU.mult),
                     reads=rk + ["lng"], writes=rk)
                P.op("dve", lambda e, i=i: e.tensor_tensor(out=r[i][:], in0=r[i][:], in1=lnb[:], op=ALU.add),
                     reads=rk + ["lnb"], writes=rk)
                P.op("pool", lambda e, i=i, t0=t0: e.dma_start(out=io["y"][t0 + i * 128:t0 + (i + 1) * 128, :], in_=r[i][:]),
                     reads=rk, writes=[("y", blk, i)], dma="y3_%d" % i)
        P.emit(ls, "c")


def build(L, stages=("p0", "s1", "s2", "s3"), dbg=False, Lp=0):
    nc = bass.Bass("TRN2", target_bir_lowering=False)
    io = {}

    def inp(name, shape):
        io[name] = nc.dram_tensor(name, shape, F32, kind="ExternalInput").ap()
    inp("x", [L, D]); inp("c", [32, 128]); inp("w_ada", [D, 3 * D]); inp("b_ada", [96, 128])
    if Lp:
        inp("xp", [Lp, D]); inp("flag", [128, 1])
    inp("w_in", [D, DIN]); inp("w_gate", [16, 1024]); inp("b_gate", [1, 1024]); inp("gnorm", [1, 512])
    inp("lam_re", [64, 128]); inp("lam_im", [64, 128]); inp("log_dt", [64, 2])
    inp("b_re", [128, 64, 16]); inp("b_im", [128, 64, 16]); inp("c_re", [128, 16, 64]); inp("c_im", [128, 16, 64])
    inp("s5_d", [128, 16]); inp("w_glu", [2048, 2048]); inp("b_glu", [16, 128])
    inp("w_out", [D, D]); inp("ln_g", [1, D]); inp("ln_b", [1, D])
    io["y"] = nc.dram_tensor("y", [L, D], F32, kind="ExternalOutput").ap()
    scr = {}
    okind = ("ExternalInput" if ("s1" not in stages and "s2" not in stages and "s2p" not in stages) else "ExternalOutput") if dbg else "Internal"
    scr["oT"] = nc.dram_tensor("oT_scr", [D, L], BF16, kind=okind).ap()
    scr["win"] = nc.dram_tensor("win_scr", [D, DIN], BF16).ap()
    scr["wglu"] = nc.dram_tensor("wglu_scr", [2048, 2048], BF16).ap()
    scr["wout"] = nc.dram_tensor("wout_scr", [D, D], BF16).ap()
    scr["gate"] = nc.dram_tensor("gate_scr", [32, 128], F32).ap()
    scr["hT"] = nc.dram_tensor("hT_scr", [D, L + Lp], BF16).ap()
    scr["mint"] = nc.dram_tensor("mint_scr", [128, 128, 128], BF16).ap()
    scr["zt"] = nc.dram_tensor("zt_scr", [64, 2, 128, 128], BF16).ap()
    scr["ff"] = nc.dram_tensor("ff_scr", [64, 2, 128, 128], BF16).ap()
    with ExitStack() as st:
        pers = {}
        pers["sh"] = st.enter_context(nc.sbuf_tensor("sh_c", [128, 32], F32))
        pers["s1p"] = st.enter_context(nc.sbuf_tensor("s1p_c", [128, 32], F32))
        pers["flag"] = st.enter_context(nc.sbuf_tensor("flagt", [128, 1], F32))
        pers["A8c"] = st.enter_context(nc.sbuf_tensor("A8c", [128, 2, 64], F32))
        pers["A8n"] = st.enter_context(nc.sbuf_tensor("A8n", [128, 2, 64], F32))
        pers["bgluc"] = st.enter_context(nc.sbuf_tensor("bgluc", [128, 16], F32))
        if "p0" in stages:
            phase0(nc, st, io, scr, pers, L, Lp)
        if "s1" in stages:
            sweep1(nc, st, io, scr, pers, L, Lp)
        if "s2" in stages or "s2p" in stages:
            s5_prologue(nc, st, io, scr, pers)
        if "s2" in stages:
            sweep2(nc, st, io, scr, pers, L, Lp)
        if "s3" in stages:
            sweep3(nc, st, io, scr, L)
    return nc


def make_in_map(inputs, b, L, s=0, Lp=0):
    f = lambda a: np.ascontiguousarray(a, dtype=np.float32)
    i = inputs
    m = {
        "x": f(i["x"][b, s * L:(s + 1) * L]), "c": f(i["c"][b].reshape(32, 128)), "w_ada": f(i["w_ada"][0]),
        "b_ada": f(i["b_ada"][0].reshape(96, 128)), "w_in": f(i["w_in"][0]), "w_gate": f(i["w_gla_gate"][0]),
        "b_gate": f(i["b_gla_gate"][0].reshape(1, 1024)), "gnorm": f(i["gla_norm_g"][0].reshape(1, 512)),
        "lam_re": f(i["s5_lambda_re"][0].reshape(64, 128)), "lam_im": f(i["s5_lambda_im"][0].reshape(64, 128)),
        "log_dt": f(i["s5_log_dt"][0].reshape(64, 2)), "b_re": f(i["s5_b_re"][0]), "b_im": f(i["s5_b_im"][0]),
        "c_re": f(i["s5_c_re"][0]), "c_im": f(i["s5_c_im"][0]), "s5_d": f(i["s5_d"][0].reshape(128, 16)),
        "w_glu": f(i["w_glu"][0]), "b_glu": f(i["b_glu"][0].reshape(16, 128)), "w_out": f(i["w_out"][0]),
        "ln_g": f(i["ln_g"][0].reshape(1, D)), "ln_b": f(i["ln_b"][0].reshape(1, D)),
    }
    if Lp:
        m["xp"] = f(i["x"][b, 0:Lp])
        m["flag"] = np.full((128, 1), float(s), dtype=np.float32)
    return m


def kernel(**inputs):
    B, S = inputs["x"].shape[0], inputs["x"].shape[1]
    L = S // 2
    nc = build(L, Lp=L)
    in_maps = [make_in_map(inputs, b, L, s, L) for b in range(B) for s in range(2)]
    res = run_bass_kernel_spmd(nc, in_maps, core_ids=list(range(2 * B)))
    out = np.empty((B, S, D), dtype=np.float32)
    for b in range(B):
        for s in range(2):
            out[b, s * L:(s + 1) * L] = np.asarray(res.results[2 * b + s]["y"], dtype=np.float32)
    return out
```

```python
import numpy as np
from contextlib import ExitStack
import concourse.bass as bass
import concourse.mybir as mybir
from concourse.bass_utils import run_bass_kernel_spmd

F32 = mybir.dt.float32
BF16 = mybir.dt.bfloat16
I32 = mybir.dt.int32
ALU = mybir.AluOpType
AF = mybir.ActivationFunctionType
AX = mybir.AxisListType

D = 4096
DIN = 10256
ALPHA = 2.0 ** 0.25
EPS = 1e-5
C_Q, C_K, C_V, C_G, C_ZG, C_U, C_ZS = 0, 1024, 2048, 4096, 4112, 6160, 8208


class Prog:
    ENGS = ("pe", "act", "dve", "pool", "sp")

    def __init__(self, nc):
        self.nc = nc
        self.ops = []
        self.last_w = {}
        self.readers = {}

    def op(self, eng, fn, reads=(), writes=(), dma=None):
        idx = len(self.ops)
        deps = set()
        for k in reads:
            if k in self.last_w:
                deps.add(self.last_w[k])
        for k in writes:
            if k in self.last_w:
                deps.add(self.last_w[k])
            for r in self.readers.get(k, ()):
                deps.add(r)
        self.ops.append(dict(eng=eng, fn=fn, deps=deps, dma=dma, needed=False))
        for k in reads:
            self.readers.setdefault(k, []).append(idx)
        for k in writes:
            self.last_w[k] = idx
            self.readers[k] = []
        return idx

    def emit(self, stack, tag):
        nc = self.nc
        ops = self.ops
        for i, o in enumerate(ops):
            latest = {}
            for d in o["deps"]:
                od = ops[d]
                if od["dma"] is not None:
                    continue
                if od["eng"] == "pe" and o["eng"] == "pe" and o["dma"] is None:
                    continue
                if od["eng"] != "pe":
                    od["needed"] = True
                    continue
                if od["eng"] not in latest or d > latest[od["eng"]]:
                    latest[od["eng"]] = d
            for d in latest.values():
                ops[d]["needed"] = True
        sems = {e: stack.enter_context(nc.semaphore(tag + "s_" + e)) for e in self.ENGS}
        dma_sems, dma_cnt = {}, {}
        cnt = {e: 0 for e in self.ENGS}
        ev = [None] * len(ops)
        for i, o in enumerate(ops):
            if o["dma"] is not None:
                name = o["dma"]
                if name not in dma_sems:
                    dma_sems[name] = stack.enter_context(nc.semaphore(tag + "d_" + name))
                    dma_cnt[name] = 0
                dma_cnt[name] += 16
                ev[i] = (dma_sems[name], dma_cnt[name], "dma:" + name)
            elif o["needed"]:
                cnt[o["eng"]] += 1
                ev[i] = (sems[o["eng"]], cnt[o["eng"]], o["eng"])
        final_dma = {n: (dma_sems[n], dma_cnt[n]) for n in dma_sems}
        per_eng = {e: [] for e in self.ENGS}
        for i, o in enumerate(ops):
            per_eng[o["eng"]].append(i)
        with nc.Block() as block:
            getters = dict(pe=block.tensor, act=block.scalar, dve=block.vector,
                           pool=block.gpsimd, sp=block.sync)
            for e in self.ENGS:
                idxs = per_eng[e]

                def body(engine, e=e, idxs=idxs):
                    waited = {}
                    for i in idxs:
                        o = ops[i]
                        need = {}
                        for d in o["deps"]:
                            if ev[d] is None:
                                continue
                            s, v, tg = ev[d]
                            if tg not in need or need[tg][1] < v:
                                need[tg] = (s, v)
                        for tg, (s, v) in need.items():
                            if waited.get(tg, 0) >= v:
                                continue
                            engine.wait_ge(s, v)
                            waited[tg] = v
                        ins = o["fn"](engine)
                        if ev[i] is not None:
                            ins.then_inc(ev[i][0], 16 if o["dma"] is not None else 1)
                    if e == "sp":
                        for n, (s, v) in final_dma.items():
                            engine.wait_ge(s, v)
                        for e2 in ("pe", "act", "dve", "pool"):
                            if cnt[e2] > 0:
                                engine.wait_ge(sems[e2], cnt[e2])
                getters[e](body)


class Ring:
    def __init__(self, nc, st, name, n, shape, dtype):
        self.t = [st.enter_context(nc.sbuf_tensor(f"{name}{i}", shape, dtype)) for i in range(n)]
        self.n = n
        self.i = 0
        self.name = name

    def next(self):
        s = self.i % self.n
        self.i += 1
        return s, self.t[s], (self.name, s)


def dram_rows(ap, r0, nkc, c0, nc_):
    return ap[r0 * 128:(r0 + nkc) * 128, c0:c0 + nc_].rearrange("(kc p) c -> p kc c", p=128)


def make_iota_mask(P, nc, st, name, shape, pattern, base, cm, op, key):
    ti = st.enter_context(nc.sbuf_tensor(name + "_i", shape, I32))
    tf = st.enter_context(nc.sbuf_tensor(name, shape, F32))
    P.op("pool", lambda e: e.iota(ti[:], pattern=pattern, base=base, channel_multiplier=cm),
         writes=[key + "_i"])
    P.op("dve", lambda e: e.tensor_scalar(out=tf[:], in0=ti[:], scalar1=0.0, scalar2=None, op0=op),
         reads=[key + "_i"], writes=[key])
    return tf


def phase0(nc, st, io, scr, pers, L, Lp=0):
    P = Prog(nc)
    with ExitStack() as ls:
        ident = make_iota_mask(P, nc, ls, "ident0", [128, 128], [[1, 128]], 0, -1, ALU.is_equal, "ident")
        c32 = ls.enter_context(nc.sbuf_tensor("c32", [32, 128], F32))
        ba96 = ls.enter_context(nc.sbuf_tensor("ba96", [96, 128], F32))
        scol = ls.enter_context(nc.sbuf_tensor("scol", [128, 32, 2], F32))
        bac = ls.enter_context(nc.sbuf_tensor("bac", [128, 96], F32))
        modc = ls.enter_context(nc.sbuf_tensor("modc", [128, 96], F32))
        g32 = ls.enter_context(nc.sbuf_tensor("g32", [32, 128], F32))
        wa = [ls.enter_context(nc.sbuf_tensor(f"wa{i}", [128, 12288], F32)) for i in range(2)]
        grow = ls.enter_context(nc.sbuf_tensor("grow", [128, 4096], F32))
        wf = [ls.enter_context(nc.sbuf_tensor(f"wf{i}", [128, 4096], F32)) for i in range(2)]
        wb = [ls.enter_context(nc.sbuf_tensor(f"wb{i}", [128, 4096], BF16)) for i in range(2)]
        pst = ls.enter_context(nc.psum_tensor("p0t", [128, 512], F32))[:, 0:128]
        psm = ls.enter_context(nc.psum_tensor("p0m", [128, 256, 2], F32))[:, 0:96, :]

        for r in range(32):
            P.op("pool", lambda e, r=r: e.dma_start(out=scr["win"][r * 128:(r + 1) * 128, :],
                                                  in_=io["w_in"][r * 128:(r + 1) * 128, :],
                                                  max_dma_last_dim=4096),
                 writes=[("win", r)], dma="cast")
        for r in range(16):
            P.op("pool", lambda e, r=r: e.dma_start(out=scr["wglu"][r * 128:(r + 1) * 128, :],
                                                  in_=io["w_glu"][r * 128:(r + 1) * 128, :],
                                                  max_dma_last_dim=4096),
                 writes=[("wglu", r)], dma="cast")

        if Lp:
            P.op("sp", lambda e: e.dma_start(out=pers["flag"][:], in_=io["flag"][:, :]), writes=["flag"], dma="ld0_1")
        P.op("sp", lambda e: e.dma_start(out=c32[:], in_=io["c"][:, :]), writes=["c32"], dma="ld0_2")
        P.op("sp", lambda e: e.dma_start(out=ba96[:], in_=io["b_ada"][:, :]), writes=["ba96"], dma="ld0_3")
        P.op("pe", lambda e: e.transpose(pst[:, 0:32], c32[:], ident[0:32, 0:32]),
             reads=["c32", "ident"], writes=["pst"])
        for j in range(2):
            P.op("act", lambda e, j=j: e.activation(out=scol[:, :, j], in_=pst[:, 0:32], func=AF.Silu),
                 reads=["pst"], writes=["scol"])
        P.op("pe", lambda e: e.transpose(pst[:, 0:96], ba96[:], ident[0:96, 0:96]),
             reads=["ba96", "ident", "scol"], writes=["pst"])
        P.op("dve", lambda e: e.tensor_copy(out=bac[:], in_=pst[:, 0:96]), reads=["pst"], writes=["bac"])
        for kc in range(32):
            P.op("sp", lambda e, kc=kc: e.dma_start(out=wa[kc % 2][:], in_=io["w_ada"][kc * 128:(kc + 1) * 128, :]),
                 writes=[("wa", kc % 2)], dma="wa%d" % (kc % 2))
            for ct in range(96):
                P.op("pe", lambda e, kc=kc, ct=ct: e.matmul(
                    psm[:, ct, :], lhsT=wa[kc % 2][:, ct * 128:(ct + 1) * 128], rhs=scol[:, kc, :],
                    start=(kc == 0 and ct == 0), stop=(kc == 31 and ct == 95), skip_group_check=True),
                    reads=[("wa", kc % 2), "scol"], writes=["psm"])
        P.op("dve", lambda e: e.tensor_tensor(out=modc[:], in0=psm[:, :, 0], in1=bac[:], op=ALU.add),
             reads=["psm", "bac"], writes=["modc"])
        P.op("dve", lambda e: e.tensor_copy(out=pers["sh"][:], in_=modc[:, 0:32]), reads=["modc"], writes=["sh"])
        P.op("dve", lambda e: e.tensor_scalar(out=pers["s1p"][:], in0=modc[:, 32:64], scalar1=1.0, scalar2=None,
                                              op0=ALU.add), reads=["modc"], writes=["s1p"])
        P.op("pe", lambda e: e.transpose(pst[0:32, :], modc[:, 64:96], ident[:]),
             reads=["modc", "ident", "bac"], writes=["pst"])
        P.op("dve", lambda e: e.tensor_copy(out=g32[:], in_=pst[0:32, :]), reads=["pst"], writes=["g32"])
        P.op("sp", lambda e: e.dma_start(out=scr["gate"][:, :], in_=g32[:]), reads=["g32"], writes=["gscr"], dma="ld0_4")
        P.op("sp", lambda e: e.dma_start(
            out=grow[:], in_=scr["gate"].rearrange("a b -> (a b)")[None, :].broadcast_to([128, 4096])),
            reads=["gscr"], writes=["grow"], dma="ld0_5")
        for r in range(32):
            P.op("sp", lambda e, r=r: e.dma_start(out=wf[r % 2][:], in_=io["w_out"][r * 128:(r + 1) * 128, :]),
                 writes=[("wf", r % 2)], dma="wf%d" % (r % 2))
            P.op("dve", lambda e, r=r: e.tensor_tensor(out=wb[r % 2][:], in0=wf[r % 2][:], in1=grow[:], op=ALU.mult),
                 reads=[("wf", r % 2), "grow"], writes=[("wb", r % 2)])
            P.op("sp", lambda e, r=r: e.dma_start(out=scr["wout"][r * 128:(r + 1) * 128, :], in_=wb[r % 2][:]),
                 reads=[("wb", r % 2)], writes=[("wout", r)], dma="wst%d" % (r % 2))
        P.emit(ls, "a")


def load_hT(P, nc, xsrc, pers, xs, hT, psb, ident, t0, tagq):
    for i in range(4):
        xb = xs[i % len(xs)]
        xk = ("xs", i % len(xs))
        P.op("pool", lambda e, xb=xb, i=i: e.dma_start(out=xb[:], in_=xsrc[t0 + i * 128:t0 + (i + 1) * 128, :]),
             writes=[xk], dma="%s_%d" % (tagq, i % len(xs)))
        for kg in range(8):
            b = kg % len(psb)
            for k4 in range(4):
                kc = kg * 4 + k4
                P.op("pe", lambda e, xb=xb, b=b, k4=k4, kc=kc: e.transpose(
                    psb[b][:, k4 * 128:(k4 + 1) * 128], xb[:, kc * 128:(kc + 1) * 128], ident[:]),
                    reads=[xk, "identf"], writes=[("B", b)])
            for k4 in range(4):
                kc = kg * 4 + k4
                if kg % 2 == 0:
                    P.op("act", lambda e, b=b, k4=k4, kc=kc, i=i: e.activation(
                        out=hT[:, kc, i * 128:(i + 1) * 128], in_=psb[b][:, k4 * 128:(k4 + 1) * 128],
                        func=AF.Identity, bias=pers["sh"][:, kc:kc + 1], scale=pers["s1p"][:, kc:kc + 1]),
                        reads=[("B", b)], writes=[("hT", kc // 8)])
                else:
                    P.op("dve", lambda e, b=b, k4=k4, kc=kc, i=i: e.tensor_scalar(
                        out=hT[:, kc, i * 128:(i + 1) * 128], in0=psb[b][:, k4 * 128:(k4 + 1) * 128],
                        scalar1=pers["s1p"][:, kc:kc + 1], scalar2=pers["sh"][:, kc:kc + 1],
                        op0=ALU.mult, op1=ALU.add),
                        reads=[("B", b)], writes=[("hT", kc // 8)])


def sweep1(nc, st, io, scr, pers, L, Lp=0):
    P = Prog(nc)
    nblk = L // 512
    with ExitStack() as ls:
        sb = lambda name, shape, dt: ls.enter_context(nc.sbuf_tensor(name, shape, dt))
        identf = make_iota_mask(P, nc, ls, "identf1", [128, 128], [[1, 128]], 0, -1, ALU.is_equal, "identf")
        identb = sb("identb1", [128, 128], BF16)
        P.op("dve", lambda e: e.tensor_copy(out=identb[:], in_=identf[:]), reads=["identf"], writes=["identb"])
        m_i = sb("m64i", [128, 64], I32)
        mask64 = sb("mask64", [128, 64], F32)
        for hf in range(2):
            P.op("pool", lambda e, hf=hf: e.iota(m_i[hf * 64:(hf + 1) * 64, :], pattern=[[1, 64]], base=0,
                                               channel_multiplier=-1), writes=["m64i"])
        P.op("dve", lambda e: e.tensor_scalar(out=mask64[:], in0=m_i[:], scalar1=0.0, scalar2=None, op0=ALU.is_ge),
             reads=["m64i"], writes=["mask64"])
        tri = sb("tri", [128, 128], F32)
        P.op("dve", lambda e: e.memset(tri[:], 0.0), writes=["tri"])
        for hf in range(2):
            P.op("dve", lambda e, hf=hf: e.tensor_copy(out=tri[hf * 64:(hf + 1) * 64, hf * 64:(hf + 1) * 64],
                                                     in_=mask64[hf * 64:(hf + 1) * 64, :]),
                 reads=["mask64", "tri"], writes=["tri"])
        wgate = sb("wgate", [16, 1024], F32)
        bgate = sb("bgate", [1, 1024], F32)
        ones = sb("ones1", [1, 128], F32)
        gnb = sb("gnb", [128, 512], F32)
        P.op("sp", lambda e: e.dma_start(out=wgate[:], in_=io["w_gate"][:, :]), writes=["wgate"], dma="c1_1")
        P.op("sp", lambda e: e.dma_start(out=bgate[:], in_=io["b_gate"][:, :]), writes=["bgate"], dma="c1_2")
        P.op("sp", lambda e: e.dma_start(out=gnb[:], in_=io["gnorm"][0:1, :].broadcast_to([128, 512])),
             writes=["gnb"], dma="c1_3")
        P.op("dve", lambda e: e.memset(ones[:], 1.0), writes=["ones"])
        T = sb("Tst", [128, 8, 512], F32)
        Sbf = sb("Sbf", [128, 8, 512], BF16)
        eblp = sb("eblp", [128, 8], F32)
        P.op("dve", lambda e: e.memset(T[:], 0.0), writes=[("T", m) for m in range(8)])
        P.op("pool", lambda e: e.memset(Sbf[:], 0.0), writes=[("Sbf", m) for m in range(8)])
        P.op("dve", lambda e: e.memset(eblp[:], 1.0), writes=[("eblp", m) for m in range(8)])
        xs = [sb(f"xs1_{i}", [128, 4096], F32) for i in range(2)]
        hT = sb("hT1", [128, 32, 512], BF16)
        ring = Ring(nc, ls, "w1_", 3, [128, 4096], BF16)
        wg16 = sb("wg16", [128, 32, 16], BF16)
        glrT = sb("glrT", [16, 512], F32)
        nls = sb("nls", [128, 4, 1024], F32)
        ebt = [sb(f"ebt{e}", [128, 512], F32) for e in range(2)]
        eit = [sb(f"eit{e}", [128, 512], F32) for e in range(2)]
        qdec = [sb(f"qdec{e}", [128, 512], BF16) for e in range(2)]
        kinvT = [sb(f"kinvT{e}", [128, 512], BF16) for e in range(2)]
        kinv_tok = sb("kinvtok", [128, 4, 256], BF16)
        v_tok = sb("vtok", [128, 4, 512], BF16)
        gz = sb("gz", [128, 4, 512], F32)
        zs = sb("zs", [128, 512], F32)
        o_tok = sb("otok", [128, 4, 512], BF16)
        oTs = sb("oTs", [128, 4, 512], BF16)
        att_s = sb("atts", [128, 64], BF16)
        junk = sb("junk1", [128, 512], BF16)
        ssq = sb("ssq", [128, 2], F32)
        B = [ls.enter_context(nc.psum_tensor(f"B{i}", [128, 512], F32)) for i in range(5)]
        B5 = ls.enter_context(nc.psum_tensor("B5", [128, 1024], BF16))
        B6 = ls.enter_context(nc.psum_tensor("B6", [128, 512], F32))
        B7 = ls.enter_context(nc.psum_tensor("B7", [128, 512], F32))

        def wload(r0, nkc, c0, ncol):
            s, wt, wk = ring.next()
            view = wt[:].rearrange("p (a b) -> p a b", b=ncol)
            P.op("sp", lambda e: e.dma_start(out=view, in_=dram_rows(scr["win"], r0, nkc, c0, ncol)),
                 reads=[("win", r) for r in range(r0, r0 + nkc)], writes=[wk], dma="w1_%d" % s)
            return view, wk

        P.op("sp", lambda e: e.dma_start(out=wg16[:], in_=dram_rows(scr["win"], 0, 32, C_G, 16)),
             writes=["wg16"], dma="w1g")
        blocks = [("pre", i) for i in range(Lp // 512)] + [("own", i) for i in range(nblk)]
        for mode, blk in blocks:
            own = mode == "own"
            last_pre = (mode == "pre" and blk == Lp // 512 - 1)
            t0 = blk * 512
            load_hT(P, nc, io["x"] if own else io["xp"], pers, xs, hT, B[0:4], identf, t0, "x1")
            tok0 = t0 + (Lp if own else 0)
            for q in range(4):
                P.op("pool", lambda e, q=q, tok0=tok0: e.dma_start(
                    out=dram_rows(scr["hT"], q * 8, 8, tok0, 512), in_=hT[:, q * 8:(q + 1) * 8, :]),
                    reads=[("hT", q)], writes=[("hTscr", tok0, q)], dma="hs_%d" % q)
            for kc in range(32):
                P.op("pe", lambda e, kc=kc: e.matmul(B[4][0:16, :], lhsT=wg16[:, kc, :], rhs=hT[:, kc, :],
                                                     start=(kc == 0), stop=(kc == 31)),
                     reads=["wg16", ("hT", kc // 8)], writes=["B4"])
            P.op("dve", lambda e: e.tensor_copy(out=glrT[:], in_=B[4][0:16, :]), reads=["B4"], writes=["glrT"])
            for i in range(4):
                for e2 in range(2):
                    sl = slice(e2 * 512, (e2 + 1) * 512)
                    P.op("pe", lambda e, i=i, sl=sl: e.matmul(B[4][:], lhsT=glrT[:, i * 128:(i + 1) * 128],
                                                            rhs=wgate[:, sl], start=True, stop=False),
                         reads=["glrT", "wgate"], writes=["B4"])
                    P.op("pe", lambda e, sl=sl: e.matmul(B[4][:], lhsT=ones[:, :], rhs=bgate[:, sl],
                                                       start=False, stop=True),
                         reads=["ones", "bgate"], writes=["B4"])
                    P.op("act", lambda e, i=i, sl=sl: e.activation(out=nls[:, i, sl], in_=B[4][:], func=AF.Exp, scale=-1.0),
                         reads=["B4"], writes=[("nls", i)])
                    P.op("act", lambda e, i=i, sl=sl: e.activation(out=nls[:, i, sl], in_=nls[:, i, sl], func=AF.Ln, bias=1.0),
                         reads=[("nls", i)], writes=[("nls", i)])
            for h in range(4):
                for e2 in range(2):
                    m = 2 * h + e2
                    for i in range(4):
                        P.op("pe", lambda e, i=i, m=m: e.matmul(B[4][:, i * 128:(i + 1) * 128],
                                                              lhsT=nls[:, i, m * 128:(m + 1) * 128], rhs=tri[:],
                                                              start=True, stop=True),
                             reads=[("nls", i), "tri"], writes=["B4"])
                    P.op("act", lambda e, e2=e2: e.activation(out=ebt[e2][:], in_=B[4][:], func=AF.Exp, scale=-1.0 / 16),
                         reads=["B4"], writes=[("ebt", e2)])
                    P.op("act", lambda e, e2=e2: e.activation(out=eit[e2][:], in_=B[4][:], func=AF.Exp, scale=1.0 / 16),
                         reads=["B4"], writes=[("eit", e2)])
                for which, c0 in (("q", C_Q + 256 * h), ("k", C_K + 256 * h)):
                    if which == "q" and not own:
                        continue
                    pb = (0, 1) if which == "q" else (2, 3)
                    for kh in range(2):
                        view, wk = wload(kh * 16, 16, c0, 256)
                        for e2 in range(2):
                            for k16 in range(16):
                                kc = kh * 16 + k16
                                P.op("pe", lambda e, view=view, e2=e2, k16=k16, kc=kc, pb=pb: e.matmul(
                                    B[pb[e2]][:], lhsT=view[:, k16, e2 * 128:(e2 + 1) * 128], rhs=hT[:, kc, :],
                                    start=(kc == 0), stop=(kc == 31)),
                                    reads=[wk, ("hT", kc // 8)], writes=[("B", pb[e2])])
                    for e2 in range(2):
                        if which == "q":
                            P.op("dve", lambda e, e2=e2, pb=pb: e.scalar_tensor_tensor(
                                out=qdec[e2][:], in0=B[pb[e2]][:], scalar=1.0 / 16, in1=ebt[e2][:],
                                op0=ALU.mult, op1=ALU.mult),
                                reads=[("B", pb[e2]), ("ebt", e2)], writes=[("qdec", e2)])
                        else:
                            P.op("dve", lambda e, e2=e2, pb=pb: e.tensor_tensor(
                                out=kinvT[e2][:], in0=B[pb[e2]][:], in1=eit[e2][:], op=ALU.mult),
                                reads=[("B", pb[e2]), ("eit", e2)], writes=[("kinvT", e2)])
                for e2 in range(2):
                    for i in range(4):
                        P.op("pe", lambda e, e2=e2, i=i: e.transpose(
                            B5[:, (i * 2 + e2) * 128:(i * 2 + e2 + 1) * 128], kinvT[e2][:, i * 128:(i + 1) * 128], identb[:]),
                            reads=[("kinvT", e2), "identb"], writes=["B5"])
                P.op("act", lambda e: e.activation(out=kinv_tok[:].rearrange("p a b -> p (a b)"), in_=B5[:], func=AF.Copy),
                     reads=["B5"], writes=["kinvtok"])
                for which, c0 in (("v", C_V + 512 * h), ("z", C_ZG + 512 * h)):
                    if which == "z" and not own:
                        continue
                    for kq in range(4):
                        view, wk = wload(kq * 8, 8, c0, 512)
                        for i in range(4):
                            for k8 in range(8):
                                kc = kq * 8 + k8
                                P.op("pe", lambda e, view=view, i=i, k8=k8, kc=kc: e.matmul(
                                    B[i][:], lhsT=hT[:, kc, i * 128:(i + 1) * 128], rhs=view[:, k8, :],
                                    start=(kc == 0), stop=(kc == 31)),
                                    reads=[wk, ("hT", kc // 8)], writes=[("B", i)])
                    for i in range(4):
                        if which == "v":
                            P.op("act", lambda e, i=i: e.activation(out=v_tok[:, i, :], in_=B[i][:], func=AF.Copy),
                                 reads=[("B", i)], writes=[("vtok", i)])
                        else:
                            P.op("act", lambda e, i=i: e.activation(out=zs[:], in_=B[i][:], func=AF.Silu),
                                 reads=[("B", i)], writes=["zs"])
                            P.op("dve", lambda e, i=i: e.tensor_tensor(out=gz[:, i, :], in0=zs[:], in1=gnb[:], op=ALU.mult),
                                 reads=["zs", "gnb"], writes=[("gz", i)])
                for c in range(8):
                    par, i = c % 2, c // 2
                    rows = slice(64 * par, 64 * par + 64)
                    cols = slice(64 * c, 64 * c + 64)
                    for e2 in range(2 if own else 0):
                        P.op("pe", lambda e, e2=e2, rows=rows, cols=cols: e.matmul(
                            B7[rows, 0:64], lhsT=kinvT[e2][:, cols], rhs=qdec[e2][:, cols],
                            start=(e2 == 0), stop=(e2 == 1)),
                            reads=[("kinvT", e2), ("qdec", e2)], writes=["B7"])
                    if not own:
                        for e2 in range(2):
                            m = 2 * h + e2
                            ebl_prev = eblp[:, m:m + 1] if c == 0 else ebt[e2][:, 64 * c - 1:64 * c]
                            ebl_cur = ebt[e2][:, 64 * c + 63:64 * c + 64]
                            P.op("pe", lambda e, rows=rows, i=i, e2=e2: e.matmul(
                                B[2 + e2][:], lhsT=kinv_tok[rows, i, e2 * 128:(e2 + 1) * 128], rhs=v_tok[rows, i, :],
                                start=True, stop=True),
                                reads=["kinvtok", ("vtok", i)], writes=[("B", 2 + e2)])
                            P.op("dve", lambda e, m=m, e2=e2, ebl_prev=ebl_prev: e.scalar_tensor_tensor(
                                out=T[:, m, :], in0=T[:, m, :], scalar=ebl_prev, in1=B[2 + e2][:], op0=ALU.mult, op1=ALU.add),
                                reads=[("T", m), ("B", 2 + e2), ("ebt", e2), ("eblp", m)], writes=[("T", m)])
                            if last_pre and c == 7:
                                P.op("act", lambda e, m=m, ebl_cur=ebl_cur: e.activation(
                                    out=Sbf[:, m, :], in_=T[:, m, :], func=AF.Copy, scale=ebl_cur),
                                    reads=[("T", m), ("ebt", e2)], writes=[("Sbf", m)])
                        continue
                    P.op("dve", lambda e, rows=rows: e.tensor_tensor(out=att_s[rows, :], in0=B7[rows, 0:64],
                                                                   in1=mask64[rows, :], op=ALU.mult),
                         reads=["B7", "mask64"], writes=["atts"])
                    P.op("pe", lambda e, rows=rows, i=i: e.matmul(B6[rows, :], lhsT=att_s[rows, :], rhs=v_tok[rows, i, :],
                                                                start=True, stop=False),
                         reads=["atts", ("vtok", i)], writes=["B6"])
                    for e2 in range(2):
                        m = 2 * h + e2
                        P.op("pe", lambda e, rows=rows, cols=cols, e2=e2, m=m: e.matmul(
                            B6[rows, :], lhsT=qdec[e2][:, cols], rhs=Sbf[:, m, :], start=False, stop=(e2 == 1)),
                            reads=[("qdec", e2), ("Sbf", m)], writes=["B6"])
                    P.op("act", lambda e, rows=rows: e.activation(out=junk[rows, :], in_=B6[rows, :], func=AF.Square,
                                                                accum_out=ssq[rows, 0:1]),
                         reads=["B6"], writes=["ssq", "junk"])
                    P.op("dve", lambda e, rows=rows: e.tensor_scalar(out=ssq[rows, 1:2], in0=ssq[rows, 0:1],
                                                                   scalar1=1.0 / 512, scalar2=EPS, op0=ALU.mult, op1=ALU.add),
                         reads=["ssq"], writes=["ssq"])
                    P.op("act", lambda e, rows=rows: e.activation(out=ssq[rows, 1:2], in_=ssq[rows, 1:2], func=AF.Sqrt),
                         reads=["ssq"], writes=["ssq"])
                    P.op("dve", lambda e, rows=rows: e.reciprocal(out=ssq[rows, 1:2], in_=ssq[rows, 1:2]),
                         reads=["ssq"], writes=["ssq"])
                    P.op("dve", lambda e, rows=rows, i=i: e.scalar_tensor_tensor(
                        out=o_tok[rows, i, :], in0=B6[rows, :], scalar=ssq[rows, 1:2], in1=gz[rows, i, :],
                        op0=ALU.mult, op1=ALU.mult),
                        reads=["B6", "ssq", ("gz", i)], writes=[("otok", i)])
                    for e2 in range(2):
                        m = 2 * h + e2
                        ebl_prev = eblp[:, m:m + 1] if c == 0 else ebt[e2][:, 64 * c - 1:64 * c]
                        ebl_cur = ebt[e2][:, 64 * c + 63:64 * c + 64]
                        P.op("pe", lambda e, rows=rows, i=i, e2=e2: e.matmul(
                            B[2 + e2][:], lhsT=kinv_tok[rows, i, e2 * 128:(e2 + 1) * 128], rhs=v_tok[rows, i, :],
                            start=True, stop=True),
                            reads=["kinvtok", ("vtok", i)], writes=[("B", 2 + e2)])
                        P.op("dve", lambda e, m=m, e2=e2, ebl_prev=ebl_prev: e.scalar_tensor_tensor(
                            out=T[:, m, :], in0=T[:, m, :], scalar=ebl_prev, in1=B[2 + e2][:], op0=ALU.mult, op1=ALU.add),
                            reads=[("T", m), ("B", 2 + e2), ("ebt", e2), ("eblp", m)], writes=[("T", m)])
                        P.op("act", lambda e, m=m, ebl_cur=ebl_cur: e.activation(
                            out=Sbf[:, m, :], in_=T[:, m, :], func=AF.Copy, scale=ebl_cur),
                            reads=[("T", m), ("ebt", e2)], writes=[("Sbf", m)])
                for e2 in range(2):
                    m = 2 * h + e2
                    P.op("dve", lambda e, m=m, e2=e2: e.tensor_copy(out=eblp[:, m:m + 1], in_=ebt[e2][:, 511:512]),
                         reads=[("ebt", e2)], writes=[("eblp", m)])
                for half in range(2 if own else 0):
                    for cc2 in range(2):
                        cc = half * 2 + cc2
                        for i in range(4):
                            P.op("pe", lambda e, cc=cc, cc2=cc2, i=i: e.transpose(
                                B5[:, (cc2 * 4 + i) * 128:(cc2 * 4 + i + 1) * 128], o_tok[:, i, cc * 128:(cc + 1) * 128], identb[:]),
                                reads=[("otok", i), "identb"], writes=["B5"])
                    P.op("dve", lambda e, half=half: e.tensor_copy(
                        out=oTs[:, half * 2:half * 2 + 2, :].rearrange("p a b -> p (a b)"), in_=B5[:]),
                        reads=["B5"], writes=["oTs"])
                if own:
                    P.op("pool", lambda e, h=h, t0=t0: e.dma_start(out=dram_rows(scr["oT"], h * 4, 4, t0, 512), in_=oTs[:]),
                         reads=["oTs"], writes=[("oTscr", blk, h)], dma="o1")
            if last_pre:
                allT = [("T", m) for m in range(8)]
                allS = [("Sbf", m) for m in range(8)]
                P.op("dve", lambda e: e.tensor_scalar(out=T[:].rearrange("p a b -> p (a b)"), in0=T[:].rearrange("p a b -> p (a b)"),
                                                      scalar1=pers["flag"][:, 0:1], scalar2=None, op0=ALU.mult),
                     reads=allT + ["flag"], writes=allT)
                P.op("dve", lambda e: e.tensor_scalar(out=Sbf[:].rearrange("p a b -> p (a b)"), in0=Sbf[:].rearrange("p a b -> p (a b)"),
                                                      scalar1=pers["flag"][:, 0:1], scalar2=None, op0=ALU.mult),
                     reads=allS + ["flag"], writes=allS)
        P.emit(ls, "b")


TWO_PI = 6.283185307179586
PI = 3.141592653589793


def s5_prologue(nc, st, io, scr, pers):
    P = Prog(nc)
    with ExitStack() as ls:
        sb = lambda name, shape, dt=F32: ls.enter_context(nc.sbuf_tensor(name, shape, dt))
        cnt = [0]

        def dve(fn, r, w):
            P.op("dve", fn, reads=r, writes=w)

        def TT(out, a, b, op, r, w):
            dve(lambda e: e.tensor_tensor(out=out, in0=a, in1=b, op=op), r, w)

        def TS(out, a, s1, s2, op0, op1, r, w):
            if op1 is None:
                dve(lambda e: e.tensor_scalar(out=out, in0=a, scalar1=s1, scalar2=None, op0=op0), r, w)
            else:
                dve(lambda e: e.tensor_scalar(out=out, in0=a, scalar1=s1, scalar2=s2, op0=op0, op1=op1), r, w)

        def ACT(out, a, func, r, w, **kw):
            P.op("act", lambda e: e.activation(out=out, in_=a, func=func, **kw), reads=r, writes=w)

        identf = make_iota_mask(P, nc, ls, "identfp", [128, 128], [[1, 128]], 0, -1, ALU.is_equal, "identf")
        maskc = make_iota_mask(P, nc, ls, "maskc", [128, 8, 16], [[16, 8], [0, 16]], 15, -1, ALU.is_ge, "maskc")
        rep = make_iota_mask(P, nc, ls, "rep", [16, 8, 16], [[0, 8], [1, 16]], 0, -1, ALU.is_equal, "rep")
        pt = ls.enter_context(nc.psum_tensor("pqt", [128, 512], F32))[:, 0:128]
        pm = [ls.enter_context(nc.psum_tensor(f"pqm{i}", [128, 512], F32))[:, 0:128] for i in range(2)]

        def transp(out_sb, in_ap, npart, nfree, rk, wk, odt_copy="dve"):
            P.op("pe", lambda e: e.transpose(pt[0:nfree, 0:npart], in_ap, identf[0:npart, 0:npart]),
                 reads=rk + ["identf"], writes=["pt"])
            dve(lambda e: e.tensor_copy(out=out_sb, in_=pt[0:nfree, 0:npart]), ["pt"], wk)

        lt = [sb(f"lt{i}", [64, 128]) for i in range(2)]
        ldt = sb("ldt", [64, 2]); dtx = sb("dtx", [64, 2, 64])
        P.op("sp", lambda e: e.dma_start(out=lt[0][:], in_=io["lam_re"][:, :]), writes=["lt0"], dma="q0_1")
        P.op("sp", lambda e: e.dma_start(out=lt[1][:], in_=io["lam_im"][:, :]), writes=["lt1"], dma="q0_2")
        P.op("sp", lambda e: e.dma_start(out=ldt[:], in_=io["log_dt"][:, :]), writes=["ldt"], dma="q0_3")
        ACT(ldt[:], ldt[:], AF.Exp, ["ldt"], ["ldt"])
        dve(lambda e: e.tensor_copy(out=dtx[:], in_=ldt[:, :, None].to_broadcast([64, 2, 64])), ["ldt"], ["dtx"])
        zt = [sb(f"ztt{i}", [64, 128]) for i in range(2)]
        for i in range(2):
            TT(zt[i][:], lt[i][:], dtx[:].rearrange("p a b -> p (a b)"), ALU.mult, [f"lt{i}", "dtx"], [f"ztt{i}"])
        lam = [sb(f"lam{i}", [128, 64]) for i in range(2)]
        z = [sb(f"z{i}", [128, 64]) for i in range(2)]
        for i in range(2):
            transp(lam[i][:], lt[i][:], 64, 128, [f"lt{i}"], [f"lam{i}"])
            transp(z[i][:], zt[i][:], 64, 128, [f"ztt{i}"], [f"z{i}"])
        ki = sb("ki", [128, 64], I32); kf = sb("kf", [128, 64]); rr = sb("rr", [128, 64]); mm_ = sb("mm_", [128, 64])
        xs_ = sb("xsft", [128, 64])

        def sin_of(out, x_ap, xk, ok):
            TS(ki[:], x_ap, 1.0 / TWO_PI, None, ALU.mult, None, [xk], ["ki"])
            dve(lambda e: e.tensor_copy(out=kf[:], in_=ki[:]), ["ki"], ["kf"])
            dve(lambda e: e.scalar_tensor_tensor(out=rr[:], in0=kf[:], scalar=-TWO_PI, in1=x_ap, op0=ALU.mult, op1=ALU.add),
                ["kf", xk], ["rr"])
            TS(mm_[:], rr[:], PI, -TWO_PI, ALU.is_gt, ALU.mult, ["rr"], ["mm_"])
            TT(rr[:], rr[:], mm_[:], ALU.add, ["rr", "mm_"], ["rr"])
            TS(mm_[:], rr[:], -PI, TWO_PI, ALU.is_lt, ALU.mult, ["rr"], ["mm_"])
            TT(rr[:], rr[:], mm_[:], ALU.add, ["rr", "mm_"], ["rr"])
            ACT(out, rr[:], AF.Sin, ["rr"], [ok])

        sn = sb("sn", [128, 64]); cs = sb("cs", [128, 64]); mag = sb("mag", [128, 64]); imag = sb("imag", [128, 64])
        sin_of(sn[:], z[1][:], "z1", "sn")
        TS(xs_[:], z[1][:], PI / 2, None, ALU.add, None, ["z1"], ["xsft"])
        sin_of(cs[:], xs_[:], "xsft", "cs")
        ACT(mag[:], z[0][:], AF.Exp, ["z0"], ["mag"])
        ACT(imag[:], z[0][:], AF.Exp, ["z0"], ["imag"], scale=-1.0)
        APr = sb("APr", [128, 9, 64]); APi = sb("APi", [128, 9, 64]); AMr = sb("AMr", [128, 8, 64]); AMi = sb("AMi", [128, 8, 64])
        t1 = sb("t1", [128, 64]); t2 = sb("t2", [128, 64])
        for Tn, nm in ((APr, "APr"), (AMr, "AMr")):
            dve(lambda e, Tn=Tn: e.memset(Tn[:, 0, :], 1.0), [], [nm])
        for Tn, nm in ((APi, "APi"), (AMi, "AMi")):
            dve(lambda e, Tn=Tn: e.memset(Tn[:, 0, :], 0.0), [], [nm])
        TT(APr[:, 1, :], mag[:], cs[:], ALU.mult, ["mag", "cs"], ["APr"])
        TT(APi[:, 1, :], mag[:], sn[:], ALU.mult, ["mag", "sn"], ["APi"])
        TT(AMr[:, 1, :], imag[:], cs[:], ALU.mult, ["imag", "cs"], ["AMr"])
        dve(lambda e: e.scalar_tensor_tensor(out=AMi[:, 1, :], in0=imag[:], scalar=-1.0, in1=sn[:], op0=ALU.mult, op1=ALU.mult),
            ["imag", "sn"], ["AMi"])

        def cmul(o_re, o_im, a_re, a_im, b_re, b_im, tmpa, tmpb, r, w):
            TT(tmpa, a_re, b_re, ALU.mult, r, ["cm_a"])
            TT(tmpb, a_im, b_im, ALU.mult, r, ["cm_b"])
            TT(o_re, tmpa, tmpb, ALU.subtract, ["cm_a", "cm_b"], w)
            TT(tmpa, a_re, b_im, ALU.mult, r + w, ["cm_a"])
            TT(tmpb, a_im, b_re, ALU.mult, r + w, ["cm_b"])
            TT(o_im, tmpa, tmpb, ALU.add, ["cm_a", "cm_b"], w)

        for k in range(2, 9):
            cmul(APr[:, k, :], APi[:, k, :], APr[:, k - 1, :], APi[:, k - 1, :], APr[:, 1, :], APi[:, 1, :],
                 t1[:], t2[:], ["APr", "APi"], ["APr", "APi"])
        for k in range(2, 8):
            cmul(AMr[:, k, :], AMi[:, k, :], AMr[:, k - 1, :], AMi[:, k - 1, :], AMr[:, 1, :], AMi[:, 1, :],
                 t1[:], t2[:], ["AMr", "AMi"], ["AMr", "AMi"])
        for c in range(2):
            dve(lambda e, c=c: e.tensor_copy(out=pers["A8c"][:, c, :], in_=APr[:, 8, :]), ["APr"], ["A8c"])
        dve(lambda e: e.tensor_copy(out=pers["A8n"][:, 1, :], in_=APi[:, 8, :]), ["APi"], ["A8n"])
        TS(pers["A8n"][:, 0, :], APi[:, 8, :], -1.0, None, ALU.mult, None, ["APi"], ["A8n"])
        den = sb("den", [128, 64]); nre = sb("nre", [128, 64]); fre = sb("fre", [128, 64]); fim = sb("fim", [128, 64])
        TT(t1[:], lam[0][:], lam[0][:], ALU.mult, ["lam0"], ["t1"])
        TT(t2[:], lam[1][:], lam[1][:], ALU.mult, ["lam1"], ["t2"])
        TT(den[:], t1[:], t2[:], ALU.add, ["t1", "t2"], ["den"])
        dve(lambda e: e.reciprocal(out=den[:], in_=den[:]), ["den"], ["den"])
        TS(nre[:], APr[:, 1, :], -1.0, None, ALU.add, None, ["APr"], ["nre"])
        TT(t1[:], nre[:], lam[0][:], ALU.mult, ["nre", "lam0"], ["t1"])
        TT(t2[:], APi[:, 1, :], lam[1][:], ALU.mult, ["APi", "lam1"], ["t2"])
        TT(fre[:], t1[:], t2[:], ALU.add, ["t1", "t2"], ["fre"])
        TT(fre[:], fre[:], den[:], ALU.mult, ["fre", "den"], ["fre"])
        TT(t1[:], APi[:, 1, :], lam[0][:], ALU.mult, ["APi", "lam0"], ["t1"])
        TT(t2[:], nre[:], lam[1][:], ALU.mult, ["nre", "lam1"], ["t2"])
        TT(fim[:], t1[:], t2[:], ALU.subtract, ["t1", "t2"], ["fim"])
        TT(fim[:], fim[:], den[:], ALU.mult, ["fim", "den"], ["fim"])
        Bt = [sb(f"Bt{i}", [128, 64, 16]) for i in range(2)]
        bb = [sb(f"bb{i}", [128, 64, 16]) for i in range(2)]
        Ct = [sb(f"Ct{i}", [128, 64, 16]) for i in range(2)]
        u1 = sb("u1", [128, 64, 16]); u2 = sb("u2", [128, 64, 16])
        for i, nm in enumerate(("b_re", "b_im")):
            for g2 in range(2):
                P.op("sp", lambda e, i=i, nm=nm, g2=g2: e.dma_start(
                    out=Bt[i][g2 * 64:(g2 + 1) * 64, :, :],
                    in_=io[nm].rearrange("(t g2) p h -> g2 p t h", g2=2)[g2]), writes=[f"Bt{i}"], dma="q1_%d_%d" % (i, g2))
        bc = lambda ap: ap[:, :, None].to_broadcast([128, 64, 16])
        cmul(bb[0][:], bb[1][:], bc(fre[:]), bc(fim[:]), Bt[0][:], Bt[1][:], u1[:], u2[:],
             ["fre", "fim", "Bt0", "Bt1"], ["bb0", "bb1"])
        cin = sb("cin", [128, 128])
        for i, nm in enumerate(("c_re", "c_im")):
            for tb in range(8):
                for tl in range(8):
                    tt_ = tb * 8 + tl
                    P.op("sp", lambda e, nm=nm, tl=tl, tt_=tt_: e.dma_start(
                        out=cin[tl * 16:(tl + 1) * 16, :].rearrange("h (g p) -> h g p", g=2),
                        in_=io[nm][2 * tt_:2 * tt_ + 2].rearrange("g h p -> h g p")), writes=["cin"], dma="q2")
                transp(Ct[i][:, tb * 8:(tb + 1) * 8, :].rearrange("p a b -> p (a b)"), cin[:], 128, 128, ["cin"], [f"Ct{i}"])
        d16 = sb("d16", [128, 16]); dgh = sb("dgh", [16, 128]); dcol = sb("dcol", [128, 128])
        P.op("sp", lambda e: e.dma_start(out=d16[:], in_=io["s5_d"][:, :]), writes=["d16"], dma="q0_4")
        transp(dgh[:], d16[:], 128, 16, ["d16"], ["dgh"])
        P.op("pe", lambda e: e.matmul(pt[:, :], lhsT=rep[:].rearrange("p a b -> p (a b)"), rhs=dgh[:], start=True, stop=True),
             reads=["rep", "dgh"], writes=["pt"])
        dve(lambda e: e.tensor_copy(out=dcol[:], in_=pt[:, :]), ["pt"], ["dcol"])
        bg16 = sb("bg16", [16, 128])
        P.op("sp", lambda e: e.dma_start(out=bg16[:], in_=io["b_glu"][:, :]), writes=["bg16"], dma="q0_5")
        transp(pers["bgluc"][:], bg16[:], 16, 128, ["bg16"], ["bgluc"])
        NB = 8
        X = [sb(f"X{i}", [128, NB, 8, 16]) for i in range(2)]
        Y = [sb(f"Y{i}", [128, NB, 8, 16]) for i in range(2)]
        Zm = [sb(f"Zm{i}", [128, NB, 8, 16]) for i in range(2)]
        Fb = sb("Fb", [128, NB, 2, 128], BF16)
        ZTb = sb("ZTb", [128, NB, 2, 128], BF16)
        Mb = sb("Mb", [128, 2 * NB, 128], BF16)
        w1 = sb("w1", [128, NB, 16]); w2 = sb("w2", [128, NB, 16]); mtmp = sb("mtmp", [128, 128])
        for q in range(64 // NB):
            ts = slice(q * NB, (q + 1) * NB)
            bcp = lambda ap: ap[:, :, None].to_broadcast([128, NB, 16])
            for j in range(8):
                cmul(X[0][:, :, j, :], X[1][:, :, j, :], bcp(AMr[:, j, ts]), bcp(AMi[:, j, ts]), bb[0][:, ts, :], bb[1][:, ts, :],
                     w1[:], w2[:], ["AMr", "AMi", "bb0", "bb1"], ["X0", "X1"])
                cmul(Y[0][:, :, j, :], Y[1][:, :, j, :], bcp(APr[:, j, ts]), bcp(APi[:, j, ts]), Ct[0][:, ts, :], Ct[1][:, ts, :],
                     w1[:], w2[:], ["APr", "APi", "Ct0", "Ct1"], ["Y0", "Y1"])
                cmul(Zm[0][:, :, j, :], Zm[1][:, :, j, :], bcp(APr[:, 7 - j, ts]), bcp(APi[:, 7 - j, ts]), bb[0][:, ts, :], bb[1][:, ts, :],
                     w1[:], w2[:], ["APr", "APi", "bb0", "bb1"], ["Z0", "Z1"])
                TT(w1[:], bcp(APr[:, j + 1, ts]), Ct[0][:, ts, :], ALU.mult, ["APr", "Ct0"], ["cm_a"])
                TT(w2[:], bcp(APi[:, j + 1, ts]), Ct[1][:, ts, :], ALU.mult, ["APi", "Ct1"], ["cm_b"])
                TT(Fb[:, :, 0, j * 16:(j + 1) * 16], w1[:], w2[:], ALU.subtract, ["cm_a", "cm_b"], ["Fb"])
                TT(w1[:], bcp(APr[:, j + 1, ts]), Ct[1][:, ts, :], ALU.mult, ["APr", "Ct1", "Fb"], ["cm_a"])
                TT(w2[:], bcp(APi[:, j + 1, ts]), Ct[0][:, ts, :], ALU.mult, ["APi", "Ct0", "Fb"], ["cm_b"])
                dve(lambda e, j=j: e.scalar_tensor_tensor(out=Fb[:, :, 1, j * 16:(j + 1) * 16], in0=w1[:], scalar=-1.0, in1=w2[:],
                                                          op0=ALU.mult, op1=ALU.subtract), ["cm_a", "cm_b"], ["Fb"])
            TS(X[1][:], X[1][:], -1.0, None, ALU.mult, None, ["X1"], ["X1"])
            for tl in range(NB):
                t = q * NB + tl
                for c in range(2):
                    P.op("pe", lambda e, c=c, tl=tl: e.transpose(pt[:, :], Zm[c][:, tl, :, :].rearrange("p a b -> p (a b)"), identf[:]),
                         reads=[f"Z{c}", "identf"], writes=["pt"])
                    dve(lambda e, c=c, tl=tl: e.tensor_copy(out=ZTb[:, tl, c, :], in_=pt[:, :]), ["pt"], ["ZTb"])
                for g2 in range(2):
                    rows = slice(g2 * 64, (g2 + 1) * 64)
                    pmb = pm[g2]
                    for c in range(2):
                        P.op("pe", lambda e, c=c, tl=tl, rows=rows, pmb=pmb: e.matmul(
                            pmb[:, :], lhsT=X[c][rows, tl, :, :].rearrange("p a b -> p (a b)"),
                            rhs=Y[c][rows, tl, :, :].rearrange("p a b -> p (a b)"), start=(c == 0), stop=(c == 1)),
                            reads=["X0", "X1", "Y0", "Y1"], writes=[("pm", g2)])
                    g = 2 * t + g2
                    TT(mtmp[:], pmb[:, :], maskc[:].rearrange("p a b -> p (a b)"), ALU.mult, [("pm", g2), "maskc"], ["mtmp"])
                    dve(lambda e, g=g, tl=tl, g2=g2: e.scalar_tensor_tensor(
                        out=Mb[:, tl * 2 + g2, :], in0=identf[:], scalar=dcol[:, g:g + 1], in1=mtmp[:],
                        op0=ALU.mult, op1=ALU.add), ["mtmp", "identf", "dcol"], ["Mb"])
            P.op("sp", lambda e, q=q: e.dma_start(out=scr["mint"][q * 2 * NB:(q + 1) * 2 * NB].rearrange("g a b -> a g b"), in_=Mb[:]),
                 reads=["Mb"], writes=[("mint", q)], dma="q3m")
            P.op("sp", lambda e, q=q: e.dma_start(out=scr["zt"][q * NB:(q + 1) * NB].rearrange("t c a b -> a t c b"), in_=ZTb[:]),
                 reads=["ZTb"], writes=[("zts", q)], dma="q3z")
            P.op("sp", lambda e, q=q: e.dma_start(out=scr["ff"][q * NB:(q + 1) * NB].rearrange("t c a b -> a t c b"), in_=Fb[:]),
                 reads=["Fb"], writes=[("ffs", q)], dma="q3f")
        P.emit(ls, "q")


import os
S2_STOP = int(os.environ.get("S2_STOP", "0"))


def sweep2(nc, st, io, scr, pers, L, Lp=0):
    P = Prog(nc)
    TB = 256
    NCH = TB // 8
    nblk = L // TB
    GELU_C = 1.5957691216057308
    with ExitStack() as ls:
        sb = lambda name, shape, dt=F32: ls.enter_context(nc.sbuf_tensor(name, shape, dt))
        identf = make_iota_mask(P, nc, ls, "identf2", [128, 128], [[1, 128]], 0, -1, ALU.is_equal, "identf")
        identb = sb("identb2", [128, 128], BF16)
        P.op("dve", lambda e: e.tensor_copy(out=identb[:], in_=identf[:]), reads=["identf"], writes=["identb"])
        hTs = [sb(f"hT2_{i}", [128, 32, TB], BF16) for i in range(2)]
        hpar = 0
        ring = Ring(nc, ls, "w2_", 3, [128, 8, 512], BF16)
        uT = sb("uT", [128, 4, TB], BF16)
        UUs = [sb(f"UU{i}", [NCH, 8, 8, 16], BF16) for i in range(2)]
        UD = sb("UD", [128, 128, NCH], BF16)
        ZTc = [sb(f"ZTc{i}", [128, 8, 2, 128], BF16) for i in range(2)]
        Fc = [sb(f"Fc{i}", [128, 8, 2, 128], BF16) for i in range(2)]
        Mc = [sb(f"Mc{i}", [128, 16, 128], BF16) for i in range(2)]
        W = sb("Wst", [128, NCH, 2, 64])
        Sbf = [sb(f"Sbf2_{i}", [128, NCH + 1, 2, 64], BF16) for i in range(2)]
        for i_ in range(2):
            P.op("pool", lambda e, i_=i_: e.memset(Sbf[i_][:], 0.0), writes=["Sbf"])
        carry = sb("carry", [128, 2, 64])
        tA = sb("tA", [128, 2, 64]); tBm = sb("tBm", [128, 2, 64])
        ysbs = [sb(f"ysb{i}", [128, 8 * NCH]) for i in range(2)]
        ysqs = [sb(f"ysq{i}", [128, 8 * NCH]) for i in range(2)]
        ysgs = [sb(f"ysg{i}", [128, 8 * NCH]) for i in range(2)]
        YDs = [sb(f"YD{i}", [128, 8, NCH], BF16) for i in range(2)]
        YYs = [sb(f"YY{i}", [NCH, 8, 8, 16], BF16) for i in range(2)]
        ygT = sb("ygT", [128, 16, TB], BF16)
        sg = sb("sg2", [128, TB])
        zcb = sb("zc2", [128, 16, TB], BF16); zgb = sb("zg2", [128, 16, TB], BF16)
        oT5 = sb("oT5", [128, 4, TB], BF16)
        B = [ls.enter_context(nc.psum_tensor(f"C{i}", [128, 512], F32)) for i in range(4)]
        B5 = ls.enter_context(nc.psum_tensor("C5", [128, 1024], BF16))
        B5b = ls.enter_context(nc.psum_tensor("C5b", [128, 1024], BF16))
        B6 = ls.enter_context(nc.psum_tensor("C6", [128, 512], F32))
        B7 = ls.enter_context(nc.psum_tensor("C7", [128, 512], F32))
        P.op("dve", lambda e: e.memset(carry[:], 0.0), writes=["carry"])

        def wload(src, r0, nkc, c0, ncol, rk):
            s, wt, wk = ring.next()
            view = wt[:, 0:nkc, 0:ncol]
            P.op("sp", lambda e: e.dma_start(out=view, in_=dram_rows(src, r0, nkc, c0, ncol)), reads=rk, writes=[wk], dma="w2_%d" % s)
            return view, wk

        blocks = [("pre", i) for i in range(Lp // TB)] + [("own", i) for i in range(nblk)]
        for mode, blk in blocks:
            own = mode == "own"
            last_pre = (mode == "pre" and blk == Lp // TB - 1)
            xsrc = io["x"] if own else io["xp"]
            t0 = blk * TB
            tok0 = t0 + (Lp if own else 0)
            hT = hTs[hpar % 2]
            hkq = lambda q_, hp_=hpar % 2: ("hT", hp_, q_)
            for q in range(4):
                P.op("sp", lambda e, q=q, tok0=tok0, hT=hT: e.dma_start(
                    out=hT[:, q * 8:(q + 1) * 8, :], in_=dram_rows(scr["hT"], q * 8, 8, tok0, TB)),
                    writes=[hkq(q)], dma="h2_%d_%d" % (hpar % 2, q))
            hpar += 1
            for cg in range(4):
                for kq in range(4):
                    view, wk = wload(scr["win"], kq * 8, 8, C_U + cg * 512, 512, [])
                    for c4 in range(4):
                        for k8 in range(8):
                            kc = kq * 8 + k8
                            P.op("pe", lambda e, view=view, c4=c4, k8=k8, kc=kc, hT=hT: e.matmul(
                                B[c4][:, 0:TB], lhsT=view[:, k8, c4 * 128:(c4 + 1) * 128], rhs=hT[:, kc, :],
                                start=(kc == 0), stop=(kc == 31)),
                                reads=[wk, hkq(kc // 8)], writes=[("B", c4)])
                for c4 in range(4):
                    ct = cg * 4 + c4
                    pp = ct % 2
                    UU = UUs[pp]
                    T5, T5k = (B5, "B5") if pp == 0 else (B6[:].bitcast(BF16), "B67_0")
                    T5b, T5bk = (B5b, "B5b") if pp == 0 else (B7[:].bitcast(BF16), "B67_1")
                    T5 = T5 if pp == 1 else T5[:]
                    T5b = T5b if pp == 1 else T5b[:]
                    P.op("act", lambda e, c4=c4: e.activation(out=uT[:, c4, :], in_=B[c4][:, 0:TB], func=AF.Copy),
                         reads=[("B", c4)], writes=[("uT", c4)])
                    for j in range(8):
                        P.op("pe", lambda e, c4=c4, j=j, T5=T5: e.transpose(
                            T5[0:NCH, j * 128:(j + 1) * 128], uT[:, c4, j::8], identb[:]),
                            reads=[("uT", c4), "identb"], writes=[T5k])
                    P.op("act", lambda e, UU=UU, T5=T5: e.activation(
                        out=UU[:].rearrange("n g j h -> n j g h"),
                        in_=T5[0:NCH, :].rearrange("n (j g h) -> n j g h", j=8, g=8), func=AF.Copy), reads=[T5k], writes=[("UU", pp)])
                    for gl in range(8):
                        P.op("pe", lambda e, gl=gl, UU=UU, T5b=T5b: e.transpose(
                            T5b[:, gl * NCH:(gl + 1) * NCH], UU[:, gl, :, :].rearrange("n j h -> n (j h)"), identb[0:NCH, 0:NCH]),
                            reads=[("UU", pp), "identb"], writes=[T5bk])
                    P.op("act", lambda e, ct=ct, T5b=T5b: e.activation(
                        out=UD[:, ct * 8:(ct + 1) * 8, :].rearrange("p g n -> p (g n)"), in_=T5b[:, 0:8 * NCH], func=AF.Copy),
                        reads=[T5bk], writes=[("UD", ct)])
            if S2_STOP == 1:
                continue
            for q in range(8):
                zc = ZTc[q % 2]
                P.op("sp", lambda e, zc=zc, q=q: e.dma_start(out=zc[:], in_=scr["zt"][q * 8:(q + 1) * 8].rearrange("t c a b -> a t c b")),
                     writes=[("ZTc", q % 2)], dma="m2z%d" % (q % 2))
                for tl in range(8):
                    t = q * 8 + tl
                    for g2 in range(2):
                        for c, Bc in ((0, B6), (1, B7)):
                            P.op("pe", lambda e, zc=zc, tl=tl, g2=g2, c=c, Bc=Bc, t=t: e.matmul(
                                Bc[g2 * 64:(g2 + 1) * 64, tl * NCH:(tl + 1) * NCH], lhsT=zc[:, tl, c, g2 * 64:(g2 + 1) * 64],
                                rhs=UD[:, 2 * t + g2, :], start=True, stop=True),
                                reads=[("ZTc", q % 2), ("UD", (2 * t + g2) // 8)], writes=["B67_%d" % c])
                for c, Bc in ((0, B6), (1, B7)):
                    P.op("act", lambda e, c=c, Bc=Bc, q=q: e.activation(
                        out=W[:, :, c, q * 8:(q + 1) * 8].rearrange("p n t -> p t n"),
                        in_=Bc[:, 0:8 * NCH].rearrange("p (t n) -> p t n", t=8), func=AF.Copy),
                        reads=["B67_%d" % c], writes=["W"])
            if own:
                for cgo in range(4):
                    for kq in range(4):
                        view, wk = wload(scr["win"], kq * 8, 8, C_ZS + cgo * 512, 512, [])
                        for c4 in range(4):
                            for k8 in range(8):
                                kc = kq * 8 + k8
                                P.op("pe", lambda e, view=view, c4=c4, k8=k8, kc=kc, hT=hT: e.matmul(
                                    B[c4][:, 0:TB], lhsT=view[:, k8, c4 * 128:(c4 + 1) * 128], rhs=hT[:, kc, :],
                                    start=(kc == 0), stop=(kc == 31)),
                                    reads=[wk, hkq(kc // 8)], writes=[("B", c4)])
                    for c4 in range(4):
                        ct = cgo * 4 + c4
                        P.op("act", lambda e, c4=c4, ct=ct: e.activation(out=zcb[:, ct, :], in_=B[c4][:, 0:TB], func=AF.Copy),
                             reads=[("B", c4)], writes=[("zc", ct)])
                        P.op("act", lambda e, c4=c4, ct=ct: e.activation(out=zgb[:, ct, :], in_=B[c4][:, 0:TB], func=AF.Sigmoid),
                             reads=[("B", c4)], writes=[("zg", ct)])
                        P.op("pool", lambda e, ct=ct: e.tensor_tensor(out=zcb[:, ct, :], in0=zcb[:, ct, :], in1=zgb[:, ct, :], op=ALU.mult),
                             reads=[("zc", ct), ("zg", ct)], writes=[("zc", ct)])
            if S2_STOP == 2:
                continue
            for g2_ in range(2):
                hp = slice(g2_ * 64, (g2_ + 1) * 64)
                P.op("act", lambda e, g2_=g2_, hp=hp: e.activation(out=Sbf[g2_][hp, 0, :, :], in_=carry[hp], func=AF.Copy),
                     reads=["carry"], writes=["Sbf"])
            for n in range(NCH):
                prev = carry[:] if n == 0 else W[:, n - 1, :, :]
                P.op("dve", lambda e, prev=prev: e.tensor_tensor(out=tA[:], in0=pers["A8c"][:], in1=prev, op=ALU.mult),
                     reads=["W", "carry"], writes=["tA"])
                P.op("dve", lambda e, prev=prev: e.tensor_tensor(out=tBm[:, 0, :], in0=pers["A8n"][:, 0, :], in1=prev[:, 1, :], op=ALU.mult),
                     reads=["W", "carry"], writes=["tB"])
                P.op("dve", lambda e, prev=prev: e.tensor_tensor(out=tBm[:, 1, :], in0=pers["A8n"][:, 1, :], in1=prev[:, 0, :], op=ALU.mult),
                     reads=["W", "carry"], writes=["tB"])
                P.op("dve", lambda e, n=n: e.tensor_tensor(out=W[:, n, :, :], in0=W[:, n, :, :], in1=tA[:], op=ALU.add),
                     reads=["W", "tA"], writes=["W"])
                P.op("dve", lambda e, n=n: e.tensor_tensor(out=W[:, n, :, :], in0=W[:, n, :, :], in1=tBm[:], op=ALU.add),
                     reads=["W", "tB"], writes=["W"])
            P.op("dve", lambda e: e.tensor_copy(out=carry[:], in_=W[:, NCH - 1, :, :]), reads=["W"], writes=["carry"])
            if last_pre:
                P.op("dve", lambda e: e.tensor_scalar(out=carry[:].rearrange("p a b -> p (a b)"), in0=carry[:].rearrange("p a b -> p (a b)"),
                                                      scalar1=pers["flag"][:, 0:1], scalar2=None, op0=ALU.mult),
                     reads=["carry", "flag"], writes=["carry"])
            if not own:
                continue
            for g2_ in range(2):
                hp = slice(g2_ * 64, (g2_ + 1) * 64)
                P.op("act", lambda e, g2_=g2_, hp=hp: e.activation(
                    out=Sbf[g2_][hp, 1:NCH + 1, :, :].rearrange("p n c t -> p (n c t)"),
                    in_=W[hp].rearrange("p n c t -> p (n c t)"), func=AF.Copy),
                    reads=["W"], writes=["Sbf"])
            if S2_STOP == 3:
                continue
            for ct in range(16):
                q = ct // 2
                if ct % 2 == 0:
                    fc, mc = Fc[q % 2], Mc[q % 2]
                    P.op("sp", lambda e, fc=fc, q=q: e.dma_start(out=fc[:], in_=scr["ff"][q * 8:(q + 1) * 8].rearrange("t c a b -> a t c b")),
                         writes=[("Fc", q % 2)], dma="m2f%d" % (q % 2))
                    P.op("sp", lambda e, mc=mc, q=q: e.dma_start(out=mc[:], in_=scr["mint"][q * 16:(q + 1) * 16].rearrange("g a b -> a g b")),
                         writes=[("Mc", q % 2)], dma="m2m%d" % (q % 2))
                yb = B[ct % 2]
                for gl in range(8):
                    g = ct * 8 + gl
                    t, g2 = g // 2, g % 2
                    tl = t - q * 8
                    rows = slice(g2 * 64, (g2 + 1) * 64)
                    osl = yb[:, gl * NCH:(gl + 1) * NCH]
                    P.op("pe", lambda e, osl=osl, mc=mc, g=g, q=q: e.matmul(osl, lhsT=mc[:, g - q * 16, :], rhs=UD[:, g, :],
                                                                       start=True, stop=False),
                         reads=[("Mc", q % 2), ("UD", ct)], writes=[("B", ct % 2)])
                    for c in range(2):
                        P.op("pe", lambda e, osl=osl, fc=fc, tl=tl, c=c, g2=g2, t=t: e.matmul(
                            osl, lhsT=fc[:, tl, c, :], rhs=Sbf[g2][:, 0:NCH, c, t], start=False, stop=(c == 1)),
                            reads=[("Fc", q % 2), "Sbf"], writes=[("B", ct % 2)])
                ysl = yb[:, 0:8 * NCH]
                pp = ct % 2
                ysb, ysq, ysg, YD, YY = ysbs[pp], ysqs[pp], ysgs[pp], YDs[pp], YYs[pp]
                T5, T5k = (B5[:], "B5") if pp == 0 else (B6[:].bitcast(BF16), "B67_0")
                T5b, T5bk = (B5b[:], "B5b") if pp == 0 else (B7[:].bitcast(BF16), "B67_1")
                kk = lambda s: (s, pp)
                P.op("act", lambda e, ysl=ysl, ysb=ysb: e.activation(out=ysb[:], in_=ysl, func=AF.Copy), reads=[("B", ct % 2)], writes=[kk("ysb")])
                P.op("act", lambda e, ysl=ysl, ysq=ysq: e.activation(out=ysq[:], in_=ysl, func=AF.Square), reads=[("B", ct % 2)], writes=[kk("ysq")])
                P.op("dve", lambda e, ysq=ysq: e.tensor_scalar(out=ysq[:], in0=ysq[:], scalar1=0.044715, scalar2=1.0, op0=ALU.mult, op1=ALU.add),
                     reads=[kk("ysq")], writes=[kk("ysq")])
                P.op("dve", lambda e, ysq=ysq, ysb=ysb: e.tensor_tensor(out=ysq[:], in0=ysq[:], in1=ysb[:], op=ALU.mult),
                     reads=[kk("ysq"), kk("ysb")], writes=[kk("ysq")])
                P.op("act", lambda e, ysq=ysq, ysg=ysg: e.activation(out=ysg[:], in_=ysq[:], func=AF.Sigmoid, scale=GELU_C),
                     reads=[kk("ysq")], writes=[kk("ysg")])
                P.op("dve", lambda e, YD=YD, ysb=ysb, ysg=ysg: e.tensor_tensor(out=YD[:].rearrange("p g n -> p (g n)"), in0=ysb[:], in1=ysg[:], op=ALU.mult),
                     reads=[kk("ysb"), kk("ysg")], writes=[kk("YD")])
                for gl in range(8):
                    P.op("pe", lambda e, gl=gl, T5=T5, YD=YD: e.transpose(T5[0:NCH, gl * 128:(gl + 1) * 128], YD[:, gl, :], identb[:]),
                         reads=[kk("YD"), "identb"], writes=[T5k])
                P.op("dve", lambda e, YY=YY, T5=T5: e.tensor_copy(out=YY[:].rearrange("n j g h -> n g j h"),
                                                    in_=T5[0:NCH, :].rearrange("n (g j h) -> n g j h", g=8, j=8)),
                     reads=[T5k], writes=[kk("YY")])
                for j in range(8):
                    P.op("pe", lambda e, j=j, YY=YY, T5b=T5b: e.transpose(T5b[:, j * NCH:(j + 1) * NCH], YY[:, j, :, :].rearrange("n g h -> n (g h)"), identb[0:NCH, 0:NCH]),
                         reads=[kk("YY"), "identb"], writes=[T5bk])
                P.op("act", lambda e, ct=ct, T5b=T5b: e.activation(
                    out=ygT[:, ct, :].rearrange("p (n j) -> p j n", j=8),
                    in_=T5b[:, 0:8 * NCH].rearrange("p (j n) -> p j n", j=8), func=AF.Copy),
                    reads=[T5bk], writes=[("ygT", ct)])
            if S2_STOP == 4:
                continue
            for cgo in range(4):
                for kq in range(2):
                    view, wk = wload(scr["wglu"], kq * 8, 8, cgo * 512, 512, [])
                    for c4 in range(4):
                        for k8 in range(8):
                            kc = kq * 8 + k8
                            P.op("pe", lambda e, view=view, c4=c4, k8=k8, kc=kc: e.matmul(
                                B[c4][:, 0:TB], lhsT=view[:, k8, c4 * 128:(c4 + 1) * 128], rhs=ygT[:, kc, :],
                                start=(kc == 0), stop=(kc == 15)),
                                reads=[wk, ("ygT", kc)], writes=[("B", c4)])
                for c4 in range(4):
                    ct = cgo * 4 + c4
                    P.op("act", lambda e, c4=c4, ct=ct: e.activation(out=sg[:], in_=B[c4][:, 0:TB], func=AF.Sigmoid,
                                                                    bias=pers["bgluc"][:, ct:ct + 1]),
                         reads=[("B", c4)], writes=["sg"])
                    P.op("dve", lambda e, ct=ct: e.tensor_tensor(out=sg[:], in0=sg[:], in1=ygT[:, ct, :], op=ALU.mult),
                         reads=["sg", ("ygT", ct)], writes=["sg"])
                    P.op("dve", lambda e, c4=c4, ct=ct: e.tensor_tensor(out=oT5[:, c4, :], in0=sg[:], in1=zcb[:, ct, :], op=ALU.mult),
                         reads=["sg", ("zc", ct)], writes=["oT5"])
                P.op("pool", lambda e, cgo=cgo, t0=t0: e.dma_start(out=dram_rows(scr["oT"], 16 + cgo * 4, 4, t0, TB), in_=oT5[:]),
                     reads=["oT5"], writes=[("oTs", blk, cgo)], dma="o2")
        P.emit(ls, "d")


def sweep3(nc, st, io, scr, L):
    P = Prog(nc)
    nblk = L // 512
    with ExitStack() as ls:
        oT = ls.enter_context(nc.sbuf_tensor("oT3", [128, 32, 512], BF16))
        r = [ls.enter_context(nc.sbuf_tensor(f"r3_{i}", [128, 4096], F32)) for i in range(4)]
        lng = ls.enter_context(nc.sbuf_tensor("lng", [128, 4096], F32))
        lnb = ls.enter_context(nc.sbuf_tensor("lnb", [128, 4096], F32))
        stats = [ls.enter_context(nc.sbuf_tensor(f"st3_{i}", [128, 8, 6], F32)) for i in range(4)]
        mv = [ls.enter_context(nc.sbuf_tensor(f"mv3_{i}", [128, 4], F32)) for i in range(4)]
        ring = Ring(nc, ls, "w3_", 4, [128, 8, 512], BF16)
        ps = [ls.enter_context(nc.psum_tensor(f"ps3_{i}", [128, 512], F32)) for i in range(8)]
        P.op("sp", lambda e: e.dma_start(out=lng[:], in_=io["ln_g"][0:1, :].broadcast_to([128, 4096])),
             writes=["lng"], dma="c3_1")
        P.op("sp", lambda e: e.dma_start(out=lnb[:], in_=io["ln_b"][0:1, :].broadcast_to([128, 4096])),
             writes=["lnb"], dma="c3_2")
        for blk in range(nblk):
            t0 = blk * 512
            for q in range(4):
                P.op("sp", lambda e, q=q, t0=t0: e.dma_start(
                    out=oT[:, q * 8:(q + 1) * 8, :], in_=dram_rows(scr["oT"], q * 8, 8, t0, 512)),
                    writes=[("oT", q)], dma="oT_%d" % q)
            for i in range(4):
                P.op("pool", lambda e, i=i, t0=t0: e.dma_start(out=r[i][:], in_=io["x"][t0 + i * 128:t0 + (i + 1) * 128, :]),
                     writes=[("r", i, c) for c in range(8)], dma="x3_%d" % i)
            for cg in range(8):
                for kq in range(4):
                    s, wt, wk = ring.next()
                    P.op("sp", lambda e, wt=wt, kq=kq, cg=cg: e.dma_start(
                        out=wt[:], in_=dram_rows(scr["wout"], kq * 8, 8, cg * 512, 512)),
                        writes=[wk], dma="w3_%d" % s)
                    for i in range(4):
                        b = (cg % 2) * 4 + i
                        for k8 in range(8):
                            kc = kq * 8 + k8
                            P.op("pe", lambda e, b=b, kc=kc, i=i, wt=wt, k8=k8: e.matmul(
                                ps[b][:], lhsT=oT[:, kc, i * 128:(i + 1) * 128], rhs=wt[:, k8, :],
                                start=(kc == 0), stop=(kc == 31)),
                                reads=[wk, ("oT", kq)], writes=[("ps", b)])
                for i in range(4):
                    b = (cg % 2) * 4 + i
                    sl = slice(cg * 512, (cg + 1) * 512)
                    P.op("dve", lambda e, i=i, b=b, sl=sl: e.scalar_tensor_tensor(
                        out=r[i][:, sl], in0=r[i][:, sl], scalar=ALPHA, in1=ps[b][:], op0=ALU.mult, op1=ALU.add),
                        reads=[("ps", b), ("r", i, cg)], writes=[("r", i, cg)])
                    P.op("dve", lambda e, i=i, cg=cg, sl=sl: e.bn_stats(out=stats[i][:, cg, :], in_=r[i][:, sl]),
                         reads=[("r", i, cg)], writes=[("st", i)])
            for i in range(4):
                rk = [("r", i, c) for c in range(8)]
                P.op("dve", lambda e, i=i: e.bn_aggr(out=mv[i][:, 0:2], in_=stats[i][:].rearrange("p a b -> p (a b)")),
                     reads=[("st", i)], writes=[("mv", i)])
                P.op("dve", lambda e, i=i: e.tensor_scalar(out=mv[i][:, 2:3], in0=mv[i][:, 1:2], scalar1=EPS,
                                                           scalar2=None, op0=ALU.add),
                     reads=[("mv", i)], writes=[("mv", i)])
                P.op("act", lambda e, i=i: e.activation(out=mv[i][:, 2:3], in_=mv[i][:, 2:3], func=AF.Sqrt),
                     reads=[("mv", i)], writes=[("mv", i)])
                P.op("dve", lambda e, i=i: e.reciprocal(out=mv[i][:, 2:3], in_=mv[i][:, 2:3]),
                     reads=[("mv", i)], writes=[("mv", i)])
                P.op("dve", lambda e, i=i: e.tensor_scalar(out=mv[i][:, 3:4], in0=mv[i][:, 0:1], scalar1=mv[i][:, 2:3],
                                                           scalar2=-1.0, op0=ALU.mult, op1=ALU.mult),
                     reads=[("mv", i)], writes=[("mv", i)])
                P.op("act", lambda e, i=i: e.activation(out=r[i][:], in_=r[i][:], func=AF.Identity,
                                                        bias=mv[i][:, 3:4], scale=mv[i][:, 2:3]),
                     reads=[("mv", i)] + rk, writes=rk)
                P.op("pool", lambda e, i=i: e.tensor_tensor(out=r[i][:], in0=r[i][:], in1=lng[:], op=ALU.mult),
                     reads=rk + ["lng"], writes=rk)
                P.op("dve", lambda e, i=i: e.tensor_tensor(out=r[i][:], in0=r[i][:], in1=lnb[:], op=ALU.add),
                     reads=rk + ["lnb"], writes=rk)
                P.op("pool", lambda e, i=i, t0=t0: e.dma_start(out=io["y"][t0 + i * 128:t0 + (i + 1) * 128, :], in_=r[i][:]),
                     reads=rk, writes=[("y", blk, i)], dma="y3_%d" % i)
        P.emit(ls, "c")


def build(L, stages=("p0", "s1", "s2", "s3"), dbg=False, Lp=0):
    nc = bass.Bass("TRN2", target_bir_lowering=False)
    io = {}

    def inp(name, shape):
        io[name] = nc.dram_tensor(name, shape, F32, kind="ExternalInput").ap()
    inp("x", [L, D]); inp("c", [32, 128]); inp("w_ada", [D, 3 * D]); inp("b_ada", [96, 128])
    if Lp:
        inp("xp", [Lp, D]); inp("flag", [128, 1])
    inp("w_in", [D, DIN]); inp("w_gate", [16, 1024]); inp("b_gate", [1, 1024]); inp("gnorm", [1, 512])
    inp("lam_re", [64, 128]); inp("lam_im", [64, 128]); inp("log_dt", [64, 2])
    inp("b_re", [128, 64, 16]); inp("b_im", [128, 64, 16]); inp("c_re", [128, 16, 64]); inp("c_im", [128, 16, 64])
    inp("s5_d", [128, 16]); inp("w_glu", [2048, 2048]); inp("b_glu", [16, 128])
    inp("w_out", [D, D]); inp("ln_g", [1, D]); inp("ln_b", [1, D])
    io["y"] = nc.dram_tensor("y", [L, D], F32, kind="ExternalOutput").ap()
    scr = {}
    okind = ("ExternalInput" if ("s1" not in stages and "s2" not in stages and "s2p" not in stages) else "ExternalOutput") if dbg else "Internal"
    scr["oT"] = nc.dram_tensor("oT_scr", [D, L], BF16, kind=okind).ap()
    scr["win"] = nc.dram_tensor("win_scr", [D, DIN], BF16).ap()
    scr["wglu"] = nc.dram_tensor("wglu_scr", [2048, 2048], BF16).ap()
    scr["wout"] = nc.dram_tensor("wout_scr", [D, D], BF16).ap()
    scr["gate"] = nc.dram_tensor("gate_scr", [32, 128], F32).ap()
    scr["hT"] = nc.dram_tensor("hT_scr", [D, L + Lp], BF16).ap()
    scr["mint"] = nc.dram_tensor("mint_scr", [128, 128, 128], BF16).ap()
    scr["zt"] = nc.dram_tensor("zt_scr", [64, 2, 128, 128], BF16).ap()
    scr["ff"] = nc.dram_tensor("ff_scr", [64, 2, 128, 128], BF16).ap()
    with ExitStack() as st:
        pers = {}
        pers["sh"] = st.enter_context(nc.sbuf_tensor("sh_c", [128, 32], F32))
        pers["s1p"] = st.enter_context(nc.sbuf_tensor("s1p_c", [128, 32], F32))
        pers["flag"] = st.enter_context(nc.sbuf_tensor("flagt", [128, 1], F32))
        pers["A8c"] = st.enter_context(nc.sbuf_tensor("A8c", [128, 2, 64], F32))
        pers["A8n"] = st.enter_context(nc.sbuf_tensor("A8n", [128, 2, 64], F32))
        pers["bgluc"] = st.enter_context(nc.sbuf_tensor("bgluc", [128, 16], F32))
        if "p0" in stages:
            phase0(nc, st, io, scr, pers, L, Lp)
        if "s1" in stages:
            sweep1(nc, st, io, scr, pers, L, Lp)
        if "s2" in stages or "s2p" in stages:
            s5_prologue(nc, st, io, scr, pers)
        if "s2" in stages:
            sweep2(nc, st, io, scr, pers, L, Lp)
        if "s3" in stages:
            sweep3(nc, st, io, scr, L)
    return nc


def make_in_map(inputs, b, L, s=0, Lp=0):
    f = lambda a: np.ascontiguousarray(a, dtype=np.float32)
    i = inputs
    m = {
        "x": f(i["x"][b, s * L:(s + 1) * L]), "c": f(i["c"][b].reshape(32, 128)), "w_ada": f(i["w_ada"][0]),
        "b_ada": f(i["b_ada"][0].reshape(96, 128)), "w_in": f(i["w_in"][0]), "w_gate": f(i["w_gla_gate"][0]),
        "b_gate": f(i["b_gla_gate"][0].reshape(1, 1024)), "gnorm": f(i["gla_norm_g"][0].reshape(1, 512)),
        "lam_re": f(i["s5_lambda_re"][0].reshape(64, 128)), "lam_im": f(i["s5_lambda_im"][0].reshape(64, 128)),
        "log_dt": f(i["s5_log_dt"][0].reshape(64, 2)), "b_re": f(i["s5_b_re"][0]), "b_im": f(i["s5_b_im"][0]),
        "c_re": f(i["s5_c_re"][0]), "c_im": f(i["s5_c_im"][0]), "s5_d": f(i["s5_d"][0].reshape(128, 16)),
        "w_glu": f(i["w_glu"][0]), "b_glu": f(i["b_glu"][0].reshape(16, 128)), "w_out": f(i["w_out"][0]),
        "ln_g": f(i["ln_g"][0].reshape(1, D)), "ln_b": f(i["ln_b"][0].reshape(1, D)),
    }
    if Lp:
        m["xp"] = f(i["x"][b, 0:Lp])
        m["flag"] = np.full((128, 1), float(s), dtype=np.float32)
    return m


def kernel(**inputs):
    B, S = inputs["x"].shape[0], inputs["x"].shape[1]
    L = S // 2
    nc = build(L, Lp=L)
    in_maps = [make_in_map(inputs, b, L, s, L) for b in range(B) for s in range(2)]
    res = run_bass_kernel_spmd(nc, in_maps, core_ids=list(range(2 * B)))
    out = np.empty((B, S, D), dtype=np.float32)
    for b in range(B):
        for s in range(2):
            out[b, s * L:(s + 1) * L] = np.asarray(res.results[2 * b + s]["y"], dtype=np.float32)
    return out
```
